# Optimizing a Trainium2 kernel written in Bass

```python
import math
import jax, jax.numpy as jnp
from jax import lax
import numpy as np

D_MODEL = 1024
BATCH = 16
SEQ = 256
DEPTH = 4
DEC_BATCH = 8
DEC_SEQ = 1024
PAST_LEN = 256

GRID_W = 64
N_EVEN = (DEPTH + 1) // 2
N_ODD = DEPTH // 2
D_A = D_MODEL
HEADDIM = 64
N_HEADS_A = D_A // HEADDIM
N_GROUPS_A = 4
D_STATE = 128
CONV_A = 3
CHUNK = 128
XBC_A = D_A + 2 * N_GROUPS_A * D_STATE
D_B = D_MODEL
CONV_B = 3
D_C = 2 * D_MODEL
POOL_WINDOWS = (2, 4, 8, 16)
N_POOL = len(POOL_WINDOWS)
D_POOL_GRP = D_C // N_POOL
SPLIT_EVEN = (D_A, D_A + XBC_A, D_A + XBC_A + N_HEADS_A, D_A + XBC_A + 2 * N_HEADS_A,
              D_A + XBC_A + 2 * N_HEADS_A + D_B, D_A + XBC_A + 2 * N_HEADS_A + 2 * D_B,
              D_A + XBC_A + 2 * N_HEADS_A + 3 * D_B)
IN_EVEN = D_A + XBC_A + 2 * N_HEADS_A + 4 * D_B
IN_ODD = 2 * D_C
ALPHA = (2 * DEPTH) ** 0.25
BETA = (8 * DEPTH) ** -0.25
LN_EPS = 1e-5
RMS_EPS = 1e-5
POS_BASE = 10000.0

kernel_name = 'hybrid_ssd_shortconv_pool_flow_step'

f32 = jnp.float32


def layer_norm(x, g, b):
    xf = x.astype(f32)
    mu = jnp.mean(xf, axis=-1, keepdims=True)
    var = jnp.mean(jnp.square(xf - mu), axis=-1, keepdims=True)
    return ((xf - mu) * lax.rsqrt(var + LN_EPS) * g.astype(f32) + b.astype(f32)).astype(x.dtype)


def rms_norm(x, g):
    xf = x.astype(f32)
    return xf * lax.rsqrt(jnp.mean(jnp.square(xf), axis=-1, keepdims=True) + RMS_EPS) * g.astype(f32)


def dwconv_centred(x, w):
    K = w.shape[0]
    p = K // 2
    L = x.shape[1]
    xp = jnp.pad(x, ((0, 0), (p, p), (0, 0)))
    return sum(xp[:, k:k + L] * w[k] for k in range(K))


def pos_embed_2d(L, dim):
    rows = L // GRID_W
    t = jnp.arange(rows * GRID_W)
    r = (t // GRID_W).astype(f32)
    col = (t % GRID_W).astype(f32)
    nf = dim // 4
    omega = 1.0 / (POS_BASE ** (jnp.arange(nf, dtype=f32) / nf))

    def emb(pos):
        a = pos[:, None] * omega[None, :]
        return jnp.concatenate([jnp.sin(a), jnp.cos(a)], axis=-1)

    return jnp.concatenate([emb(r), emb(col)], axis=-1)


def ssd_chunked(x, dt, A, bm, cm, h0):
    b, L, H, P = x.shape
    G, N = bm.shape[2], bm.shape[3]
    R = H // G
    Q = CHUNK
    nc = L // Q
    x = x.reshape(b, nc, Q, G, R, P)
    dt = dt.reshape(b, nc, Q, G, R)
    bm = bm.reshape(b, nc, Q, G, N)
    cm = cm.reshape(b, nc, Q, G, N)
    a_cum = jnp.cumsum(dt * A.reshape(G, R), axis=2)
    lower = jnp.tril(jnp.ones((Q, Q), dtype=bool))[None, None, :, :, None, None]
    seg = a_cum[:, :, :, None] - a_cum[:, :, None, :]
    decay = jnp.exp(jnp.where(lower, seg, -jnp.inf))
    xdt = x * dt[..., None]
    cb = jnp.einsum('bcign,bcjgn->bcijg', cm, bm)
    y_diag = jnp.einsum('bcijgr,bcjgrp->bcigrp', cb[..., None] * decay, xdt)
    decay_end = jnp.exp(a_cum[:, :, -1:] - a_cum)
    chunk_states = jnp.einsum('bcjgn,bcjgrp->bcgrpn', bm, decay_end[..., None] * xdt)
    chunk_decay = jnp.exp(a_cum[:, :, -1])

    def step(h, inp):
        s, d = inp
        return h * d[..., None, None] + s, h

    h_last, h_starts = lax.scan(step, h0.astype(f32).reshape(b, G, R, P, N),
                                (jnp.moveaxis(chunk_states, 1, 0), jnp.moveaxis(chunk_decay, 1, 0)))
    h_starts = jnp.moveaxis(h_starts, 0, 1)
    y_off = jnp.einsum('bcign,bcgrpn->bcigrp', cm, h_starts) * jnp.exp(a_cum)[..., None]
    return (y_diag + y_off).reshape(b, L, H, P), h_last.reshape(b, H, P, N)


def ssd_mixer(z, xbc, dtf, dtb, conv_w, conv_b, a_log, dt_bias, d_skip, norm_g, h0_f, h0_b):
    Bsz, L, _ = z.shape
    xbc = jax.nn.silu(dwconv_centred(xbc, conv_w) + conv_b)
    xs, bm, cm = jnp.split(xbc, [D_A, D_A + N_GROUPS_A * D_STATE], axis=-1)
    xs = xs.reshape(Bsz, L, N_HEADS_A, HEADDIM).astype(f32)
    bm = bm.reshape(Bsz, L, N_GROUPS_A, D_STATE).astype(f32)
    cm = cm.reshape(Bsz, L, N_GROUPS_A, D_STATE).astype(f32)
    A = -jnp.exp(a_log.astype(f32))
    dt = jax.nn.softplus(jnp.stack([dtf, dtb], 0).astype(f32) + dt_bias.astype(f32)[:, None, None, :])
    y_f, h_f = ssd_chunked(xs, dt[0], A[0], bm, cm, h0_f)
    y_b, h_b = ssd_chunked(xs[:, ::-1], dt[1][:, ::-1], A[1], bm[:, ::-1], cm[:, ::-1], h0_b)
    y = y_f + y_b[:, ::-1] + d_skip.astype(f32)[:, None] * xs
    y = y.reshape(Bsz, L, D_A) * jax.nn.silu(z.astype(f32))
    return rms_norm(y, norm_g).astype(z.dtype), h_f, h_b


def multiscale_pool(v):
    L = v.shape[1]
    vf = v.astype(f32)
    cs = jnp.pad(jnp.cumsum(vf, axis=1), ((0, 0), (1, 0), (0, 0)))
    t = jnp.arange(L)
    outs = []
    for k, w in enumerate(POOL_WINDOWS):
        lo = jnp.clip(t - w // 2, 0, L)
        hi = jnp.clip(t + w - w // 2, 0, L)
        sl = slice(k * D_POOL_GRP, (k + 1) * D_POOL_GRP)
        seg = cs[:, :, sl]
        cnt = (hi - lo).astype(f32)[None, :, None]
        outs.append((seg[:, hi] - seg[:, lo]) / cnt - vf[:, :, sl])
    return jnp.concatenate(outs, axis=-1).astype(v.dtype)


def trunk(x, cvec, h0, w_mod, b_mod, ln_g, ln_b, w_in_even, conv_a_w, conv_a_b, a_log, dt_bias,
          d_skip, norm_a_g, conv_b_w, w_out_even, w_in_odd, w_pool, pool_scale, w_out_odd):
    Bsz, L, _ = x.shape
    states = []
    for i in range(DEPTH):
        j = i // 2
        mod = jnp.dot(jax.nn.silu(cvec), w_mod[i]) + b_mod[i]
        shift, scale, gate = jnp.split(mod[:, None, :], 3, axis=-1)
        u = x * (1 + scale) + shift
        if i % 2 == 0:
            proj = u @ w_in_even[j]
            z, xbc, dtf, dtb, g, bg, cg, h_in = jnp.split(proj, SPLIT_EVEN, axis=-1)
            y_a, h_f, h_b = ssd_mixer(z, xbc, dtf, dtb, conv_a_w[j], conv_a_b[j], a_log[j], dt_bias[j],
                                      d_skip[j], norm_a_g[j], h0[:, j, 0], h0[:, j, 1])
            y_b = bg * dwconv_centred(cg * h_in, conv_b_w[j]) * jax.nn.silu(g)
            y = jnp.concatenate([y_a, y_b], axis=-1) @ w_out_even[j]
            states.append(jnp.stack([h_f, h_b], axis=1).astype(h0.dtype))
        else:
            v, g = jnp.split(u @ w_in_odd[j], 2, axis=-1)
            p = multiscale_pool(v).reshape(Bsz, L, N_POOL, D_POOL_GRP)
            p = jnp.einsum('blkc,kcd->blkd', p, w_pool[j]).reshape(Bsz, L, D_C) * pool_scale[j]
            y = (p * jax.nn.silu(g)) @ w_out_odd[j]
        x = layer_norm(ALPHA * x + (1 + gate) * y, ln_g[i], ln_b[i])
    return x, jnp.stack(states, axis=1)


def setup_inputs(seed: int = 0) -> dict:
    key = jax.random.key(seed)
    ks = jax.random.split(key, 24)
    D = D_MODEL
    nrm = jax.random.normal
    dt0 = jnp.exp(jax.random.uniform(ks[11], (N_EVEN, 2, N_HEADS_A), minval=math.log(1e-3), maxval=math.log(1e-1)))
    return {
        'x_prompt': nrm(ks[0], (BATCH, SEQ, D), f32),
        'x_sample': nrm(ks[1], (DEC_BATCH, DEC_SEQ, D), f32),
        'state_ssd': 0.1 * nrm(ks[2], (DEC_BATCH, N_EVEN, 2, N_HEADS_A, HEADDIM, D_STATE), f32),
        'c': nrm(ks[3], (DEC_BATCH, D), f32),
        'c_ctx': nrm(ks[4], (D,), f32),
        'w_mod': 0.5 * D ** -0.5 * nrm(ks[5], (DEPTH, D, 3 * D), f32),
        'b_mod': 0.01 * nrm(ks[6], (DEPTH, 3 * D), f32),
        'ln_g': 1.0 + 0.02 * nrm(ks[7], (DEPTH, D), f32),
        'ln_b': 0.02 * nrm(ks[8], (DEPTH, D), f32),
        'w_in_even': D ** -0.5 * nrm(ks[9], (N_EVEN, D, IN_EVEN), f32),
        'conv_a_w': CONV_A ** -0.5 * nrm(ks[10], (N_EVEN, CONV_A, XBC_A), f32),
        'conv_a_b': 0.02 * nrm(ks[12], (N_EVEN, XBC_A), f32),
        'a_log': jnp.log(jax.random.uniform(ks[13], (N_EVEN, 2, N_HEADS_A), minval=1.0, maxval=16.0)),
        'dt_bias': dt0 + jnp.log(-jnp.expm1(-dt0)),
        'd_skip': 1.0 + 0.1 * nrm(ks[14], (N_EVEN, N_HEADS_A), f32),
        'norm_a_g': 1.0 + 0.02 * nrm(ks[15], (N_EVEN, D_A), f32),
        'conv_b_w': CONV_B ** -0.5 * nrm(ks[16], (N_EVEN, CONV_B, D_B), f32),
        'w_out_even': BETA * (D_A + D_B) ** -0.5 * nrm(ks[17], (N_EVEN, D_A + D_B, D), f32),
        'w_in_odd': D ** -0.5 * nrm(ks[18], (N_ODD, D, IN_ODD), f32),
        'w_pool': D_POOL_GRP ** -0.5 * nrm(ks[19], (N_ODD, N_POOL, D_POOL_GRP, D_POOL_GRP), f32),
        'pool_scale': 1.0 + 0.1 * nrm(ks[20], (N_ODD, D_C), f32),
        'w_out_odd': BETA * D_C ** -0.5 * nrm(ks[21], (N_ODD, D_C, D), f32),
    }


def reference(x_prompt, x_sample, state_ssd, c, c_ctx, w_mod, b_mod, ln_g, ln_b, w_in_even, conv_a_w,
              conv_a_b, a_log, dt_bias, d_skip, norm_a_g, conv_b_w, w_out_even, w_in_odd, w_pool,
              pool_scale, w_out_odd):
    h0_ctx = jnp.zeros((x_prompt.shape[0], N_EVEN, 2, N_HEADS_A, HEADDIM, D_STATE), dtype=x_prompt.dtype)
    y_prompt, new_state_ssd = trunk(x_prompt, c_ctx[None, :], h0_ctx, w_mod, b_mod, ln_g, ln_b, w_in_even,
                                    conv_a_w, conv_a_b, a_log, dt_bias, d_skip, norm_a_g, conv_b_w,
                                    w_out_even, w_in_odd, w_pool, pool_scale, w_out_odd)
    L = x_sample.shape[1]
    xs = x_sample + pos_embed_2d(L, x_sample.shape[2]).astype(x_sample.dtype)[None]
    y_sample, _ = trunk(xs, c, state_ssd, w_mod, b_mod, ln_g, ln_b, w_in_even, conv_a_w, conv_a_b,
                        a_log, dt_bias, d_skip, norm_a_g, conv_b_w, w_out_even, w_in_odd, w_pool,
                        pool_scale, w_out_odd)
    return (y_prompt, y_sample, new_state_ssd)
```

```python
import math
from contextlib import ExitStack

import numpy as np
import concourse.bass as bass
import concourse.mybir as mybir
from concourse.bass_utils import run_bass_kernel_spmd

F32 = mybir.dt.float32
BF16 = mybir.dt.bfloat16
I32 = mybir.dt.int32
AF = mybir.ActivationFunctionType
ALU = mybir.AluOpType

NCORES = 8
D = 1024
T = 1536
KC = 8
NT = 3
SEQS = [(0, 256), (256, 256), (512, 1024)]
NTC = 12
ALPHA = (2 * 4) ** 0.25
LN_EPS = 1e-5
RMS_EPS = 1e-5
IN_EVEN = 7200
PW = 1600
POFF = [16, 280, 544]
WINS = (2, 4, 8, 16)
NSLOT = 2
SLOT = 4096

DEBUG_STOP = None


class Tok:
    __slots__ = ("w", "r", "name")
    ALL = []

    def __init__(self, name=""):
        self.w = None
        self.r = {}
        self.name = name
        Tok.ALL.append(self)


class Sched:
    ENG = ("pe", "dve", "act", "pool")

    def __init__(self, nc, es):
        self.nc = nc
        self.es = es
        self.engs = {"pe": nc.tensor, "dve": nc.vector, "act": nc.scalar, "pool": nc.gpsimd, "sp": nc.sync}
        self.sem = {k: es.enter_context(nc.semaphore("sem_" + k)) for k in self.ENG}
        self.cnt = {k: 0 for k in self.ENG}
        self.waited = {k: {} for k in list(self.ENG) + ["sp"]}
        self.dma_sems = []
        self.dry = False
        self.plan = False
        self.needed = {k: set() for k in self.ENG}
        self.semval = {k: 0 for k in self.ENG}
        self.valof = {k: {} for k in self.ENG}
        self.nwaits = 0

    def reset(self):
        for t in Tok.ALL:
            t.w = None
            t.r = {}
        self.cnt = {k: 0 for k in self.ENG}
        self.waited = {k: {} for k in list(self.ENG) + ["sp"]}
        self.semval = {k: 0 for k in self.ENG}
        self.valof = {k: {} for k in self.ENG}
        self.nwaits = 0
        for ent in self.dma_sems:
            ent[1] = 0

    def _wait(self, eng, dep):
        if dep is None:
            return
        src, val = dep
        if isinstance(src, str):
            if src == eng and eng == "pe":
                return
            key = src
            sem = self.sem[src]
        else:
            key = ("dma", src.num)
            sem = src
        if self.waited[eng].get(key, 0) >= val:
            return
        self.nwaits += 1
        self.waited[eng][key] = val
        if self.plan:
            if isinstance(src, str):
                self.needed[src].add(val)
            return
        if isinstance(src, str):
            self.engs[eng].wait_ge(sem, self.valof[src][val])
        else:
            self.engs[eng].wait_ge(sem, val)

    def op(self, eng, fn, reads=(), writes=()):
        if self.dry:
            return
        for t in reads:
            self._wait(eng, t.w)
        for t in writes:
            self._wait(eng, t.w)
            for src, val in t.r.items():
                if src == eng and eng == "pe":
                    continue
                self._wait(eng, (src, val))
        self.cnt[eng] += 1
        c = self.cnt[eng]
        if not self.plan:
            ins = fn(self.engs[eng])
            if c in self.needed[eng]:
                ins.then_inc(self.sem[eng], 1)
                self.semval[eng] += 1
                self.valof[eng][c] = self.semval[eng]
        for t in reads:
            t.r[eng] = c
        for t in writes:
            t.w = (eng, c)
            t.r = {}

    def new_dma_sem(self):
        s = self.es.enter_context(self.nc.semaphore("dsem%d" % len(self.dma_sems)))
        self.dma_sems.append([s, 0])
        return len(self.dma_sems) - 1

    def dma(self, q, pairs, reads=(), writes=(), semi=None, **kw):
        if self.dry:
            return
        for t in reads:
            self._wait(q, t.w)
        for t in writes:
            self._wait(q, t.w)
            for src_, val in t.r.items():
                self._wait(q, (src_, val))
        ent = self.dma_sems[semi]
        for (o, i_) in pairs:
            ent[1] += 16
            if not self.plan:
                ins = self.engs[q].dma_start(out=o, in_=i_, **kw)
                ins.then_inc(ent[0], 16)
        dep = (ent[0], ent[1])
        for t in reads:
            t.r[ent[0]] = ent[1]
        for t in writes:
            t.w = dep
            t.r = {}


class Ring:
    def __init__(self, S, buf):
        self.S = S
        self.buf = buf
        self.toks = [Tok("ring%d" % i) for i in range(NSLOT)]
        self.sems = [S.new_dma_sem() for _ in range(NSLOT)]
        self.plan = []
        self.issued = 0
        self.cur = 0

    def slot_ap(self, s):
        return self.buf[:, s, :]

    def _issue(self):
        p = self.issued
        s = p % NSLOT
        pairs = self.plan[p](self.slot_ap(s))
        self.S.dma("pool", pairs, writes=[self.toks[s]], semi=self.sems[s])
        self.issued += 1

    def next(self, pairs_fn, prefetch=True):
        if self.S.dry:
            self.plan.append(pairs_fn)
            return self.slot_ap(0), self.toks[0]
        p = self.cur
        while self.issued < min(len(self.plan), (p + NSLOT) if prefetch else (p + 1)):
            self._issue()
        s = p % NSLOT
        self.cur += 1
        return self.slot_ap(s), self.toks[s]

    def release(self):
        pass

    def kick(self):
        if self.S.dry:
            return
        while self.issued < min(len(self.plan), NSLOT):
            self._issue()


def build_program():
    nc = bass.Bass("TRN2", target_bir_lowering=False)

    def din(name, shape):
        return nc.dram_tensor(name, list(shape), F32, kind="ExternalInput").ap()

    xin = din("xin", [D, T])
    state = din("state", [512, 1024])
    cvec = din("cvec", [2, D])
    w_mod = din("w_mod", [4, D, 3072])
    b_mod = din("b_mod", [4, 3072])
    ln_g = din("ln_g", [4, D])
    ln_b = din("ln_b", [4, D])
    w_in_even = din("w_in_even", [2, D, IN_EVEN])
    conv_a_w = din("conv_a_w", [2, 3, 2048])
    conv_a_b = din("conv_a_b", [2, 2048])
    a_log = din("a_log", [2, 2, 16])
    dt_bias = din("dt_bias", [2, 2, 16])
    d_skip = din("d_skip", [2, 16])
    norm_a_g = din("norm_a_g", [2, D])
    conv_b_w = din("conv_b_w", [2, 3, D])
    w_out_even = din("w_out_even", [2, 2048, D])
    w_in_odd = din("w_in_odd", [2, D, 4096])
    w_pool = din("w_pool", [2, 4, 512, 512])
    pool_scale = din("pool_scale", [2, 2048])
    w_out_odd = din("w_out_odd", [2, 2048, D])
    yout = nc.dram_tensor("yout", [D, T], F32, kind="ExternalOutput").ap()
    nstate = nc.dram_tensor("nstate", [1024, 1024], F32, kind="ExternalOutput").ap()

    with ExitStack() as es:
        S = Sched(nc, es)

        def sb(name, shape, dt):
            return es.enter_context(nc.sbuf_tensor(name, list(shape), dt))

        x = sb("x", [128, KC, T], F32)
        xtok = [Tok("x%d" % c) for c in range(KC)]
        u = sb("u", [128, KC, T], BF16)
        utok = [Tok("u%d" % c) for c in range(KC)]
        yb = sb("ybuf", [128, KC, T], BF16)
        ytok = [Tok("y%d" % c) for c in range(KC)]
        ringbuf = sb("ring", [128, NSLOT, SLOT], BF16)
        ring = Ring(S, ringbuf)
        Fb = [sb("F%d" % i, [128, PW], F32) for i in range(3)]
        Ftok = [Tok("F%d" % i) for i in range(3)]
        Hb = [sb("H%d" % i, [128, T], BF16) for i in range(4)]
        Hb45 = sb("H45", [128, 2, T], BF16)
        Hb += [Hb45[:, 0, :], Hb45[:, 1, :]]
        Hb += [sb("H%d" % i, [128, T], BF16) for i in (6, 7)]
        Htok = [Tok("H%d" % i) for i in range(8)]
        pv = sb("pv", [128, 384], F32)
        pvtok = Tok("pv")
        pvA = sb("pvA", [128, 64], F32)
        modv = sb("modv", [128, 4, 24, 2], F32)
        modtok = Tok("modv")
        ident = sb("ident", [128, 128], F32)
        identb = sb("identb", [128, 128], BF16)
        onesb = sb("onesb", [128, 128], BF16)
        onesm = sb("onesm", [128, 128], BF16)
        onesf = sb("onesf", [128, 128], F32)
        Lfb = sb("Lfb", [128, 2, 128], BF16)
        Lf = Lfb[:, 0, :]
        Lb = Lfb[:, 1, :]
        Uf = sb("Uf", [128, 128], BF16)
        Ub = sb("Ub", [128, 128], BF16)
        ctok = Tok("consts")
        mtmp = sb("mtmp", [128, 128], F32)
        mtmptok = Tok("mtmp")
        A_bc = sb("A_bc", [128, 64], F32)
        D_bc = sb("D_bc", [128, 32], F32)
        dtb_bc = sb("dtb_bc", [128, 64], F32)
        abtok = Tok("abc")
        posc = sb("posc", [128, 4, 64], F32)
        postok = Tok("pos")
        invtab = sb("invtab", [128, 4, 16], F32)
        invtok = Tok("invtab")
        scb = sb("scb", [128, 8, 2], BF16)
        scbtok = Tok("scb")
        dt_tok = sb("dt_tok", [128, 2, NTC, 16], F32)
        a_bf = sb("a_bf", [128, 2, NTC, 16], BF16)
        wde = sb("wde", [128, 2, NTC, 16], F32)
        cd_bc = sb("cd_bc", [128, 2, NTC, 16], F32)
        dttok = Tok("dtstuff")
        Dmat = sb("Dmat", [128, 4, 128], BF16)
        dmtok = Tok("Dmat")
        stin_sem = [S.new_dma_sem(), S.new_dma_sem()]
        x_tok = sb("x_tok", [128, NTC, 256], BF16)
        xtoktok = Tok("x_tok")
        B_tok = sb("B_tok", [128, NTC, 128], BF16)
        btoktok = Tok("B_tok")
        hbuf = sb("hbuf", [128, 2048], F32)
        hst = hbuf[:].bitcast(BF16).rearrange("p (d t q) -> p d t q", d=2, t=8)
        invc_alias = hbuf[:, 0:PW]
        hsttok = Tok("hst")
        hcur = sb("hcur", [128, 256], F32)
        hcurtok = Tok("hcur")
        htmp = sb("htmp", [128, 256], F32)
        htmptok = Tok("htmp")
        NB = 2
        Rb2 = sb("Rb2", [128, 2, 4, 128], BF16)
        Rtok = Tok()
        eA2 = sb("eA2", [128, 2, 4, 128], BF16)
        eAtok = [Tok() for _ in range(2)]
        eE2 = sb("eE2", [128, 2, 4, 128], BF16)
        eEtok = [Tok() for _ in range(2)]
        MT2 = [sb("MT2%d" % q, [128, 2, 4, 128], BF16) for q in range(2)]
        MTtok = [Tok() for q in range(2)]
        Cs2 = [sb("Cs2%d" % q, [128, 2, 4, 128], BF16) for q in range(2)]
        Cstok = [Tok() for q in range(2)]
        xdt2 = [sb("xdt2%d" % q, [128, 2, 4, 64], BF16) for q in range(2)]
        xdttok = [Tok() for q in range(2)]
        h7f = Hb[7][:, :].bitcast(F32)
        hcur2 = h7f[:, 0:256]
        htmp2 = h7f[:, 256:512]
        xdd = [[sb("xdd", [128, 4, 64], BF16)[:], sb("xddb", [128, 4, 64], BF16)[:]],
               [Hb[7][:, 1024:1280].rearrange("p (h q) -> p h q", h=4), Hb[7][:, 1280:1536].rearrange("p (h q) -> p h q", h=4)]]
        xddtok = [[Tok("xdd00"), Tok("xdd01")], [Tok("xdd10"), Tok("xdd11")]]
        hcur2tok = Tok("hcur2")
        htmp2tok = Tok("htmp2")
        CBm2 = sb("CBm2", [128, 2, 128], BF16)
        CBmtok = Tok()
        sto = [sb("sto%d" % i, [128, 2, 128], F32) for i in range(2)]
        stotok = [Tok() for _ in range(2)]
        sto_sem = [S.new_dma_sem() for _ in range(2)]
        xs = [Fb[1][:, 0:D], Fb[2][:, 0:D]]
        xstok = [Ftok[1], Ftok[2]]
        xs_sem = [S.new_dma_sem() for _ in range(2)]
        pst = [Fb[2][:, i * 128:(i + 1) * 128] for i in range(3)]
        cst = Fb[2][0:16, 384:512]
        psttok = Ftok[2]
        pst_sem = S.new_dma_sem()
        misc_sem = S.new_dma_sem()

        pA = es.enter_context(nc.psum_tensor("pA", [128, 1536], F32))
        pB = es.enter_context(nc.psum_tensor("pB", [128, 1536], F32))
        p6 = es.enter_context(nc.psum_tensor("p6", [128, 512], F32))
        p7 = es.enter_context(nc.psum_tensor("p7", [128, 512], F32))
        bk = [Tok("bank%d" % i) for i in range(8)]
        PG = [(pA, bk[0:3]), (pB, bk[3:6])]
        pg_state = [0]
        trB = pB[:, 0:512].bitcast(BF16)
        trP = p6[:, :].bitcast(BF16)
        bank4 = pB[:, 512:1024]
        bank5 = pB[:, 1024:1536]

        def next_pg():
            g = PG[pg_state[0] % 2]
            pg_state[0] += 1
            return g

        def tt(eng, out, in0, in1, op, R, W):
            S.op(eng, lambda e: e.tensor_tensor(out, in0, in1, op), R, W)

        def ts(eng, out, in0, s1, s2, op0, op1, R, W):
            if s2 is None:
                S.op(eng, lambda e: e.tensor_scalar(out, in0, s1, None, op0), R, W)
            else:
                S.op(eng, lambda e: e.tensor_scalar(out, in0, s1, s2, op0, op1), R, W)

        def stt(eng, out, in0, scalar, in1, op0, op1, R, W):
            S.op(eng, lambda e: e.scalar_tensor_tensor(out=out, in0=in0, scalar=scalar, in1=in1, op0=op0, op1=op1), R, W)

        def act(out, in_, func, R, W, bias=None, scale=None):
            kw = {}
            if bias is not None:
                kw["bias"] = bias
            if scale is not None:
                kw["scale"] = scale
            S.op("act", lambda e: e.activation(out=out, in_=in_, func=func, **kw), R, W)

        def cp(eng, out, in_, R, W):
            if eng == "act":
                act(out, in_, AF.Copy, R, W)
            else:
                S.op(eng, lambda e: e.tensor_copy(out, in_), R, W)

        def mm(out, lhsT, rhs, start, stop, R, W, tp=None):
            if tp is None:
                S.op("pe", lambda e: e.matmul(out, lhsT, rhs, start=start, stop=stop), R, W)
            else:
                S.op("pe", lambda e: e.matmul(out, lhsT, rhs, start=start, stop=stop, tile_position=tp), R, W)

        def tr(out, in_, idn, R, W):
            S.op("pe", lambda e: e.transpose(out, in_, idn), R, W)

        def pvc(col):
            return pv[:, col:col + 1]

        def c_bmod(i, m): return i * 24 + m
        def c_lng(i, c): return 96 + i * 8 + c
        def c_caw(j, k, c): return 128 + j * 48 + k * 16 + c
        def c_lnb(i, c): return 128 + 96 + i * 8 + c
        def c_cab(j, c): return 256 + j * 16 + c
        def c_nag(j, c): return 256 + 32 + j * 8 + c
        def c_cbw(j, k, c): return 256 + 48 + j * 24 + k * 8 + c
        def c_psc(j, c): return 256 + 96 + j * 16 + c

        tiles = [(n * 512, 512) for n in range(NT)]

        def emit():
            mod_reset()
            ring.kick()
            def mask(dst, pattern, cm, cmp_):
                S.op("pool", lambda e: e.memset(mtmp[:], 1.0), (), [mtmptok])
                S.op("pool", lambda e: e.affine_select(out=mtmp[:], in_=mtmp[:], pattern=pattern, compare_op=cmp_,
                                                       fill=0.0, base=0, channel_multiplier=cm), (), [mtmptok])
                cp("dve", dst[:], mtmp[:], [mtmptok], [ctok])

            S.op("pool", lambda e: e.memset(ident[:], 0.0), (), [ctok])
            S.op("pool", lambda e: e.affine_select(out=ident[:], in_=ident[:], pattern=[[-1, 128]], compare_op=ALU.not_equal,
                                                   fill=1.0, base=0, channel_multiplier=1), (), [ctok])
            cp("dve", identb[:], ident[:], [ctok], [ctok])
            S.op("pool", lambda e: e.memset(onesb[:], 1.0), (), [ctok])
            S.op("pool", lambda e: e.memset(onesm[:], 1.0 / 1024), (), [ctok])
            S.op("pool", lambda e: e.memset(onesf[:], 1.0 / 1024), (), [ctok])
            mask(Lf, [[1, 128]], -1, ALU.is_ge)
            mask(Lb, [[-1, 128]], 1, ALU.is_ge)
            mask(Uf, [[-1, 128]], 1, ALU.is_gt)
            mask(Ub, [[1, 128]], -1, ALU.is_gt)

            def rows(ap2d):
                return ap2d
            loads = [
                (pst[0][0:96, :], b_mod.rearrange("i (m p) -> (i m) p", p=128)),
                (pst[0][96:128, :], ln_g.rearrange("i (c p) -> (i c) p", p=128)),
                (pst[1][0:96, :], conv_a_w.rearrange("j k (c p) -> (j k c) p", p=128)),
                (pst[1][96:128, :], ln_b.rearrange("i (c p) -> (i c) p", p=128)),
                (pst[2][0:32, :], conv_a_b.rearrange("j (c p) -> (j c) p", p=128)),
                (pst[2][32:48, :], norm_a_g.rearrange("j (c p) -> (j c) p", p=128)),
                (pst[2][48:96, :], conv_b_w.rearrange("j k (c p) -> (j k c) p", p=128)),
                (pst[2][96:128, :], pool_scale.rearrange("j (c p) -> (j c) p", p=128)),
                (cst[:, :], cvec.rearrange("v (k p) -> (v k) p", p=128)),
            ]
            S.dma("sp", loads, writes=[psttok], semi=pst_sem)
            S.dma("sp", [(A_bc[:], a_log.rearrange("j d h -> (j d h)").partition_broadcast(128)),
                         (D_bc[:], d_skip.rearrange("j h -> (j h)").partition_broadcast(128))],
                  writes=[abtok], semi=misc_sem)
            S.dma("sp", [(dtb_bc[:], dt_bias.rearrange("j d h -> (j d h)").partition_broadcast(128))],
                  writes=[abtok], semi=misc_sem)
            for i in range(3):
                tr(pA[:, i * 128:(i + 1) * 128], pst[i][:], ident[:], [psttok, ctok], [bk[0]])
            tr(pA[:, 384:400], cst[:], ident[0:16, 0:16], [psttok, ctok], [bk[0]])
            cp("dve", pv[:], pA[:, 0:384], [], [bk[0], pvtok])
            ts("dve", pvA[:, 0:32], pv[:, 96:128], ALPHA, None, ALU.mult, None, [], [pvtok])
            ts("dve", pvA[:, 32:64], pv[:, 224:256], ALPHA, None, ALU.mult, None, [], [pvtok])
            act(scb[:].rearrange("p k v -> p v k"), pA[:, 384:400].rearrange("p (v k) -> p v k", v=2), AF.Silu,
                [], [bk[0], scbtok])
            act(A_bc[:], A_bc[:], AF.Exp, [], [abtok])
            ts("dve", A_bc[:], A_bc[:], -1.0, None, ALU.mult, None, [], [abtok])

            kidx = Fb[0][:, 0:2]
            pvals = Fb[0][:, 64:128]
            S.op("pool", lambda e: e.iota(kidx, pattern=[[128, 2]], base=0, channel_multiplier=1,
                                          allow_small_or_imprecise_dtypes=True), (), [Ftok[0]])
            S.op("pool", lambda e: e.iota(pvals, pattern=[[1, 64]], base=0, channel_multiplier=0,
                                          allow_small_or_imprecise_dtypes=True), (), [Ftok[0]])
            omega = Fb[0][:, 2:4]
            act(omega, kidx, AF.Exp, [], [Ftok[0]], scale=-math.log(10000.0) / 256.0)
            ang = Fb[1][:, 0:256].rearrange("p (a b) -> p a b", a=4)
            for ch in range(4):
                cc = ch % 2
                ts("dve", ang[:, ch, :], pvals, omega[:, cc:cc + 1], 1.0 / (2 * math.pi), ALU.mult, ALU.mult,
                   [Ftok[0]], [Ftok[1]])
                if ch >= 2:
                    ts("dve", ang[:, ch, :], ang[:, ch, :], 0.25, None, ALU.add, None, [], [Ftok[1]])
            angi = Fb[1][:, 256:512].bitcast(I32)
            angf = Fb[1][:, 512:768]
            cp("dve", angi, Fb[1][:, 0:256], [], [Ftok[1]])
            cp("dve", angf, angi, [], [Ftok[1]])
            tt("dve", Fb[1][:, 0:256], Fb[1][:, 0:256], angf, ALU.subtract, [], [Ftok[1]])
            act(posc[:].rearrange("p a b -> p (a b)"), Fb[1][:, 0:256], AF.Sin, [Ftok[1]], [postok], scale=6.283185)

            ev = Fb[0][:, 128:136]
            S.op("pool", lambda e: e.iota(ev, pattern=[[1, 8]], base=0, channel_multiplier=0,
                                          allow_small_or_imprecise_dtypes=True), (), [Ftok[0]])
            for k, w in enumerate(WINS):
                ts("dve", invtab[:, k, 0:8], ev, float(w // 2), float(w), ALU.add, ALU.min, [Ftok[0]], [invtok])
                ts("dve", invtab[:, k, 8:16], ev, -1.0, float(8 + w // 2), ALU.mult, ALU.add, [Ftok[0]], [invtok])
                ts("dve", invtab[:, k, 8:16], invtab[:, k, 8:16], float(w), None, ALU.min, None, [], [invtok])
            S.op("dve", lambda e: e.reciprocal(invtab[:].rearrange("p a b -> p (a b)"),
                                               invtab[:].rearrange("p a b -> p (a b)")), [], [invtok])

            S.dma("sp", [(x[:, c, :], xin[c * 128:(c + 1) * 128, :]) for c in range(KC)], writes=xtok, semi=xs_sem[0])
            xs4 = x[:, 0:4, 512:T].rearrange("p c (r q) -> p c r q", q=64)
            tt("dve", xs4, xs4, posc[:, :, 0:16].unsqueeze(3).to_broadcast([128, 4, 16, 64]), ALU.add,
               [postok], xtok[0:4])
            xs8 = x[:, 4:8, 512:T].rearrange("p c (r q) -> p c r q", q=64)
            tt("dve", xs8, xs8, posc[:, :, :].unsqueeze(2).to_broadcast([128, 4, 16, 64]), ALU.add,
               [postok], xtok[4:8])

            for _ in range(6):
                mod_unit()

            for i in range(4):
                if i % 2 == 0:
                    even_layer(i)
                else:
                    odd_layer(i)
                layer_norm(i)
                if DEBUG_STOP is not None and i == DEBUG_STOP:
                    break

            S.dma("sp", [(yout[c * 128:(c + 1) * 128, :], x[:, c, :]) for c in range(KC)], reads=xtok, semi=xs_sem[1])

        mod_pending = []

        def mod_reset():
            del mod_pending[:]
            for i in range(4):
                for m4 in range(6):
                    mod_pending.append((i, m4))

        def mod_unit():
            if not mod_pending:
                return
            i, m4 = mod_pending.pop(0)

            def pairs(slot, i=i, m4=m4):
                return [(slot.rearrange("p (k n) -> p k n", k=8),
                         w_mod[i, :, m4 * 512:(m4 + 1) * 512].rearrange("(k p) n -> p k n", p=128))]
            sl, stok = ring.next(pairs)
            w = sl.rearrange("p (k n) -> p k n", k=8)
            for mmi in range(4):
                m = m4 * 4 + mmi
                for kc in range(KC):
                    mm(p7[:, m * 2:(m + 1) * 2], w[:, kc, mmi * 128:(mmi + 1) * 128], scb[:, kc, :],
                       kc == 0, kc == KC - 1, [stok, scbtok], [bk[7]])
            if m4 == 5:
                tt("dve", modv[:, i, :, :], p7[:, 0:48].rearrange("p (m v) -> p m v", v=2),
                   pv[:, i * 24:(i + 1) * 24].unsqueeze(2).to_broadcast([128, 24, 2]), ALU.add,
                   [pvtok], [bk[7], modtok])
                ts("dve", modv[:, i, 8:24, :], modv[:, i, 8:24, :], 1.0, None, ALU.add, None, [], [modtok])
                if i >= 1:
                    ts("dve", modv[:, i, 8:16, :], modv[:, i, 8:16, :], 1.0 / ALPHA, None, ALU.mult, None, [], [modtok])

        def modulate(i):
            for c in range(KC):
                for v, (t0, tl) in enumerate([(0, 512), (512, 1024)]):
                    act(u[:, c, t0:t0 + tl], x[:, c, t0:t0 + tl], AF.Identity, [xtok[c], modtok], [utok[c]],
                        bias=modv[:, i, c, v:v + 1], scale=modv[:, i, 8 + c, v:v + 1])
                if i == 0:
                    ts("dve", x[:, c, :], x[:, c, :], ALPHA, None, ALU.mult, None, [], [xtok[c]])

        def proj_chunk(wv, wtok, ncols, col0, kchunks, rhs_fn, rtoks):
            pg, ptoks = next_pg()
            for n, (t0, tl) in enumerate(tiles):
                for kc in range(kchunks):
                    mm(pg[0:ncols, t0:t0 + tl], wv[:, kc, col0:col0 + ncols], rhs_fn(kc, t0, tl),
                       kc == 0, kc == kchunks - 1, [wtok] + rtoks(kc), [ptoks[n]])
            return pg, ptoks

        def u_rhs(kc, t0, tl):
            return u[:, kc, t0:t0 + tl]

        def u_toks(kc):
            return [utok[kc]]

        def y_rhs(kc, t0, tl):
            return yb[:, kc, t0:t0 + tl]

        def y_toks(kc):
            return [ytok[kc]]

        def conv3(P, ptoks, dstF, dstFtok, w0, w1, w2, bias, srctoks=()):
            if bias is not None:
                act(dstF[:, 0:T], P[:, 0:T], AF.Identity, list(srctoks), list(ptoks) + [dstFtok], bias=bias, scale=w1)
            else:
                act(dstF[:, 0:T], P[:, 0:T], AF.Identity, list(srctoks), list(ptoks) + [dstFtok], bias=0.0, scale=w1)
            for (s0, L) in SEQS:
                stt("dve", dstF[:, s0 + 1:s0 + L], P[:, s0:s0 + L - 1], w0, dstF[:, s0 + 1:s0 + L], ALU.mult, ALU.add,
                    list(srctoks), list(ptoks) + [dstFtok])
                stt("dve", dstF[:, s0:s0 + L - 1], P[:, s0 + 1:s0 + L], w2, dstF[:, s0:s0 + L - 1], ALU.mult, ALU.add,
                    list(srctoks), list(ptoks) + [dstFtok])

        conv_ctr = [0]

        def conv_evac(pg, ptoks):
            k = conv_ctr[0] % 2
            conv_ctr[0] += 1
            praw, prt = (Fb[0], Ftok[0]) if k == 0 else (Fb[2], Ftok[2])
            cp("act", praw[:, 0:T], pg[:, 0:T], [], list(ptoks) + [prt])
            return k

        def conv_taps(k, w0, w1, w2, bias, dst, dsttok):
            praw, prt = (Fb[0], Ftok[0]) if k == 0 else (Fb[2], Ftok[2])
            cvb, cvt = (Fb[1][:, 0:T], Ftok[1]) if k == 0 else (hbuf[:, 0:T], hsttok)
            ts("dve", cvb, praw[:, 0:T], w1, bias, ALU.mult, ALU.add, [prt, pvtok], [cvt])
            for (s0, L) in SEQS:
                stt("dve", cvb[:, s0 + 1:s0 + L], praw[:, s0:s0 + L - 1], w0, cvb[:, s0 + 1:s0 + L], ALU.mult, ALU.add,
                    [prt], [cvt])
                stt("dve", cvb[:, s0:s0 + L - 1], praw[:, s0 + 1:s0 + L], w2, cvb[:, s0:s0 + L - 1], ALU.mult, ALU.add,
                    [prt], [cvt])
            act(dst, cvb, AF.Silu, [cvt], [dsttok])

        def ln_stats_chunk(c):
            q = c % 2
            xbf, xbft = Hb[q], Htok[q]
            sqf, sqft = Hb[2 + q], Htok[2 + q]
            cp("dve", xbf[:, :], x[:, c, :], [xtok[c]], [xbft])
            act(sqf[:, :], x[:, c, :], AF.Square, [xtok[c]], [sqft])
            for n, (t0, tl) in enumerate(tiles):
                mm(pA[:, t0:t0 + tl], onesm[:], xbf[:, t0:t0 + tl], c == 0, c == KC - 1, [ctok, xbft], [bk[n]])
                mm(pB[:, t0:t0 + tl], onesm[:], sqf[:, t0:t0 + tl], c == 0, c == KC - 1, [ctok, sqft], [bk[3 + n]])

        def out_proj_partial(i, wdram, row0, ln_stats=False):
            pend = []
            cnt = 0
            for half in range(2):
                def pairs(slot, half=half):
                    return [(slot.rearrange("p (k n) -> p k n", k=8),
                             wdram[row0:row0 + 1024, half * 512:(half + 1) * 512].rearrange("(k p) n -> p k n", p=128))]
                sl, stok = ring.next(pairs)
                wv = sl.rearrange("p (k n) -> p k n", k=8)
                for dd in range(4):
                    dc = half * 4 + dd
                    if not ln_stats:
                        pg, ptoks = proj_chunk(wv, stok, 128, dd * 128, KC, y_rhs, y_toks)
                        for v, (t0, tl) in enumerate([(0, 512), (512, 1024)]):
                            bt = ptoks[0:1] if v == 0 else ptoks[1:3]
                            stt("dve", x[:, dc, t0:t0 + tl], pg[:, t0:t0 + tl], modv[:, i, 16 + dc, v:v + 1],
                                x[:, dc, t0:t0 + tl], ALU.mult, ALU.add, [modtok], bt + [xtok[dc]])
                    else:
                        for n, (t0, tl) in enumerate(tiles):
                            pb, pbt = (p6, bk[6]) if cnt % 2 == 0 else (p7, bk[7])
                            cnt += 1
                            for kc in range(KC):
                                mm(pb[:, 0:tl], wv[:, kc, dd * 128:(dd + 1) * 128], yb[:, kc, t0:t0 + tl],
                                   kc == 0, kc == KC - 1, [stok, ytok[kc]], [pbt])
                            v = 0 if n == 0 else 1
                            stt("dve", x[:, dc, t0:t0 + tl], pb[:, 0:tl], modv[:, i, 16 + dc, v:v + 1],
                                x[:, dc, t0:t0 + tl], ALU.mult, ALU.add, [modtok], [pbt, xtok[dc]])
                        if pend:
                            pend.pop(0)()
                        pend.append(lambda dc=dc: ln_stats_chunk(dc))
                ring.release()
            while pend:
                pend.pop(0)()

        def layer_norm(i):
            mean_sb = Fb[0]
            rstd_sb = Fb[1]
            sq = [Fb[2], Fb[2]]
            sqt = [Ftok[2], Ftok[2]]
            if i < 3:
                for _ in range(6):
                    mod_unit()
            cp("act", mean_sb[:, 0:T], pA[:, 0:T], [], bk[0:3] + [Ftok[0]])
            tt("dve", rstd_sb[:, 0:T], mean_sb[:, 0:T], mean_sb[:, 0:T], ALU.mult, [Ftok[0]], [Ftok[1]])
            tt("dve", rstd_sb[:, 0:T], pB[:, 0:T], rstd_sb[:, 0:T], ALU.subtract, [], bk[3:6] + [Ftok[1]])
            act(rstd_sb[:, 0:T], rstd_sb[:, 0:T], AF.Ln, [], [Ftok[1]], bias=LN_EPS)
            act(rstd_sb[:, 0:T], rstd_sb[:, 0:T], AF.Exp, [], [Ftok[1]], scale=-0.5)
            for c in range(KC):
                tt("dve", x[:, c, :], x[:, c, :], mean_sb[:, 0:T], ALU.subtract, [Ftok[0]], [xtok[c]])
                tt("dve", x[:, c, :], x[:, c, :], rstd_sb[:, 0:T], ALU.mult, [Ftok[1]], [xtok[c]])
                if i < 3:
                    act(x[:, c, :], x[:, c, :], AF.Identity, [pvtok], [xtok[c]],
                        bias=pvA[:, 32 + i * 8 + c:32 + i * 8 + c + 1], scale=pvA[:, i * 8 + c:i * 8 + c + 1])
                else:
                    act(x[:, c, :], x[:, c, :], AF.Identity, [pvtok], [xtok[c]], bias=pvc(c_lnb(i, c)), scale=pvc(c_lng(i, c)))

        def even_layer(i):
            j = i // 2
            modulate(i)
            def pairs_dt(slot):
                return [(slot[:, 0:256].rearrange("p (k n) -> p k n", k=8),
                         w_in_even[j, :, 3072:3104].rearrange("(k p) n -> p k n", p=128))]
            sl, stok = ring.next(pairs_dt)
            wv = sl[:, 0:256].rearrange("p (k n) -> p k n", k=8)
            for tc in range(NTC):
                for kc in range(KC):
                    mm(p6[:, tc * 32:(tc + 1) * 32], u[:, kc, tc * 128:(tc + 1) * 128], wv[:, kc, 0:32],
                       kc == 0, kc == KC - 1, [stok, utok[kc]], [bk[6]])
            ring.release()
            dtmp = wde[:].rearrange("p d t h -> p (d t h)").rearrange("p (t c) -> p t c", c=32)
            tt("dve", dtmp, p6[:, 0:384].rearrange("p (t c) -> p t c", c=32),
               dtb_bc[:, j * 32:(j + 1) * 32].unsqueeze(1).to_broadcast([128, NTC, 32]), ALU.add,
               [abtok], [bk[6], dttok])
            act(dtmp, dtmp, AF.Exp, [], [dttok])
            act(dt_tok[:], dtmp.rearrange("p t (d h) -> p d t h", d=2), AF.Ln, [], [dttok], bias=1.0)
            tt("dve", a_bf[:], dt_tok[:],
               A_bc[:, j * 32:(j + 1) * 32].rearrange("p (d h) -> p d h", d=2).unsqueeze(2).to_broadcast([128, 2, NTC, 16]),
               ALU.mult, [abtok], [dttok])
            mm(p6[:, 0:192], Uf[:], a_bf[:, 0, :, :].rearrange("p t h -> p (t h)"), True, True, [ctok, dttok], [bk[6]])
            mm(p6[:, 192:384], Ub[:], a_bf[:, 1, :, :].rearrange("p t h -> p (t h)"), True, True, [ctok, dttok], [bk[6]])
            mm(p7[:, 0:384], onesb[:], a_bf[:].rearrange("p d t h -> p (d t h)"), True, True, [ctok, dttok], [bk[7]])
            act(wde[:].rearrange("p d t h -> p (d t h)"), p6[:, 0:384], AF.Exp, [], [bk[6], dttok])
            act(cd_bc[:].rearrange("p d t h -> p (d t h)"), p7[:, 0:384], AF.Exp, [], [bk[7], dttok])
            tt("dve", wde[:], wde[:], dt_tok[:], ALU.mult, [], [dttok])

            zs = [Hb[4], Hb[5]]
            zst = [Htok[4], Htok[5]]
            xT = [Hb[0], Hb[1]]
            xTt = [Htok[0], Htok[1]]
            BT, BTt = Hb[2], Htok[2]
            CT, CTt = Hb[3], Htok[3]
            cv, cvt = Fb[1], Ftok[1]
            def tr_block(srcH, srcT, dst_fn, dtok):
                for q in range(3):
                    hq = q % 2
                    for t4 in range(4):
                        tc = q * 4 + t4
                        tr(trP[:, hq * 512 + t4 * 128:hq * 512 + (t4 + 1) * 128], srcH[:, tc * 128:(tc + 1) * 128],
                           identb[:], [srcT, ctok], [bk[6]])
                    cp("act", dst_fn(q), trP[:, hq * 512:(hq + 1) * 512].rearrange("p (t f) -> p t f", t=4),
                       [], [bk[6], dtok])

            for g in range(4):
                def pairsX(slot, g=g):
                    sv = slot.rearrange("p (k n) -> p k n", k=8)
                    return [(sv[:, :, 0:256], w_in_even[j, :, 1024 + g * 256:1024 + (g + 1) * 256].rearrange("(k p) n -> p k n", p=128)),
                            (sv[:, :, 256:384], w_in_even[j, :, 2048 + g * 128:2048 + (g + 1) * 128].rearrange("(k p) n -> p k n", p=128)),
                            (sv[:, :, 384:512], w_in_even[j, :, 2560 + g * 128:2560 + (g + 1) * 128].rearrange("(k p) n -> p k n", p=128))]
                sl, stok = ring.next(pairsX)
                wv = sl.rearrange("p (k n) -> p k n", k=8)
                chunks = [(0, g * 2, xT[0][:, :], xTt[0]), (128, g * 2 + 1, xT[1][:, :], xTt[1]),
                          (256, 8 + g, BT[:, :], BTt), (384, 12 + g, CT[:, :], CTt)]
                kbuf = [None] * 4

                def conv_of(ci):
                    col0, ch, dst, dtok = chunks[ci]
                    conv_taps(kbuf[ci], pvc(c_caw(j, 0, ch)), pvc(c_caw(j, 1, ch)), pvc(c_caw(j, 2, ch)), pvc(c_cab(j, ch)),
                              dst, dtok)

                def tr_x(cc):
                    tr_block(xT[cc], xTt[cc], lambda q, cc=cc: x_tok[:, q * 4:(q + 1) * 4, cc * 128:(cc + 1) * 128], xtoktok)

                for ci in range(4):
                    pg, ptoks = proj_chunk(wv, stok, 128, chunks[ci][0], KC, u_rhs, u_toks)
                    kbuf[ci] = conv_evac(pg, ptoks)
                    if ci >= 1:
                        conv_of(ci - 1)
                    if ci == 3:
                        tr_x(0)
                ring.release()
                def pairsZ(slot, g=g):
                    sv = slot[:, 0:2048].rearrange("p (k n) -> p k n", k=8)
                    return [(sv, w_in_even[j, :, g * 256:(g + 1) * 256].rearrange("(k p) n -> p k n", p=128))]
                sl, stok = ring.next(pairsZ)
                wv = sl[:, 0:2048].rearrange("p (k n) -> p k n", k=8)
                for cc in range(2):
                    pg, ptoks = proj_chunk(wv, stok, 128, cc * 128, KC, u_rhs, u_toks)
                    act(zs[cc][:, :], pg[:, 0:T], AF.Silu, [], ptoks + [zst[cc]])
                    if cc == 0:
                        conv_of(3)
                        tr_x(1)
                    else:
                        tr_block(BT, BTt, lambda q: B_tok[:, q * 4:(q + 1) * 4, :], btoktok)
                ring.release()
                ssd_group(j, g, zs, zst, BT, BTt, CT, CTt)

            for c in range(KC):
                sqb, sqbt = Hb[c % 2], Htok[c % 2]
                act(sqb[:, :], yb[:, c, :], AF.Square, [ytok[c]], [sqbt])
                for n, (t0, tl) in enumerate(tiles):
                    mm(pA[:, t0:t0 + tl], onesm[:], sqb[:, t0:t0 + tl], c == 0, c == KC - 1, [ctok, sqbt], [bk[n]])
            act(Fb[2][:, 0:T], pA[:, 0:T], AF.Ln, [], bk[0:3] + [Ftok[2]], bias=RMS_EPS)
            act(Fb[2][:, 0:T], Fb[2][:, 0:T], AF.Exp, [], [Ftok[2]], scale=-0.5)
            for c in range(KC):
                stt("dve", yb[:, c, :], yb[:, c, :], pvc(c_nag(j, c)), Fb[2][:, 0:T], ALU.mult, ALU.mult,
                    [pvtok, Ftok[2]], [ytok[c]])
            out_proj_partial(i, w_out_even[j], 0)

            for c in range(KC):
                def pairsM(slot, c=c):
                    sv = slot.rearrange("p (k n) -> p k n", k=8)
                    return [(sv[:, :, q * 128:(q + 1) * 128],
                             w_in_even[j, :, 3104 + q * 1024 + c * 128:3104 + q * 1024 + (c + 1) * 128].rearrange("(k p) n -> p k n", p=128))
                            for q in range(4)]
                sl, stok = ring.next(pairsM)
                wv = sl.rearrange("p (k n) -> p k n", k=8)
                sg, sgt = Hb[0], Htok[0]
                pg, ptoks = proj_chunk(wv, stok, 128, 0, KC, u_rhs, u_toks)
                act(sg[:, :], pg[:, 0:T], AF.Silu, [], ptoks + [sgt])
                pg, ptoks = proj_chunk(wv, stok, 128, 128, KC, u_rhs, u_toks)
                tt("dve", Fb[0][:, 0:T], pg[:, 0:T], sg[:, :], ALU.mult, [sgt], ptoks + [Ftok[0]])
                pg, ptoks = proj_chunk(wv, stok, 128, 256, KC, u_rhs, u_toks)
                cp("act", Hb[1][:, :], pg[:, 0:T], [], ptoks + [Htok[1]])
                pg, ptoks = proj_chunk(wv, stok, 128, 384, KC, u_rhs, u_toks)
                tt("dve", Fb[1][:, 0:T], pg[:, 0:T], Hb[1][:, :], ALU.mult, [Htok[1]], ptoks + [Ftok[1]])
                ring.release()
                conv3(Fb[1], [], Fb[2], Ftok[2], pvc(c_cbw(j, 0, c)), pvc(c_cbw(j, 1, c)), pvc(c_cbw(j, 2, c)), None,
                      srctoks=[Ftok[1], pvtok])
                tt("dve", yb[:, c, :], Fb[2][:, 0:T], Fb[0][:, 0:T], ALU.mult, [Ftok[2], Ftok[0]], [ytok[c]])
            out_proj_partial(i, w_out_even[j], 1024, ln_stats=True)

        def ssd_group(j, g, zs, zst, BT, BTt, CT, CTt):
            gs = slice(g * 4, g * 4 + 4)
            for h4 in range(4):
                hh = j * 16 + g * 4 + h4
                ts("dve", Dmat[:, h4, :], identb[:], D_bc[:, hh:hh + 1], None, ALU.mult, None, [ctok, abtok], [dmtok])
            HC = [(hcur[:], hcurtok), (hcur2, hcur2tok)]
            HT = [(htmp[:], htmptok), (htmp2, htmp2tok)]

            def v4(ap):
                return ap.rearrange("p (h q) -> p h q", h=4)

            hstP = Hb[6][:, 0:1024].rearrange("p (d s q) -> p d s q", d=2, s=2)

            def hst_slot(si, d, tc):
                if si == 2:
                    return hst[:, d, tc - 4, :], hsttok
                return hstP[:, d, si, :], Htok[6]

            def seq_of(tc):
                si = 0 if tc < 2 else (1 if tc < 4 else 2)
                s0, L = SEQS[si]
                return si, s0 // 128, L // 128

            def state_rounds(si):
                s0, L = SEQS[si]
                nch = L // 128
                tc0 = s0 // 128
                is_sample = (si == 2)
                orders = [list(range(tc0, tc0 + nch)), list(range(tc0 + nch - 1, tc0 - 1, -1))]
                have = [False, False]
                fns = []

                def init():
                    for d in range(2):
                        r0 = (j * 2 + d) * 128
                        S.dma("sp", [(HC[d][0], state[r0:r0 + 128, g * 256:(g + 1) * 256])],
                              writes=[HC[d][1]], semi=stin_sem[d])
                        have[d] = True
                if is_sample:
                    fns.append(init)

                def round_(idx):
                    for d in range(2):
                        tc = orders[d][idx]
                        hc, hct = HC[d]
                        ht, htt = HT[d]
                        par = idx % 2
                        if is_sample:
                            if par == 0:
                                stb = pA[:, 1024:1280] if d == 0 else pA[:, 1280:1536]
                                stk = bk[2]
                            else:
                                stb = bank4[:, 128:384] if d == 0 else bank5[:, 256:512]
                                stk = bk[4] if d == 0 else bk[5]
                        else:
                            if par == 0:
                                stb = bank4[:, 128:384] if d == 0 else bank5[:, 256:512]
                                stk = bk[4] if d == 0 else bk[5]
                            else:
                                stb = p6[:, 0:256] if d == 0 else p7[:, 0:256]
                                stk = bk[6] if d == 0 else bk[7]
                        xdb, xdbt = xdd[d][par], xddtok[d][par]
                        if have[d]:
                            hdst, hdtok = hst_slot(si, d, tc)
                            cp("act", hdst, hc, [hct], [hdtok])
                        if idx == nch - 1 and is_sample:
                            continue
                        tt("dve", xdb, v4(x_tok[:, tc, :]),
                           wde[:, d, tc, gs].unsqueeze(2).to_broadcast([128, 4, 64]), ALU.mult,
                           [xtoktok, dttok], [xdbt])
                        mm(stb, B_tok[:, tc, :], xdb.rearrange("p h q -> p (h q)"), True, True,
                           [btoktok, xdbt], [stk])
                        if have[d]:
                            tt("dve", v4(ht), v4(hc),
                               cd_bc[:, d, tc, gs].unsqueeze(2).to_broadcast([128, 4, 64]), ALU.mult,
                               [hct, dttok], [htt])
                            tt("dve", hc, stb, ht, ALU.add, [htt], [stk, hct])
                        else:
                            cp("dve", hc, stb, [], [stk, hct])
                            have[d] = True
                for idx in range(nch):
                    fns.append(lambda idx=idx: round_(idx))

                def final():
                    for d in range(2):
                        hc, hct = HC[d]
                        stv = sto[d][:].rearrange("p a b -> p (a b)")
                        cp("act", stv, hc, [hct], [stotok[d]])
                        r0 = ((si * 2 + j) * 2 + d) * 128
                        S.dma("sp", [(nstate[r0:r0 + 128, g * 256:(g + 1) * 256], stv)],
                              reads=[stotok[d]], semi=sto_sem[d])
                if not is_sample:
                    fns.append(final)
                return fns

            for si in range(2):
                for fn in state_rounds(si):
                    fn()
            pending_rounds = state_rounds(2)

            if True:
                its = list(range(NTC))
                ARG = [(p6[:], bk[6], p7[:], bk[7]), (pA[:, 0:512], bk[0], pA[:, 512:1024], bk[1])]
                YB = [(bank5, bk[5]), (pB[:, 0:512], bk[3])]

                def stage_pre_a(n):
                    tc = its[n]
                    k0 = tc * 128
                    mm(bank4[:, 0:128], BT[:, k0:k0 + 128], CT[:, k0:k0 + 128], True, True, [BTt, CTt], [bk[4]])
                    tt("dve", Rb2[:], Lfb[:].unsqueeze(2).to_broadcast([128, 2, 4, 128]),
                       a_bf[:, :, tc, gs].unsqueeze(3).to_broadcast([128, 2, 4, 128]), ALU.mult,
                       [ctok, dttok], [Rtok])
                    for d in range(2):
                        Um = Uf if d == 0 else Ub
                        pa, pat, pe_, pet = ARG[d]
                        Rflat = Rb2[:, d].rearrange("p h i -> p (h i)")
                        mm(pa, Um[:], Rflat, True, True, [ctok, Rtok], [pat])
                        mm(pe_, onesb[:], Rflat, True, True, [ctok, Rtok], [pet])

                def stage_pre_b(n):
                    for d in range(2):
                        pa, pat, pe_, pet = ARG[d]
                        act(eA2[:, d].rearrange("p h i -> p (h i)"), pa, AF.Exp, [], [pat, eAtok[d]])
                        act(eE2[:, d].rearrange("p h i -> p (h i)"), pe_, AF.Exp, [], [pet, eEtok[d]])

                def stage_cbm():
                    tt("dve", CBm2[:], bank4[:, 0:128].unsqueeze(1).to_broadcast([128, 2, 128]), Lfb[:], ALU.mult,
                       [ctok], [bk[4], CBmtok])

                def stage_mid(n):
                    tc = its[n]
                    k0 = tc * 128
                    q = n % 2
                    tt("dve", MT2[q][:], eA2[:], CBm2[:].unsqueeze(2).to_broadcast([128, 2, 4, 128]), ALU.mult,
                       [eAtok[0], eAtok[1], CBmtok], [MTtok[q]])
                    tt("dve", Cs2[q][:], eE2[:],
                       CT[:, k0:k0 + 128].unsqueeze(1).unsqueeze(1).to_broadcast([128, 2, 4, 128]), ALU.mult,
                       [eEtok[0], eEtok[1], CTt], [Cstok[q]])
                    tt("dve", xdt2[q][:], v4(x_tok[:, tc, :]).unsqueeze(1).to_broadcast([128, 2, 4, 64]),
                       dt_tok[:, :, tc, gs].unsqueeze(3).to_broadcast([128, 2, 4, 64]), ALU.mult,
                       [xtoktok, dttok], [xdttok[q]])

                def stage_y(n):
                    tc = its[n]
                    q = n % 2
                    yb_, ybt = YB[q]
                    si, tc0, nch = seq_of(tc)
                    has_f = (si == 2) or (tc != tc0)
                    has_b = (si == 2) or (tc != tc0 + nch - 1)
                    hf, hft = hst_slot(si, 0, tc)
                    hb_, hbt = hst_slot(si, 1, tc)
                    for h4 in range(4):
                        po = (h4 % 2) * 64
                        out = yb_[po:po + 64, (h4 // 2) * 128:(h4 // 2) * 128 + 128]
                        tp = (0, 64) if po else None
                        mm(out, xdt2[q][:, 0, h4, :], MT2[q][:, 0, h4, :], True, False, [xdttok[q], MTtok[q]], [ybt], tp)
                        mm(out, xdt2[q][:, 1, h4, :], MT2[q][:, 1, h4, :], False, False, [xdttok[q], MTtok[q]], [ybt], tp)
                        if has_f:
                            mm(out, hf[:, h4 * 64:(h4 + 1) * 64], Cs2[q][:, 0, h4, :], False, False,
                               [hft, Cstok[q]], [ybt], tp)
                        if has_b:
                            mm(out, hb_[:, h4 * 64:(h4 + 1) * 64], Cs2[q][:, 1, h4, :], False, False,
                               [hbt, Cstok[q]], [ybt], tp)
                        mm(out, x_tok[:, tc, h4 * 64:(h4 + 1) * 64], Dmat[:, h4, :], False, True, [xtoktok, dmtok], [ybt], tp)

                def stage_evac(n):
                    tc = its[n]
                    k0 = tc * 128
                    yb_, ybt = YB[n % 2]
                    tt("dve", yb[:, g * 2:g * 2 + 2, k0:k0 + 128], yb_[:, 0:256].rearrange("p (r i) -> p r i", r=2),
                       Hb45[:, :, k0:k0 + 128], ALU.mult, [zst[0], zst[1]], [ybt, ytok[g * 2], ytok[g * 2 + 1]])

                N = len(its)
                stage_pre_a(0)
                stage_pre_b(0)
                stage_cbm()
                for n in range(N):
                    if n == 4:
                        while pending_rounds:
                            pending_rounds.pop(0)()
                    if n + 1 < N:
                        stage_pre_a(n + 1)
                    stage_mid(n)
                    if n + 1 < N:
                        stage_pre_b(n + 1)
                    if n >= 1:
                        stage_evac(n - 1)
                    if n + 1 < N:
                        stage_cbm()
                    stage_y(n)
                    if n < 4:
                        for _ in range(3):
                            if pending_rounds:
                                pending_rounds.pop(0)()
                stage_evac(N - 1)

        def odd_layer(i):
            j = i // 2
            modulate(i)
            S.op("dve", lambda e: e.memset(Fb[0][:], 0.0), (), [Ftok[0]])
            S.op("dve", lambda e: e.memset(Fb[1][:], 0.0), (), [Ftok[1]])
            S.op("dve", lambda e: e.memset(Fb[2][:], 0.0), (), [Ftok[2]])
            for half in range(2):
                for kk in range(2):
                    k = half * 2 + kk
                    w = WINS[k]
                    invc, invct = invc_alias, hsttok
                    S.op("dve", lambda e: e.memset(invc, 1.0 / w), (), [invct])
                    for si, (s0, L) in enumerate(SEQS):
                        o = POFF[si]
                        cp("dve", invc[:, o:o + 8], invtab[:, k, 0:8], [invtok], [invct])
                        cp("dve", invc[:, o + L - 8:o + L], invtab[:, k, 8:16], [invtok], [invct])
                    def pairsV(slot, k=k):
                        return [(slot.rearrange("p (k n) -> p k n", k=8),
                                 w_in_odd[j, :, k * 512:(k + 1) * 512].rearrange("(k p) n -> p k n", p=128))]

                    def pairsG(slot, k=k):
                        return [(slot.rearrange("p (k n) -> p k n", k=8),
                                 w_in_odd[j, :, 2048 + k * 512:2048 + (k + 1) * 512].rearrange("(k p) n -> p k n", p=128))]
                    slV, stokV = ring.next(pairsV)
                    wvV = slV.rearrange("p (k n) -> p k n", k=8)
                    slG, stokG = ring.next(pairsG, prefetch=False)
                    wvG = slG.rearrange("p (k n) -> p k n", k=8)
                    for cc in range(4):
                        pg, ptoks = proj_chunk(wvV, stokV, 128, cc * 128, KC, u_rhs, u_toks)
                        vp, vpt = Fb[0], Ftok[0]
                        for si, (s0, L) in enumerate(SEQS):
                            o = POFF[si]
                            bt = ptoks[0:1] if si < 2 else ptoks[1:3]
                            cp("act", vp[:, o:o + L], pg[:, s0:s0 + L], [], bt + [vpt])
                        pg2, ptoks2 = proj_chunk(wvG, stokG, 128, cc * 128, KC, u_rhs, u_toks)
                        act(Hb[cc][:, :], pg2[:, 0:T], AF.Silu, [], ptoks2 + [Htok[cc]])
                        src, srct = vp, vpt
                        bufs = [(Fb[1], Ftok[1]), (Fb[2], Ftok[2])]
                        sh = [(1, 0), (1, 1), (2, 2), (4, 4)]
                        for lev in range(k + 1):
                            dst, dstt = bufs[lev % 2]
                            a_, b_ = sh[lev]
                            tt("dve", dst[:, 8:PW - 8], src[:, 8 - a_:PW - 8 - a_], src[:, 8 + b_:PW - 8 + b_], ALU.add,
                               [srct], [dstt])
                            src, srct = dst, dstt
                        oth, otht = bufs[(k + 1) % 2]
                        tt("dve", oth[:, 8:PW - 8], src[:, 8:PW - 8], invc[:, 8:PW - 8], ALU.mult, [srct, invct], [otht])
                        for si, (s0, L) in enumerate(SEQS):
                            o = POFF[si]
                            tt("dve", Hb[4 + cc][:, s0:s0 + L], oth[:, o:o + L], vp[:, o:o + L], ALU.subtract,
                               [otht, vpt], [Htok[4 + cc]])
                    def pairsP(slot, k=k):
                        return [(slot[:, 0:2048].rearrange("p (k n) -> p k n", k=4),
                                 w_pool[j, k, :, :].rearrange("(k p) n -> p k n", p=128))]
                    sl, stok = ring.next(pairsP)
                    wv = sl[:, 0:2048].rearrange("p (k n) -> p k n", k=4)
                    for dd in range(4):
                        pg, ptoks = proj_chunk(wv, stok, 128, dd * 128, 4,
                                               lambda kc, t0, tl: Hb[4 + kc][:, t0:t0 + tl], lambda kc: [Htok[4 + kc]])
                        yc = kk * 4 + dd
                        stt("dve", yb[:, yc, :], pg[:, 0:T], pvc(c_psc(j, k * 4 + dd)), Hb[dd][:, :], ALU.mult, ALU.mult,
                            [pvtok, Htok[dd]], ptoks + [ytok[yc]])
                    ring.release()
                out_proj_partial(i, w_out_odd[j], half * 1024, ln_stats=(half == 1))

        S.dry = True
        emit()
        S.dry = False
        S.plan = True
        pg_state[0] = 0
        emit()
        S.plan = False
        S.reset()
        ring.issued = 0
        ring.cur = 0
        pg_state[0] = 0
        emit()
        for semi in list(xs_sem) + list(sto_sem):
            ent = S.dma_sems[semi]
            if ent[1] > 0:
                nc.sync.wait_ge(ent[0], ent[1])
        print("program: insts", S.cnt, "incs", S.semval, "waits", S.nwaits, "pieces", len(ring.plan))
    return nc


_NC_CACHE = {}


def kernel(x_prompt, x_sample, state_ssd, c, c_ctx, w_mod, b_mod, ln_g, ln_b, w_in_even, conv_a_w,
           conv_a_b, a_log, dt_bias, d_skip, norm_a_g, conv_b_w, w_out_even, w_in_odd, w_pool,
           pool_scale, w_out_odd):
    f = lambda a: np.ascontiguousarray(np.asarray(a), dtype=np.float32)
    if "nc" not in _NC_CACHE:
        _NC_CACHE["nc"] = build_program()
    nc = _NC_CACHE["nc"]
    x_prompt = f(x_prompt)
    x_sample = f(x_sample)
    state_ssd = f(state_ssd)
    c = f(c)
    c_ctx = f(c_ctx)
    shared = dict(w_mod=f(w_mod), b_mod=f(b_mod), ln_g=f(ln_g), ln_b=f(ln_b), w_in_even=f(w_in_even),
                  conv_a_w=f(conv_a_w), conv_a_b=f(conv_a_b), a_log=f(a_log), dt_bias=f(dt_bias), d_skip=f(d_skip),
                  norm_a_g=f(norm_a_g), conv_b_w=f(conv_b_w), w_out_even=f(w_out_even), w_in_odd=f(w_in_odd),
                  w_pool=f(w_pool), pool_scale=f(pool_scale), w_out_odd=f(w_out_odd))
    in_maps = []
    for k in range(NCORES):
        m = dict(shared)
        m["xin"] = np.ascontiguousarray(np.concatenate([x_prompt[2 * k], x_prompt[2 * k + 1], x_sample[k]], axis=0).T)
        m["state"] = np.ascontiguousarray(state_ssd[k].transpose(0, 1, 4, 2, 3).reshape(512, 1024))
        m["cvec"] = np.ascontiguousarray(np.stack([c_ctx, c[k]], axis=0))
        in_maps.append(m)
    res = run_bass_kernel_spmd(nc, in_maps, core_ids=list(range(NCORES)))
    y_prompt = np.empty((16, 256, D), np.float32)
    y_sample = np.empty((8, 1024, D), np.float32)
    new_state = np.empty((16, 2, 2, 16, 64, 128), np.float32)
    for k in range(NCORES):
        r = res.results[k]
        yo = np.asarray(r["yout"]).T
        y_prompt[2 * k] = yo[0:256]
        y_prompt[2 * k + 1] = yo[256:512]
        y_sample[k] = yo[512:1536]
        ns = np.asarray(r["nstate"]).reshape(2, 2, 2, 128, 16, 64).transpose(0, 1, 2, 4, 5, 3)
        new_state[2 * k] = ns[0]
        new_state[2 * k + 1] = ns[1]
    return (y_prompt, y_sample, new_state)
```

```python
import math
from contextlib import ExitStack

import numpy as np
import concourse.bass as bass
import concourse.mybir as mybir
from concourse.bass_utils import run_bass_kernel_spmd

F32 = mybir.dt.float32
BF16 = mybir.dt.bfloat16
I32 = mybir.dt.int32
AF = mybir.ActivationFunctionType
ALU = mybir.AluOpType

NCORES = 8
D = 1024
T = 1536
KC = 8
NT = 3
SEQS = [(0, 256), (256, 256), (512, 1024)]
NTC = 12
ALPHA = (2 * 4) ** 0.25
LN_EPS = 1e-5
RMS_EPS = 1e-5
IN_EVEN = 7200
PW = 1600
POFF = [16, 280, 544]
WINS = (2, 4, 8, 16)
NSLOT = 2
SLOT = 4096

DEBUG_STOP = None


class Tok:
    __slots__ = ("w", "r", "name")
    ALL = []

    def __init__(self, name=""):
        self.w = None
        self.r = {}
        self.name = name
        Tok.ALL.append(self)


class Sched:
    ENG = ("pe", "dve", "act", "pool")

    def __init__(self, nc, es):
        self.nc = nc
        self.es = es
        self.engs = {"pe": nc.tensor, "dve": nc.vector, "act": nc.scalar, "pool": nc.gpsimd, "sp": nc.sync}
        self.sem = {k: es.enter_context(nc.semaphore("sem_" + k)) for k in self.ENG}
        self.cnt = {k: 0 for k in self.ENG}
        self.waited = {k: {} for k in list(self.ENG) + ["sp"]}
        self.dma_sems = []
        self.dry = False
        self.plan = False
        self.needed = {k: set() for k in self.ENG}
        self.semval = {k: 0 for k in self.ENG}
        self.valof = {k: {} for k in self.ENG}
        self.nwaits = 0

    def reset(self):
        for t in Tok.ALL:
            t.w = None
            t.r = {}
        self.cnt = {k: 0 for k in self.ENG}
        self.waited = {k: {} for k in list(self.ENG) + ["sp"]}
        self.semval = {k: 0 for k in self.ENG}
        self.valof = {k: {} for k in self.ENG}
        self.nwaits = 0
        for ent in self.dma_sems:
            ent[1] = 0

    def _wait(self, eng, dep):
        if dep is None:
            return
        src, val = dep
        if isinstance(src, str):
            if src == eng and eng == "pe":
                return
            key = src
            sem = self.sem[src]
        else:
            key = ("dma", src.num)
            sem = src
        if self.waited[eng].get(key, 0) >= val:
            return
        self.nwaits += 1
        self.waited[eng][key] = val
        if self.plan:
            if isinstance(src, str):
                self.needed[src].add(val)
            return
        if isinstance(src, str):
            self.engs[eng].wait_ge(sem, self.valof[src][val])
        else:
            self.engs[eng].wait_ge(sem, val)

    def op(self, eng, fn, reads=(), writes=()):
        if self.dry:
            return
        for t in reads:
            self._wait(eng, t.w)
        for t in writes:
            self._wait(eng, t.w)
            for src, val in t.r.items():
                if src == eng and eng == "pe":
                    continue
                self._wait(eng, (src, val))
        self.cnt[eng] += 1
        c = self.cnt[eng]
        if not self.plan:
            ins = fn(self.engs[eng])
            if c in self.needed[eng]:
                ins.then_inc(self.sem[eng], 1)
                self.semval[eng] += 1
                self.valof[eng][c] = self.semval[eng]
        for t in reads:
            t.r[eng] = c
        for t in writes:
            t.w = (eng, c)
            t.r = {}

    def new_dma_sem(self):
        s = self.es.enter_context(self.nc.semaphore("dsem%d" % len(self.dma_sems)))
        self.dma_sems.append([s, 0])
        return len(self.dma_sems) - 1

    def dma(self, q, pairs, reads=(), writes=(), semi=None, **kw):
        if self.dry:
            return
        for t in reads:
            self._wait(q, t.w)
        for t in writes:
            self._wait(q, t.w)
            for src_, val in t.r.items():
                self._wait(q, (src_, val))
        ent = self.dma_sems[semi]
        for (o, i_) in pairs:
            ent[1] += 16
            if not self.plan:
                ins = self.engs[q].dma_start(out=o, in_=i_, **kw)
                ins.then_inc(ent[0], 16)
        dep = (ent[0], ent[1])
        for t in reads:
            t.r[ent[0]] = ent[1]
        for t in writes:
            t.w = dep
            t.r = {}


class Ring:
    def __init__(self, S, buf):
        self.S = S
        self.buf = buf
        self.toks = [Tok("ring%d" % i) for i in range(NSLOT)]
        self.sems = [S.new_dma_sem() for _ in range(NSLOT)]
        self.plan = []
        self.issued = 0
        self.cur = 0

    def slot_ap(self, s):
        return self.buf[:, s, :]

    def _issue(self):
        p = self.issued
        s = p % NSLOT
        pairs = self.plan[p](self.slot_ap(s))
        self.S.dma("pool", pairs, writes=[self.toks[s]], semi=self.sems[s])
        self.issued += 1

    def next(self, pairs_fn, prefetch=True):
        if self.S.dry:
            self.plan.append(pairs_fn)
            return self.slot_ap(0), self.toks[0]
        p = self.cur
        while self.issued < min(len(self.plan), (p + NSLOT) if prefetch else (p + 1)):
            self._issue()
        s = p % NSLOT
        self.cur += 1
        return self.slot_ap(s), self.toks[s]

    def release(self):
        pass

    def kick(self):
        if self.S.dry:
            return
        while self.issued < min(len(self.plan), NSLOT):
            self._issue()


def build_program():
    nc = bass.Bass("TRN2", target_bir_lowering=False)

    def din(name, shape):
        return nc.dram_tensor(name, list(shape), F32, kind="ExternalInput").ap()

    xin = din("xin", [D, T])
    state = din("state", [512, 1024])
    cvec = din("cvec", [2, D])
    w_mod = din("w_mod", [4, D, 3072])
    b_mod = din("b_mod", [4, 3072])
    ln_g = din("ln_g", [4, D])
    ln_b = din("ln_b", [4, D])
    w_in_even = din("w_in_even", [2, D, IN_EVEN])
    conv_a_w = din("conv_a_w", [2, 3, 2048])
    conv_a_b = din("conv_a_b", [2, 2048])
    a_log = din("a_log", [2, 2, 16])
    dt_bias = din("dt_bias", [2, 2, 16])
    d_skip = din("d_skip", [2, 16])
    norm_a_g = din("norm_a_g", [2, D])
    conv_b_w = din("conv_b_w", [2, 3, D])
    w_out_even = din("w_out_even", [2, 2048, D])
    w_in_odd = din("w_in_odd", [2, D, 4096])
    w_pool = din("w_pool", [2, 4, 512, 512])
    pool_scale = din("pool_scale", [2, 2048])
    w_out_odd = din("w_out_odd", [2, 2048, D])
    yout = nc.dram_tensor("yout", [D, T], F32, kind="ExternalOutput").ap()
    nstate = nc.dram_tensor("nstate", [1024, 1024], F32, kind="ExternalOutput").ap()

    with ExitStack() as es:
        S = Sched(nc, es)

        def sb(name, shape, dt):
            return es.enter_context(nc.sbuf_tensor(name, list(shape), dt))

        x = sb("x", [128, KC, T], F32)
        xtok = [Tok("x%d" % c) for c in range(KC)]
        u = sb("u", [128, KC, T], BF16)
        utok = [Tok("u%d" % c) for c in range(KC)]
        yb = sb("ybuf", [128, KC, T], BF16)
        ytok = [Tok("y%d" % c) for c in range(KC)]
        ringbuf = sb("ring", [128, NSLOT, SLOT], BF16)
        ring = Ring(S, ringbuf)
        Fb = [sb("F%d" % i, [128, PW], F32) for i in range(3)]
        Ftok = [Tok("F%d" % i) for i in range(3)]
        Hb = [sb("H%d" % i, [128, T], BF16) for i in range(4)]
        Hb45 = sb("H45", [128, 2, T], BF16)
        Hb += [Hb45[:, 0, :], Hb45[:, 1, :]]
        Hb += [sb("H%d" % i, [128, T], BF16) for i in (6, 7)]
        Htok = [Tok("H%d" % i) for i in range(8)]
        pv = sb("pv", [128, 384], F32)
        pvtok = Tok("pv")
        pvA = sb("pvA", [128, 64], F32)
        modv = sb("modv", [128, 4, 24, 2], F32)
        modtok = Tok("modv")
        ident = sb("ident", [128, 128], F32)
        identb = sb("identb", [128, 128], BF16)
        onesb = sb("onesb", [128, 128], BF16)
        onesm = sb("onesm", [128, 128], BF16)
        onesf = sb("onesf", [128, 128], F32)
        Lfb = sb("Lfb", [128, 2, 128], BF16)
        Lf = Lfb[:, 0, :]
        Lb = Lfb[:, 1, :]
        Uf = sb("Uf", [128, 128], BF16)
        Ub = sb("Ub", [128, 128], BF16)
        ctok = Tok("consts")
        mtmp = sb("mtmp", [128, 128], F32)
        mtmptok = Tok("mtmp")
        A_bc = sb("A_bc", [128, 64], F32)
        D_bc = sb("D_bc", [128, 32], F32)
        dtb_bc = sb("dtb_bc", [128, 64], F32)
        abtok = Tok("abc")
        posc = sb("posc", [128, 4, 64], F32)
        postok = Tok("pos")
        invtab = sb("invtab", [128, 4, 16], F32)
        invtok = Tok("invtab")
        scb = sb("scb", [128, 8, 2], BF16)
        scbtok = Tok("scb")
        dt_tok = sb("dt_tok", [128, 2, NTC, 16], F32)
        a_bf = sb("a_bf", [128, 2, NTC, 16], BF16)
        wde = sb("wde", [128, 2, NTC, 16], F32)
        cd_bc = sb("cd_bc", [128, 2, NTC, 16], F32)
        dttok = Tok("dtstuff")
        Dmat = sb("Dmat", [128, 4, 128], BF16)
        dmtok = Tok("Dmat")
        stin_sem = [S.new_dma_sem(), S.new_dma_sem()]
        x_tok = sb("x_tok", [128, NTC, 256], BF16)
        xtoktok = Tok("x_tok")
        B_tok = sb("B_tok", [128, NTC, 128], BF16)
        btoktok = Tok("B_tok")
        hbuf = sb("hbuf", [128, 2048], F32)
        hst = hbuf[:].bitcast(BF16).rearrange("p (d t q) -> p d t q", d=2, t=8)
        invc_alias = hbuf[:, 0:PW]
        hsttok = Tok("hst")
        hcur = sb("hcur", [128, 256], F32)
        hcurtok = Tok("hcur")
        htmp = sb("htmp", [128, 256], F32)
        htmptok = Tok("htmp")
        NB = 2
        Rb2 = sb("Rb2", [128, 2, 4, 128], BF16)
        Rtok = Tok()
        eA2 = sb("eA2", [128, 2, 4, 128], BF16)
        eAtok = [Tok() for _ in range(2)]
        eE2 = sb("eE2", [128, 2, 4, 128], BF16)
        eEtok = [Tok() for _ in range(2)]
        MT2 = [sb("MT2%d" % q, [128, 2, 4, 128], BF16) for q in range(2)]
        MTtok = [Tok() for q in range(2)]
        Cs2 = [sb("Cs2%d" % q, [128, 2, 4, 128], BF16) for q in range(2)]
        Cstok = [Tok() for q in range(2)]
        xdt2 = [sb("xdt2%d" % q, [128, 2, 4, 64], BF16) for q in range(2)]
        xdttok = [Tok() for q in range(2)]
        h7f = Hb[7][:, :].bitcast(F32)
        hcur2 = h7f[:, 0:256]
        htmp2 = h7f[:, 256:512]
        xdd = [[sb("xdd", [128, 4, 64], BF16)[:], sb("xddb", [128, 4, 64], BF16)[:]],
               [Hb[7][:, 1024:1280].rearrange("p (h q) -> p h q", h=4), Hb[7][:, 1280:1536].rearrange("p (h q) -> p h q", h=4)]]
        xddtok = [[Tok("xdd00"), Tok("xdd01")], [Tok("xdd10"), Tok("xdd11")]]
        hcur2tok = Tok("hcur2")
        htmp2tok = Tok("htmp2")
        CBm2 = sb("CBm2", [128, 2, 128], BF16)
        CBmtok = Tok()
        sto = [sb("sto%d" % i, [128, 2, 128], F32) for i in range(2)]
        stotok = [Tok() for _ in range(2)]
        sto_sem = [S.new_dma_sem() for _ in range(2)]
        xs = [Fb[1][:, 0:D], Fb[2][:, 0:D]]
        xstok = [Ftok[1], Ftok[2]]
        xs_sem = [S.new_dma_sem() for _ in range(2)]
        pst = [Fb[2][:, i * 128:(i + 1) * 128] for i in range(3)]
        cst = Fb[2][0:16, 384:512]
        psttok = Ftok[2]
        pst_sem = S.new_dma_sem()
        misc_sem = S.new_dma_sem()

        pA = es.enter_context(nc.psum_tensor("pA", [128, 1536], F32))
        pB = es.enter_context(nc.psum_tensor("pB", [128, 1536], F32))
        p6 = es.enter_context(nc.psum_tensor("p6", [128, 512], F32))
        p7 = es.enter_context(nc.psum_tensor("p7", [128, 512], F32))
        bk = [Tok("bank%d" % i) for i in range(8)]
        PG = [(pA, bk[0:3]), (pB, bk[3:6])]
        pg_state = [0]
        trB = pB[:, 0:512].bitcast(BF16)
        trP = p6[:, :].bitcast(BF16)
        bank4 = pB[:, 512:1024]
        bank5 = pB[:, 1024:1536]

        def next_pg():
            g = PG[pg_state[0] % 2]
            pg_state[0] += 1
            return g

        def tt(eng, out, in0, in1, op, R, W):
            S.op(eng, lambda e: e.tensor_tensor(out, in0, in1, op), R, W)

        def ts(eng, out, in0, s1, s2, op0, op1, R, W):
            if s2 is None:
                S.op(eng, lambda e: e.tensor_scalar(out, in0, s1, None, op0), R, W)
            else:
                S.op(eng, lambda e: e.tensor_scalar(out, in0, s1, s2, op0, op1), R, W)

        def stt(eng, out, in0, scalar, in1, op0, op1, R, W):
            S.op(eng, lambda e: e.scalar_tensor_tensor(out=out, in0=in0, scalar=scalar, in1=in1, op0=op0, op1=op1), R, W)

        def act(out, in_, func, R, W, bias=None, scale=None):
            kw = {}
            if bias is not None:
                kw["bias"] = bias
            if scale is not None:
                kw["scale"] = scale
            S.op("act", lambda e: e.activation(out=out, in_=in_, func=func, **kw), R, W)

        def cp(eng, out, in_, R, W):
            if eng == "act":
                act(out, in_, AF.Copy, R, W)
            else:
                S.op(eng, lambda e: e.tensor_copy(out, in_), R, W)

        def mm(out, lhsT, rhs, start, stop, R, W, tp=None):
            if tp is None:
                S.op("pe", lambda e: e.matmul(out, lhsT, rhs, start=start, stop=stop), R, W)
            else:
                S.op("pe", lambda e: e.matmul(out, lhsT, rhs, start=start, stop=stop, tile_position=tp), R, W)

        def tr(out, in_, idn, R, W):
            S.op("pe", lambda e: e.transpose(out, in_, idn), R, W)

        def pvc(col):
            return pv[:, col:col + 1]

        def c_bmod(i, m): return i * 24 + m
        def c_lng(i, c): return 96 + i * 8 + c
        def c_caw(j, k, c): return 128 + j * 48 + k * 16 + c
        def c_lnb(i, c): return 128 + 96 + i * 8 + c
        def c_cab(j, c): return 256 + j * 16 + c
        def c_nag(j, c): return 256 + 32 + j * 8 + c
        def c_cbw(j, k, c): return 256 + 48 + j * 24 + k * 8 + c
        def c_psc(j, c): return 256 + 96 + j * 16 + c

        tiles = [(n * 512, 512) for n in range(NT)]

        def emit():
            mod_reset()
            ring.kick()
            def mask(dst, pattern, cm, cmp_):
                S.op("pool", lambda e: e.memset(mtmp[:], 1.0), (), [mtmptok])
                S.op("pool", lambda e: e.affine_select(out=mtmp[:], in_=mtmp[:], pattern=pattern, compare_op=cmp_,
                                                       fill=0.0, base=0, channel_multiplier=cm), (), [mtmptok])
                cp("dve", dst[:], mtmp[:], [mtmptok], [ctok])

            S.op("pool", lambda e: e.memset(ident[:], 0.0), (), [ctok])
            S.op("pool", lambda e: e.affine_select(out=ident[:], in_=ident[:], pattern=[[-1, 128]], compare_op=ALU.not_equal,
                                                   fill=1.0, base=0, channel_multiplier=1), (), [ctok])
            cp("dve", identb[:], ident[:], [ctok], [ctok])
            S.op("pool", lambda e: e.memset(onesb[:], 1.0), (), [ctok])
            S.op("pool", lambda e: e.memset(onesm[:], 1.0 / 1024), (), [ctok])
            S.op("pool", lambda e: e.memset(onesf[:], 1.0 / 1024), (), [ctok])
            mask(Lf, [[1, 128]], -1, ALU.is_ge)
            mask(Lb, [[-1, 128]], 1, ALU.is_ge)
            mask(Uf, [[-1, 128]], 1, ALU.is_gt)
            mask(Ub, [[1, 128]], -1, ALU.is_gt)

            def rows(ap2d):
                return ap2d
            loads = [
                (pst[0][0:96, :], b_mod.rearrange("i (m p) -> (i m) p", p=128)),
                (pst[0][96:128, :], ln_g.rearrange("i (c p) -> (i c) p", p=128)),
                (pst[1][0:96, :], conv_a_w.rearrange("j k (c p) -> (j k c) p", p=128)),
                (pst[1][96:128, :], ln_b.rearrange("i (c p) -> (i c) p", p=128)),
                (pst[2][0:32, :], conv_a_b.rearrange("j (c p) -> (j c) p", p=128)),
                (pst[2][32:48, :], norm_a_g.rearrange("j (c p) -> (j c) p", p=128)),
                (pst[2][48:96, :], conv_b_w.rearrange("j k (c p) -> (j k c) p", p=128)),
                (pst[2][96:128, :], pool_scale.rearrange("j (c p) -> (j c) p", p=128)),
                (cst[:, :], cvec.rearrange("v (k p) -> (v k) p", p=128)),
            ]
            S.dma("sp", loads, writes=[psttok], semi=pst_sem)
            S.dma("sp", [(A_bc[:], a_log.rearrange("j d h -> (j d h)").partition_broadcast(128)),
                         (D_bc[:], d_skip.rearrange("j h -> (j h)").partition_broadcast(128))],
                  writes=[abtok], semi=misc_sem)
            S.dma("sp", [(dtb_bc[:], dt_bias.rearrange("j d h -> (j d h)").partition_broadcast(128))],
                  writes=[abtok], semi=misc_sem)
            for i in range(3):
                tr(pA[:, i * 128:(i + 1) * 128], pst[i][:], ident[:], [psttok, ctok], [bk[0]])
            tr(pA[:, 384:400], cst[:], ident[0:16, 0:16], [psttok, ctok], [bk[0]])
            cp("dve", pv[:], pA[:, 0:384], [], [bk[0], pvtok])
            ts("dve", pvA[:, 0:32], pv[:, 96:128], ALPHA, None, ALU.mult, None, [], [pvtok])
            ts("dve", pvA[:, 32:64], pv[:, 224:256], ALPHA, None, ALU.mult, None, [], [pvtok])
            act(scb[:].rearrange("p k v -> p v k"), pA[:, 384:400].rearrange("p (v k) -> p v k", v=2), AF.Silu,
                [], [bk[0], scbtok])
            act(A_bc[:], A_bc[:], AF.Exp, [], [abtok])
            ts("dve", A_bc[:], A_bc[:], -1.0, None, ALU.mult, None, [], [abtok])

            kidx = Fb[0][:, 0:2]
            pvals = Fb[0][:, 64:128]
            S.op("pool", lambda e: e.iota(kidx, pattern=[[128, 2]], base=0, channel_multiplier=1,
                                          allow_small_or_imprecise_dtypes=True), (), [Ftok[0]])
            S.op("pool", lambda e: e.iota(pvals, pattern=[[1, 64]], base=0, channel_multiplier=0,
                                          allow_small_or_imprecise_dtypes=True), (), [Ftok[0]])
            omega = Fb[0][:, 2:4]
            act(omega, kidx, AF.Exp, [], [Ftok[0]], scale=-math.log(10000.0) / 256.0)
            ang = Fb[1][:, 0:256].rearrange("p (a b) -> p a b", a=4)
            for ch in range(4):
                cc = ch % 2
                ts("dve", ang[:, ch, :], pvals, omega[:, cc:cc + 1], 1.0 / (2 * math.pi), ALU.mult, ALU.mult,
                   [Ftok[0]], [Ftok[1]])
                if ch >= 2:
                    ts("dve", ang[:, ch, :], ang[:, ch, :], 0.25, None, ALU.add, None, [], [Ftok[1]])
            angi = Fb[1][:, 256:512].bitcast(I32)
            angf = Fb[1][:, 512:768]
            cp("dve", angi, Fb[1][:, 0:256], [], [Ftok[1]])
            cp("dve", angf, angi, [], [Ftok[1]])
            tt("dve", Fb[1][:, 0:256], Fb[1][:, 0:256], angf, ALU.subtract, [], [Ftok[1]])
            act(posc[:].rearrange("p a b -> p (a b)"), Fb[1][:, 0:256], AF.Sin, [Ftok[1]], [postok], scale=6.283185)

            ev = Fb[0][:, 128:136]
            S.op("pool", lambda e: e.iota(ev, pattern=[[1, 8]], base=0, channel_multiplier=0,
                                          allow_small_or_imprecise_dtypes=True), (), [Ftok[0]])
            for k, w in enumerate(WINS):
                ts("dve", invtab[:, k, 0:8], ev, float(w // 2), float(w), ALU.add, ALU.min, [Ftok[0]], [invtok])
                ts("dve", invtab[:, k, 8:16], ev, -1.0, float(8 + w // 2), ALU.mult, ALU.add, [Ftok[0]], [invtok])
                ts("dve", invtab[:, k, 8:16], invtab[:, k, 8:16], float(w), None, ALU.min, None, [], [invtok])
            S.op("dve", lambda e: e.reciprocal(invtab[:].rearrange("p a b -> p (a b)"),
                                               invtab[:].rearrange("p a b -> p (a b)")), [], [invtok])

            S.dma("sp", [(x[:, c, :], xin[c * 128:(c + 1) * 128, :]) for c in range(KC)], writes=xtok, semi=xs_sem[0])
            xs4 = x[:, 0:4, 512:T].rearrange("p c (r q) -> p c r q", q=64)
            tt("dve", xs4, xs4, posc[:, :, 0:16].unsqueeze(3).to_broadcast([128, 4, 16, 64]), ALU.add,
               [postok], xtok[0:4])
            xs8 = x[:, 4:8, 512:T].rearrange("p c (r q) -> p c r q", q=64)
            tt("dve", xs8, xs8, posc[:, :, :].unsqueeze(2).to_broadcast([128, 4, 16, 64]), ALU.add,
               [postok], xtok[4:8])

            for _ in range(6):
                mod_unit()

            for i in range(4):
                if i % 2 == 0:
                    even_layer(i)
                else:
                    odd_layer(i)
                layer_norm(i)
                if DEBUG_STOP is not None and i == DEBUG_STOP:
                    break

            S.dma("sp", [(yout[c * 128:(c + 1) * 128, :], x[:, c, :]) for c in range(KC)], reads=xtok, semi=xs_sem[1])

        mod_pending = []

        def mod_reset():
            del mod_pending[:]
            for i in range(4):
                for m4 in range(6):
                    mod_pending.append((i, m4))

        def mod_unit():
            if not mod_pending:
                return
            i, m4 = mod_pending.pop(0)

            def pairs(slot, i=i, m4=m4):
                return [(slot.rearrange("p (k n) -> p k n", k=8),
                         w_mod[i, :, m4 * 512:(m4 + 1) * 512].rearrange("(k p) n -> p k n", p=128))]
            sl, stok = ring.next(pairs)
            w = sl.rearrange("p (k n) -> p k n", k=8)
            for mmi in range(4):
                m = m4 * 4 + mmi
                for kc in range(KC):
                    mm(p7[:, m * 2:(m + 1) * 2], w[:, kc, mmi * 128:(mmi + 1) * 128], scb[:, kc, :],
                       kc == 0, kc == KC - 1, [stok, scbtok], [bk[7]])
            if m4 == 5:
                tt("dve", modv[:, i, :, :], p7[:, 0:48].rearrange("p (m v) -> p m v", v=2),
                   pv[:, i * 24:(i + 1) * 24].unsqueeze(2).to_broadcast([128, 24, 2]), ALU.add,
                   [pvtok], [bk[7], modtok])
                ts("dve", modv[:, i, 8:24, :], modv[:, i, 8:24, :], 1.0, None, ALU.add, None, [], [modtok])
                if i >= 1:
                    ts("dve", modv[:, i, 8:16, :], modv[:, i, 8:16, :], 1.0 / ALPHA, None, ALU.mult, None, [], [modtok])

        def modulate(i):
            for c in range(KC):
                for v, (t0, tl) in enumerate([(0, 512), (512, 1024)]):
                    act(u[:, c, t0:t0 + tl], x[:, c, t0:t0 + tl], AF.Identity, [xtok[c], modtok], [utok[c]],
                        bias=modv[:, i, c, v:v + 1], scale=modv[:, i, 8 + c, v:v + 1])
                if i == 0:
                    ts("dve", x[:, c, :], x[:, c, :], ALPHA, None, ALU.mult, None, [], [xtok[c]])

        def proj_chunk(wv, wtok, ncols, col0, kchunks, rhs_fn, rtoks):
            pg, ptoks = next_pg()
            for n, (t0, tl) in enumerate(tiles):
                for kc in range(kchunks):
                    mm(pg[0:ncols, t0:t0 + tl], wv[:, kc, col0:col0 + ncols], rhs_fn(kc, t0, tl),
                       kc == 0, kc == kchunks - 1, [wtok] + rtoks(kc), [ptoks[n]])
            return pg, ptoks

        def u_rhs(kc, t0, tl):
            return u[:, kc, t0:t0 + tl]

        def u_toks(kc):
            return [utok[kc]]

        def y_rhs(kc, t0, tl):
            return yb[:, kc, t0:t0 + tl]

        def y_toks(kc):
            return [ytok[kc]]

        def conv3(P, ptoks, dstF, dstFtok, w0, w1, w2, bias, srctoks=()):
            if bias is not None:
                act(dstF[:, 0:T], P[:, 0:T], AF.Identity, list(srctoks), list(ptoks) + [dstFtok], bias=bias, scale=w1)
            else:
                act(dstF[:, 0:T], P[:, 0:T], AF.Identity, list(srctoks), list(ptoks) + [dstFtok], bias=0.0, scale=w1)
            for (s0, L) in SEQS:
                stt("dve", dstF[:, s0 + 1:s0 + L], P[:, s0:s0 + L - 1], w0, dstF[:, s0 + 1:s0 + L], ALU.mult, ALU.add,
                    list(srctoks), list(ptoks) + [dstFtok])
                stt("dve", dstF[:, s0:s0 + L - 1], P[:, s0 + 1:s0 + L], w2, dstF[:, s0:s0 + L - 1], ALU.mult, ALU.add,
                    list(srctoks), list(ptoks) + [dstFtok])

        conv_ctr = [0]

        def conv_evac(pg, ptoks):
            k = conv_ctr[0] % 2
            conv_ctr[0] += 1
            praw, prt = (Fb[0], Ftok[0]) if k == 0 else (Fb[2], Ftok[2])
            cp("act", praw[:, 0:T], pg[:, 0:T], [], list(ptoks) + [prt])
            return k

        def conv_taps(k, w0, w1, w2, bias, dst, dsttok):
            praw, prt = (Fb[0], Ftok[0]) if k == 0 else (Fb[2], Ftok[2])
            cvb, cvt = (Fb[1][:, 0:T], Ftok[1]) if k == 0 else (hbuf[:, 0:T], hsttok)
            ts("dve", cvb, praw[:, 0:T], w1, bias, ALU.mult, ALU.add, [prt, pvtok], [cvt])
            for (s0, L) in SEQS:
                stt("dve", cvb[:, s0 + 1:s0 + L], praw[:, s0:s0 + L - 1], w0, cvb[:, s0 + 1:s0 + L], ALU.mult, ALU.add,
                    [prt], [cvt])
                stt("dve", cvb[:, s0:s0 + L - 1], praw[:, s0 + 1:s0 + L], w2, cvb[:, s0:s0 + L - 1], ALU.mult, ALU.add,
                    [prt], [cvt])
            act(dst, cvb, AF.Silu, [cvt], [dsttok])

        def out_proj_partial(i, wdram, row0):
            for half in range(2):
                def pairs(slot, half=half):
                    return [(slot.rearrange("p (k n) -> p k n", k=8),
                             wdram[row0:row0 + 1024, half * 512:(half + 1) * 512].rearrange("(k p) n -> p k n", p=128))]
                sl, stok = ring.next(pairs)
                wv = sl.rearrange("p (k n) -> p k n", k=8)
                for dd in range(4):
                    dc = half * 4 + dd
                    pg, ptoks = proj_chunk(wv, stok, 128, dd * 128, KC, y_rhs, y_toks)
                    for v, (t0, tl) in enumerate([(0, 512), (512, 1024)]):
                        bt = ptoks[0:1] if v == 0 else ptoks[1:3]
                        stt("dve", x[:, dc, t0:t0 + tl], pg[:, t0:t0 + tl], modv[:, i, 16 + dc, v:v + 1],
                            x[:, dc, t0:t0 + tl], ALU.mult, ALU.add, [modtok], bt + [xtok[dc]])
                ring.release()

        def layer_norm(i):
            mean_sb = Fb[0]
            rstd_sb = Fb[1]
            sq = [Fb[2], Fb[2]]
            sqt = [Ftok[2], Ftok[2]]
            for c in range(KC):
                q = c % 2
                xbf, xbft = Hb[q], Htok[q]
                sqf, sqft = Hb[2 + q], Htok[2 + q]
                cp("dve", xbf[:, :], x[:, c, :], [xtok[c]], [xbft])
                act(sqf[:, :], x[:, c, :], AF.Square, [xtok[c]], [sqft])
                for n, (t0, tl) in enumerate(tiles):
                    mm(pA[:, t0:t0 + tl], onesm[:], xbf[:, t0:t0 + tl], c == 0, c == KC - 1, [ctok, xbft], [bk[n]])
                    mm(pB[:, t0:t0 + tl], onesm[:], sqf[:, t0:t0 + tl], c == 0, c == KC - 1, [ctok, sqft], [bk[3 + n]])
            if i < 3:
                for _ in range(6):
                    mod_unit()
            cp("act", mean_sb[:, 0:T], pA[:, 0:T], [], bk[0:3] + [Ftok[0]])
            tt("dve", rstd_sb[:, 0:T], mean_sb[:, 0:T], mean_sb[:, 0:T], ALU.mult, [Ftok[0]], [Ftok[1]])
            tt("dve", rstd_sb[:, 0:T], pB[:, 0:T], rstd_sb[:, 0:T], ALU.subtract, [], bk[3:6] + [Ftok[1]])
            act(rstd_sb[:, 0:T], rstd_sb[:, 0:T], AF.Ln, [], [Ftok[1]], bias=LN_EPS)
            act(rstd_sb[:, 0:T], rstd_sb[:, 0:T], AF.Exp, [], [Ftok[1]], scale=-0.5)
            for c in range(KC):
                tt("dve", x[:, c, :], x[:, c, :], mean_sb[:, 0:T], ALU.subtract, [Ftok[0]], [xtok[c]])
                tt("dve", x[:, c, :], x[:, c, :], rstd_sb[:, 0:T], ALU.mult, [Ftok[1]], [xtok[c]])
                if i < 3:
                    act(x[:, c, :], x[:, c, :], AF.Identity, [pvtok], [xtok[c]],
                        bias=pvA[:, 32 + i * 8 + c:32 + i * 8 + c + 1], scale=pvA[:, i * 8 + c:i * 8 + c + 1])
                else:
                    act(x[:, c, :], x[:, c, :], AF.Identity, [pvtok], [xtok[c]], bias=pvc(c_lnb(i, c)), scale=pvc(c_lng(i, c)))

        def even_layer(i):
            j = i // 2
            modulate(i)
            def pairs_dt(slot):
                return [(slot[:, 0:256].rearrange("p (k n) -> p k n", k=8),
                         w_in_even[j, :, 3072:3104].rearrange("(k p) n -> p k n", p=128))]
            sl, stok = ring.next(pairs_dt)
            wv = sl[:, 0:256].rearrange("p (k n) -> p k n", k=8)
            for tc in range(NTC):
                for kc in range(KC):
                    mm(p6[:, tc * 32:(tc + 1) * 32], u[:, kc, tc * 128:(tc + 1) * 128], wv[:, kc, 0:32],
                       kc == 0, kc == KC - 1, [stok, utok[kc]], [bk[6]])
            ring.release()
            dtmp = wde[:].rearrange("p d t h -> p (d t h)").rearrange("p (t c) -> p t c", c=32)
            tt("dve", dtmp, p6[:, 0:384].rearrange("p (t c) -> p t c", c=32),
               dtb_bc[:, j * 32:(j + 1) * 32].unsqueeze(1).to_broadcast([128, NTC, 32]), ALU.add,
               [abtok], [bk[6], dttok])
            act(dtmp, dtmp, AF.Exp, [], [dttok])
            act(dt_tok[:], dtmp.rearrange("p t (d h) -> p d t h", d=2), AF.Ln, [], [dttok], bias=1.0)
            tt("dve", a_bf[:], dt_tok[:],
               A_bc[:, j * 32:(j + 1) * 32].rearrange("p (d h) -> p d h", d=2).unsqueeze(2).to_broadcast([128, 2, NTC, 16]),
               ALU.mult, [abtok], [dttok])
            mm(p6[:, 0:192], Uf[:], a_bf[:, 0, :, :].rearrange("p t h -> p (t h)"), True, True, [ctok, dttok], [bk[6]])
            mm(p6[:, 192:384], Ub[:], a_bf[:, 1, :, :].rearrange("p t h -> p (t h)"), True, True, [ctok, dttok], [bk[6]])
            mm(p7[:, 0:384], onesb[:], a_bf[:].rearrange("p d t h -> p (d t h)"), True, True, [ctok, dttok], [bk[7]])
            act(wde[:].rearrange("p d t h -> p (d t h)"), p6[:, 0:384], AF.Exp, [], [bk[6], dttok])
            act(cd_bc[:].rearrange("p d t h -> p (d t h)"), p7[:, 0:384], AF.Exp, [], [bk[7], dttok])
            tt("dve", wde[:], wde[:], dt_tok[:], ALU.mult, [], [dttok])

            zs = [Hb[4], Hb[5]]
            zst = [Htok[4], Htok[5]]
            xT = [Hb[0], Hb[1]]
            xTt = [Htok[0], Htok[1]]
            BT, BTt = Hb[2], Htok[2]
            CT, CTt = Hb[3], Htok[3]
            cv, cvt = Fb[1], Ftok[1]
            def tr_block(srcH, srcT, dst_fn, dtok):
                for q in range(3):
                    hq = q % 2
                    for t4 in range(4):
                        tc = q * 4 + t4
                        tr(trP[:, hq * 512 + t4 * 128:hq * 512 + (t4 + 1) * 128], srcH[:, tc * 128:(tc + 1) * 128],
                           identb[:], [srcT, ctok], [bk[6]])
                    cp("act", dst_fn(q), trP[:, hq * 512:(hq + 1) * 512].rearrange("p (t f) -> p t f", t=4),
                       [], [bk[6], dtok])

            for g in range(4):
                def pairsX(slot, g=g):
                    sv = slot.rearrange("p (k n) -> p k n", k=8)
                    return [(sv[:, :, 0:256], w_in_even[j, :, 1024 + g * 256:1024 + (g + 1) * 256].rearrange("(k p) n -> p k n", p=128)),
                            (sv[:, :, 256:384], w_in_even[j, :, 2048 + g * 128:2048 + (g + 1) * 128].rearrange("(k p) n -> p k n", p=128)),
                            (sv[:, :, 384:512], w_in_even[j, :, 2560 + g * 128:2560 + (g + 1) * 128].rearrange("(k p) n -> p k n", p=128))]
                sl, stok = ring.next(pairsX)
                wv = sl.rearrange("p (k n) -> p k n", k=8)
                chunks = [(0, g * 2, xT[0][:, :], xTt[0]), (128, g * 2 + 1, xT[1][:, :], xTt[1]),
                          (256, 8 + g, BT[:, :], BTt), (384, 12 + g, CT[:, :], CTt)]
                kbuf = [None] * 4

                def conv_of(ci):
                    col0, ch, dst, dtok = chunks[ci]
                    conv_taps(kbuf[ci], pvc(c_caw(j, 0, ch)), pvc(c_caw(j, 1, ch)), pvc(c_caw(j, 2, ch)), pvc(c_cab(j, ch)),
                              dst, dtok)

                def tr_x(cc):
                    tr_block(xT[cc], xTt[cc], lambda q, cc=cc: x_tok[:, q * 4:(q + 1) * 4, cc * 128:(cc + 1) * 128], xtoktok)

                for ci in range(4):
                    pg, ptoks = proj_chunk(wv, stok, 128, chunks[ci][0], KC, u_rhs, u_toks)
                    kbuf[ci] = conv_evac(pg, ptoks)
                    if ci >= 1:
                        conv_of(ci - 1)
                    if ci == 3:
                        tr_x(0)
                ring.release()
                def pairsZ(slot, g=g):
                    sv = slot[:, 0:2048].rearrange("p (k n) -> p k n", k=8)
                    return [(sv, w_in_even[j, :, g * 256:(g + 1) * 256].rearrange("(k p) n -> p k n", p=128))]
                sl, stok = ring.next(pairsZ)
                wv = sl[:, 0:2048].rearrange("p (k n) -> p k n", k=8)
                for cc in range(2):
                    pg, ptoks = proj_chunk(wv, stok, 128, cc * 128, KC, u_rhs, u_toks)
                    act(zs[cc][:, :], pg[:, 0:T], AF.Silu, [], ptoks + [zst[cc]])
                    if cc == 0:
                        conv_of(3)
                        tr_x(1)
                    else:
                        tr_block(BT, BTt, lambda q: B_tok[:, q * 4:(q + 1) * 4, :], btoktok)
                ring.release()
                ssd_group(j, g, zs, zst, BT, BTt, CT, CTt)

            for c in range(KC):
                sqb, sqbt = Hb[c % 2], Htok[c % 2]
                act(sqb[:, :], yb[:, c, :], AF.Square, [ytok[c]], [sqbt])
                for n, (t0, tl) in enumerate(tiles):
                    mm(pA[:, t0:t0 + tl], onesm[:], sqb[:, t0:t0 + tl], c == 0, c == KC - 1, [ctok, sqbt], [bk[n]])
            act(Fb[2][:, 0:T], pA[:, 0:T], AF.Ln, [], bk[0:3] + [Ftok[2]], bias=RMS_EPS)
            act(Fb[2][:, 0:T], Fb[2][:, 0:T], AF.Exp, [], [Ftok[2]], scale=-0.5)
            for c in range(KC):
                stt("dve", yb[:, c, :], yb[:, c, :], pvc(c_nag(j, c)), Fb[2][:, 0:T], ALU.mult, ALU.mult,
                    [pvtok, Ftok[2]], [ytok[c]])
            def mixb_part1(c):
                def pairsM(slot, c=c):
                    sv = slot.rearrange("p (k n) -> p k n", k=8)
                    return [(sv[:, :, q * 128:(q + 1) * 128],
                             w_in_even[j, :, 3104 + q * 1024 + c * 128:3104 + q * 1024 + (c + 1) * 128].rearrange("(k p) n -> p k n", p=128))
                            for q in range(4)]
                sl, stok = ring.next(pairsM)
                wv = sl.rearrange("p (k n) -> p k n", k=8)
                sg, sgt = Hb[0], Htok[0]
                pg, ptoks = proj_chunk(wv, stok, 128, 0, KC, u_rhs, u_toks)
                act(sg[:, :], pg[:, 0:T], AF.Silu, [], ptoks + [sgt])
                pg, ptoks = proj_chunk(wv, stok, 128, 128, KC, u_rhs, u_toks)
                tt("dve", Fb[0][:, 0:T], pg[:, 0:T], sg[:, :], ALU.mult, [sgt], ptoks + [Ftok[0]])
                pg, ptoks = proj_chunk(wv, stok, 128, 256, KC, u_rhs, u_toks)
                cp("act", Hb[1][:, :], pg[:, 0:T], [], ptoks + [Htok[1]])
                pg, ptoks = proj_chunk(wv, stok, 128, 384, KC, u_rhs, u_toks)
                tt("dve", Fb[1][:, 0:T], pg[:, 0:T], Hb[1][:, :], ALU.mult, [Htok[1]], ptoks + [Ftok[1]])
                ring.release()

            def mixb_part2(c):
                conv3(Fb[1], [], Fb[2], Ftok[2], pvc(c_cbw(j, 0, c)), pvc(c_cbw(j, 1, c)), pvc(c_cbw(j, 2, c)), None,
                      srctoks=[Ftok[1], pvtok])
                tt("dve", yb[:, c, :], Fb[2][:, 0:T], Fb[0][:, 0:T], ALU.mult, [Ftok[2], Ftok[0]], [ytok[c]])

            mixb_part1(0)
            out_proj_partial(i, w_out_even[j], 0)
            mixb_part2(0)
            for c in range(1, KC):
                mixb_part1(c)
                mixb_part2(c)
            out_proj_partial(i, w_out_even[j], 1024)

        def ssd_group(j, g, zs, zst, BT, BTt, CT, CTt):
            gs = slice(g * 4, g * 4 + 4)
            for h4 in range(4):
                hh = j * 16 + g * 4 + h4
                ts("dve", Dmat[:, h4, :], identb[:], D_bc[:, hh:hh + 1], None, ALU.mult, None, [ctok, abtok], [dmtok])
            HC = [(hcur[:], hcurtok), (hcur2, hcur2tok)]
            HT = [(htmp[:], htmptok), (htmp2, htmp2tok)]

            def v4(ap):
                return ap.rearrange("p (h q) -> p h q", h=4)

            hstP = Hb[6][:, 0:1024].rearrange("p (d s q) -> p d s q", d=2, s=2)

            def hst_slot(si, d, tc):
                if si == 2:
                    return hst[:, d, tc - 4, :], hsttok
                return hstP[:, d, si, :], Htok[6]

            def seq_of(tc):
                si = 0 if tc < 2 else (1 if tc < 4 else 2)
                s0, L = SEQS[si]
                return si, s0 // 128, L // 128

            def state_rounds(si):
                s0, L = SEQS[si]
                nch = L // 128
                tc0 = s0 // 128
                is_sample = (si == 2)
                orders = [list(range(tc0, tc0 + nch)), list(range(tc0 + nch - 1, tc0 - 1, -1))]
                have = [False, False]
                fns = []

                def init():
                    for d in range(2):
                        r0 = (j * 2 + d) * 128
                        S.dma("sp", [(HC[d][0], state[r0:r0 + 128, g * 256:(g + 1) * 256])],
                              writes=[HC[d][1]], semi=stin_sem[d])
                        have[d] = True
                if is_sample:
                    fns.append(init)

                def round_(idx):
                    for d in range(2):
                        tc = orders[d][idx]
                        hc, hct = HC[d]
                        ht, htt = HT[d]
                        par = idx % 2
                        if is_sample:
                            if par == 0:
                                stb = pA[:, 1024:1280] if d == 0 else pA[:, 1280:1536]
                                stk = bk[2]
                            else:
                                stb = bank4[:, 128:384] if d == 0 else bank5[:, 256:512]
                                stk = bk[4] if d == 0 else bk[5]
                        else:
                            if par == 0:
                                stb = bank4[:, 128:384] if d == 0 else bank5[:, 256:512]
                                stk = bk[4] if d == 0 else bk[5]
                            else:
                                stb = p6[:, 0:256] if d == 0 else p7[:, 0:256]
                                stk = bk[6] if d == 0 else bk[7]
                        xdb, xdbt = xdd[d][par], xddtok[d][par]
                        if have[d]:
                            hdst, hdtok = hst_slot(si, d, tc)
                            cp("act", hdst, hc, [hct], [hdtok])
                        if idx == nch - 1 and is_sample:
                            continue
                        tt("dve", xdb, v4(x_tok[:, tc, :]),
                           wde[:, d, tc, gs].unsqueeze(2).to_broadcast([128, 4, 64]), ALU.mult,
                           [xtoktok, dttok], [xdbt])
                        mm(stb, B_tok[:, tc, :], xdb.rearrange("p h q -> p (h q)"), True, True,
                           [btoktok, xdbt], [stk])
                        if have[d]:
                            tt("dve", v4(ht), v4(hc),
                               cd_bc[:, d, tc, gs].unsqueeze(2).to_broadcast([128, 4, 64]), ALU.mult,
                               [hct, dttok], [htt])
                            tt("dve", hc, stb, ht, ALU.add, [htt], [stk, hct])
                        else:
                            cp("dve", hc, stb, [], [stk, hct])
                            have[d] = True
                for idx in range(nch):
                    fns.append(lambda idx=idx: round_(idx))

                def final():
                    for d in range(2):
                        hc, hct = HC[d]
                        stv = sto[d][:].rearrange("p a b -> p (a b)")
                        cp("act", stv, hc, [hct], [stotok[d]])
                        r0 = ((si * 2 + j) * 2 + d) * 128
                        S.dma("sp", [(nstate[r0:r0 + 128, g * 256:(g + 1) * 256], stv)],
                              reads=[stotok[d]], semi=sto_sem[d])
                if not is_sample:
                    fns.append(final)
                return fns

            for si in range(2):
                for fn in state_rounds(si):
                    fn()
            pending_rounds = state_rounds(2)

            if True:
                its = list(range(NTC))
                ARG = [(p6[:], bk[6], p7[:], bk[7]), (pA[:, 0:512], bk[0], pA[:, 512:1024], bk[1])]
                YB = [(bank5, bk[5]), (pB[:, 0:512], bk[3])]

                def stage_pre_a(n):
                    tc = its[n]
                    k0 = tc * 128
                    mm(bank4[:, 0:128], BT[:, k0:k0 + 128], CT[:, k0:k0 + 128], True, True, [BTt, CTt], [bk[4]])
                    tt("dve", Rb2[:], Lfb[:].unsqueeze(2).to_broadcast([128, 2, 4, 128]),
                       a_bf[:, :, tc, gs].unsqueeze(3).to_broadcast([128, 2, 4, 128]), ALU.mult,
                       [ctok, dttok], [Rtok])
                    for d in range(2):
                        Um = Uf if d == 0 else Ub
                        pa, pat, pe_, pet = ARG[d]
                        Rflat = Rb2[:, d].rearrange("p h i -> p (h i)")
                        mm(pa, Um[:], Rflat, True, True, [ctok, Rtok], [pat])
                        mm(pe_, onesb[:], Rflat, True, True, [ctok, Rtok], [pet])

                def stage_pre_b(n):
                    for d in range(2):
                        pa, pat, pe_, pet = ARG[d]
                        act(eA2[:, d].rearrange("p h i -> p (h i)"), pa, AF.Exp, [], [pat, eAtok[d]])
                        act(eE2[:, d].rearrange("p h i -> p (h i)"), pe_, AF.Exp, [], [pet, eEtok[d]])

                def stage_cbm():
                    tt("dve", CBm2[:], bank4[:, 0:128].unsqueeze(1).to_broadcast([128, 2, 128]), Lfb[:], ALU.mult,
                       [ctok], [bk[4], CBmtok])

                def stage_mid(n):
                    tc = its[n]
                    k0 = tc * 128
                    q = n % 2
                    tt("dve", MT2[q][:], eA2[:], CBm2[:].unsqueeze(2).to_broadcast([128, 2, 4, 128]), ALU.mult,
                       [eAtok[0], eAtok[1], CBmtok], [MTtok[q]])
                    tt("dve", Cs2[q][:], eE2[:],
                       CT[:, k0:k0 + 128].unsqueeze(1).unsqueeze(1).to_broadcast([128, 2, 4, 128]), ALU.mult,
                       [eEtok[0], eEtok[1], CTt], [Cstok[q]])
                    tt("dve", xdt2[q][:], v4(x_tok[:, tc, :]).unsqueeze(1).to_broadcast([128, 2, 4, 64]),
                       dt_tok[:, :, tc, gs].unsqueeze(3).to_broadcast([128, 2, 4, 64]), ALU.mult,
                       [xtoktok, dttok], [xdttok[q]])

                def stage_y(n):
                    tc = its[n]
                    q = n % 2
                    yb_, ybt = YB[q]
                    si, tc0, nch = seq_of(tc)
                    has_f = (si == 2) or (tc != tc0)
                    has_b = (si == 2) or (tc != tc0 + nch - 1)
                    hf, hft = hst_slot(si, 0, tc)
                    hb_, hbt = hst_slot(si, 1, tc)
                    for h4 in range(4):
                        po = (h4 % 2) * 64
                        out = yb_[po:po + 64, (h4 // 2) * 128:(h4 // 2) * 128 + 128]
                        tp = (0, 64) if po else None
                        mm(out, xdt2[q][:, 0, h4, :], MT2[q][:, 0, h4, :], True, False, [xdttok[q], MTtok[q]], [ybt], tp)
                        mm(out, xdt2[q][:, 1, h4, :], MT2[q][:, 1, h4, :], False, False, [xdttok[q], MTtok[q]], [ybt], tp)
                        if has_f:
                            mm(out, hf[:, h4 * 64:(h4 + 1) * 64], Cs2[q][:, 0, h4, :], False, False,
                               [hft, Cstok[q]], [ybt], tp)
                        if has_b:
                            mm(out, hb_[:, h4 * 64:(h4 + 1) * 64], Cs2[q][:, 1, h4, :], False, False,
                               [hbt, Cstok[q]], [ybt], tp)
                        mm(out, x_tok[:, tc, h4 * 64:(h4 + 1) * 64], Dmat[:, h4, :], False, True, [xtoktok, dmtok], [ybt], tp)

                def stage_evac(n):
                    tc = its[n]
                    k0 = tc * 128
                    yb_, ybt = YB[n % 2]
                    tt("dve", yb[:, g * 2:g * 2 + 2, k0:k0 + 128], yb_[:, 0:256].rearrange("p (r i) -> p r i", r=2),
                       Hb45[:, :, k0:k0 + 128], ALU.mult, [zst[0], zst[1]], [ybt, ytok[g * 2], ytok[g * 2 + 1]])

                N = len(its)
                stage_pre_a(0)
                stage_pre_b(0)
                stage_cbm()
                for n in range(N):
                    if n == 4:
                        while pending_rounds:
                            pending_rounds.pop(0)()
                    if n + 1 < N:
                        stage_pre_a(n + 1)
                    stage_mid(n)
                    if n + 1 < N:
                        stage_pre_b(n + 1)
                    if n >= 1:
                        stage_evac(n - 1)
                    if n + 1 < N:
                        stage_cbm()
                    stage_y(n)
                    if n < 4:
                        for _ in range(3):
                            if pending_rounds:
                                pending_rounds.pop(0)()
                stage_evac(N - 1)

        def odd_layer(i):
            j = i // 2
            modulate(i)
            S.op("dve", lambda e: e.memset(Fb[0][:], 0.0), (), [Ftok[0]])
            S.op("dve", lambda e: e.memset(Fb[1][:], 0.0), (), [Ftok[1]])
            S.op("dve", lambda e: e.memset(Fb[2][:], 0.0), (), [Ftok[2]])
            for half in range(2):
                for kk in range(2):
                    k = half * 2 + kk
                    w = WINS[k]
                    invc, invct = invc_alias, hsttok
                    S.op("dve", lambda e: e.memset(invc, 1.0 / w), (), [invct])
                    for si, (s0, L) in enumerate(SEQS):
                        o = POFF[si]
                        cp("dve", invc[:, o:o + 8], invtab[:, k, 0:8], [invtok], [invct])
                        cp("dve", invc[:, o + L - 8:o + L], invtab[:, k, 8:16], [invtok], [invct])
                    def pairsV(slot, k=k):
                        return [(slot.rearrange("p (k n) -> p k n", k=8),
                                 w_in_odd[j, :, k * 512:(k + 1) * 512].rearrange("(k p) n -> p k n", p=128))]

                    def pairsG(slot, k=k):
                        return [(slot.rearrange("p (k n) -> p k n", k=8),
                                 w_in_odd[j, :, 2048 + k * 512:2048 + (k + 1) * 512].rearrange("(k p) n -> p k n", p=128))]
                    slV, stokV = ring.next(pairsV)
                    wvV = slV.rearrange("p (k n) -> p k n", k=8)
                    slG, stokG = ring.next(pairsG, prefetch=False)
                    wvG = slG.rearrange("p (k n) -> p k n", k=8)
                    for cc in range(4):
                        pg, ptoks = proj_chunk(wvV, stokV, 128, cc * 128, KC, u_rhs, u_toks)
                        vp, vpt = Fb[0], Ftok[0]
                        for si, (s0, L) in enumerate(SEQS):
                            o = POFF[si]
                            bt = ptoks[0:1] if si < 2 else ptoks[1:3]
                            cp("act", vp[:, o:o + L], pg[:, s0:s0 + L], [], bt + [vpt])
                        pg2, ptoks2 = proj_chunk(wvG, stokG, 128, cc * 128, KC, u_rhs, u_toks)
                        act(Hb[cc][:, :], pg2[:, 0:T], AF.Silu, [], ptoks2 + [Htok[cc]])
                        src, srct = vp, vpt
                        bufs = [(Fb[1], Ftok[1]), (Fb[2], Ftok[2])]
                        sh = [(1, 0), (1, 1), (2, 2), (4, 4)]
                        for lev in range(k + 1):
                            dst, dstt = bufs[lev % 2]
                            a_, b_ = sh[lev]
                            tt("dve", dst[:, 8:PW - 8], src[:, 8 - a_:PW - 8 - a_], src[:, 8 + b_:PW - 8 + b_], ALU.add,
                               [srct], [dstt])
                            src, srct = dst, dstt
                        oth, otht = bufs[(k + 1) % 2]
                        tt("dve", oth[:, 8:PW - 8], src[:, 8:PW - 8], invc[:, 8:PW - 8], ALU.mult, [srct, invct], [otht])
                        for si, (s0, L) in enumerate(SEQS):
                            o = POFF[si]
                            tt("dve", Hb[4 + cc][:, s0:s0 + L], oth[:, o:o + L], vp[:, o:o + L], ALU.subtract,
                               [otht, vpt], [Htok[4 + cc]])
                    def pairsP(slot, k=k):
                        return [(slot[:, 0:2048].rearrange("p (k n) -> p k n", k=4),
                                 w_pool[j, k, :, :].rearrange("(k p) n -> p k n", p=128))]
                    sl, stok = ring.next(pairsP)
                    wv = sl[:, 0:2048].rearrange("p (k n) -> p k n", k=4)
                    for dd in range(4):
                        pg, ptoks = proj_chunk(wv, stok, 128, dd * 128, 4,
                                               lambda kc, t0, tl: Hb[4 + kc][:, t0:t0 + tl], lambda kc: [Htok[4 + kc]])
                        yc = kk * 4 + dd
                        stt("dve", yb[:, yc, :], pg[:, 0:T], pvc(c_psc(j, k * 4 + dd)), Hb[dd][:, :], ALU.mult, ALU.mult,
                            [pvtok, Htok[dd]], ptoks + [ytok[yc]])
                    ring.release()
                out_proj_partial(i, w_out_odd[j], half * 1024)

        S.dry = True
        emit()
        S.dry = False
        S.plan = True
        pg_state[0] = 0
        emit()
        S.plan = False
        S.reset()
        ring.issued = 0
        ring.cur = 0
        pg_state[0] = 0
        emit()
        for semi in list(xs_sem) + list(sto_sem):
            ent = S.dma_sems[semi]
            if ent[1] > 0:
                nc.sync.wait_ge(ent[0], ent[1])
        print("program: insts", S.cnt, "incs", S.semval, "waits", S.nwaits, "pieces", len(ring.plan))
    return nc


_NC_CACHE = {}


def kernel(x_prompt, x_sample, state_ssd, c, c_ctx, w_mod, b_mod, ln_g, ln_b, w_in_even, conv_a_w,
           conv_a_b, a_log, dt_bias, d_skip, norm_a_g, conv_b_w, w_out_even, w_in_odd, w_pool,
           pool_scale, w_out_odd):
    f = lambda a: np.ascontiguousarray(np.asarray(a), dtype=np.float32)
    if "nc" not in _NC_CACHE:
        _NC_CACHE["nc"] = build_program()
    nc = _NC_CACHE["nc"]
    x_prompt = f(x_prompt)
    x_sample = f(x_sample)
    state_ssd = f(state_ssd)
    c = f(c)
    c_ctx = f(c_ctx)
    shared = dict(w_mod=f(w_mod), b_mod=f(b_mod), ln_g=f(ln_g), ln_b=f(ln_b), w_in_even=f(w_in_even),
                  conv_a_w=f(conv_a_w), conv_a_b=f(conv_a_b), a_log=f(a_log), dt_bias=f(dt_bias), d_skip=f(d_skip),
                  norm_a_g=f(norm_a_g), conv_b_w=f(conv_b_w), w_out_even=f(w_out_even), w_in_odd=f(w_in_odd),
                  w_pool=f(w_pool), pool_scale=f(pool_scale), w_out_odd=f(w_out_odd))
    in_maps = []
    for k in range(NCORES):
        m = dict(shared)
        m["xin"] = np.ascontiguousarray(np.concatenate([x_prompt[2 * k], x_prompt[2 * k + 1], x_sample[k]], axis=0).T)
        m["state"] = np.ascontiguousarray(state_ssd[k].transpose(0, 1, 4, 2, 3).reshape(512, 1024))
        m["cvec"] = np.ascontiguousarray(np.stack([c_ctx, c[k]], axis=0))
        in_maps.append(m)
    res = run_bass_kernel_spmd(nc, in_maps, core_ids=list(range(NCORES)))
    y_prompt = np.empty((16, 256, D), np.float32)
    y_sample = np.empty((8, 1024, D), np.float32)
    new_state = np.empty((16, 2, 2, 16, 64, 128), np.float32)
    for k in range(NCORES):
        r = res.results[k]
        yo = np.asarray(r["yout"]).T
        y_prompt[2 * k] = yo[0:256]
        y_prompt[2 * k + 1] = yo[256:512]
        y_sample[k] = yo[512:1536]
        ns = np.asarray(r["nstate"]).reshape(2, 2, 2, 128, 16, 64).transpose(0, 1, 2, 4, 5, 3)
        new_state[2 * k] = ns[0]
        new_state[2 * k + 1] = ns[1]
    return (y_prompt, y_sample, new_state)
```

```python
import math
from contextlib import ExitStack

import numpy as np
import concourse.bass as bass
import concourse.mybir as mybir
from concourse.bass_utils import run_bass_kernel_spmd

F32 = mybir.dt.float32
BF16 = mybir.dt.bfloat16
I32 = mybir.dt.int32
AF = mybir.ActivationFunctionType
ALU = mybir.AluOpType

NCORES = 8
D = 1024
T = 1536
KC = 8
NT = 3
SEQS = [(0, 256), (256, 256), (512, 1024)]
NTC = 12
ALPHA = (2 * 4) ** 0.25
LN_EPS = 1e-5
RMS_EPS = 1e-5
IN_EVEN = 7200
PW = 1600
POFF = [16, 280, 544]
WINS = (2, 4, 8, 16)
NSLOT = 2
SLOT = 4096

DEBUG_STOP = None


class Tok:
    __slots__ = ("w", "r", "name")
    ALL = []

    def __init__(self, name=""):
        self.w = None
        self.r = {}
        self.name = name
        Tok.ALL.append(self)


class Sched:
    ENG = ("pe", "dve", "act", "pool")

    def __init__(self, nc, es):
        self.nc = nc
        self.es = es
        self.engs = {"pe": nc.tensor, "dve": nc.vector, "act": nc.scalar, "pool": nc.gpsimd, "sp": nc.sync}
        self.sem = {k: es.enter_context(nc.semaphore("sem_" + k)) for k in self.ENG}
        self.cnt = {k: 0 for k in self.ENG}
        self.waited = {k: {} for k in list(self.ENG) + ["sp"]}
        self.dma_sems = []
        self.dry = False
        self.plan = False
        self.needed = {k: set() for k in self.ENG}
        self.semval = {k: 0 for k in self.ENG}
        self.valof = {k: {} for k in self.ENG}
        self.nwaits = 0

    def reset(self):
        for t in Tok.ALL:
            t.w = None
            t.r = {}
        self.cnt = {k: 0 for k in self.ENG}
        self.waited = {k: {} for k in list(self.ENG) + ["sp"]}
        self.semval = {k: 0 for k in self.ENG}
        self.valof = {k: {} for k in self.ENG}
        self.nwaits = 0
        for ent in self.dma_sems:
            ent[1] = 0

    def _wait(self, eng, dep):
        if dep is None:
            return
        src, val = dep
        if isinstance(src, str):
            if src == eng and eng == "pe":
                return
            key = src
            sem = self.sem[src]
        else:
            key = ("dma", src.num)
            sem = src
        if self.waited[eng].get(key, 0) >= val:
            return
        self.nwaits += 1
        self.waited[eng][key] = val
        if self.plan:
            if isinstance(src, str):
                self.needed[src].add(val)
            return
        if isinstance(src, str):
            self.engs[eng].wait_ge(sem, self.valof[src][val])
        else:
            self.engs[eng].wait_ge(sem, val)

    def op(self, eng, fn, reads=(), writes=()):
        if self.dry:
            return
        for t in reads:
            self._wait(eng, t.w)
        for t in writes:
            self._wait(eng, t.w)
            for src, val in t.r.items():
                if src == eng and eng == "pe":
                    continue
                self._wait(eng, (src, val))
        self.cnt[eng] += 1
        c = self.cnt[eng]
        if not self.plan:
            ins = fn(self.engs[eng])
            if c in self.needed[eng]:
                ins.then_inc(self.sem[eng], 1)
                self.semval[eng] += 1
                self.valof[eng][c] = self.semval[eng]
        for t in reads:
            t.r[eng] = c
        for t in writes:
            t.w = (eng, c)
            t.r = {}

    def new_dma_sem(self):
        s = self.es.enter_context(self.nc.semaphore("dsem%d" % len(self.dma_sems)))
        self.dma_sems.append([s, 0])
        return len(self.dma_sems) - 1

    def dma(self, q, pairs, reads=(), writes=(), semi=None, **kw):
        if self.dry:
            return
        for t in reads:
            self._wait(q, t.w)
        for t in writes:
            self._wait(q, t.w)
            for src_, val in t.r.items():
                self._wait(q, (src_, val))
        ent = self.dma_sems[semi]
        for (o, i_) in pairs:
            ent[1] += 16
            if not self.plan:
                ins = self.engs[q].dma_start(out=o, in_=i_, **kw)
                ins.then_inc(ent[0], 16)
        dep = (ent[0], ent[1])
        for t in reads:
            t.r[ent[0]] = ent[1]
        for t in writes:
            t.w = dep
            t.r = {}


class Ring:
    def __init__(self, S, buf):
        self.S = S
        self.buf = buf
        self.toks = [Tok("ring%d" % i) for i in range(NSLOT)]
        self.sems = [S.new_dma_sem() for _ in range(NSLOT)]
        self.plan = []
        self.issued = 0
        self.cur = 0

    def slot_ap(self, s):
        return self.buf[:, s, :]

    def _issue(self):
        p = self.issued
        s = p % NSLOT
        pairs = self.plan[p](self.slot_ap(s))
        self.S.dma("pool", pairs, writes=[self.toks[s]], semi=self.sems[s])
        self.issued += 1

    def next(self, pairs_fn, prefetch=True):
        if self.S.dry:
            self.plan.append(pairs_fn)
            return self.slot_ap(0), self.toks[0]
        p = self.cur
        while self.issued < min(len(self.plan), (p + NSLOT) if prefetch else (p + 1)):
            self._issue()
        s = p % NSLOT
        self.cur += 1
        return self.slot_ap(s), self.toks[s]

    def release(self):
        pass

    def kick(self):
        if self.S.dry:
            return
        while self.issued < min(len(self.plan), NSLOT):
            self._issue()


def build_program():
    nc = bass.Bass("TRN2", target_bir_lowering=False)

    def din(name, shape):
        return nc.dram_tensor(name, list(shape), F32, kind="ExternalInput").ap()

    xin = din("xin", [D, T])
    state = din("state", [512, 1024])
    cvec = din("cvec", [2, D])
    w_mod = din("w_mod", [4, D, 3072])
    b_mod = din("b_mod", [4, 3072])
    ln_g = din("ln_g", [4, D])
    ln_b = din("ln_b", [4, D])
    w_in_even = din("w_in_even", [2, D, IN_EVEN])
    conv_a_w = din("conv_a_w", [2, 3, 2048])
    conv_a_b = din("conv_a_b", [2, 2048])
    a_log = din("a_log", [2, 2, 16])
    dt_bias = din("dt_bias", [2, 2, 16])
    d_skip = din("d_skip", [2, 16])
    norm_a_g = din("norm_a_g", [2, D])
    conv_b_w = din("conv_b_w", [2, 3, D])
    w_out_even = din("w_out_even", [2, 2048, D])
    w_in_odd = din("w_in_odd", [2, D, 4096])
    w_pool = din("w_pool", [2, 4, 512, 512])
    pool_scale = din("pool_scale", [2, 2048])
    w_out_odd = din("w_out_odd", [2, 2048, D])
    yout = nc.dram_tensor("yout", [D, T], F32, kind="ExternalOutput").ap()
    nstate = nc.dram_tensor("nstate", [1024, 1024], F32, kind="ExternalOutput").ap()

    with ExitStack() as es:
        S = Sched(nc, es)

        def sb(name, shape, dt):
            return es.enter_context(nc.sbuf_tensor(name, list(shape), dt))

        x = sb("x", [128, KC, T], F32)
        xtok = [Tok("x%d" % c) for c in range(KC)]
        u = sb("u", [128, KC, T], BF16)
        utok = [Tok("u%d" % c) for c in range(KC)]
        yb = sb("ybuf", [128, KC, T], BF16)
        ytok = [Tok("y%d" % c) for c in range(KC)]
        ringbuf = sb("ring", [128, NSLOT, SLOT], BF16)
        ring = Ring(S, ringbuf)
        Fb = [sb("F%d" % i, [128, PW], F32) for i in range(3)]
        Ftok = [Tok("F%d" % i) for i in range(3)]
        Hb = [sb("H%d" % i, [128, T], BF16) for i in range(4)]
        Hb45 = sb("H45", [128, 2, T], BF16)
        Hb += [Hb45[:, 0, :], Hb45[:, 1, :]]
        Hb += [sb("H%d" % i, [128, T], BF16) for i in (6, 7)]
        Htok = [Tok("H%d" % i) for i in range(8)]
        pv = sb("pv", [128, 384], F32)
        pvtok = Tok("pv")
        pvA = sb("pvA", [128, 64], F32)
        modv = sb("modv", [128, 4, 24, 2], F32)
        modtok = Tok("modv")
        ident = sb("ident", [128, 128], F32)
        identb = sb("identb", [128, 128], BF16)
        onesb = sb("onesb", [128, 128], BF16)
        onesm = sb("onesm", [128, 128], BF16)
        onesf = sb("onesf", [128, 128], F32)
        Lfb = sb("Lfb", [128, 2, 128], BF16)
        Lf = Lfb[:, 0, :]
        Lb = Lfb[:, 1, :]
        Uf = sb("Uf", [128, 128], BF16)
        Ub = sb("Ub", [128, 128], BF16)
        ctok = Tok("consts")
        mtmp = sb("mtmp", [128, 128], F32)
        mtmptok = Tok("mtmp")
        A_bc = sb("A_bc", [128, 64], F32)
        D_bc = sb("D_bc", [128, 32], F32)
        dtb_bc = sb("dtb_bc", [128, 64], F32)
        abtok = Tok("abc")
        posc = sb("posc", [128, 4, 64], F32)
        postok = Tok("pos")
        invtab = sb("invtab", [128, 4, 16], F32)
        invtok = Tok("invtab")
        scb = sb("scb", [128, 8, 2], BF16)
        scbtok = Tok("scb")
        dt_tok = sb("dt_tok", [128, 2, NTC, 16], F32)
        a_bf = sb("a_bf", [128, 2, NTC, 16], BF16)
        wde = sb("wde", [128, 2, NTC, 16], F32)
        cd_bc = sb("cd_bc", [128, 2, NTC, 16], F32)
        dttok = Tok("dtstuff")
        Dmat = sb("Dmat", [128, 4, 128], BF16)
        dmtok = Tok("Dmat")
        stin_sem = [S.new_dma_sem(), S.new_dma_sem()]
        x_tok = sb("x_tok", [128, NTC, 256], BF16)
        xtoktok = Tok("x_tok")
        B_tok = sb("B_tok", [128, NTC, 128], BF16)
        btoktok = Tok("B_tok")
        hbuf = sb("hbuf", [128, 2048], F32)
        hst = hbuf[:].bitcast(BF16).rearrange("p (d t q) -> p d t q", d=2, t=8)
        invc_alias = hbuf[:, 0:PW]
        hsttok = Tok("hst")
        hcur = sb("hcur", [128, 256], F32)
        hcurtok = Tok("hcur")
        htmp = sb("htmp", [128, 256], F32)
        htmptok = Tok("htmp")
        NB = 2
        Rb2 = sb("Rb2", [128, 2, 4, 128], BF16)
        Rtok = Tok()
        eA2 = sb("eA2", [128, 2, 4, 128], BF16)
        eAtok = [Tok() for _ in range(2)]
        eE2 = sb("eE2", [128, 2, 4, 128], BF16)
        eEtok = [Tok() for _ in range(2)]
        MT2 = [sb("MT2%d" % q, [128, 2, 4, 128], BF16) for q in range(2)]
        MTtok = [Tok() for q in range(2)]
        Cs2 = [sb("Cs2%d" % q, [128, 2, 4, 128], BF16) for q in range(2)]
        Cstok = [Tok() for q in range(2)]
        xdt2 = [sb("xdt2%d" % q, [128, 2, 4, 64], BF16) for q in range(2)]
        xdttok = [Tok() for q in range(2)]
        h7f = Hb[7][:, :].bitcast(F32)
        hcur2 = h7f[:, 0:256]
        htmp2 = h7f[:, 256:512]
        xdd = [[sb("xdd", [128, 4, 64], BF16)[:], sb("xddb", [128, 4, 64], BF16)[:]],
               [Hb[7][:, 1024:1280].rearrange("p (h q) -> p h q", h=4), Hb[7][:, 1280:1536].rearrange("p (h q) -> p h q", h=4)]]
        xddtok = [[Tok("xdd00"), Tok("xdd01")], [Tok("xdd10"), Tok("xdd11")]]
        hcur2tok = Tok("hcur2")
        htmp2tok = Tok("htmp2")
        CBm2 = sb("CBm2", [128, 2, 128], BF16)
        CBmtok = Tok()
        sto = [sb("sto%d" % i, [128, 2, 128], F32) for i in range(2)]
        stotok = [Tok() for _ in range(2)]
        sto_sem = [S.new_dma_sem() for _ in range(2)]
        xs = [Fb[1][:, 0:D], Fb[2][:, 0:D]]
        xstok = [Ftok[1], Ftok[2]]
        xs_sem = [S.new_dma_sem() for _ in range(2)]
        pst = [Fb[2][:, i * 128:(i + 1) * 128] for i in range(3)]
        cst = Fb[2][0:16, 384:512]
        psttok = Ftok[2]
        pst_sem = S.new_dma_sem()
        misc_sem = S.new_dma_sem()

        pA = es.enter_context(nc.psum_tensor("pA", [128, 1536], F32))
        pB = es.enter_context(nc.psum_tensor("pB", [128, 1536], F32))
        p6 = es.enter_context(nc.psum_tensor("p6", [128, 512], F32))
        p7 = es.enter_context(nc.psum_tensor("p7", [128, 512], F32))
        bk = [Tok("bank%d" % i) for i in range(8)]
        PG = [(pA, bk[0:3]), (pB, bk[3:6])]
        pg_state = [0]
        trB = pB[:, 0:512].bitcast(BF16)
        trP = p6[:, :].bitcast(BF16)
        bank4 = pB[:, 512:1024]
        bank5 = pB[:, 1024:1536]

        def next_pg():
            g = PG[pg_state[0] % 2]
            pg_state[0] += 1
            return g

        def tt(eng, out, in0, in1, op, R, W):
            S.op(eng, lambda e: e.tensor_tensor(out, in0, in1, op), R, W)

        def ts(eng, out, in0, s1, s2, op0, op1, R, W):
            if s2 is None:
                S.op(eng, lambda e: e.tensor_scalar(out, in0, s1, None, op0), R, W)
            else:
                S.op(eng, lambda e: e.tensor_scalar(out, in0, s1, s2, op0, op1), R, W)

        def stt(eng, out, in0, scalar, in1, op0, op1, R, W):
            S.op(eng, lambda e: e.scalar_tensor_tensor(out=out, in0=in0, scalar=scalar, in1=in1, op0=op0, op1=op1), R, W)

        def act(out, in_, func, R, W, bias=None, scale=None):
            kw = {}
            if bias is not None:
                kw["bias"] = bias
            if scale is not None:
                kw["scale"] = scale
            S.op("act", lambda e: e.activation(out=out, in_=in_, func=func, **kw), R, W)

        def cp(eng, out, in_, R, W):
            if eng == "act":
                act(out, in_, AF.Copy, R, W)
            else:
                S.op(eng, lambda e: e.tensor_copy(out, in_), R, W)

        def mm(out, lhsT, rhs, start, stop, R, W, tp=None):
            if tp is None:
                S.op("pe", lambda e: e.matmul(out, lhsT, rhs, start=start, stop=stop), R, W)
            else:
                S.op("pe", lambda e: e.matmul(out, lhsT, rhs, start=start, stop=stop, tile_position=tp), R, W)

        def tr(out, in_, idn, R, W):
            S.op("pe", lambda e: e.transpose(out, in_, idn), R, W)

        def pvc(col):
            return pv[:, col:col + 1]

        def c_bmod(i, m): return i * 24 + m
        def c_lng(i, c): return 96 + i * 8 + c
        def c_caw(j, k, c): return 128 + j * 48 + k * 16 + c
        def c_lnb(i, c): return 128 + 96 + i * 8 + c
        def c_cab(j, c): return 256 + j * 16 + c
        def c_nag(j, c): return 256 + 32 + j * 8 + c
        def c_cbw(j, k, c): return 256 + 48 + j * 24 + k * 8 + c
        def c_psc(j, c): return 256 + 96 + j * 16 + c

        tiles = [(n * 512, 512) for n in range(NT)]

        def emit():
            mod_reset()
            ring.kick()
            def mask(dst, pattern, cm, cmp_):
                S.op("pool", lambda e: e.memset(mtmp[:], 1.0), (), [mtmptok])
                S.op("pool", lambda e: e.affine_select(out=mtmp[:], in_=mtmp[:], pattern=pattern, compare_op=cmp_,
                                                       fill=0.0, base=0, channel_multiplier=cm), (), [mtmptok])
                cp("dve", dst[:], mtmp[:], [mtmptok], [ctok])

            S.op("pool", lambda e: e.memset(ident[:], 0.0), (), [ctok])
            S.op("pool", lambda e: e.affine_select(out=ident[:], in_=ident[:], pattern=[[-1, 128]], compare_op=ALU.not_equal,
                                                   fill=1.0, base=0, channel_multiplier=1), (), [ctok])
            cp("dve", identb[:], ident[:], [ctok], [ctok])
            S.op("pool", lambda e: e.memset(onesb[:], 1.0), (), [ctok])
            S.op("pool", lambda e: e.memset(onesm[:], 1.0 / 1024), (), [ctok])
            S.op("pool", lambda e: e.memset(onesf[:], 1.0 / 1024), (), [ctok])
            mask(Lf, [[1, 128]], -1, ALU.is_ge)
            mask(Lb, [[-1, 128]], 1, ALU.is_ge)
            mask(Uf, [[-1, 128]], 1, ALU.is_gt)
            mask(Ub, [[1, 128]], -1, ALU.is_gt)

            def rows(ap2d):
                return ap2d
            loads = [
                (pst[0][0:96, :], b_mod.rearrange("i (m p) -> (i m) p", p=128)),
                (pst[0][96:128, :], ln_g.rearrange("i (c p) -> (i c) p", p=128)),
                (pst[1][0:96, :], conv_a_w.rearrange("j k (c p) -> (j k c) p", p=128)),
                (pst[1][96:128, :], ln_b.rearrange("i (c p) -> (i c) p", p=128)),
                (pst[2][0:32, :], conv_a_b.rearrange("j (c p) -> (j c) p", p=128)),
                (pst[2][32:48, :], norm_a_g.rearrange("j (c p) -> (j c) p", p=128)),
                (pst[2][48:96, :], conv_b_w.rearrange("j k (c p) -> (j k c) p", p=128)),
                (pst[2][96:128, :], pool_scale.rearrange("j (c p) -> (j c) p", p=128)),
                (cst[:, :], cvec.rearrange("v (k p) -> (v k) p", p=128)),
            ]
            S.dma("sp", loads, writes=[psttok], semi=pst_sem)
            S.dma("sp", [(A_bc[:], a_log.rearrange("j d h -> (j d h)").partition_broadcast(128)),
                         (D_bc[:], d_skip.rearrange("j h -> (j h)").partition_broadcast(128))],
                  writes=[abtok], semi=misc_sem)
            S.dma("sp", [(dtb_bc[:], dt_bias.rearrange("j d h -> (j d h)").partition_broadcast(128))],
                  writes=[abtok], semi=misc_sem)
            for i in range(3):
                tr(pA[:, i * 128:(i + 1) * 128], pst[i][:], ident[:], [psttok, ctok], [bk[0]])
            tr(pA[:, 384:400], cst[:], ident[0:16, 0:16], [psttok, ctok], [bk[0]])
            cp("dve", pv[:], pA[:, 0:384], [], [bk[0], pvtok])
            ts("dve", pvA[:, 0:32], pv[:, 96:128], ALPHA, None, ALU.mult, None, [], [pvtok])
            ts("dve", pvA[:, 32:64], pv[:, 224:256], ALPHA, None, ALU.mult, None, [], [pvtok])
            act(scb[:].rearrange("p k v -> p v k"), pA[:, 384:400].rearrange("p (v k) -> p v k", v=2), AF.Silu,
                [], [bk[0], scbtok])
            act(A_bc[:], A_bc[:], AF.Exp, [], [abtok])
            ts("dve", A_bc[:], A_bc[:], -1.0, None, ALU.mult, None, [], [abtok])

            kidx = Fb[0][:, 0:2]
            pvals = Fb[0][:, 64:128]
            S.op("pool", lambda e: e.iota(kidx, pattern=[[128, 2]], base=0, channel_multiplier=1,
                                          allow_small_or_imprecise_dtypes=True), (), [Ftok[0]])
            S.op("pool", lambda e: e.iota(pvals, pattern=[[1, 64]], base=0, channel_multiplier=0,
                                          allow_small_or_imprecise_dtypes=True), (), [Ftok[0]])
            omega = Fb[0][:, 2:4]
            act(omega, kidx, AF.Exp, [], [Ftok[0]], scale=-math.log(10000.0) / 256.0)
            ang = Fb[1][:, 0:256].rearrange("p (a b) -> p a b", a=4)
            for ch in range(4):
                cc = ch % 2
                ts("dve", ang[:, ch, :], pvals, omega[:, cc:cc + 1], 1.0 / (2 * math.pi), ALU.mult, ALU.mult,
                   [Ftok[0]], [Ftok[1]])
                if ch >= 2:
                    ts("dve", ang[:, ch, :], ang[:, ch, :], 0.25, None, ALU.add, None, [], [Ftok[1]])
            angi = Fb[1][:, 256:512].bitcast(I32)
            angf = Fb[1][:, 512:768]
            cp("dve", angi, Fb[1][:, 0:256], [], [Ftok[1]])
            cp("dve", angf, angi, [], [Ftok[1]])
            tt("dve", Fb[1][:, 0:256], Fb[1][:, 0:256], angf, ALU.subtract, [], [Ftok[1]])
            act(posc[:].rearrange("p a b -> p (a b)"), Fb[1][:, 0:256], AF.Sin, [Ftok[1]], [postok], scale=6.283185)

            ev = Fb[0][:, 128:136]
            S.op("pool", lambda e: e.iota(ev, pattern=[[1, 8]], base=0, channel_multiplier=0,
                                          allow_small_or_imprecise_dtypes=True), (), [Ftok[0]])
            for k, w in enumerate(WINS):
                ts("dve", invtab[:, k, 0:8], ev, float(w // 2), float(w), ALU.add, ALU.min, [Ftok[0]], [invtok])
                ts("dve", invtab[:, k, 8:16], ev, -1.0, float(8 + w // 2), ALU.mult, ALU.add, [Ftok[0]], [invtok])
                ts("dve", invtab[:, k, 8:16], invtab[:, k, 8:16], float(w), None, ALU.min, None, [], [invtok])
            S.op("dve", lambda e: e.reciprocal(invtab[:].rearrange("p a b -> p (a b)"),
                                               invtab[:].rearrange("p a b -> p (a b)")), [], [invtok])

            S.dma("sp", [(x[:, c, :], xin[c * 128:(c + 1) * 128, :]) for c in range(KC)], writes=xtok, semi=xs_sem[0])
            xs4 = x[:, 0:4, 512:T].rearrange("p c (r q) -> p c r q", q=64)
            tt("dve", xs4, xs4, posc[:, :, 0:16].unsqueeze(3).to_broadcast([128, 4, 16, 64]), ALU.add,
               [postok], xtok[0:4])
            xs8 = x[:, 4:8, 512:T].rearrange("p c (r q) -> p c r q", q=64)
            tt("dve", xs8, xs8, posc[:, :, :].unsqueeze(2).to_broadcast([128, 4, 16, 64]), ALU.add,
               [postok], xtok[4:8])

            for _ in range(6):
                mod_unit()

            for i in range(4):
                if i % 2 == 0:
                    even_layer(i)
                else:
                    odd_layer(i)
                layer_norm(i)
                if DEBUG_STOP is not None and i == DEBUG_STOP:
                    break

            if DEBUG_STOP is not None:
                S.dma("sp", [(yout[c * 128:(c + 1) * 128, :], x[:, c, :]) for c in range(KC)], reads=xtok, semi=xs_sem[1])

        mod_pending = []

        def mod_reset():
            del mod_pending[:]
            for i in range(4):
                for m4 in range(6):
                    mod_pending.append((i, m4))

        def mod_unit():
            if not mod_pending:
                return
            i, m4 = mod_pending.pop(0)

            def pairs(slot, i=i, m4=m4):
                return [(slot.rearrange("p (k n) -> p k n", k=8),
                         w_mod[i, :, m4 * 512:(m4 + 1) * 512].rearrange("(k p) n -> p k n", p=128))]
            sl, stok = ring.next(pairs)
            w = sl.rearrange("p (k n) -> p k n", k=8)
            for mmi in range(4):
                m = m4 * 4 + mmi
                for kc in range(KC):
                    mm(p7[:, m * 2:(m + 1) * 2], w[:, kc, mmi * 128:(mmi + 1) * 128], scb[:, kc, :],
                       kc == 0, kc == KC - 1, [stok, scbtok], [bk[7]])
            if m4 == 5:
                tt("dve", modv[:, i, :, :], p7[:, 0:48].rearrange("p (m v) -> p m v", v=2),
                   pv[:, i * 24:(i + 1) * 24].unsqueeze(2).to_broadcast([128, 24, 2]), ALU.add,
                   [pvtok], [bk[7], modtok])
                ts("dve", modv[:, i, 8:24, :], modv[:, i, 8:24, :], 1.0, None, ALU.add, None, [], [modtok])
                if i >= 1:
                    ts("dve", modv[:, i, 8:16, :], modv[:, i, 8:16, :], 1.0 / ALPHA, None, ALU.mult, None, [], [modtok])

        def modulate(i):
            for c in range(KC):
                for v, (t0, tl) in enumerate([(0, 512), (512, 1024)]):
                    act(u[:, c, t0:t0 + tl], x[:, c, t0:t0 + tl], AF.Identity, [xtok[c], modtok], [utok[c]],
                        bias=modv[:, i, c, v:v + 1], scale=modv[:, i, 8 + c, v:v + 1])
                if i == 0:
                    ts("dve", x[:, c, :], x[:, c, :], ALPHA, None, ALU.mult, None, [], [xtok[c]])

        def proj_chunk(wv, wtok, ncols, col0, kchunks, rhs_fn, rtoks):
            pg, ptoks = next_pg()
            for n, (t0, tl) in enumerate(tiles):
                for kc in range(kchunks):
                    mm(pg[0:ncols, t0:t0 + tl], wv[:, kc, col0:col0 + ncols], rhs_fn(kc, t0, tl),
                       kc == 0, kc == kchunks - 1, [wtok] + rtoks(kc), [ptoks[n]])
            return pg, ptoks

        def u_rhs(kc, t0, tl):
            return u[:, kc, t0:t0 + tl]

        def u_toks(kc):
            return [utok[kc]]

        def y_rhs(kc, t0, tl):
            return yb[:, kc, t0:t0 + tl]

        def y_toks(kc):
            return [ytok[kc]]

        def conv3(P, ptoks, dstF, dstFtok, w0, w1, w2, bias, srctoks=()):
            if bias is not None:
                act(dstF[:, 0:T], P[:, 0:T], AF.Identity, list(srctoks), list(ptoks) + [dstFtok], bias=bias, scale=w1)
            else:
                act(dstF[:, 0:T], P[:, 0:T], AF.Identity, list(srctoks), list(ptoks) + [dstFtok], bias=0.0, scale=w1)
            for (s0, L) in SEQS:
                stt("dve", dstF[:, s0 + 1:s0 + L], P[:, s0:s0 + L - 1], w0, dstF[:, s0 + 1:s0 + L], ALU.mult, ALU.add,
                    list(srctoks), list(ptoks) + [dstFtok])
                stt("dve", dstF[:, s0:s0 + L - 1], P[:, s0 + 1:s0 + L], w2, dstF[:, s0:s0 + L - 1], ALU.mult, ALU.add,
                    list(srctoks), list(ptoks) + [dstFtok])

        conv_ctr = [0]

        def conv_evac(pg, ptoks):
            k = conv_ctr[0] % 2
            conv_ctr[0] += 1
            praw, prt = (Fb[0], Ftok[0]) if k == 0 else (Fb[2], Ftok[2])
            cp("act", praw[:, 0:T], pg[:, 0:T], [], list(ptoks) + [prt])
            return k

        def conv_taps(k, w0, w1, w2, bias, dst, dsttok):
            praw, prt = (Fb[0], Ftok[0]) if k == 0 else (Fb[2], Ftok[2])
            cvb, cvt = (Fb[1][:, 0:T], Ftok[1]) if k == 0 else (hbuf[:, 0:T], hsttok)
            ts("dve", cvb, praw[:, 0:T], w1, bias, ALU.mult, ALU.add, [prt, pvtok], [cvt])
            for (s0, L) in SEQS:
                stt("dve", cvb[:, s0 + 1:s0 + L], praw[:, s0:s0 + L - 1], w0, cvb[:, s0 + 1:s0 + L], ALU.mult, ALU.add,
                    [prt], [cvt])
                stt("dve", cvb[:, s0:s0 + L - 1], praw[:, s0 + 1:s0 + L], w2, cvb[:, s0:s0 + L - 1], ALU.mult, ALU.add,
                    [prt], [cvt])
            act(dst, cvb, AF.Silu, [cvt], [dsttok])

        def out_proj_partial(i, wdram, row0):
            for half in range(2):
                def pairs(slot, half=half):
                    return [(slot.rearrange("p (k n) -> p k n", k=8),
                             wdram[row0:row0 + 1024, half * 512:(half + 1) * 512].rearrange("(k p) n -> p k n", p=128))]
                sl, stok = ring.next(pairs)
                wv = sl.rearrange("p (k n) -> p k n", k=8)
                for dd in range(4):
                    dc = half * 4 + dd
                    pg, ptoks = proj_chunk(wv, stok, 128, dd * 128, KC, y_rhs, y_toks)
                    for v, (t0, tl) in enumerate([(0, 512), (512, 1024)]):
                        bt = ptoks[0:1] if v == 0 else ptoks[1:3]
                        stt("dve", x[:, dc, t0:t0 + tl], pg[:, t0:t0 + tl], modv[:, i, 16 + dc, v:v + 1],
                            x[:, dc, t0:t0 + tl], ALU.mult, ALU.add, [modtok], bt + [xtok[dc]])
                ring.release()

        def layer_norm(i):
            mean_sb = Fb[0]
            rstd_sb = Fb[1]
            sq = [Fb[2], Fb[2]]
            sqt = [Ftok[2], Ftok[2]]
            for c in range(KC):
                q = c % 2
                xbf, xbft = Hb[q], Htok[q]
                sqf, sqft = Hb[2 + q], Htok[2 + q]
                cp("dve", xbf[:, :], x[:, c, :], [xtok[c]], [xbft])
                act(sqf[:, :], x[:, c, :], AF.Square, [xtok[c]], [sqft])
                for n, (t0, tl) in enumerate(tiles):
                    mm(pA[:, t0:t0 + tl], onesm[:], xbf[:, t0:t0 + tl], c == 0, c == KC - 1, [ctok, xbft], [bk[n]])
                    mm(pB[:, t0:t0 + tl], onesm[:], sqf[:, t0:t0 + tl], c == 0, c == KC - 1, [ctok, sqft], [bk[3 + n]])
            if i < 3:
                for _ in range(6):
                    mod_unit()
            cp("act", mean_sb[:, 0:T], pA[:, 0:T], [], bk[0:3] + [Ftok[0]])
            tt("dve", rstd_sb[:, 0:T], mean_sb[:, 0:T], mean_sb[:, 0:T], ALU.mult, [Ftok[0]], [Ftok[1]])
            tt("dve", rstd_sb[:, 0:T], pB[:, 0:T], rstd_sb[:, 0:T], ALU.subtract, [], bk[3:6] + [Ftok[1]])
            act(rstd_sb[:, 0:T], rstd_sb[:, 0:T], AF.Ln, [], [Ftok[1]], bias=LN_EPS)
            act(rstd_sb[:, 0:T], rstd_sb[:, 0:T], AF.Exp, [], [Ftok[1]], scale=-0.5)
            for c in range(KC):
                tt("dve", x[:, c, :], x[:, c, :], mean_sb[:, 0:T], ALU.subtract, [Ftok[0]], [xtok[c]])
                tt("dve", x[:, c, :], x[:, c, :], rstd_sb[:, 0:T], ALU.mult, [Ftok[1]], [xtok[c]])
                if i < 3:
                    act(x[:, c, :], x[:, c, :], AF.Identity, [pvtok], [xtok[c]],
                        bias=pvA[:, 32 + i * 8 + c:32 + i * 8 + c + 1], scale=pvA[:, i * 8 + c:i * 8 + c + 1])
                else:
                    act(x[:, c, :], x[:, c, :], AF.Identity, [pvtok], [xtok[c]], bias=pvc(c_lnb(i, c)), scale=pvc(c_lng(i, c)))
                    if DEBUG_STOP is None:
                        S.dma("sp", [(yout[c * 128:(c + 1) * 128, :], x[:, c, :])], reads=[xtok[c]], semi=xs_sem[1])

        def even_layer(i):
            j = i // 2
            modulate(i)
            def pairs_dt(slot):
                return [(slot[:, 0:256].rearrange("p (k n) -> p k n", k=8),
                         w_in_even[j, :, 3072:3104].rearrange("(k p) n -> p k n", p=128))]
            sl, stok = ring.next(pairs_dt)
            wv = sl[:, 0:256].rearrange("p (k n) -> p k n", k=8)
            for tc in range(NTC):
                for kc in range(KC):
                    mm(p6[:, tc * 32:(tc + 1) * 32], u[:, kc, tc * 128:(tc + 1) * 128], wv[:, kc, 0:32],
                       kc == 0, kc == KC - 1, [stok, utok[kc]], [bk[6]])
            ring.release()
            dtmp = wde[:].rearrange("p d t h -> p (d t h)").rearrange("p (t c) -> p t c", c=32)
            tt("dve", dtmp, p6[:, 0:384].rearrange("p (t c) -> p t c", c=32),
               dtb_bc[:, j * 32:(j + 1) * 32].unsqueeze(1).to_broadcast([128, NTC, 32]), ALU.add,
               [abtok], [bk[6], dttok])
            act(dtmp, dtmp, AF.Exp, [], [dttok])
            act(dt_tok[:], dtmp.rearrange("p t (d h) -> p d t h", d=2), AF.Ln, [], [dttok], bias=1.0)
            tt("dve", a_bf[:], dt_tok[:],
               A_bc[:, j * 32:(j + 1) * 32].rearrange("p (d h) -> p d h", d=2).unsqueeze(2).to_broadcast([128, 2, NTC, 16]),
               ALU.mult, [abtok], [dttok])
            mm(p6[:, 0:192], Uf[:], a_bf[:, 0, :, :].rearrange("p t h -> p (t h)"), True, True, [ctok, dttok], [bk[6]])
            mm(p6[:, 192:384], Ub[:], a_bf[:, 1, :, :].rearrange("p t h -> p (t h)"), True, True, [ctok, dttok], [bk[6]])
            mm(p7[:, 0:384], onesb[:], a_bf[:].rearrange("p d t h -> p (d t h)"), True, True, [ctok, dttok], [bk[7]])
            act(wde[:].rearrange("p d t h -> p (d t h)"), p6[:, 0:384], AF.Exp, [], [bk[6], dttok])
            act(cd_bc[:].rearrange("p d t h -> p (d t h)"), p7[:, 0:384], AF.Exp, [], [bk[7], dttok])
            tt("dve", wde[:], wde[:], dt_tok[:], ALU.mult, [], [dttok])

            zs = [Hb[4], Hb[5]]
            zst = [Htok[4], Htok[5]]
            xT = [Hb[0], Hb[1]]
            xTt = [Htok[0], Htok[1]]
            BT, BTt = Hb[2], Htok[2]
            CT, CTt = Hb[3], Htok[3]
            cv, cvt = Fb[1], Ftok[1]
            def tr_block(srcH, srcT, dst_fn, dtok):
                for q in range(3):
                    hq = q % 2
                    for t4 in range(4):
                        tc = q * 4 + t4
                        tr(trP[:, hq * 512 + t4 * 128:hq * 512 + (t4 + 1) * 128], srcH[:, tc * 128:(tc + 1) * 128],
                           identb[:], [srcT, ctok], [bk[6]])
                    cp("act", dst_fn(q), trP[:, hq * 512:(hq + 1) * 512].rearrange("p (t f) -> p t f", t=4),
                       [], [bk[6], dtok])

            for g in range(4):
                def pairsX(slot, g=g):
                    sv = slot.rearrange("p (k n) -> p k n", k=8)
                    return [(sv[:, :, 0:256], w_in_even[j, :, 1024 + g * 256:1024 + (g + 1) * 256].rearrange("(k p) n -> p k n", p=128)),
                            (sv[:, :, 256:384], w_in_even[j, :, 2048 + g * 128:2048 + (g + 1) * 128].rearrange("(k p) n -> p k n", p=128)),
                            (sv[:, :, 384:512], w_in_even[j, :, 2560 + g * 128:2560 + (g + 1) * 128].rearrange("(k p) n -> p k n", p=128))]
                sl, stok = ring.next(pairsX)
                wv = sl.rearrange("p (k n) -> p k n", k=8)
                chunks = [(0, g * 2, xT[0][:, :], xTt[0]), (128, g * 2 + 1, xT[1][:, :], xTt[1]),
                          (256, 8 + g, BT[:, :], BTt), (384, 12 + g, CT[:, :], CTt)]
                kbuf = [None] * 4

                def conv_of(ci):
                    col0, ch, dst, dtok = chunks[ci]
                    conv_taps(kbuf[ci], pvc(c_caw(j, 0, ch)), pvc(c_caw(j, 1, ch)), pvc(c_caw(j, 2, ch)), pvc(c_cab(j, ch)),
                              dst, dtok)

                def tr_x(cc):
                    tr_block(xT[cc], xTt[cc], lambda q, cc=cc: x_tok[:, q * 4:(q + 1) * 4, cc * 128:(cc + 1) * 128], xtoktok)

                for ci in range(4):
                    pg, ptoks = proj_chunk(wv, stok, 128, chunks[ci][0], KC, u_rhs, u_toks)
                    kbuf[ci] = conv_evac(pg, ptoks)
                    if ci >= 1:
                        conv_of(ci - 1)
                    if ci == 3:
                        tr_x(0)
                ring.release()
                def pairsZ(slot, g=g):
                    sv = slot[:, 0:2048].rearrange("p (k n) -> p k n", k=8)
                    return [(sv, w_in_even[j, :, g * 256:(g + 1) * 256].rearrange("(k p) n -> p k n", p=128))]
                sl, stok = ring.next(pairsZ)
                wv = sl[:, 0:2048].rearrange("p (k n) -> p k n", k=8)
                for cc in range(2):
                    pg, ptoks = proj_chunk(wv, stok, 128, cc * 128, KC, u_rhs, u_toks)
                    act(zs[cc][:, :], pg[:, 0:T], AF.Silu, [], ptoks + [zst[cc]])
                    if cc == 0:
                        conv_of(3)
                        tr_x(1)
                    else:
                        tr_block(BT, BTt, lambda q: B_tok[:, q * 4:(q + 1) * 4, :], btoktok)
                ring.release()
                ssd_group(j, g, zs, zst, BT, BTt, CT, CTt)

            for c in range(KC):
                sqb, sqbt = Hb[c % 2], Htok[c % 2]
                act(sqb[:, :], yb[:, c, :], AF.Square, [ytok[c]], [sqbt])
                for n, (t0, tl) in enumerate(tiles):
                    mm(pA[:, t0:t0 + tl], onesm[:], sqb[:, t0:t0 + tl], c == 0, c == KC - 1, [ctok, sqbt], [bk[n]])
            act(Fb[2][:, 0:T], pA[:, 0:T], AF.Ln, [], bk[0:3] + [Ftok[2]], bias=RMS_EPS)
            act(Fb[2][:, 0:T], Fb[2][:, 0:T], AF.Exp, [], [Ftok[2]], scale=-0.5)
            for c in range(KC):
                stt("dve", yb[:, c, :], yb[:, c, :], pvc(c_nag(j, c)), Fb[2][:, 0:T], ALU.mult, ALU.mult,
                    [pvtok, Ftok[2]], [ytok[c]])
            def mixb_part1(c):
                def pairsM(slot, c=c):
                    sv = slot.rearrange("p (k n) -> p k n", k=8)
                    return [(sv[:, :, q * 128:(q + 1) * 128],
                             w_in_even[j, :, 3104 + q * 1024 + c * 128:3104 + q * 1024 + (c + 1) * 128].rearrange("(k p) n -> p k n", p=128))
                            for q in range(4)]
                sl, stok = ring.next(pairsM)
                wv = sl.rearrange("p (k n) -> p k n", k=8)
                sg, sgt = Hb[0], Htok[0]
                pg, ptoks = proj_chunk(wv, stok, 128, 0, KC, u_rhs, u_toks)
                act(sg[:, :], pg[:, 0:T], AF.Silu, [], ptoks + [sgt])
                pg, ptoks = proj_chunk(wv, stok, 128, 128, KC, u_rhs, u_toks)
                tt("dve", Fb[0][:, 0:T], pg[:, 0:T], sg[:, :], ALU.mult, [sgt], ptoks + [Ftok[0]])
                pg, ptoks = proj_chunk(wv, stok, 128, 256, KC, u_rhs, u_toks)
                cp("act", Hb[1][:, :], pg[:, 0:T], [], ptoks + [Htok[1]])
                pg, ptoks = proj_chunk(wv, stok, 128, 384, KC, u_rhs, u_toks)
                tt("dve", Fb[1][:, 0:T], pg[:, 0:T], Hb[1][:, :], ALU.mult, [Htok[1]], ptoks + [Ftok[1]])
                ring.release()

            def mixb_part2(c):
                conv3(Fb[1], [], Fb[2], Ftok[2], pvc(c_cbw(j, 0, c)), pvc(c_cbw(j, 1, c)), pvc(c_cbw(j, 2, c)), None,
                      srctoks=[Ftok[1], pvtok])
                tt("dve", yb[:, c, :], Fb[2][:, 0:T], Fb[0][:, 0:T], ALU.mult, [Ftok[2], Ftok[0]], [ytok[c]])

            mixb_part1(0)
            out_proj_partial(i, w_out_even[j], 0)
            mixb_part2(0)
            for c in range(1, KC):
                mixb_part1(c)
                mixb_part2(c)
            out_proj_partial(i, w_out_even[j], 1024)

        def ssd_group(j, g, zs, zst, BT, BTt, CT, CTt):
            gs = slice(g * 4, g * 4 + 4)
            for h4 in range(4):
                hh = j * 16 + g * 4 + h4
                ts("dve", Dmat[:, h4, :], identb[:], D_bc[:, hh:hh + 1], None, ALU.mult, None, [ctok, abtok], [dmtok])
            HC = [(hcur[:], hcurtok), (hcur2, hcur2tok)]
            HT = [(htmp[:], htmptok), (htmp2, htmp2tok)]

            def v4(ap):
                return ap.rearrange("p (h q) -> p h q", h=4)

            hstP = Hb[6][:, 0:1024].rearrange("p (d s q) -> p d s q", d=2, s=2)

            def hst_slot(si, d, tc):
                if si == 2:
                    return hst[:, d, tc - 4, :], hsttok
                return hstP[:, d, si, :], Htok[6]

            def seq_of(tc):
                si = 0 if tc < 2 else (1 if tc < 4 else 2)
                s0, L = SEQS[si]
                return si, s0 // 128, L // 128

            def state_rounds(si):
                s0, L = SEQS[si]
                nch = L // 128
                tc0 = s0 // 128
                is_sample = (si == 2)
                orders = [list(range(tc0, tc0 + nch)), list(range(tc0 + nch - 1, tc0 - 1, -1))]
                have = [False, False]
                fns = []

                def init():
                    for d in range(2):
                        r0 = (j * 2 + d) * 128
                        S.dma("sp", [(HC[d][0], state[r0:r0 + 128, g * 256:(g + 1) * 256])],
                              writes=[HC[d][1]], semi=stin_sem[d])
                        have[d] = True
                if is_sample:
                    fns.append(init)

                def round_(idx):
                    for d in range(2):
                        tc = orders[d][idx]
                        hc, hct = HC[d]
                        ht, htt = HT[d]
                        par = idx % 2
                        if is_sample:
                            if par == 0:
                                stb = pA[:, 1024:1280] if d == 0 else pA[:, 1280:1536]
                                stk = bk[2]
                            else:
                                stb = bank4[:, 128:384] if d == 0 else bank5[:, 256:512]
                                stk = bk[4] if d == 0 else bk[5]
                        else:
                            if par == 0:
                                stb = bank4[:, 128:384] if d == 0 else bank5[:, 256:512]
                                stk = bk[4] if d == 0 else bk[5]
                            else:
                                stb = p6[:, 0:256] if d == 0 else p7[:, 0:256]
                                stk = bk[6] if d == 0 else bk[7]
                        xdb, xdbt = xdd[d][par], xddtok[d][par]
                        if have[d]:
                            hdst, hdtok = hst_slot(si, d, tc)
                            cp("act", hdst, hc, [hct], [hdtok])
                        if idx == nch - 1 and is_sample:
                            continue
                        tt("dve", xdb, v4(x_tok[:, tc, :]),
                           wde[:, d, tc, gs].unsqueeze(2).to_broadcast([128, 4, 64]), ALU.mult,
                           [xtoktok, dttok], [xdbt])
                        mm(stb, B_tok[:, tc, :], xdb.rearrange("p h q -> p (h q)"), True, True,
                           [btoktok, xdbt], [stk])
                        if have[d]:
                            tt("dve", v4(ht), v4(hc),
                               cd_bc[:, d, tc, gs].unsqueeze(2).to_broadcast([128, 4, 64]), ALU.mult,
                               [hct, dttok], [htt])
                            tt("dve", hc, stb, ht, ALU.add, [htt], [stk, hct])
                        else:
                            cp("dve", hc, stb, [], [stk, hct])
                            have[d] = True
                for idx in range(nch):
                    fns.append(lambda idx=idx: round_(idx))

                def final():
                    for d in range(2):
                        hc, hct = HC[d]
                        stv = sto[d][:].rearrange("p a b -> p (a b)")
                        cp("act", stv, hc, [hct], [stotok[d]])
                        r0 = ((si * 2 + j) * 2 + d) * 128
                        S.dma("sp", [(nstate[r0:r0 + 128, g * 256:(g + 1) * 256], stv)],
                              reads=[stotok[d]], semi=sto_sem[d])
                if not is_sample:
                    fns.append(final)
                return fns

            for si in range(2):
                for fn in state_rounds(si):
                    fn()
            pending_rounds = state_rounds(2)

            if True:
                its = list(range(NTC))
                ARG = [(p6[:], bk[6], p7[:], bk[7]), (pA[:, 0:512], bk[0], pA[:, 512:1024], bk[1])]
                YB = [(bank5, bk[5]), (pB[:, 0:512], bk[3])]

                def stage_pre_a(n):
                    tc = its[n]
                    k0 = tc * 128
                    mm(bank4[:, 0:128], BT[:, k0:k0 + 128], CT[:, k0:k0 + 128], True, True, [BTt, CTt], [bk[4]])
                    tt("dve", Rb2[:], Lfb[:].unsqueeze(2).to_broadcast([128, 2, 4, 128]),
                       a_bf[:, :, tc, gs].unsqueeze(3).to_broadcast([128, 2, 4, 128]), ALU.mult,
                       [ctok, dttok], [Rtok])
                    for d in range(2):
                        Um = Uf if d == 0 else Ub
                        pa, pat, pe_, pet = ARG[d]
                        Rflat = Rb2[:, d].rearrange("p h i -> p (h i)")
                        mm(pa, Um[:], Rflat, True, True, [ctok, Rtok], [pat])
                        mm(pe_, onesb[:], Rflat, True, True, [ctok, Rtok], [pet])

                def stage_pre_b(n):
                    for d in range(2):
                        pa, pat, pe_, pet = ARG[d]
                        act(eA2[:, d].rearrange("p h i -> p (h i)"), pa, AF.Exp, [], [pat, eAtok[d]])
                        act(eE2[:, d].rearrange("p h i -> p (h i)"), pe_, AF.Exp, [], [pet, eEtok[d]])

                def stage_cbm():
                    tt("dve", CBm2[:], bank4[:, 0:128].unsqueeze(1).to_broadcast([128, 2, 128]), Lfb[:], ALU.mult,
                       [ctok], [bk[4], CBmtok])

                def stage_mid(n):
                    tc = its[n]
                    k0 = tc * 128
                    q = n % 2
                    tt("dve", MT2[q][:], eA2[:], CBm2[:].unsqueeze(2).to_broadcast([128, 2, 4, 128]), ALU.mult,
                       [eAtok[0], eAtok[1], CBmtok], [MTtok[q]])
                    tt("dve", Cs2[q][:], eE2[:],
                       CT[:, k0:k0 + 128].unsqueeze(1).unsqueeze(1).to_broadcast([128, 2, 4, 128]), ALU.mult,
                       [eEtok[0], eEtok[1], CTt], [Cstok[q]])
                    tt("dve", xdt2[q][:], v4(x_tok[:, tc, :]).unsqueeze(1).to_broadcast([128, 2, 4, 64]),
                       dt_tok[:, :, tc, gs].unsqueeze(3).to_broadcast([128, 2, 4, 64]), ALU.mult,
                       [xtoktok, dttok], [xdttok[q]])

                def stage_y(n):
                    tc = its[n]
                    q = n % 2
                    yb_, ybt = YB[q]
                    si, tc0, nch = seq_of(tc)
                    has_f = (si == 2) or (tc != tc0)
                    has_b = (si == 2) or (tc != tc0 + nch - 1)
                    hf, hft = hst_slot(si, 0, tc)
                    hb_, hbt = hst_slot(si, 1, tc)
                    for h4 in range(4):
                        po = (h4 % 2) * 64
                        out = yb_[po:po + 64, (h4 // 2) * 128:(h4 // 2) * 128 + 128]
                        tp = (0, 64) if po else None
                        mm(out, xdt2[q][:, 0, h4, :], MT2[q][:, 0, h4, :], True, False, [xdttok[q], MTtok[q]], [ybt], tp)
                        mm(out, xdt2[q][:, 1, h4, :], MT2[q][:, 1, h4, :], False, False, [xdttok[q], MTtok[q]], [ybt], tp)
                        if has_f:
                            mm(out, hf[:, h4 * 64:(h4 + 1) * 64], Cs2[q][:, 0, h4, :], False, False,
                               [hft, Cstok[q]], [ybt], tp)
                        if has_b:
                            mm(out, hb_[:, h4 * 64:(h4 + 1) * 64], Cs2[q][:, 1, h4, :], False, False,
                               [hbt, Cstok[q]], [ybt], tp)
                        mm(out, x_tok[:, tc, h4 * 64:(h4 + 1) * 64], Dmat[:, h4, :], False, True, [xtoktok, dmtok], [ybt], tp)

                def stage_evac(n):
                    tc = its[n]
                    k0 = tc * 128
                    yb_, ybt = YB[n % 2]
                    tt("dve", yb[:, g * 2:g * 2 + 2, k0:k0 + 128], yb_[:, 0:256].rearrange("p (r i) -> p r i", r=2),
                       Hb45[:, :, k0:k0 + 128], ALU.mult, [zst[0], zst[1]], [ybt, ytok[g * 2], ytok[g * 2 + 1]])

                N = len(its)
                stage_pre_a(0)
                stage_pre_b(0)
                stage_cbm()
                for n in range(N):
                    if n == 4:
                        while pending_rounds:
                            pending_rounds.pop(0)()
                    if n + 1 < N:
                        stage_pre_a(n + 1)
                    stage_mid(n)
                    if n + 1 < N:
                        stage_pre_b(n + 1)
                    if n >= 1:
                        stage_evac(n - 1)
                    if n + 1 < N:
                        stage_cbm()
                    stage_y(n)
                    if n < 4:
                        for _ in range(3):
                            if pending_rounds:
                                pending_rounds.pop(0)()
                stage_evac(N - 1)

        def odd_layer(i):
            j = i // 2
            modulate(i)
            S.op("dve", lambda e: e.memset(Fb[0][:], 0.0), (), [Ftok[0]])
            S.op("dve", lambda e: e.memset(Fb[1][:], 0.0), (), [Ftok[1]])
            S.op("dve", lambda e: e.memset(Fb[2][:], 0.0), (), [Ftok[2]])
            for half in range(2):
                for kk in range(2):
                    k = half * 2 + kk
                    w = WINS[k]
                    invc, invct = invc_alias, hsttok
                    S.op("dve", lambda e: e.memset(invc, 1.0 / w), (), [invct])
                    for si, (s0, L) in enumerate(SEQS):
                        o = POFF[si]
                        cp("dve", invc[:, o:o + 8], invtab[:, k, 0:8], [invtok], [invct])
                        cp("dve", invc[:, o + L - 8:o + L], invtab[:, k, 8:16], [invtok], [invct])
                    def pairsV(slot, k=k):
                        return [(slot.rearrange("p (k n) -> p k n", k=8),
                                 w_in_odd[j, :, k * 512:(k + 1) * 512].rearrange("(k p) n -> p k n", p=128))]

                    def pairsG(slot, k=k):
                        return [(slot.rearrange("p (k n) -> p k n", k=8),
                                 w_in_odd[j, :, 2048 + k * 512:2048 + (k + 1) * 512].rearrange("(k p) n -> p k n", p=128))]
                    slV, stokV = ring.next(pairsV)
                    wvV = slV.rearrange("p (k n) -> p k n", k=8)
                    slG, stokG = ring.next(pairsG, prefetch=False)
                    wvG = slG.rearrange("p (k n) -> p k n", k=8)
                    for cc in range(4):
                        pg, ptoks = proj_chunk(wvV, stokV, 128, cc * 128, KC, u_rhs, u_toks)
                        vp, vpt = Fb[0], Ftok[0]
                        for si, (s0, L) in enumerate(SEQS):
                            o = POFF[si]
                            bt = ptoks[0:1] if si < 2 else ptoks[1:3]
                            cp("act", vp[:, o:o + L], pg[:, s0:s0 + L], [], bt + [vpt])
                        pg2, ptoks2 = proj_chunk(wvG, stokG, 128, cc * 128, KC, u_rhs, u_toks)
                        act(Hb[cc][:, :], pg2[:, 0:T], AF.Silu, [], ptoks2 + [Htok[cc]])
                        src, srct = vp, vpt
                        bufs = [(Fb[1], Ftok[1]), (Fb[2], Ftok[2])]
                        sh = [(1, 0), (1, 1), (2, 2), (4, 4)]
                        for lev in range(k + 1):
                            dst, dstt = bufs[lev % 2]
                            a_, b_ = sh[lev]
                            tt("dve", dst[:, 8:PW - 8], src[:, 8 - a_:PW - 8 - a_], src[:, 8 + b_:PW - 8 + b_], ALU.add,
                               [srct], [dstt])
                            src, srct = dst, dstt
                        oth, otht = bufs[(k + 1) % 2]
                        tt("dve", oth[:, 8:PW - 8], src[:, 8:PW - 8], invc[:, 8:PW - 8], ALU.mult, [srct, invct], [otht])
                        for si, (s0, L) in enumerate(SEQS):
                            o = POFF[si]
                            tt("dve", Hb[4 + cc][:, s0:s0 + L], oth[:, o:o + L], vp[:, o:o + L], ALU.subtract,
                               [otht, vpt], [Htok[4 + cc]])
                    def pairsP(slot, k=k):
                        return [(slot[:, 0:2048].rearrange("p (k n) -> p k n", k=4),
                                 w_pool[j, k, :, :].rearrange("(k p) n -> p k n", p=128))]
                    sl, stok = ring.next(pairsP)
                    wv = sl[:, 0:2048].rearrange("p (k n) -> p k n", k=4)
                    for dd in range(4):
                        pg, ptoks = proj_chunk(wv, stok, 128, dd * 128, 4,
                                               lambda kc, t0, tl: Hb[4 + kc][:, t0:t0 + tl], lambda kc: [Htok[4 + kc]])
                        yc = kk * 4 + dd
                        stt("dve", yb[:, yc, :], pg[:, 0:T], pvc(c_psc(j, k * 4 + dd)), Hb[dd][:, :], ALU.mult, ALU.mult,
                            [pvtok, Htok[dd]], ptoks + [ytok[yc]])
                    ring.release()
                out_proj_partial(i, w_out_odd[j], half * 1024)

        S.dry = True
        emit()
        S.dry = False
        S.plan = True
        pg_state[0] = 0
        emit()
        S.plan = False
        S.reset()
        ring.issued = 0
        ring.cur = 0
        pg_state[0] = 0
        emit()
        for semi in list(xs_sem) + list(sto_sem):
            ent = S.dma_sems[semi]
            if ent[1] > 0:
                nc.sync.wait_ge(ent[0], ent[1])
        print("program: insts", S.cnt, "incs", S.semval, "waits", S.nwaits, "pieces", len(ring.plan))
    return nc


_NC_CACHE = {}


def kernel(x_prompt, x_sample, state_ssd, c, c_ctx, w_mod, b_mod, ln_g, ln_b, w_in_even, conv_a_w,
           conv_a_b, a_log, dt_bias, d_skip, norm_a_g, conv_b_w, w_out_even, w_in_odd, w_pool,
           pool_scale, w_out_odd):
    f = lambda a: np.ascontiguousarray(np.asarray(a), dtype=np.float32)
    if "nc" not in _NC_CACHE:
        _NC_CACHE["nc"] = build_program()
    nc = _NC_CACHE["nc"]
    x_prompt = f(x_prompt)
    x_sample = f(x_sample)
    state_ssd = f(state_ssd)
    c = f(c)
    c_ctx = f(c_ctx)
    shared = dict(w_mod=f(w_mod), b_mod=f(b_mod), ln_g=f(ln_g), ln_b=f(ln_b), w_in_even=f(w_in_even),
                  conv_a_w=f(conv_a_w), conv_a_b=f(conv_a_b), a_log=f(a_log), dt_bias=f(dt_bias), d_skip=f(d_skip),
                  norm_a_g=f(norm_a_g), conv_b_w=f(conv_b_w), w_out_even=f(w_out_even), w_in_odd=f(w_in_odd),
                  w_pool=f(w_pool), pool_scale=f(pool_scale), w_out_odd=f(w_out_odd))
    in_maps = []
    for k in range(NCORES):
        m = dict(shared)
        m["xin"] = np.ascontiguousarray(np.concatenate([x_prompt[2 * k], x_prompt[2 * k + 1], x_sample[k]], axis=0).T)
        m["state"] = np.ascontiguousarray(state_ssd[k].transpose(0, 1, 4, 2, 3).reshape(512, 1024))
        m["cvec"] = np.ascontiguousarray(np.stack([c_ctx, c[k]], axis=0))
        in_maps.append(m)
    res = run_bass_kernel_spmd(nc, in_maps, core_ids=list(range(NCORES)))
    y_prompt = np.empty((16, 256, D), np.float32)
    y_sample = np.empty((8, 1024, D), np.float32)
    new_state = np.empty((16, 2, 2, 16, 64, 128), np.float32)
    for k in range(NCORES):
        r = res.results[k]
        yo = np.asarray(r["yout"]).T
        y_prompt[2 * k] = yo[0:256]
        y_prompt[2 * k + 1] = yo[256:512]
        y_sample[k] = yo[512:1536]
        ns = np.asarray(r["nstate"]).reshape(2, 2, 2, 128, 16, 64).transpose(0, 1, 2, 4, 5, 3)
        new_state[2 * k] = ns[0]
        new_state[2 * k + 1] = ns[1]
    return (y_prompt, y_sample, new_state)
```

```python
import math
from contextlib import ExitStack

import numpy as np
import concourse.bass as bass
import concourse.mybir as mybir
from concourse.bass_utils import run_bass_kernel_spmd

F32 = mybir.dt.float32
BF16 = mybir.dt.bfloat16
I32 = mybir.dt.int32
AF = mybir.ActivationFunctionType
ALU = mybir.AluOpType

NCORES = 8
D = 1024
T = 1536
KC = 8
NT = 3
SEQS = [(0, 256), (256, 256), (512, 1024)]
NTC = 12
ALPHA = (2 * 4) ** 0.25
LN_EPS = 1e-5
RMS_EPS = 1e-5
IN_EVEN = 7200
PW = 1600
POFF = [16, 280, 544]
WINS = (2, 4, 8, 16)
NSLOT = 2
SLOT = 4096

DEBUG_STOP = None


class Tok:
    __slots__ = ("w", "r", "name")
    ALL = []

    def __init__(self, name=""):
        self.w = None
        self.r = {}
        self.name = name
        Tok.ALL.append(self)


class Sched:
    ENG = ("pe", "dve", "act", "pool")

    def __init__(self, nc, es):
        self.nc = nc
        self.es = es
        self.engs = {"pe": nc.tensor, "dve": nc.vector, "act": nc.scalar, "pool": nc.gpsimd, "sp": nc.sync}
        self.sem = {k: es.enter_context(nc.semaphore("sem_" + k)) for k in self.ENG}
        self.cnt = {k: 0 for k in self.ENG}
        self.waited = {k: {} for k in list(self.ENG) + ["sp"]}
        self.dma_sems = []
        self.dry = False
        self.plan = False
        self.needed = {k: set() for k in self.ENG}
        self.semval = {k: 0 for k in self.ENG}
        self.valof = {k: {} for k in self.ENG}
        self.nwaits = 0

    def reset(self):
        for t in Tok.ALL:
            t.w = None
            t.r = {}
        self.cnt = {k: 0 for k in self.ENG}
        self.waited = {k: {} for k in list(self.ENG) + ["sp"]}
        self.semval = {k: 0 for k in self.ENG}
        self.valof = {k: {} for k in self.ENG}
        self.nwaits = 0
        for ent in self.dma_sems:
            ent[1] = 0

    def _wait(self, eng, dep):
        if dep is None:
            return
        src, val = dep
        if isinstance(src, str):
            if src == eng and eng == "pe":
                return
            key = src
            sem = self.sem[src]
        else:
            key = ("dma", src.num)
            sem = src
        if self.waited[eng].get(key, 0) >= val:
            return
        self.nwaits += 1
        self.waited[eng][key] = val
        if self.plan:
            if isinstance(src, str):
                self.needed[src].add(val)
            return
        if isinstance(src, str):
            self.engs[eng].wait_ge(sem, self.valof[src][val])
        else:
            self.engs[eng].wait_ge(sem, val)

    def op(self, eng, fn, reads=(), writes=()):
        if self.dry:
            return
        for t in reads:
            self._wait(eng, t.w)
        for t in writes:
            self._wait(eng, t.w)
            for src, val in t.r.items():
                if src == eng and eng == "pe":
                    continue
                self._wait(eng, (src, val))
        self.cnt[eng] += 1
        c = self.cnt[eng]
        if not self.plan:
            ins = fn(self.engs[eng])
            if c in self.needed[eng]:
                ins.then_inc(self.sem[eng], 1)
                self.semval[eng] += 1
                self.valof[eng][c] = self.semval[eng]
        for t in reads:
            t.r[eng] = c
        for t in writes:
            t.w = (eng, c)
            t.r = {}

    def new_dma_sem(self):
        s = self.es.enter_context(self.nc.semaphore("dsem%d" % len(self.dma_sems)))
        self.dma_sems.append([s, 0])
        return len(self.dma_sems) - 1

    def dma(self, q, pairs, reads=(), writes=(), semi=None, **kw):
        if self.dry:
            return
        for t in reads:
            self._wait(q, t.w)
        for t in writes:
            self._wait(q, t.w)
            for src_, val in t.r.items():
                self._wait(q, (src_, val))
        ent = self.dma_sems[semi]
        for (o, i_) in pairs:
            ent[1] += 16
            if not self.plan:
                ins = self.engs[q].dma_start(out=o, in_=i_, **kw)
                ins.then_inc(ent[0], 16)
        dep = (ent[0], ent[1])
        for t in reads:
            t.r[ent[0]] = ent[1]
        for t in writes:
            t.w = dep
            t.r = {}


class Ring:
    def __init__(self, S, buf):
        self.S = S
        self.buf = buf
        self.toks = [Tok("ring%d" % i) for i in range(NSLOT)]
        self.sems = [S.new_dma_sem() for _ in range(NSLOT)]
        self.plan = []
        self.issued = 0
        self.cur = 0

    def slot_ap(self, s):
        return self.buf[:, s, :]

    def _issue(self):
        p = self.issued
        s = p % NSLOT
        pairs = self.plan[p](self.slot_ap(s))
        self.S.dma("pool", pairs, writes=[self.toks[s]], semi=self.sems[s])
        self.issued += 1

    def next(self, pairs_fn, prefetch=True):
        if self.S.dry:
            self.plan.append(pairs_fn)
            return self.slot_ap(0), self.toks[0]
        p = self.cur
        while self.issued < min(len(self.plan), (p + NSLOT) if prefetch else (p + 1)):
            self._issue()
        s = p % NSLOT
        self.cur += 1
        return self.slot_ap(s), self.toks[s]

    def release(self):
        pass

    def kick(self):
        if self.S.dry:
            return
        while self.issued < min(len(self.plan), NSLOT):
            self._issue()


def build_program():
    nc = bass.Bass("TRN2", target_bir_lowering=False)

    def din(name, shape):
        return nc.dram_tensor(name, list(shape), F32, kind="ExternalInput").ap()

    xin = din("xin", [D, T])
    state = din("state", [512, 1024])
    cvec = din("cvec", [2, D])
    w_mod = din("w_mod", [4, D, 3072])
    b_mod = din("b_mod", [4, 3072])
    ln_g = din("ln_g", [4, D])
    ln_b = din("ln_b", [4, D])
    w_in_even = din("w_in_even", [2, D, IN_EVEN])
    conv_a_w = din("conv_a_w", [2, 3, 2048])
    conv_a_b = din("conv_a_b", [2, 2048])
    a_log = din("a_log", [2, 2, 16])
    dt_bias = din("dt_bias", [2, 2, 16])
    d_skip = din("d_skip", [2, 16])
    norm_a_g = din("norm_a_g", [2, D])
    conv_b_w = din("conv_b_w", [2, 3, D])
    w_out_even = din("w_out_even", [2, 2048, D])
    w_in_odd = din("w_in_odd", [2, D, 4096])
    w_pool = din("w_pool", [2, 4, 512, 512])
    pool_scale = din("pool_scale", [2, 2048])
    w_out_odd = din("w_out_odd", [2, 2048, D])
    yout = nc.dram_tensor("yout", [D, T], F32, kind="ExternalOutput").ap()
    nstate = nc.dram_tensor("nstate", [1024, 1024], F32, kind="ExternalOutput").ap()

    with ExitStack() as es:
        S = Sched(nc, es)

        def sb(name, shape, dt):
            return es.enter_context(nc.sbuf_tensor(name, list(shape), dt))

        x = sb("x", [128, KC, T], F32)
        xtok = [Tok("x%d" % c) for c in range(KC)]
        u = sb("u", [128, KC, T], BF16)
        utok = [Tok("u%d" % c) for c in range(KC)]
        yb = sb("ybuf", [128, KC, T], BF16)
        ytok = [Tok("y%d" % c) for c in range(KC)]
        ringbuf = sb("ring", [128, NSLOT, SLOT], BF16)
        ring = Ring(S, ringbuf)
        Fb = [sb("F%d" % i, [128, PW], F32) for i in range(3)]
        Ftok = [Tok("F%d" % i) for i in range(3)]
        Hb = [sb("H%d" % i, [128, T], BF16) for i in range(4)]
        Hb45 = sb("H45", [128, 2, T], BF16)
        Hb += [Hb45[:, 0, :], Hb45[:, 1, :]]
        Hb += [sb("H%d" % i, [128, T], BF16) for i in (6, 7)]
        Htok = [Tok("H%d" % i) for i in range(8)]
        pv = sb("pv", [128, 384], F32)
        pvtok = Tok("pv")
        pvA = sb("pvA", [128, 64], F32)
        modv = sb("modv", [128, 4, 24, 2], F32)
        modtok = Tok("modv")
        ident = sb("ident", [128, 128], F32)
        identb = sb("identb", [128, 128], BF16)
        onesb = sb("onesb", [128, 128], BF16)
        onesm = sb("onesm", [128, 128], BF16)
        onesf = sb("onesf", [128, 128], F32)
        Lfb = sb("Lfb", [128, 2, 128], BF16)
        Lf = Lfb[:, 0, :]
        Lb = Lfb[:, 1, :]
        Uf = sb("Uf", [128, 128], BF16)
        Ub = sb("Ub", [128, 128], BF16)
        ctok = Tok("consts")
        mtmp = sb("mtmp", [128, 128], F32)
        mtmptok = Tok("mtmp")
        A_bc = sb("A_bc", [128, 64], F32)
        D_bc = sb("D_bc", [128, 32], F32)
        dtb_bc = sb("dtb_bc", [128, 64], F32)
        abtok = Tok("abc")
        posc = sb("posc", [128, 4, 64], F32)
        postok = Tok("pos")
        invtab = sb("invtab", [128, 4, 16], F32)
        invtok = Tok("invtab")
        scb = sb("scb", [128, 8, 2], BF16)
        scbtok = Tok("scb")
        dt_tok = sb("dt_tok", [128, 2, NTC, 16], F32)
        a_bf = sb("a_bf", [128, 2, NTC, 16], BF16)
        wde = sb("wde", [128, 2, NTC, 16], F32)
        cd_bc = sb("cd_bc", [128, 2, NTC, 16], F32)
        dttok = Tok("dtstuff")
        Dmat = sb("Dmat", [128, 4, 128], BF16)
        dmtok = Tok("Dmat")
        stin_sem = [S.new_dma_sem(), S.new_dma_sem()]
        x_tok = sb("x_tok", [128, NTC, 256], BF16)
        xtoktok = Tok("x_tok")
        B_tok = sb("B_tok", [128, NTC, 128], BF16)
        btoktok = Tok("B_tok")
        hbuf = sb("hbuf", [128, 2048], F32)
        hst = hbuf[:].bitcast(BF16).rearrange("p (d t q) -> p d t q", d=2, t=8)
        invc_alias = hbuf[:, 0:PW]
        hsttok = Tok("hst")
        hcur = sb("hcur", [128, 256], F32)
        hcurtok = Tok("hcur")
        htmp = sb("htmp", [128, 256], F32)
        htmptok = Tok("htmp")
        NB = 2
        Rb2 = sb("Rb2", [128, 2, 4, 128], BF16)
        Rtok = Tok()
        eA2 = sb("eA2", [128, 2, 4, 128], BF16)
        eAtok = [Tok() for _ in range(2)]
        eE2 = sb("eE2", [128, 2, 4, 128], BF16)
        eEtok = [Tok() for _ in range(2)]
        MT2 = [sb("MT2%d" % q, [128, 2, 4, 128], BF16) for q in range(2)]
        MTtok = [Tok() for q in range(2)]
        Cs2 = [sb("Cs2%d" % q, [128, 2, 4, 128], BF16) for q in range(2)]
        Cstok = [Tok() for q in range(2)]
        xdt2 = [sb("xdt2%d" % q, [128, 2, 4, 64], BF16) for q in range(2)]
        xdttok = [Tok() for q in range(2)]
        h7f = Hb[7][:, :].bitcast(F32)
        hcur2 = h7f[:, 0:256]
        htmp2 = h7f[:, 256:512]
        xdd = [[sb("xdd", [128, 4, 64], BF16)[:], sb("xddb", [128, 4, 64], BF16)[:]],
               [Hb[7][:, 1024:1280].rearrange("p (h q) -> p h q", h=4), Hb[7][:, 1280:1536].rearrange("p (h q) -> p h q", h=4)]]
        xddtok = [[Tok("xdd00"), Tok("xdd01")], [Tok("xdd10"), Tok("xdd11")]]
        hcur2tok = Tok("hcur2")
        htmp2tok = Tok("htmp2")
        CBm2 = sb("CBm2", [128, 2, 128], BF16)
        CBmtok = Tok()
        sto = [sb("sto%d" % i, [128, 2, 128], F32) for i in range(2)]
        stotok = [Tok() for _ in range(2)]
        sto_sem = [S.new_dma_sem() for _ in range(2)]
        xs = [Fb[1][:, 0:D], Fb[2][:, 0:D]]
        xstok = [Ftok[1], Ftok[2]]
        xs_sem = [S.new_dma_sem() for _ in range(2)]
        pst = [Fb[2][:, i * 128:(i + 1) * 128] for i in range(3)]
        cst = Fb[2][0:16, 384:512]
        psttok = Ftok[2]
        pst_sem = S.new_dma_sem()
        misc_sem = S.new_dma_sem()

        pA = es.enter_context(nc.psum_tensor("pA", [128, 1536], F32))
        pB = es.enter_context(nc.psum_tensor("pB", [128, 1536], F32))
        p6 = es.enter_context(nc.psum_tensor("p6", [128, 512], F32))
        p7 = es.enter_context(nc.psum_tensor("p7", [128, 512], F32))
        bk = [Tok("bank%d" % i) for i in range(8)]
        PG = [(pA, bk[0:3]), (pB, bk[3:6])]
        pg_state = [0]
        trB = pB[:, 0:512].bitcast(BF16)
        trP = p6[:, :].bitcast(BF16)
        bank4 = pB[:, 512:1024]
        bank5 = pB[:, 1024:1536]

        def next_pg():
            g = PG[pg_state[0] % 2]
            pg_state[0] += 1
            return g

        def tt(eng, out, in0, in1, op, R, W):
            S.op(eng, lambda e: e.tensor_tensor(out, in0, in1, op), R, W)

        def ts(eng, out, in0, s1, s2, op0, op1, R, W):
            if s2 is None:
                S.op(eng, lambda e: e.tensor_scalar(out, in0, s1, None, op0), R, W)
            else:
                S.op(eng, lambda e: e.tensor_scalar(out, in0, s1, s2, op0, op1), R, W)

        def stt(eng, out, in0, scalar, in1, op0, op1, R, W):
            S.op(eng, lambda e: e.scalar_tensor_tensor(out=out, in0=in0, scalar=scalar, in1=in1, op0=op0, op1=op1), R, W)

        def act(out, in_, func, R, W, bias=None, scale=None):
            kw = {}
            if bias is not None:
                kw["bias"] = bias
            if scale is not None:
                kw["scale"] = scale
            S.op("act", lambda e: e.activation(out=out, in_=in_, func=func, **kw), R, W)

        def cp(eng, out, in_, R, W):
            if eng == "act":
                act(out, in_, AF.Copy, R, W)
            else:
                S.op(eng, lambda e: e.tensor_copy(out, in_), R, W)

        def mm(out, lhsT, rhs, start, stop, R, W, tp=None):
            if tp is None:
                S.op("pe", lambda e: e.matmul(out, lhsT, rhs, start=start, stop=stop), R, W)
            else:
                S.op("pe", lambda e: e.matmul(out, lhsT, rhs, start=start, stop=stop, tile_position=tp), R, W)

        def tr(out, in_, idn, R, W):
            S.op("pe", lambda e: e.transpose(out, in_, idn), R, W)

        def pvc(col):
            return pv[:, col:col + 1]

        def c_bmod(i, m): return i * 24 + m
        def c_lng(i, c): return 96 + i * 8 + c
        def c_caw(j, k, c): return 128 + j * 48 + k * 16 + c
        def c_lnb(i, c): return 128 + 96 + i * 8 + c
        def c_cab(j, c): return 256 + j * 16 + c
        def c_nag(j, c): return 256 + 32 + j * 8 + c
        def c_cbw(j, k, c): return 256 + 48 + j * 24 + k * 8 + c
        def c_psc(j, c): return 256 + 96 + j * 16 + c

        tiles = [(n * 512, 512) for n in range(NT)]

        def emit():
            mod_reset()
            ring.kick()
            def mask(dst, pattern, cm, cmp_):
                S.op("pool", lambda e: e.memset(mtmp[:], 1.0), (), [mtmptok])
                S.op("pool", lambda e: e.affine_select(out=mtmp[:], in_=mtmp[:], pattern=pattern, compare_op=cmp_,
                                                       fill=0.0, base=0, channel_multiplier=cm), (), [mtmptok])
                cp("dve", dst[:], mtmp[:], [mtmptok], [ctok])

            S.op("pool", lambda e: e.memset(ident[:], 0.0), (), [ctok])
            S.op("pool", lambda e: e.affine_select(out=ident[:], in_=ident[:], pattern=[[-1, 128]], compare_op=ALU.not_equal,
                                                   fill=1.0, base=0, channel_multiplier=1), (), [ctok])
            cp("dve", identb[:], ident[:], [ctok], [ctok])
            S.op("pool", lambda e: e.memset(onesb[:], 1.0), (), [ctok])
            S.op("pool", lambda e: e.memset(onesm[:], 1.0 / 1024), (), [ctok])
            S.op("pool", lambda e: e.memset(onesf[:], 1.0 / 1024), (), [ctok])
            mask(Lf, [[1, 128]], -1, ALU.is_ge)
            mask(Lb, [[-1, 128]], 1, ALU.is_ge)
            mask(Uf, [[-1, 128]], 1, ALU.is_gt)
            mask(Ub, [[1, 128]], -1, ALU.is_gt)

            def rows(ap2d):
                return ap2d
            loads = [
                (pst[0][0:96, :], b_mod.rearrange("i (m p) -> (i m) p", p=128)),
                (pst[0][96:128, :], ln_g.rearrange("i (c p) -> (i c) p", p=128)),
                (pst[1][0:96, :], conv_a_w.rearrange("j k (c p) -> (j k c) p", p=128)),
                (pst[1][96:128, :], ln_b.rearrange("i (c p) -> (i c) p", p=128)),
                (pst[2][0:32, :], conv_a_b.rearrange("j (c p) -> (j c) p", p=128)),
                (pst[2][32:48, :], norm_a_g.rearrange("j (c p) -> (j c) p", p=128)),
                (pst[2][48:96, :], conv_b_w.rearrange("j k (c p) -> (j k c) p", p=128)),
                (pst[2][96:128, :], pool_scale.rearrange("j (c p) -> (j c) p", p=128)),
                (cst[:, :], cvec.rearrange("v (k p) -> (v k) p", p=128)),
            ]
            S.dma("sp", loads, writes=[psttok], semi=pst_sem)
            S.dma("sp", [(A_bc[:], a_log.rearrange("j d h -> (j d h)").partition_broadcast(128)),
                         (D_bc[:], d_skip.rearrange("j h -> (j h)").partition_broadcast(128))],
                  writes=[abtok], semi=misc_sem)
            S.dma("sp", [(dtb_bc[:], dt_bias.rearrange("j d h -> (j d h)").partition_broadcast(128))],
                  writes=[abtok], semi=misc_sem)
            for i in range(3):
                tr(pA[:, i * 128:(i + 1) * 128], pst[i][:], ident[:], [psttok, ctok], [bk[0]])
            tr(pA[:, 384:400], cst[:], ident[0:16, 0:16], [psttok, ctok], [bk[0]])
            cp("dve", pv[:], pA[:, 0:384], [], [bk[0], pvtok])
            ts("dve", pvA[:, 0:32], pv[:, 96:128], ALPHA, None, ALU.mult, None, [], [pvtok])
            ts("dve", pvA[:, 32:64], pv[:, 224:256], ALPHA, None, ALU.mult, None, [], [pvtok])
            act(scb[:].rearrange("p k v -> p v k"), pA[:, 384:400].rearrange("p (v k) -> p v k", v=2), AF.Silu,
                [], [bk[0], scbtok])
            act(A_bc[:], A_bc[:], AF.Exp, [], [abtok])
            ts("dve", A_bc[:], A_bc[:], -1.0, None, ALU.mult, None, [], [abtok])

            kidx = Fb[0][:, 0:2]
            pvals = Fb[0][:, 64:128]
            S.op("pool", lambda e: e.iota(kidx, pattern=[[128, 2]], base=0, channel_multiplier=1,
                                          allow_small_or_imprecise_dtypes=True), (), [Ftok[0]])
            S.op("pool", lambda e: e.iota(pvals, pattern=[[1, 64]], base=0, channel_multiplier=0,
                                          allow_small_or_imprecise_dtypes=True), (), [Ftok[0]])
            omega = Fb[0][:, 2:4]
            act(omega, kidx, AF.Exp, [], [Ftok[0]], scale=-math.log(10000.0) / 256.0)
            ang = Fb[1][:, 0:256].rearrange("p (a b) -> p a b", a=4)
            for ch in range(4):
                cc = ch % 2
                ts("dve", ang[:, ch, :], pvals, omega[:, cc:cc + 1], 1.0 / (2 * math.pi), ALU.mult, ALU.mult,
                   [Ftok[0]], [Ftok[1]])
                if ch >= 2:
                    ts("dve", ang[:, ch, :], ang[:, ch, :], 0.25, None, ALU.add, None, [], [Ftok[1]])
            angi = Fb[1][:, 256:512].bitcast(I32)
            angf = Fb[1][:, 512:768]
            cp("dve", angi, Fb[1][:, 0:256], [], [Ftok[1]])
            cp("dve", angf, angi, [], [Ftok[1]])
            tt("dve", Fb[1][:, 0:256], Fb[1][:, 0:256], angf, ALU.subtract, [], [Ftok[1]])
            act(posc[:].rearrange("p a b -> p (a b)"), Fb[1][:, 0:256], AF.Sin, [Ftok[1]], [postok], scale=6.283185)

            ev = Fb[0][:, 128:136]
            S.op("pool", lambda e: e.iota(ev, pattern=[[1, 8]], base=0, channel_multiplier=0,
                                          allow_small_or_imprecise_dtypes=True), (), [Ftok[0]])
            for k, w in enumerate(WINS):
                ts("dve", invtab[:, k, 0:8], ev, float(w // 2), float(w), ALU.add, ALU.min, [Ftok[0]], [invtok])
                ts("dve", invtab[:, k, 8:16], ev, -1.0, float(8 + w // 2), ALU.mult, ALU.add, [Ftok[0]], [invtok])
                ts("dve", invtab[:, k, 8:16], invtab[:, k, 8:16], float(w), None, ALU.min, None, [], [invtok])
            S.op("dve", lambda e: e.reciprocal(invtab[:].rearrange("p a b -> p (a b)"),
                                               invtab[:].rearrange("p a b -> p (a b)")), [], [invtok])

            S.dma("sp", [(x[:, c, :], xin[c * 128:(c + 1) * 128, :]) for c in range(KC)], writes=xtok, semi=xs_sem[0])
            xs4 = x[:, 0:4, 512:T].rearrange("p c (r q) -> p c r q", q=64)
            tt("dve", xs4, xs4, posc[:, :, 0:16].unsqueeze(3).to_broadcast([128, 4, 16, 64]), ALU.add,
               [postok], xtok[0:4])
            xs8 = x[:, 4:8, 512:T].rearrange("p c (r q) -> p c r q", q=64)
            tt("dve", xs8, xs8, posc[:, :, :].unsqueeze(2).to_broadcast([128, 4, 16, 64]), ALU.add,
               [postok], xtok[4:8])

            for _ in range(6):
                mod_unit()

            for i in range(4):
                if i % 2 == 0:
                    even_layer(i)
                else:
                    odd_layer(i)
                layer_norm(i)
                if DEBUG_STOP is not None and i == DEBUG_STOP:
                    break

            if DEBUG_STOP is not None:
                S.dma("sp", [(yout[c * 128:(c + 1) * 128, :], x[:, c, :]) for c in range(KC)], reads=xtok, semi=xs_sem[1])

        mod_pending = []

        def mod_reset():
            del mod_pending[:]
            for i in range(4):
                for m4 in range(6):
                    mod_pending.append((i, m4))

        def mod_unit():
            if not mod_pending:
                return
            i, m4 = mod_pending.pop(0)

            def pairs(slot, i=i, m4=m4):
                return [(slot.rearrange("p (k n) -> p k n", k=8),
                         w_mod[i, :, m4 * 512:(m4 + 1) * 512].rearrange("(k p) n -> p k n", p=128))]
            sl, stok = ring.next(pairs)
            w = sl.rearrange("p (k n) -> p k n", k=8)
            for mmi in range(4):
                m = m4 * 4 + mmi
                for kc in range(KC):
                    mm(p7[:, m * 2:(m + 1) * 2], w[:, kc, mmi * 128:(mmi + 1) * 128], scb[:, kc, :],
                       kc == 0, kc == KC - 1, [stok, scbtok], [bk[7]])
            if m4 == 5:
                tt("dve", modv[:, i, :, :], p7[:, 0:48].rearrange("p (m v) -> p m v", v=2),
                   pv[:, i * 24:(i + 1) * 24].unsqueeze(2).to_broadcast([128, 24, 2]), ALU.add,
                   [pvtok], [bk[7], modtok])
                ts("dve", modv[:, i, 8:24, :], modv[:, i, 8:24, :], 1.0, None, ALU.add, None, [], [modtok])
                if i >= 1:
                    ts("dve", modv[:, i, 8:16, :], modv[:, i, 8:16, :], 1.0 / ALPHA, None, ALU.mult, None, [], [modtok])

        def modulate(i):
            for c in range(KC):
                for v, (t0, tl) in enumerate([(0, 512), (512, 1024)]):
                    act(u[:, c, t0:t0 + tl], x[:, c, t0:t0 + tl], AF.Identity, [xtok[c], modtok], [utok[c]],
                        bias=modv[:, i, c, v:v + 1], scale=modv[:, i, 8 + c, v:v + 1])
                if i == 0:
                    ts("dve", x[:, c, :], x[:, c, :], ALPHA, None, ALU.mult, None, [], [xtok[c]])

        def proj_chunk(wv, wtok, ncols, col0, kchunks, rhs_fn, rtoks):
            pg, ptoks = next_pg()
            for n, (t0, tl) in enumerate(tiles):
                for kc in range(kchunks):
                    mm(pg[0:ncols, t0:t0 + tl], wv[:, kc, col0:col0 + ncols], rhs_fn(kc, t0, tl),
                       kc == 0, kc == kchunks - 1, [wtok] + rtoks(kc), [ptoks[n]])
            return pg, ptoks

        def u_rhs(kc, t0, tl):
            return u[:, kc, t0:t0 + tl]

        def u_toks(kc):
            return [utok[kc]]

        def y_rhs(kc, t0, tl):
            return yb[:, kc, t0:t0 + tl]

        def y_toks(kc):
            return [ytok[kc]]

        def conv3(P, ptoks, dstF, dstFtok, w0, w1, w2, bias, srctoks=()):
            if bias is not None:
                act(dstF[:, 0:T], P[:, 0:T], AF.Identity, list(srctoks), list(ptoks) + [dstFtok], bias=bias, scale=w1)
            else:
                act(dstF[:, 0:T], P[:, 0:T], AF.Identity, list(srctoks), list(ptoks) + [dstFtok], bias=0.0, scale=w1)
            for (s0, L) in SEQS:
                stt("dve", dstF[:, s0 + 1:s0 + L], P[:, s0:s0 + L - 1], w0, dstF[:, s0 + 1:s0 + L], ALU.mult, ALU.add,
                    list(srctoks), list(ptoks) + [dstFtok])
                stt("dve", dstF[:, s0:s0 + L - 1], P[:, s0 + 1:s0 + L], w2, dstF[:, s0:s0 + L - 1], ALU.mult, ALU.add,
                    list(srctoks), list(ptoks) + [dstFtok])

        conv_ctr = [0]

        def conv_evac(pg, ptoks):
            k = conv_ctr[0] % 2
            conv_ctr[0] += 1
            praw, prt = (Fb[0], Ftok[0]) if k == 0 else (Fb[2], Ftok[2])
            cp("act", praw[:, 0:T], pg[:, 0:T], [], list(ptoks) + [prt])
            return k

        def conv_taps(k, w0, w1, w2, bias, dst, dsttok):
            praw, prt = (Fb[0], Ftok[0]) if k == 0 else (Fb[2], Ftok[2])
            cvb, cvt = (Fb[1][:, 0:T], Ftok[1]) if k == 0 else (hbuf[:, 0:T], hsttok)
            ts("dve", cvb, praw[:, 0:T], w1, bias, ALU.mult, ALU.add, [prt, pvtok], [cvt])
            for (s0, L) in SEQS:
                stt("dve", cvb[:, s0 + 1:s0 + L], praw[:, s0:s0 + L - 1], w0, cvb[:, s0 + 1:s0 + L], ALU.mult, ALU.add,
                    [prt], [cvt])
                stt("dve", cvb[:, s0:s0 + L - 1], praw[:, s0 + 1:s0 + L], w2, cvb[:, s0:s0 + L - 1], ALU.mult, ALU.add,
                    [prt], [cvt])
            act(dst, cvb, AF.Silu, [cvt], [dsttok])

        def out_proj_partial(i, wdram, row0):
            for half in range(2):
                def pairs(slot, half=half):
                    return [(slot.rearrange("p (k n) -> p k n", k=8),
                             wdram[row0:row0 + 1024, half * 512:(half + 1) * 512].rearrange("(k p) n -> p k n", p=128))]
                sl, stok = ring.next(pairs)
                wv = sl.rearrange("p (k n) -> p k n", k=8)
                for dd in range(4):
                    dc = half * 4 + dd
                    pg, ptoks = proj_chunk(wv, stok, 128, dd * 128, KC, y_rhs, y_toks)
                    for v, (t0, tl) in enumerate([(0, 512), (512, 1024)]):
                        bt = ptoks[0:1] if v == 0 else ptoks[1:3]
                        stt("dve", x[:, dc, t0:t0 + tl], pg[:, t0:t0 + tl], modv[:, i, 16 + dc, v:v + 1],
                            x[:, dc, t0:t0 + tl], ALU.mult, ALU.add, [modtok], bt + [xtok[dc]])
                ring.release()

        def layer_norm(i):
            mean_sb = Fb[0]
            rstd_sb = Fb[1]
            sq = [Fb[2], Fb[2]]
            sqt = [Ftok[2], Ftok[2]]
            for c in range(KC):
                q = c % 2
                xbf, xbft = Hb[q], Htok[q]
                sqf, sqft = Hb[2 + q], Htok[2 + q]
                cp("dve", xbf[:, :], x[:, c, :], [xtok[c]], [xbft])
                act(sqf[:, :], x[:, c, :], AF.Square, [xtok[c]], [sqft])
                for n, (t0, tl) in enumerate(tiles):
                    mm(pA[:, t0:t0 + tl], onesm[:], xbf[:, t0:t0 + tl], c == 0, c == KC - 1, [ctok, xbft], [bk[n]])
                    mm(pB[:, t0:t0 + tl], onesm[:], sqf[:, t0:t0 + tl], c == 0, c == KC - 1, [ctok, sqft], [bk[3 + n]])
            if i < 3:
                for _ in range(6):
                    mod_unit()
            cp("act", mean_sb[:, 0:T], pA[:, 0:T], [], bk[0:3] + [Ftok[0]])
            tt("dve", rstd_sb[:, 0:T], mean_sb[:, 0:T], mean_sb[:, 0:T], ALU.mult, [Ftok[0]], [Ftok[1]])
            tt("dve", rstd_sb[:, 0:T], pB[:, 0:T], rstd_sb[:, 0:T], ALU.subtract, [], bk[3:6] + [Ftok[1]])
            act(rstd_sb[:, 0:T], rstd_sb[:, 0:T], AF.Ln, [], [Ftok[1]], bias=LN_EPS)
            act(rstd_sb[:, 0:T], rstd_sb[:, 0:T], AF.Exp, [], [Ftok[1]], scale=-0.5)
            for c in range(KC):
                tt("dve", x[:, c, :], x[:, c, :], mean_sb[:, 0:T], ALU.subtract, [Ftok[0]], [xtok[c]])
                tt("dve", x[:, c, :], x[:, c, :], rstd_sb[:, 0:T], ALU.mult, [Ftok[1]], [xtok[c]])
                if i < 3:
                    act(x[:, c, :], x[:, c, :], AF.Identity, [pvtok], [xtok[c]],
                        bias=pvA[:, 32 + i * 8 + c:32 + i * 8 + c + 1], scale=pvA[:, i * 8 + c:i * 8 + c + 1])
                else:
                    act(x[:, c, :], x[:, c, :], AF.Identity, [pvtok], [xtok[c]], bias=pvc(c_lnb(i, c)), scale=pvc(c_lng(i, c)))
                    if DEBUG_STOP is None:
                        S.dma("sp", [(yout[c * 128:(c + 1) * 128, :], x[:, c, :])], reads=[xtok[c]], semi=xs_sem[1])

        def even_layer(i):
            j = i // 2
            modulate(i)
            def pairs_dt(slot):
                return [(slot[:, 0:256].rearrange("p (k n) -> p k n", k=8),
                         w_in_even[j, :, 3072:3104].rearrange("(k p) n -> p k n", p=128))]
            sl, stok = ring.next(pairs_dt)
            wv = sl[:, 0:256].rearrange("p (k n) -> p k n", k=8)
            for tc in range(NTC):
                for kc in range(KC):
                    mm(p6[:, tc * 32:(tc + 1) * 32], u[:, kc, tc * 128:(tc + 1) * 128], wv[:, kc, 0:32],
                       kc == 0, kc == KC - 1, [stok, utok[kc]], [bk[6]])
            ring.release()
            dtmp = wde[:].rearrange("p d t h -> p (d t h)").rearrange("p (t c) -> p t c", c=32)
            tt("dve", dtmp, p6[:, 0:384].rearrange("p (t c) -> p t c", c=32),
               dtb_bc[:, j * 32:(j + 1) * 32].unsqueeze(1).to_broadcast([128, NTC, 32]), ALU.add,
               [abtok], [bk[6], dttok])
            act(dtmp, dtmp, AF.Exp, [], [dttok])
            act(dt_tok[:], dtmp.rearrange("p t (d h) -> p d t h", d=2), AF.Ln, [], [dttok], bias=1.0)
            tt("dve", a_bf[:], dt_tok[:],
               A_bc[:, j * 32:(j + 1) * 32].rearrange("p (d h) -> p d h", d=2).unsqueeze(2).to_broadcast([128, 2, NTC, 16]),
               ALU.mult, [abtok], [dttok])
            def dt_tables():
                mm(p6[:, 0:192], Uf[:], a_bf[:, 0, :, :].rearrange("p t h -> p (t h)"), True, True, [ctok, dttok], [bk[6]])
                mm(p6[:, 192:384], Ub[:], a_bf[:, 1, :, :].rearrange("p t h -> p (t h)"), True, True, [ctok, dttok], [bk[6]])
                mm(p7[:, 0:384], onesb[:], a_bf[:].rearrange("p d t h -> p (d t h)"), True, True, [ctok, dttok], [bk[7]])
                act(wde[:].rearrange("p d t h -> p (d t h)"), p6[:, 0:384], AF.Exp, [], [bk[6], dttok])
                act(cd_bc[:].rearrange("p d t h -> p (d t h)"), p7[:, 0:384], AF.Exp, [], [bk[7], dttok])
                tt("dve", wde[:], wde[:], dt_tok[:], ALU.mult, [], [dttok])

            zs = [Hb[4], Hb[5]]
            zst = [Htok[4], Htok[5]]
            xT = [Hb[0], Hb[1]]
            xTt = [Htok[0], Htok[1]]
            BT, BTt = Hb[2], Htok[2]
            CT, CTt = Hb[3], Htok[3]
            cv, cvt = Fb[1], Ftok[1]
            def tr_block(srcH, srcT, dst_fn, dtok):
                for q in range(3):
                    hq = q % 2
                    for t4 in range(4):
                        tc = q * 4 + t4
                        tr(trP[:, hq * 512 + t4 * 128:hq * 512 + (t4 + 1) * 128], srcH[:, tc * 128:(tc + 1) * 128],
                           identb[:], [srcT, ctok], [bk[6]])
                    cp("act", dst_fn(q), trP[:, hq * 512:(hq + 1) * 512].rearrange("p (t f) -> p t f", t=4),
                       [], [bk[6], dtok])

            for g in range(4):
                def pairsX(slot, g=g):
                    sv = slot.rearrange("p (k n) -> p k n", k=8)
                    return [(sv[:, :, 0:256], w_in_even[j, :, 1024 + g * 256:1024 + (g + 1) * 256].rearrange("(k p) n -> p k n", p=128)),
                            (sv[:, :, 256:384], w_in_even[j, :, 2048 + g * 128:2048 + (g + 1) * 128].rearrange("(k p) n -> p k n", p=128)),
                            (sv[:, :, 384:512], w_in_even[j, :, 2560 + g * 128:2560 + (g + 1) * 128].rearrange("(k p) n -> p k n", p=128))]
                sl, stok = ring.next(pairsX)
                wv = sl.rearrange("p (k n) -> p k n", k=8)
                chunks = [(0, g * 2, xT[0][:, :], xTt[0]), (128, g * 2 + 1, xT[1][:, :], xTt[1]),
                          (256, 8 + g, BT[:, :], BTt), (384, 12 + g, CT[:, :], CTt)]
                kbuf = [None] * 4

                def conv_of(ci):
                    col0, ch, dst, dtok = chunks[ci]
                    conv_taps(kbuf[ci], pvc(c_caw(j, 0, ch)), pvc(c_caw(j, 1, ch)), pvc(c_caw(j, 2, ch)), pvc(c_cab(j, ch)),
                              dst, dtok)

                def tr_x(cc):
                    tr_block(xT[cc], xTt[cc], lambda q, cc=cc: x_tok[:, q * 4:(q + 1) * 4, cc * 128:(cc + 1) * 128], xtoktok)

                for ci in range(4):
                    pg, ptoks = proj_chunk(wv, stok, 128, chunks[ci][0], KC, u_rhs, u_toks)
                    kbuf[ci] = conv_evac(pg, ptoks)
                    if ci >= 1:
                        conv_of(ci - 1)
                    if ci == 3:
                        tr_x(0)
                ring.release()
                if g == 0:
                    dt_tables()
                def pairsZ(slot, g=g):
                    sv = slot[:, 0:2048].rearrange("p (k n) -> p k n", k=8)
                    return [(sv, w_in_even[j, :, g * 256:(g + 1) * 256].rearrange("(k p) n -> p k n", p=128))]
                sl, stok = ring.next(pairsZ)
                wv = sl[:, 0:2048].rearrange("p (k n) -> p k n", k=8)
                for cc in range(2):
                    pg, ptoks = proj_chunk(wv, stok, 128, cc * 128, KC, u_rhs, u_toks)
                    act(zs[cc][:, :], pg[:, 0:T], AF.Silu, [], ptoks + [zst[cc]])
                    if cc == 0:
                        conv_of(3)
                        tr_x(1)
                    else:
                        tr_block(BT, BTt, lambda q: B_tok[:, q * 4:(q + 1) * 4, :], btoktok)
                ring.release()
                ssd_group(j, g, zs, zst, BT, BTt, CT, CTt)

            for c in range(KC):
                sqb, sqbt = Hb[c % 2], Htok[c % 2]
                act(sqb[:, :], yb[:, c, :], AF.Square, [ytok[c]], [sqbt])
                for n, (t0, tl) in enumerate(tiles):
                    mm(pA[:, t0:t0 + tl], onesm[:], sqb[:, t0:t0 + tl], c == 0, c == KC - 1, [ctok, sqbt], [bk[n]])
            act(Fb[2][:, 0:T], pA[:, 0:T], AF.Ln, [], bk[0:3] + [Ftok[2]], bias=RMS_EPS)
            act(Fb[2][:, 0:T], Fb[2][:, 0:T], AF.Exp, [], [Ftok[2]], scale=-0.5)
            for c in range(KC):
                stt("dve", yb[:, c, :], yb[:, c, :], pvc(c_nag(j, c)), Fb[2][:, 0:T], ALU.mult, ALU.mult,
                    [pvtok, Ftok[2]], [ytok[c]])
            def mixb_part1(c):
                def pairsM(slot, c=c):
                    sv = slot.rearrange("p (k n) -> p k n", k=8)
                    return [(sv[:, :, q * 128:(q + 1) * 128],
                             w_in_even[j, :, 3104 + q * 1024 + c * 128:3104 + q * 1024 + (c + 1) * 128].rearrange("(k p) n -> p k n", p=128))
                            for q in range(4)]
                sl, stok = ring.next(pairsM)
                wv = sl.rearrange("p (k n) -> p k n", k=8)
                sg, sgt = Hb[0], Htok[0]
                pg, ptoks = proj_chunk(wv, stok, 128, 0, KC, u_rhs, u_toks)
                act(sg[:, :], pg[:, 0:T], AF.Silu, [], ptoks + [sgt])
                pg, ptoks = proj_chunk(wv, stok, 128, 128, KC, u_rhs, u_toks)
                tt("dve", Fb[0][:, 0:T], pg[:, 0:T], sg[:, :], ALU.mult, [sgt], ptoks + [Ftok[0]])
                pg, ptoks = proj_chunk(wv, stok, 128, 256, KC, u_rhs, u_toks)
                cp("act", Hb[1][:, :], pg[:, 0:T], [], ptoks + [Htok[1]])
                pg, ptoks = proj_chunk(wv, stok, 128, 384, KC, u_rhs, u_toks)
                tt("dve", Fb[1][:, 0:T], pg[:, 0:T], Hb[1][:, :], ALU.mult, [Htok[1]], ptoks + [Ftok[1]])
                ring.release()

            def mixb_part2(c):
                conv3(Fb[1], [], Fb[2], Ftok[2], pvc(c_cbw(j, 0, c)), pvc(c_cbw(j, 1, c)), pvc(c_cbw(j, 2, c)), None,
                      srctoks=[Ftok[1], pvtok])
                tt("dve", yb[:, c, :], Fb[2][:, 0:T], Fb[0][:, 0:T], ALU.mult, [Ftok[2], Ftok[0]], [ytok[c]])

            mixb_part1(0)
            out_proj_partial(i, w_out_even[j], 0)
            mixb_part2(0)
            for c in range(1, KC):
                mixb_part1(c)
                mixb_part2(c)
            out_proj_partial(i, w_out_even[j], 1024)

        def ssd_group(j, g, zs, zst, BT, BTt, CT, CTt):
            gs = slice(g * 4, g * 4 + 4)
            for h4 in range(4):
                hh = j * 16 + g * 4 + h4
                ts("dve", Dmat[:, h4, :], identb[:], D_bc[:, hh:hh + 1], None, ALU.mult, None, [ctok, abtok], [dmtok])
            HC = [(hcur[:], hcurtok), (hcur2, hcur2tok)]
            HT = [(htmp[:], htmptok), (htmp2, htmp2tok)]

            def v4(ap):
                return ap.rearrange("p (h q) -> p h q", h=4)

            hstP = Hb[6][:, 0:1024].rearrange("p (d s q) -> p d s q", d=2, s=2)

            def hst_slot(si, d, tc):
                if si == 2:
                    return hst[:, d, tc - 4, :], hsttok
                return hstP[:, d, si, :], Htok[6]

            def seq_of(tc):
                si = 0 if tc < 2 else (1 if tc < 4 else 2)
                s0, L = SEQS[si]
                return si, s0 // 128, L // 128

            def state_rounds(si):
                s0, L = SEQS[si]
                nch = L // 128
                tc0 = s0 // 128
                is_sample = (si == 2)
                orders = [list(range(tc0, tc0 + nch)), list(range(tc0 + nch - 1, tc0 - 1, -1))]
                have = [False, False]
                fns = []

                def init():
                    for d in range(2):
                        r0 = (j * 2 + d) * 128
                        S.dma("sp", [(HC[d][0], state[r0:r0 + 128, g * 256:(g + 1) * 256])],
                              writes=[HC[d][1]], semi=stin_sem[d])
                        have[d] = True
                if is_sample:
                    fns.append(init)

                def round_(idx):
                    for d in range(2):
                        tc = orders[d][idx]
                        hc, hct = HC[d]
                        ht, htt = HT[d]
                        par = idx % 2
                        if is_sample:
                            if par == 0:
                                stb = pA[:, 1024:1280] if d == 0 else pA[:, 1280:1536]
                                stk = bk[2]
                            else:
                                stb = bank4[:, 128:384] if d == 0 else bank5[:, 256:512]
                                stk = bk[4] if d == 0 else bk[5]
                        else:
                            if par == 0:
                                stb = bank4[:, 128:384] if d == 0 else bank5[:, 256:512]
                                stk = bk[4] if d == 0 else bk[5]
                            else:
                                stb = p6[:, 0:256] if d == 0 else p7[:, 0:256]
                                stk = bk[6] if d == 0 else bk[7]
                        xdb, xdbt = xdd[d][par], xddtok[d][par]
                        if have[d]:
                            hdst, hdtok = hst_slot(si, d, tc)
                            cp("act", hdst, hc, [hct], [hdtok])
                        if idx == nch - 1 and is_sample:
                            continue
                        tt("dve", xdb, v4(x_tok[:, tc, :]),
                           wde[:, d, tc, gs].unsqueeze(2).to_broadcast([128, 4, 64]), ALU.mult,
                           [xtoktok, dttok], [xdbt])
                        mm(stb, B_tok[:, tc, :], xdb.rearrange("p h q -> p (h q)"), True, True,
                           [btoktok, xdbt], [stk])
                        if have[d]:
                            tt("dve", v4(ht), v4(hc),
                               cd_bc[:, d, tc, gs].unsqueeze(2).to_broadcast([128, 4, 64]), ALU.mult,
                               [hct, dttok], [htt])
                            tt("dve", hc, stb, ht, ALU.add, [htt], [stk, hct])
                        else:
                            cp("dve", hc, stb, [], [stk, hct])
                            have[d] = True
                for idx in range(nch):
                    fns.append(lambda idx=idx: round_(idx))

                def final():
                    for d in range(2):
                        hc, hct = HC[d]
                        stv = sto[d][:].rearrange("p a b -> p (a b)")
                        cp("act", stv, hc, [hct], [stotok[d]])
                        r0 = ((si * 2 + j) * 2 + d) * 128
                        S.dma("sp", [(nstate[r0:r0 + 128, g * 256:(g + 1) * 256], stv)],
                              reads=[stotok[d]], semi=sto_sem[d])
                if not is_sample:
                    fns.append(final)
                return fns

            for si in range(2):
                for fn in state_rounds(si):
                    fn()
            pending_rounds = state_rounds(2)

            if True:
                its = list(range(NTC))
                ARG = [(p6[:], bk[6], p7[:], bk[7]), (pA[:, 0:512], bk[0], pA[:, 512:1024], bk[1])]
                YB = [(bank5, bk[5]), (pB[:, 0:512], bk[3])]

                def stage_pre_a(n):
                    tc = its[n]
                    k0 = tc * 128
                    mm(bank4[:, 0:128], BT[:, k0:k0 + 128], CT[:, k0:k0 + 128], True, True, [BTt, CTt], [bk[4]])
                    tt("dve", Rb2[:], Lfb[:].unsqueeze(2).to_broadcast([128, 2, 4, 128]),
                       a_bf[:, :, tc, gs].unsqueeze(3).to_broadcast([128, 2, 4, 128]), ALU.mult,
                       [ctok, dttok], [Rtok])
                    for d in range(2):
                        Um = Uf if d == 0 else Ub
                        pa, pat, pe_, pet = ARG[d]
                        Rflat = Rb2[:, d].rearrange("p h i -> p (h i)")
                        mm(pa, Um[:], Rflat, True, True, [ctok, Rtok], [pat])
                        mm(pe_, onesb[:], Rflat, True, True, [ctok, Rtok], [pet])

                def stage_pre_b(n):
                    for d in range(2):
                        pa, pat, pe_, pet = ARG[d]
                        act(eA2[:, d].rearrange("p h i -> p (h i)"), pa, AF.Exp, [], [pat, eAtok[d]])
                        act(eE2[:, d].rearrange("p h i -> p (h i)"), pe_, AF.Exp, [], [pet, eEtok[d]])

                def stage_cbm():
                    tt("dve", CBm2[:], bank4[:, 0:128].unsqueeze(1).to_broadcast([128, 2, 128]), Lfb[:], ALU.mult,
                       [ctok], [bk[4], CBmtok])

                def stage_mid(n):
                    tc = its[n]
                    k0 = tc * 128
                    q = n % 2
                    tt("dve", MT2[q][:], eA2[:], CBm2[:].unsqueeze(2).to_broadcast([128, 2, 4, 128]), ALU.mult,
                       [eAtok[0], eAtok[1], CBmtok], [MTtok[q]])
                    tt("dve", Cs2[q][:], eE2[:],
                       CT[:, k0:k0 + 128].unsqueeze(1).unsqueeze(1).to_broadcast([128, 2, 4, 128]), ALU.mult,
                       [eEtok[0], eEtok[1], CTt], [Cstok[q]])
                    tt("dve", xdt2[q][:], v4(x_tok[:, tc, :]).unsqueeze(1).to_broadcast([128, 2, 4, 64]),
                       dt_tok[:, :, tc, gs].unsqueeze(3).to_broadcast([128, 2, 4, 64]), ALU.mult,
                       [xtoktok, dttok], [xdttok[q]])

                def stage_y(n):
                    tc = its[n]
                    q = n % 2
                    yb_, ybt = YB[q]
                    si, tc0, nch = seq_of(tc)
                    has_f = (si == 2) or (tc != tc0)
                    has_b = (si == 2) or (tc != tc0 + nch - 1)
                    hf, hft = hst_slot(si, 0, tc)
                    hb_, hbt = hst_slot(si, 1, tc)
                    for h4 in range(4):
                        po = (h4 % 2) * 64
                        out = yb_[po:po + 64, (h4 // 2) * 128:(h4 // 2) * 128 + 128]
                        tp = (0, 64) if po else None
                        mm(out, xdt2[q][:, 0, h4, :], MT2[q][:, 0, h4, :], True, False, [xdttok[q], MTtok[q]], [ybt], tp)
                        mm(out, xdt2[q][:, 1, h4, :], MT2[q][:, 1, h4, :], False, False, [xdttok[q], MTtok[q]], [ybt], tp)
                        if has_f:
                            mm(out, hf[:, h4 * 64:(h4 + 1) * 64], Cs2[q][:, 0, h4, :], False, False,
                               [hft, Cstok[q]], [ybt], tp)
                        if has_b:
                            mm(out, hb_[:, h4 * 64:(h4 + 1) * 64], Cs2[q][:, 1, h4, :], False, False,
                               [hbt, Cstok[q]], [ybt], tp)
                        mm(out, x_tok[:, tc, h4 * 64:(h4 + 1) * 64], Dmat[:, h4, :], False, True, [xtoktok, dmtok], [ybt], tp)

                def stage_evac(n):
                    tc = its[n]
                    k0 = tc * 128
                    yb_, ybt = YB[n % 2]
                    tt("dve", yb[:, g * 2:g * 2 + 2, k0:k0 + 128], yb_[:, 0:256].rearrange("p (r i) -> p r i", r=2),
                       Hb45[:, :, k0:k0 + 128], ALU.mult, [zst[0], zst[1]], [ybt, ytok[g * 2], ytok[g * 2 + 1]])

                N = len(its)
                stage_pre_a(0)
                stage_pre_b(0)
                stage_cbm()
                for n in range(N):
                    if n == 4:
                        while pending_rounds:
                            pending_rounds.pop(0)()
                    if n + 1 < N:
                        stage_pre_a(n + 1)
                    stage_mid(n)
                    if n + 1 < N:
                        stage_pre_b(n + 1)
                    if n >= 1:
                        stage_evac(n - 1)
                    if n + 1 < N:
                        stage_cbm()
                    stage_y(n)
                    if n < 4:
                        for _ in range(3):
                            if pending_rounds:
                                pending_rounds.pop(0)()
                stage_evac(N - 1)

        def odd_layer(i):
            j = i // 2
            modulate(i)
            S.op("dve", lambda e: e.memset(Fb[0][:], 0.0), (), [Ftok[0]])
            S.op("dve", lambda e: e.memset(Fb[1][:], 0.0), (), [Ftok[1]])
            S.op("dve", lambda e: e.memset(Fb[2][:], 0.0), (), [Ftok[2]])
            for half in range(2):
                for kk in range(2):
                    k = half * 2 + kk
                    w = WINS[k]
                    invc, invct = invc_alias, hsttok
                    S.op("dve", lambda e: e.memset(invc, 1.0 / w), (), [invct])
                    for si, (s0, L) in enumerate(SEQS):
                        o = POFF[si]
                        cp("dve", invc[:, o:o + 8], invtab[:, k, 0:8], [invtok], [invct])
                        cp("dve", invc[:, o + L - 8:o + L], invtab[:, k, 8:16], [invtok], [invct])
                    def pairsV(slot, k=k):
                        return [(slot.rearrange("p (k n) -> p k n", k=8),
                                 w_in_odd[j, :, k * 512:(k + 1) * 512].rearrange("(k p) n -> p k n", p=128))]

                    def pairsG(slot, k=k):
                        return [(slot.rearrange("p (k n) -> p k n", k=8),
                                 w_in_odd[j, :, 2048 + k * 512:2048 + (k + 1) * 512].rearrange("(k p) n -> p k n", p=128))]
                    slV, stokV = ring.next(pairsV)
                    wvV = slV.rearrange("p (k n) -> p k n", k=8)
                    slG, stokG = ring.next(pairsG, prefetch=False)
                    wvG = slG.rearrange("p (k n) -> p k n", k=8)
                    for cc in range(4):
                        pg, ptoks = proj_chunk(wvV, stokV, 128, cc * 128, KC, u_rhs, u_toks)
                        vp, vpt = Fb[0], Ftok[0]
                        for si, (s0, L) in enumerate(SEQS):
                            o = POFF[si]
                            bt = ptoks[0:1] if si < 2 else ptoks[1:3]
                            cp("act", vp[:, o:o + L], pg[:, s0:s0 + L], [], bt + [vpt])
                        pg2, ptoks2 = proj_chunk(wvG, stokG, 128, cc * 128, KC, u_rhs, u_toks)
                        act(Hb[cc][:, :], pg2[:, 0:T], AF.Silu, [], ptoks2 + [Htok[cc]])
                        src, srct = vp, vpt
                        bufs = [(Fb[1], Ftok[1]), (Fb[2], Ftok[2])]
                        sh = [(1, 0), (1, 1), (2, 2), (4, 4)]
                        for lev in range(k + 1):
                            dst, dstt = bufs[lev % 2]
                            a_, b_ = sh[lev]
                            tt("dve", dst[:, 8:PW - 8], src[:, 8 - a_:PW - 8 - a_], src[:, 8 + b_:PW - 8 + b_], ALU.add,
                               [srct], [dstt])
                            src, srct = dst, dstt
                        oth, otht = bufs[(k + 1) % 2]
                        tt("dve", oth[:, 8:PW - 8], src[:, 8:PW - 8], invc[:, 8:PW - 8], ALU.mult, [srct, invct], [otht])
                        for si, (s0, L) in enumerate(SEQS):
                            o = POFF[si]
                            tt("dve", Hb[4 + cc][:, s0:s0 + L], oth[:, o:o + L], vp[:, o:o + L], ALU.subtract,
                               [otht, vpt], [Htok[4 + cc]])
                    def pairsP(slot, k=k):
                        return [(slot[:, 0:2048].rearrange("p (k n) -> p k n", k=4),
                                 w_pool[j, k, :, :].rearrange("(k p) n -> p k n", p=128))]
                    sl, stok = ring.next(pairsP)
                    wv = sl[:, 0:2048].rearrange("p (k n) -> p k n", k=4)
                    for dd in range(4):
                        pg, ptoks = proj_chunk(wv, stok, 128, dd * 128, 4,
                                               lambda kc, t0, tl: Hb[4 + kc][:, t0:t0 + tl], lambda kc: [Htok[4 + kc]])
                        yc = kk * 4 + dd
                        stt("dve", yb[:, yc, :], pg[:, 0:T], pvc(c_psc(j, k * 4 + dd)), Hb[dd][:, :], ALU.mult, ALU.mult,
                            [pvtok, Htok[dd]], ptoks + [ytok[yc]])
                    ring.release()
                out_proj_partial(i, w_out_odd[j], half * 1024)

        S.dry = True
        emit()
        S.dry = False
        S.plan = True
        pg_state[0] = 0
        emit()
        S.plan = False
        S.reset()
        ring.issued = 0
        ring.cur = 0
        pg_state[0] = 0
        emit()
        for semi in list(xs_sem) + list(sto_sem):
            ent = S.dma_sems[semi]
            if ent[1] > 0:
                nc.sync.wait_ge(ent[0], ent[1])
        print("program: insts", S.cnt, "incs", S.semval, "waits", S.nwaits, "pieces", len(ring.plan))
    return nc


_NC_CACHE = {}


def kernel(x_prompt, x_sample, state_ssd, c, c_ctx, w_mod, b_mod, ln_g, ln_b, w_in_even, conv_a_w,
           conv_a_b, a_log, dt_bias, d_skip, norm_a_g, conv_b_w, w_out_even, w_in_odd, w_pool,
           pool_scale, w_out_odd):
    f = lambda a: np.ascontiguousarray(np.asarray(a), dtype=np.float32)
    if "nc" not in _NC_CACHE:
        _NC_CACHE["nc"] = build_program()
    nc = _NC_CACHE["nc"]
    x_prompt = f(x_prompt)
    x_sample = f(x_sample)
    state_ssd = f(state_ssd)
    c = f(c)
    c_ctx = f(c_ctx)
    shared = dict(w_mod=f(w_mod), b_mod=f(b_mod), ln_g=f(ln_g), ln_b=f(ln_b), w_in_even=f(w_in_even),
                  conv_a_w=f(conv_a_w), conv_a_b=f(conv_a_b), a_log=f(a_log), dt_bias=f(dt_bias), d_skip=f(d_skip),
                  norm_a_g=f(norm_a_g), conv_b_w=f(conv_b_w), w_out_even=f(w_out_even), w_in_odd=f(w_in_odd),
                  w_pool=f(w_pool), pool_scale=f(pool_scale), w_out_odd=f(w_out_odd))
    in_maps = []
    for k in range(NCORES):
        m = dict(shared)
        m["xin"] = np.ascontiguousarray(np.concatenate([x_prompt[2 * k], x_prompt[2 * k + 1], x_sample[k]], axis=0).T)
        m["state"] = np.ascontiguousarray(state_ssd[k].transpose(0, 1, 4, 2, 3).reshape(512, 1024))
        m["cvec"] = np.ascontiguousarray(np.stack([c_ctx, c[k]], axis=0))
        in_maps.append(m)
    res = run_bass_kernel_spmd(nc, in_maps, core_ids=list(range(NCORES)))
    y_prompt = np.empty((16, 256, D), np.float32)
    y_sample = np.empty((8, 1024, D), np.float32)
    new_state = np.empty((16, 2, 2, 16, 64, 128), np.float32)
    for k in range(NCORES):
        r = res.results[k]
        yo = np.asarray(r["yout"]).T
        y_prompt[2 * k] = yo[0:256]
        y_prompt[2 * k + 1] = yo[256:512]
        y_sample[k] = yo[512:1536]
        ns = np.asarray(r["nstate"]).reshape(2, 2, 2, 128, 16, 64).transpose(0, 1, 2, 4, 5, 3)
        new_state[2 * k] = ns[0]
        new_state[2 * k + 1] = ns[1]
    return (y_prompt, y_sample, new_state)
```

```python
import math
from contextlib import ExitStack

import numpy as np
import concourse.bass as bass
import concourse.mybir as mybir
from concourse.bass_utils import run_bass_kernel_spmd

F32 = mybir.dt.float32
BF16 = mybir.dt.bfloat16
I32 = mybir.dt.int32
AF = mybir.ActivationFunctionType
ALU = mybir.AluOpType

NCORES = 8
D = 1024
T = 1536
KC = 8
NT = 3
SEQS = [(0, 256), (256, 256), (512, 1024)]
NTC = 12
ALPHA = (2 * 4) ** 0.25
LN_EPS = 1e-5
RMS_EPS = 1e-5
IN_EVEN = 7200
PW = 1600
POFF = [16, 280, 544]
WINS = (2, 4, 8, 16)
NSLOT = 2
SLOT = 4096

DEBUG_STOP = None


class Tok:
    __slots__ = ("w", "r", "name")
    ALL = []

    def __init__(self, name=""):
        self.w = None
        self.r = {}
        self.name = name
        Tok.ALL.append(self)


class Sched:
    ENG = ("pe", "dve", "act", "pool")

    def __init__(self, nc, es):
        self.nc = nc
        self.es = es
        self.engs = {"pe": nc.tensor, "dve": nc.vector, "act": nc.scalar, "pool": nc.gpsimd, "sp": nc.sync}
        self.sem = {k: es.enter_context(nc.semaphore("sem_" + k)) for k in self.ENG}
        self.cnt = {k: 0 for k in self.ENG}
        self.waited = {k: {} for k in list(self.ENG) + ["sp"]}
        self.dma_sems = []
        self.dry = False
        self.plan = False
        self.needed = {k: set() for k in self.ENG}
        self.semval = {k: 0 for k in self.ENG}
        self.valof = {k: {} for k in self.ENG}
        self.nwaits = 0

    def reset(self):
        for t in Tok.ALL:
            t.w = None
            t.r = {}
        self.cnt = {k: 0 for k in self.ENG}
        self.waited = {k: {} for k in list(self.ENG) + ["sp"]}
        self.semval = {k: 0 for k in self.ENG}
        self.valof = {k: {} for k in self.ENG}
        self.nwaits = 0
        for ent in self.dma_sems:
            ent[1] = 0

    def _wait(self, eng, dep):
        if dep is None:
            return
        src, val = dep
        if isinstance(src, str):
            if src == eng and eng == "pe":
                return
            key = src
            sem = self.sem[src]
        else:
            key = ("dma", src.num)
            sem = src
        if self.waited[eng].get(key, 0) >= val:
            return
        self.nwaits += 1
        self.waited[eng][key] = val
        if self.plan:
            if isinstance(src, str):
                self.needed[src].add(val)
            return
        if isinstance(src, str):
            self.engs[eng].wait_ge(sem, self.valof[src][val])
        else:
            self.engs[eng].wait_ge(sem, val)

    def op(self, eng, fn, reads=(), writes=()):
        if self.dry:
            return
        for t in reads:
            self._wait(eng, t.w)
        for t in writes:
            self._wait(eng, t.w)
            for src, val in t.r.items():
                if src == eng and eng == "pe":
                    continue
                self._wait(eng, (src, val))
        self.cnt[eng] += 1
        c = self.cnt[eng]
        if not self.plan:
            ins = fn(self.engs[eng])
            if c in self.needed[eng]:
                ins.then_inc(self.sem[eng], 1)
                self.semval[eng] += 1
                self.valof[eng][c] = self.semval[eng]
        for t in reads:
            t.r[eng] = c
        for t in writes:
            t.w = (eng, c)
            t.r = {}

    def new_dma_sem(self):
        s = self.es.enter_context(self.nc.semaphore("dsem%d" % len(self.dma_sems)))
        self.dma_sems.append([s, 0])
        return len(self.dma_sems) - 1

    def dma(self, q, pairs, reads=(), writes=(), semi=None, **kw):
        if self.dry:
            return
        for t in reads:
            self._wait(q, t.w)
        for t in writes:
            self._wait(q, t.w)
            for src_, val in t.r.items():
                self._wait(q, (src_, val))
        ent = self.dma_sems[semi]
        for (o, i_) in pairs:
            ent[1] += 16
            if not self.plan:
                ins = self.engs[q].dma_start(out=o, in_=i_, **kw)
                ins.then_inc(ent[0], 16)
        dep = (ent[0], ent[1])
        for t in reads:
            t.r[ent[0]] = ent[1]
        for t in writes:
            t.w = dep
            t.r = {}


class Ring:
    def __init__(self, S, buf):
        self.S = S
        self.buf = buf
        self.toks = [Tok("ring%d" % i) for i in range(NSLOT)]
        self.sems = [S.new_dma_sem() for _ in range(NSLOT)]
        self.plan = []
        self.issued = 0
        self.cur = 0

    def slot_ap(self, s):
        return self.buf[:, s, :]

    def _issue(self):
        p = self.issued
        s = p % NSLOT
        pairs = self.plan[p](self.slot_ap(s))
        self.S.dma("pool", pairs, writes=[self.toks[s]], semi=self.sems[s])
        self.issued += 1

    def next(self, pairs_fn, prefetch=True):
        if self.S.dry:
            self.plan.append(pairs_fn)
            return self.slot_ap(0), self.toks[0]
        p = self.cur
        while self.issued < min(len(self.plan), (p + NSLOT) if prefetch else (p + 1)):
            self._issue()
        s = p % NSLOT
        self.cur += 1
        return self.slot_ap(s), self.toks[s]

    def release(self):
        pass

    def kick(self):
        if self.S.dry:
            return
        while self.issued < min(len(self.plan), NSLOT):
            self._issue()


def build_program():
    nc = bass.Bass("TRN2", target_bir_lowering=False)

    def din(name, shape):
        return nc.dram_tensor(name, list(shape), F32, kind="ExternalInput").ap()

    xin = din("xin", [D, T])
    state = din("state", [512, 1024])
    cvec = din("cvec", [2, D])
    w_mod = din("w_mod", [4, D, 3072])
    b_mod = din("b_mod", [4, 3072])
    ln_g = din("ln_g", [4, D])
    ln_b = din("ln_b", [4, D])
    w_in_even = din("w_in_even", [2, D, IN_EVEN])
    conv_a_w = din("conv_a_w", [2, 3, 2048])
    conv_a_b = din("conv_a_b", [2, 2048])
    a_log = din("a_log", [2, 2, 16])
    dt_bias = din("dt_bias", [2, 2, 16])
    d_skip = din("d_skip", [2, 16])
    norm_a_g = din("norm_a_g", [2, D])
    conv_b_w = din("conv_b_w", [2, 3, D])
    w_out_even = din("w_out_even", [2, 2048, D])
    w_in_odd = din("w_in_odd", [2, D, 4096])
    w_pool = din("w_pool", [2, 4, 512, 512])
    pool_scale = din("pool_scale", [2, 2048])
    w_out_odd = din("w_out_odd", [2, 2048, D])
    yout = nc.dram_tensor("yout", [D, T], F32, kind="ExternalOutput").ap()
    nstate = nc.dram_tensor("nstate", [1024, 1024], F32, kind="ExternalOutput").ap()

    with ExitStack() as es:
        S = Sched(nc, es)

        def sb(name, shape, dt):
            return es.enter_context(nc.sbuf_tensor(name, list(shape), dt))

        x = sb("x", [128, KC, T], F32)
        xtok = [Tok("x%d" % c) for c in range(KC)]
        u = sb("u", [128, KC, T], BF16)
        utok = [Tok("u%d" % c) for c in range(KC)]
        yb = sb("ybuf", [128, KC, T], BF16)
        ytok = [Tok("y%d" % c) for c in range(KC)]
        ringbuf = sb("ring", [128, NSLOT, SLOT], BF16)
        ring = Ring(S, ringbuf)
        Fb = [sb("F%d" % i, [128, PW], F32) for i in range(3)]
        Ftok = [Tok("F%d" % i) for i in range(3)]
        Hb = [sb("H%d" % i, [128, T], BF16) for i in range(4)]
        Hb45 = sb("H45", [128, 2, T], BF16)
        Hb += [Hb45[:, 0, :], Hb45[:, 1, :]]
        Hb += [sb("H%d" % i, [128, T], BF16) for i in (6, 7)]
        Htok = [Tok("H%d" % i) for i in range(8)]
        pv = sb("pv", [128, 384], F32)
        pvtok = Tok("pv")
        pvA = sb("pvA", [128, 64], F32)
        modv = sb("modv", [128, 4, 24, 2], F32)
        modtok = Tok("modv")
        ident = sb("ident", [128, 128], F32)
        identb = sb("identb", [128, 128], BF16)
        onesb = sb("onesb", [128, 128], BF16)
        onesm = sb("onesm", [128, 128], BF16)
        onesf = sb("onesf", [128, 128], F32)
        Lfb = sb("Lfb", [128, 2, 128], BF16)
        Lf = Lfb[:, 0, :]
        Lb = Lfb[:, 1, :]
        Uf = sb("Uf", [128, 128], BF16)
        Ub = sb("Ub", [128, 128], BF16)
        ctok = Tok("consts")
        mtmp = Fb[0][:, 1024:1152]
        mtmptok = Tok("mtmp")
        A_bc = sb("A_bc", [128, 64], F32)
        D_bc = sb("D_bc", [128, 32], F32)
        dtb_bc = sb("dtb_bc", [128, 64], F32)
        abtok = Tok("abc")
        posc = sb("posc", [128, 4, 64], F32)
        postok = Tok("pos")
        invtab = sb("invtab", [128, 4, 16], F32)
        invtok = Tok("invtab")
        scb = sb("scb", [128, 8, 2], BF16)
        scbtok = Tok("scb")
        dt_tok = sb("dt_tok", [128, 2, NTC, 16], F32)
        a_bf = sb("a_bf", [128, 2, NTC, 16], BF16)
        dt_bf = sb("dt_bf", [128, 2, NTC, 16], BF16)
        wde = sb("wde", [128, 2, NTC, 16], F32)
        cd_bc = sb("cd_bc", [128, 2, NTC, 16], F32)
        dttok = Tok("dtstuff")
        Dmat = sb("Dmat", [128, 4, 128], BF16)
        dmtok = Tok("Dmat")
        stin_sem = [S.new_dma_sem(), S.new_dma_sem()]
        x_tok = sb("x_tok", [128, NTC, 256], BF16)
        xtoktok = Tok("x_tok")
        B_tok = sb("B_tok", [128, NTC, 128], BF16)
        btoktok = Tok("B_tok")
        hbuf = sb("hbuf", [128, 2048], F32)
        hst = hbuf[:].bitcast(BF16).rearrange("p (d t q) -> p d t q", d=2, t=8)
        invc_alias = hbuf[:, 0:PW]
        hsttok = Tok("hst")
        hcur = sb("hcur", [128, 256], F32)
        hcurtok = Tok("hcur")
        htmp = sb("htmp", [128, 256], F32)
        htmptok = Tok("htmp")
        NB = 2
        Rb2 = sb("Rb2", [128, 2, 4, 128], BF16)
        Rtok = Tok()
        eA2 = sb("eA2", [128, 2, 4, 128], BF16)
        eAtok = [Tok() for _ in range(2)]
        eE2 = sb("eE2", [128, 2, 4, 128], BF16)
        eEtok = [Tok() for _ in range(2)]
        MT2 = [sb("MT2%d" % q, [128, 2, 4, 128], BF16) for q in range(2)]
        MTtok = [Tok() for q in range(2)]
        Cs2 = [sb("Cs2%d" % q, [128, 2, 4, 128], BF16) for q in range(2)]
        Cstok = [Tok() for q in range(2)]
        xdt2 = [sb("xdt2%d" % q, [128, 2, 4, 64], BF16) for q in range(2)]
        xdttok = [Tok() for q in range(2)]
        h7f = Hb[7][:, :].bitcast(F32)
        hcur2 = h7f[:, 0:256]
        htmp2 = h7f[:, 256:512]
        xdd = [[sb("xdd", [128, 4, 64], BF16)[:], sb("xddb", [128, 4, 64], BF16)[:]],
               [Hb[7][:, 1024:1280].rearrange("p (h q) -> p h q", h=4), Hb[7][:, 1280:1536].rearrange("p (h q) -> p h q", h=4)]]
        xddtok = [[Tok("xdd00"), Tok("xdd01")], [Tok("xdd10"), Tok("xdd11")]]
        hcur2tok = Tok("hcur2")
        htmp2tok = Tok("htmp2")
        CBm2 = sb("CBm2", [128, 2, 128], BF16)
        CBmtok = Tok()
        sto = [sb("sto%d" % i, [128, 2, 128], F32) for i in range(2)]
        stotok = [Tok() for _ in range(2)]
        sto_sem = [S.new_dma_sem() for _ in range(2)]
        xs = [Fb[1][:, 0:D], Fb[2][:, 0:D]]
        xstok = [Ftok[1], Ftok[2]]
        xs_sem = [S.new_dma_sem() for _ in range(2)]
        pst = [Fb[2][:, i * 128:(i + 1) * 128] for i in range(3)]
        cst = Fb[2][0:16, 384:512]
        psttok = Ftok[2]
        pst_sem = S.new_dma_sem()
        misc_sem = S.new_dma_sem()

        pA = es.enter_context(nc.psum_tensor("pA", [128, 1536], F32))
        pB = es.enter_context(nc.psum_tensor("pB", [128, 1536], F32))
        p6 = es.enter_context(nc.psum_tensor("p6", [128, 512], F32))
        p7 = es.enter_context(nc.psum_tensor("p7", [128, 512], F32))
        bk = [Tok("bank%d" % i) for i in range(8)]
        PG = [(pA, bk[0:3]), (pB, bk[3:6])]
        pg_state = [0]
        trB = pB[:, 0:512].bitcast(BF16)
        trP = p6[:, :].bitcast(BF16)
        bank4 = pB[:, 512:1024]
        bank5 = pB[:, 1024:1536]

        def next_pg():
            g = PG[pg_state[0] % 2]
            pg_state[0] += 1
            return g

        def tt(eng, out, in0, in1, op, R, W):
            S.op(eng, lambda e: e.tensor_tensor(out, in0, in1, op), R, W)

        def ts(eng, out, in0, s1, s2, op0, op1, R, W):
            if s2 is None:
                S.op(eng, lambda e: e.tensor_scalar(out, in0, s1, None, op0), R, W)
            else:
                S.op(eng, lambda e: e.tensor_scalar(out, in0, s1, s2, op0, op1), R, W)

        def stt(eng, out, in0, scalar, in1, op0, op1, R, W):
            S.op(eng, lambda e: e.scalar_tensor_tensor(out=out, in0=in0, scalar=scalar, in1=in1, op0=op0, op1=op1), R, W)

        def act(out, in_, func, R, W, bias=None, scale=None):
            kw = {}
            if bias is not None:
                kw["bias"] = bias
            if scale is not None:
                kw["scale"] = scale
            S.op("act", lambda e: e.activation(out=out, in_=in_, func=func, **kw), R, W)

        def cp(eng, out, in_, R, W):
            if eng == "act":
                act(out, in_, AF.Copy, R, W)
            else:
                S.op(eng, lambda e: e.tensor_copy(out, in_), R, W)

        def mm(out, lhsT, rhs, start, stop, R, W, tp=None):
            if tp is None:
                S.op("pe", lambda e: e.matmul(out, lhsT, rhs, start=start, stop=stop), R, W)
            else:
                S.op("pe", lambda e: e.matmul(out, lhsT, rhs, start=start, stop=stop, tile_position=tp), R, W)

        def tr(out, in_, idn, R, W):
            S.op("pe", lambda e: e.transpose(out, in_, idn), R, W)

        def pvc(col):
            return pv[:, col:col + 1]

        def c_bmod(i, m): return i * 24 + m
        def c_lng(i, c): return 96 + i * 8 + c
        def c_caw(j, k, c): return 128 + j * 48 + k * 16 + c
        def c_lnb(i, c): return 128 + 96 + i * 8 + c
        def c_cab(j, c): return 256 + j * 16 + c
        def c_nag(j, c): return 256 + 32 + j * 8 + c
        def c_cbw(j, k, c): return 256 + 48 + j * 24 + k * 8 + c
        def c_psc(j, c): return 256 + 96 + j * 16 + c

        tiles = [(n * 512, 512) for n in range(NT)]

        def emit():
            mod_reset()
            ring.kick()
            def mask(dst, pattern, cm, cmp_):
                S.op("pool", lambda e: e.memset(mtmp[:], 1.0), (), [mtmptok])
                S.op("pool", lambda e: e.affine_select(out=mtmp[:], in_=mtmp[:], pattern=pattern, compare_op=cmp_,
                                                       fill=0.0, base=0, channel_multiplier=cm), (), [mtmptok])
                cp("dve", dst[:], mtmp[:], [mtmptok], [ctok])

            S.op("pool", lambda e: e.memset(ident[:], 0.0), (), [ctok])
            S.op("pool", lambda e: e.affine_select(out=ident[:], in_=ident[:], pattern=[[-1, 128]], compare_op=ALU.not_equal,
                                                   fill=1.0, base=0, channel_multiplier=1), (), [ctok])
            cp("dve", identb[:], ident[:], [ctok], [ctok])
            S.op("pool", lambda e: e.memset(onesb[:], 1.0), (), [ctok])
            S.op("pool", lambda e: e.memset(onesm[:], 1.0 / 1024), (), [ctok])
            S.op("pool", lambda e: e.memset(onesf[:], 1.0 / 1024), (), [ctok])
            mask(Lf, [[1, 128]], -1, ALU.is_ge)
            mask(Lb, [[-1, 128]], 1, ALU.is_ge)
            mask(Uf, [[-1, 128]], 1, ALU.is_gt)
            mask(Ub, [[1, 128]], -1, ALU.is_gt)

            def rows(ap2d):
                return ap2d
            loads = [
                (pst[0][0:96, :], b_mod.rearrange("i (m p) -> (i m) p", p=128)),
                (pst[0][96:128, :], ln_g.rearrange("i (c p) -> (i c) p", p=128)),
                (pst[1][0:96, :], conv_a_w.rearrange("j k (c p) -> (j k c) p", p=128)),
                (pst[1][96:128, :], ln_b.rearrange("i (c p) -> (i c) p", p=128)),
                (pst[2][0:32, :], conv_a_b.rearrange("j (c p) -> (j c) p", p=128)),
                (pst[2][32:48, :], norm_a_g.rearrange("j (c p) -> (j c) p", p=128)),
                (pst[2][48:96, :], conv_b_w.rearrange("j k (c p) -> (j k c) p", p=128)),
                (pst[2][96:128, :], pool_scale.rearrange("j (c p) -> (j c) p", p=128)),
                (cst[:, :], cvec.rearrange("v (k p) -> (v k) p", p=128)),
            ]
            S.dma("sp", loads, writes=[psttok], semi=pst_sem)
            S.dma("sp", [(A_bc[:], a_log.rearrange("j d h -> (j d h)").partition_broadcast(128)),
                         (D_bc[:], d_skip.rearrange("j h -> (j h)").partition_broadcast(128))],
                  writes=[abtok], semi=misc_sem)
            S.dma("sp", [(dtb_bc[:], dt_bias.rearrange("j d h -> (j d h)").partition_broadcast(128))],
                  writes=[abtok], semi=misc_sem)
            for i in range(3):
                tr(pA[:, i * 128:(i + 1) * 128], pst[i][:], ident[:], [psttok, ctok], [bk[0]])
            tr(pA[:, 384:400], cst[:], ident[0:16, 0:16], [psttok, ctok], [bk[0]])
            cp("dve", pv[:], pA[:, 0:384], [], [bk[0], pvtok])
            ts("dve", pvA[:, 0:32], pv[:, 96:128], ALPHA, None, ALU.mult, None, [], [pvtok])
            ts("dve", pvA[:, 32:64], pv[:, 224:256], ALPHA, None, ALU.mult, None, [], [pvtok])
            act(scb[:].rearrange("p k v -> p v k"), pA[:, 384:400].rearrange("p (v k) -> p v k", v=2), AF.Silu,
                [], [bk[0], scbtok])
            act(A_bc[:], A_bc[:], AF.Exp, [], [abtok])
            ts("dve", A_bc[:], A_bc[:], -1.0, None, ALU.mult, None, [], [abtok])

            kidx = Fb[0][:, 0:2]
            pvals = Fb[0][:, 64:128]
            S.op("pool", lambda e: e.iota(kidx, pattern=[[128, 2]], base=0, channel_multiplier=1,
                                          allow_small_or_imprecise_dtypes=True), (), [Ftok[0]])
            S.op("pool", lambda e: e.iota(pvals, pattern=[[1, 64]], base=0, channel_multiplier=0,
                                          allow_small_or_imprecise_dtypes=True), (), [Ftok[0]])
            omega = Fb[0][:, 2:4]
            act(omega, kidx, AF.Exp, [], [Ftok[0]], scale=-math.log(10000.0) / 256.0)
            ang = Fb[1][:, 0:256].rearrange("p (a b) -> p a b", a=4)
            for ch in range(4):
                cc = ch % 2
                ts("dve", ang[:, ch, :], pvals, omega[:, cc:cc + 1], 1.0 / (2 * math.pi), ALU.mult, ALU.mult,
                   [Ftok[0]], [Ftok[1]])
                if ch >= 2:
                    ts("dve", ang[:, ch, :], ang[:, ch, :], 0.25, None, ALU.add, None, [], [Ftok[1]])
            angi = Fb[1][:, 256:512].bitcast(I32)
            angf = Fb[1][:, 512:768]
            cp("dve", angi, Fb[1][:, 0:256], [], [Ftok[1]])
            cp("dve", angf, angi, [], [Ftok[1]])
            tt("dve", Fb[1][:, 0:256], Fb[1][:, 0:256], angf, ALU.subtract, [], [Ftok[1]])
            act(posc[:].rearrange("p a b -> p (a b)"), Fb[1][:, 0:256], AF.Sin, [Ftok[1]], [postok], scale=6.283185)

            ev = Fb[0][:, 128:136]
            S.op("pool", lambda e: e.iota(ev, pattern=[[1, 8]], base=0, channel_multiplier=0,
                                          allow_small_or_imprecise_dtypes=True), (), [Ftok[0]])
            for k, w in enumerate(WINS):
                ts("dve", invtab[:, k, 0:8], ev, float(w // 2), float(w), ALU.add, ALU.min, [Ftok[0]], [invtok])
                ts("dve", invtab[:, k, 8:16], ev, -1.0, float(8 + w // 2), ALU.mult, ALU.add, [Ftok[0]], [invtok])
                ts("dve", invtab[:, k, 8:16], invtab[:, k, 8:16], float(w), None, ALU.min, None, [], [invtok])
            S.op("dve", lambda e: e.reciprocal(invtab[:].rearrange("p a b -> p (a b)"),
                                               invtab[:].rearrange("p a b -> p (a b)")), [], [invtok])

            S.dma("sp", [(x[:, c, :], xin[c * 128:(c + 1) * 128, :]) for c in range(KC)], writes=xtok, semi=xs_sem[0])
            xs4 = x[:, 0:4, 512:T].rearrange("p c (r q) -> p c r q", q=64)
            tt("dve", xs4, xs4, posc[:, :, 0:16].unsqueeze(3).to_broadcast([128, 4, 16, 64]), ALU.add,
               [postok], xtok[0:4])
            xs8 = x[:, 4:8, 512:T].rearrange("p c (r q) -> p c r q", q=64)
            tt("dve", xs8, xs8, posc[:, :, :].unsqueeze(2).to_broadcast([128, 4, 16, 64]), ALU.add,
               [postok], xtok[4:8])

            for _ in range(6):
                mod_unit()

            for i in range(4):
                if i % 2 == 0:
                    even_layer(i)
                else:
                    odd_layer(i)
                layer_norm(i)
                if DEBUG_STOP is not None and i == DEBUG_STOP:
                    break

            if DEBUG_STOP is not None:
                S.dma("sp", [(yout[c * 128:(c + 1) * 128, :], x[:, c, :]) for c in range(KC)], reads=xtok, semi=xs_sem[1])

        mod_pending = []

        def mod_reset():
            del mod_pending[:]
            for i in range(4):
                for m4 in range(6):
                    mod_pending.append((i, m4))

        def mod_unit():
            if not mod_pending:
                return
            i, m4 = mod_pending.pop(0)

            def pairs(slot, i=i, m4=m4):
                return [(slot.rearrange("p (k n) -> p k n", k=8),
                         w_mod[i, :, m4 * 512:(m4 + 1) * 512].rearrange("(k p) n -> p k n", p=128))]
            sl, stok = ring.next(pairs)
            w = sl.rearrange("p (k n) -> p k n", k=8)
            for mmi in range(4):
                m = m4 * 4 + mmi
                for kc in range(KC):
                    mm(p7[:, m * 2:(m + 1) * 2], w[:, kc, mmi * 128:(mmi + 1) * 128], scb[:, kc, :],
                       kc == 0, kc == KC - 1, [stok, scbtok], [bk[7]])
            if m4 == 5:
                tt("dve", modv[:, i, :, :], p7[:, 0:48].rearrange("p (m v) -> p m v", v=2),
                   pv[:, i * 24:(i + 1) * 24].unsqueeze(2).to_broadcast([128, 24, 2]), ALU.add,
                   [pvtok], [bk[7], modtok])
                ts("dve", modv[:, i, 8:24, :], modv[:, i, 8:24, :], 1.0, None, ALU.add, None, [], [modtok])
                if i >= 1:
                    ts("dve", modv[:, i, 8:16, :], modv[:, i, 8:16, :], 1.0 / ALPHA, None, ALU.mult, None, [], [modtok])

        def modulate(i):
            for c in range(KC):
                for v, (t0, tl) in enumerate([(0, 512), (512, 1024)]):
                    act(u[:, c, t0:t0 + tl], x[:, c, t0:t0 + tl], AF.Identity, [xtok[c], modtok], [utok[c]],
                        bias=modv[:, i, c, v:v + 1], scale=modv[:, i, 8 + c, v:v + 1])
                if i == 0:
                    ts("dve", x[:, c, :], x[:, c, :], ALPHA, None, ALU.mult, None, [], [xtok[c]])

        def proj_chunk(wv, wtok, ncols, col0, kchunks, rhs_fn, rtoks):
            pg, ptoks = next_pg()
            for n, (t0, tl) in enumerate(tiles):
                for kc in range(kchunks):
                    mm(pg[0:ncols, t0:t0 + tl], wv[:, kc, col0:col0 + ncols], rhs_fn(kc, t0, tl),
                       kc == 0, kc == kchunks - 1, [wtok] + rtoks(kc), [ptoks[n]])
            return pg, ptoks

        def u_rhs(kc, t0, tl):
            return u[:, kc, t0:t0 + tl]

        def u_toks(kc):
            return [utok[kc]]

        def y_rhs(kc, t0, tl):
            return yb[:, kc, t0:t0 + tl]

        def y_toks(kc):
            return [ytok[kc]]

        def conv3(P, ptoks, dstF, dstFtok, w0, w1, w2, bias, srctoks=()):
            if bias is not None:
                act(dstF[:, 0:T], P[:, 0:T], AF.Identity, list(srctoks), list(ptoks) + [dstFtok], bias=bias, scale=w1)
            else:
                act(dstF[:, 0:T], P[:, 0:T], AF.Identity, list(srctoks), list(ptoks) + [dstFtok], bias=0.0, scale=w1)
            for (s0, L) in SEQS:
                stt("dve", dstF[:, s0 + 1:s0 + L], P[:, s0:s0 + L - 1], w0, dstF[:, s0 + 1:s0 + L], ALU.mult, ALU.add,
                    list(srctoks), list(ptoks) + [dstFtok])
                stt("dve", dstF[:, s0:s0 + L - 1], P[:, s0 + 1:s0 + L], w2, dstF[:, s0:s0 + L - 1], ALU.mult, ALU.add,
                    list(srctoks), list(ptoks) + [dstFtok])

        conv_ctr = [0]

        def conv_evac(pg, ptoks):
            k = conv_ctr[0] % 2
            conv_ctr[0] += 1
            praw, prt = (Fb[0], Ftok[0]) if k == 0 else (Fb[2], Ftok[2])
            cp("act", praw[:, 0:T], pg[:, 0:T], [], list(ptoks) + [prt])
            return k

        def conv_taps(k, w0, w1, w2, bias, dst, dsttok):
            praw, prt = (Fb[0], Ftok[0]) if k == 0 else (Fb[2], Ftok[2])
            cvb, cvt = (Fb[1][:, 0:T], Ftok[1]) if k == 0 else (hbuf[:, 0:T], hsttok)
            ts("dve", cvb, praw[:, 0:T], w1, bias, ALU.mult, ALU.add, [prt, pvtok], [cvt])
            for (s0, L) in SEQS:
                stt("dve", cvb[:, s0 + 1:s0 + L], praw[:, s0:s0 + L - 1], w0, cvb[:, s0 + 1:s0 + L], ALU.mult, ALU.add,
                    [prt], [cvt])
                stt("dve", cvb[:, s0:s0 + L - 1], praw[:, s0 + 1:s0 + L], w2, cvb[:, s0:s0 + L - 1], ALU.mult, ALU.add,
                    [prt], [cvt])
            act(dst, cvb, AF.Silu, [cvt], [dsttok])

        def out_proj_partial(i, wdram, row0):
            for half in range(2):
                def pairs(slot, half=half):
                    return [(slot.rearrange("p (k n) -> p k n", k=8),
                             wdram[row0:row0 + 1024, half * 512:(half + 1) * 512].rearrange("(k p) n -> p k n", p=128))]
                sl, stok = ring.next(pairs)
                wv = sl.rearrange("p (k n) -> p k n", k=8)
                for dd in range(4):
                    dc = half * 4 + dd
                    pg, ptoks = proj_chunk(wv, stok, 128, dd * 128, KC, y_rhs, y_toks)
                    for v, (t0, tl) in enumerate([(0, 512), (512, 1024)]):
                        bt = ptoks[0:1] if v == 0 else ptoks[1:3]
                        stt("dve", x[:, dc, t0:t0 + tl], pg[:, t0:t0 + tl], modv[:, i, 16 + dc, v:v + 1],
                            x[:, dc, t0:t0 + tl], ALU.mult, ALU.add, [modtok], bt + [xtok[dc]])
                ring.release()

        def layer_norm(i):
            mean_sb = Fb[0]
            rstd_sb = Fb[1]
            sq = [Fb[2], Fb[2]]
            sqt = [Ftok[2], Ftok[2]]
            for c in range(KC):
                q = c % 2
                xbf, xbft = Hb[q], Htok[q]
                sqf, sqft = Hb[2 + q], Htok[2 + q]
                cp("dve", xbf[:, :], x[:, c, :], [xtok[c]], [xbft])
                act(sqf[:, :], x[:, c, :], AF.Square, [xtok[c]], [sqft])
                for n, (t0, tl) in enumerate(tiles):
                    mm(pA[:, t0:t0 + tl], onesm[:], xbf[:, t0:t0 + tl], c == 0, c == KC - 1, [ctok, xbft], [bk[n]])
                    mm(pB[:, t0:t0 + tl], onesm[:], sqf[:, t0:t0 + tl], c == 0, c == KC - 1, [ctok, sqft], [bk[3 + n]])
            if i < 3:
                for _ in range(6):
                    mod_unit()
            cp("act", mean_sb[:, 0:T], pA[:, 0:T], [], bk[0:3] + [Ftok[0]])
            tt("dve", rstd_sb[:, 0:T], mean_sb[:, 0:T], mean_sb[:, 0:T], ALU.mult, [Ftok[0]], [Ftok[1]])
            tt("dve", rstd_sb[:, 0:T], pB[:, 0:T], rstd_sb[:, 0:T], ALU.subtract, [], bk[3:6] + [Ftok[1]])
            act(rstd_sb[:, 0:T], rstd_sb[:, 0:T], AF.Ln, [], [Ftok[1]], bias=LN_EPS)
            act(rstd_sb[:, 0:T], rstd_sb[:, 0:T], AF.Exp, [], [Ftok[1]], scale=-0.5)
            for c in range(KC):
                tt("dve", x[:, c, :], x[:, c, :], mean_sb[:, 0:T], ALU.subtract, [Ftok[0]], [xtok[c]])
                tt("dve", x[:, c, :], x[:, c, :], rstd_sb[:, 0:T], ALU.mult, [Ftok[1]], [xtok[c]])
                if i < 3:
                    act(x[:, c, :], x[:, c, :], AF.Identity, [pvtok], [xtok[c]],
                        bias=pvA[:, 32 + i * 8 + c:32 + i * 8 + c + 1], scale=pvA[:, i * 8 + c:i * 8 + c + 1])
                else:
                    act(x[:, c, :], x[:, c, :], AF.Identity, [pvtok], [xtok[c]], bias=pvc(c_lnb(i, c)), scale=pvc(c_lng(i, c)))
                    if DEBUG_STOP is None:
                        S.dma("sp", [(yout[c * 128:(c + 1) * 128, :], x[:, c, :])], reads=[xtok[c]], semi=xs_sem[1])

        def even_layer(i):
            j = i // 2
            modulate(i)
            def pairs_dt(slot):
                return [(slot[:, 0:256].rearrange("p (k n) -> p k n", k=8),
                         w_in_even[j, :, 3072:3104].rearrange("(k p) n -> p k n", p=128))]
            sl, stok = ring.next(pairs_dt)
            wv = sl[:, 0:256].rearrange("p (k n) -> p k n", k=8)
            for tc in range(NTC):
                for kc in range(KC):
                    mm(p6[:, tc * 32:(tc + 1) * 32], u[:, kc, tc * 128:(tc + 1) * 128], wv[:, kc, 0:32],
                       kc == 0, kc == KC - 1, [stok, utok[kc]], [bk[6]])
            ring.release()
            dtmp = wde[:].rearrange("p d t h -> p (d t h)").rearrange("p (t c) -> p t c", c=32)
            tt("dve", dtmp, p6[:, 0:384].rearrange("p (t c) -> p t c", c=32),
               dtb_bc[:, j * 32:(j + 1) * 32].unsqueeze(1).to_broadcast([128, NTC, 32]), ALU.add,
               [abtok], [bk[6], dttok])
            act(dtmp, dtmp, AF.Exp, [], [dttok])
            act(dt_tok[:], dtmp.rearrange("p t (d h) -> p d t h", d=2), AF.Ln, [], [dttok], bias=1.0)
            cp("act", dt_bf[:], dt_tok[:], [], [dttok])
            tt("dve", a_bf[:], dt_tok[:],
               A_bc[:, j * 32:(j + 1) * 32].rearrange("p (d h) -> p d h", d=2).unsqueeze(2).to_broadcast([128, 2, NTC, 16]),
               ALU.mult, [abtok], [dttok])
            def dt_tables():
                mm(p6[:, 0:192], Uf[:], a_bf[:, 0, :, :].rearrange("p t h -> p (t h)"), True, True, [ctok, dttok], [bk[6]])
                mm(p6[:, 192:384], Ub[:], a_bf[:, 1, :, :].rearrange("p t h -> p (t h)"), True, True, [ctok, dttok], [bk[6]])
                mm(p7[:, 0:384], onesb[:], a_bf[:].rearrange("p d t h -> p (d t h)"), True, True, [ctok, dttok], [bk[7]])
                act(wde[:].rearrange("p d t h -> p (d t h)"), p6[:, 0:384], AF.Exp, [], [bk[6], dttok])
                act(cd_bc[:].rearrange("p d t h -> p (d t h)"), p7[:, 0:384], AF.Exp, [], [bk[7], dttok])
                tt("dve", wde[:], wde[:], dt_tok[:], ALU.mult, [], [dttok])

            zs = [Hb[4], Hb[5]]
            zst = [Htok[4], Htok[5]]
            xT = [Hb[0], Hb[1]]
            xTt = [Htok[0], Htok[1]]
            BT, BTt = Hb[2], Htok[2]
            CT, CTt = Hb[3], Htok[3]
            cv, cvt = Fb[1], Ftok[1]
            def tr_block(srcH, srcT, dst_fn, dtok):
                for q in range(3):
                    hq = q % 2
                    for t4 in range(4):
                        tc = q * 4 + t4
                        tr(trP[:, hq * 512 + t4 * 128:hq * 512 + (t4 + 1) * 128], srcH[:, tc * 128:(tc + 1) * 128],
                           identb[:], [srcT, ctok], [bk[6]])
                    cp("act", dst_fn(q), trP[:, hq * 512:(hq + 1) * 512].rearrange("p (t f) -> p t f", t=4),
                       [], [bk[6], dtok])

            for g in range(4):
                def pairsX(slot, g=g):
                    sv = slot.rearrange("p (k n) -> p k n", k=8)
                    return [(sv[:, :, 0:256], w_in_even[j, :, 1024 + g * 256:1024 + (g + 1) * 256].rearrange("(k p) n -> p k n", p=128)),
                            (sv[:, :, 256:384], w_in_even[j, :, 2048 + g * 128:2048 + (g + 1) * 128].rearrange("(k p) n -> p k n", p=128)),
                            (sv[:, :, 384:512], w_in_even[j, :, 2560 + g * 128:2560 + (g + 1) * 128].rearrange("(k p) n -> p k n", p=128))]
                sl, stok = ring.next(pairsX)
                wv = sl.rearrange("p (k n) -> p k n", k=8)
                chunks = [(0, g * 2, xT[0][:, :], xTt[0]), (128, g * 2 + 1, xT[1][:, :], xTt[1]),
                          (256, 8 + g, BT[:, :], BTt), (384, 12 + g, CT[:, :], CTt)]
                kbuf = [None] * 4

                def conv_of(ci):
                    col0, ch, dst, dtok = chunks[ci]
                    conv_taps(kbuf[ci], pvc(c_caw(j, 0, ch)), pvc(c_caw(j, 1, ch)), pvc(c_caw(j, 2, ch)), pvc(c_cab(j, ch)),
                              dst, dtok)

                def tr_x(cc):
                    tr_block(xT[cc], xTt[cc], lambda q, cc=cc: x_tok[:, q * 4:(q + 1) * 4, cc * 128:(cc + 1) * 128], xtoktok)

                for ci in range(4):
                    pg, ptoks = proj_chunk(wv, stok, 128, chunks[ci][0], KC, u_rhs, u_toks)
                    kbuf[ci] = conv_evac(pg, ptoks)
                    if ci >= 1:
                        conv_of(ci - 1)
                    if ci == 3:
                        tr_x(0)
                ring.release()
                if g == 0:
                    dt_tables()
                def pairsZ(slot, g=g):
                    sv = slot[:, 0:2048].rearrange("p (k n) -> p k n", k=8)
                    return [(sv, w_in_even[j, :, g * 256:(g + 1) * 256].rearrange("(k p) n -> p k n", p=128))]
                sl, stok = ring.next(pairsZ)
                wv = sl[:, 0:2048].rearrange("p (k n) -> p k n", k=8)
                for cc in range(2):
                    pg, ptoks = proj_chunk(wv, stok, 128, cc * 128, KC, u_rhs, u_toks)
                    act(zs[cc][:, :], pg[:, 0:T], AF.Silu, [], ptoks + [zst[cc]])
                    if cc == 0:
                        conv_of(3)
                        tr_x(1)
                    else:
                        tr_block(BT, BTt, lambda q: B_tok[:, q * 4:(q + 1) * 4, :], btoktok)
                ring.release()
                ssd_group(j, g, zs, zst, BT, BTt, CT, CTt)

            for c in range(KC):
                sqb, sqbt = Hb[c % 2], Htok[c % 2]
                act(sqb[:, :], yb[:, c, :], AF.Square, [ytok[c]], [sqbt])
                for n, (t0, tl) in enumerate(tiles):
                    mm(pA[:, t0:t0 + tl], onesm[:], sqb[:, t0:t0 + tl], c == 0, c == KC - 1, [ctok, sqbt], [bk[n]])
            act(Fb[2][:, 0:T], pA[:, 0:T], AF.Ln, [], bk[0:3] + [Ftok[2]], bias=RMS_EPS)
            act(Fb[2][:, 0:T], Fb[2][:, 0:T], AF.Exp, [], [Ftok[2]], scale=-0.5)
            for c in range(KC):
                stt("dve", yb[:, c, :], yb[:, c, :], pvc(c_nag(j, c)), Fb[2][:, 0:T], ALU.mult, ALU.mult,
                    [pvtok, Ftok[2]], [ytok[c]])
            def mixb_part1(c):
                def pairsM(slot, c=c):
                    sv = slot.rearrange("p (k n) -> p k n", k=8)
                    return [(sv[:, :, q * 128:(q + 1) * 128],
                             w_in_even[j, :, 3104 + q * 1024 + c * 128:3104 + q * 1024 + (c + 1) * 128].rearrange("(k p) n -> p k n", p=128))
                            for q in range(4)]
                sl, stok = ring.next(pairsM)
                wv = sl.rearrange("p (k n) -> p k n", k=8)
                sg, sgt = Hb[0], Htok[0]
                pg, ptoks = proj_chunk(wv, stok, 128, 0, KC, u_rhs, u_toks)
                act(sg[:, :], pg[:, 0:T], AF.Silu, [], ptoks + [sgt])
                pg, ptoks = proj_chunk(wv, stok, 128, 128, KC, u_rhs, u_toks)
                tt("dve", Fb[0][:, 0:T], pg[:, 0:T], sg[:, :], ALU.mult, [sgt], ptoks + [Ftok[0]])
                pg, ptoks = proj_chunk(wv, stok, 128, 256, KC, u_rhs, u_toks)
                cp("act", Hb[1][:, :], pg[:, 0:T], [], ptoks + [Htok[1]])
                pg, ptoks = proj_chunk(wv, stok, 128, 384, KC, u_rhs, u_toks)
                tt("dve", Fb[1][:, 0:T], pg[:, 0:T], Hb[1][:, :], ALU.mult, [Htok[1]], ptoks + [Ftok[1]])
                ring.release()

            def mixb_part2(c):
                conv3(Fb[1], [], Fb[2], Ftok[2], pvc(c_cbw(j, 0, c)), pvc(c_cbw(j, 1, c)), pvc(c_cbw(j, 2, c)), None,
                      srctoks=[Ftok[1], pvtok])
                tt("dve", yb[:, c, :], Fb[2][:, 0:T], Fb[0][:, 0:T], ALU.mult, [Ftok[2], Ftok[0]], [ytok[c]])

            mixb_part1(0)
            out_proj_partial(i, w_out_even[j], 0)
            mixb_part2(0)
            for c in range(1, KC):
                mixb_part1(c)
                mixb_part2(c)
            out_proj_partial(i, w_out_even[j], 1024)

        def ssd_group(j, g, zs, zst, BT, BTt, CT, CTt):
            gs = slice(g * 4, g * 4 + 4)
            for h4 in range(4):
                hh = j * 16 + g * 4 + h4
                ts("dve", Dmat[:, h4, :], identb[:], D_bc[:, hh:hh + 1], None, ALU.mult, None, [ctok, abtok], [dmtok])
            HC = [(hcur[:], hcurtok), (hcur2, hcur2tok)]
            HT = [(htmp[:], htmptok), (htmp2, htmp2tok)]

            def v4(ap):
                return ap.rearrange("p (h q) -> p h q", h=4)

            hstP = Hb[6][:, 0:1024].rearrange("p (d s q) -> p d s q", d=2, s=2)

            def hst_slot(si, d, tc):
                if si == 2:
                    return hst[:, d, tc - 4, :], hsttok
                return hstP[:, d, si, :], Htok[6]

            def seq_of(tc):
                si = 0 if tc < 2 else (1 if tc < 4 else 2)
                s0, L = SEQS[si]
                return si, s0 // 128, L // 128

            def state_rounds(si):
                s0, L = SEQS[si]
                nch = L // 128
                tc0 = s0 // 128
                is_sample = (si == 2)
                orders = [list(range(tc0, tc0 + nch)), list(range(tc0 + nch - 1, tc0 - 1, -1))]
                have = [False, False]
                fns = []

                def init():
                    for d in range(2):
                        r0 = (j * 2 + d) * 128
                        S.dma("sp", [(HC[d][0], state[r0:r0 + 128, g * 256:(g + 1) * 256])],
                              writes=[HC[d][1]], semi=stin_sem[d])
                        have[d] = True
                if is_sample:
                    fns.append(init)

                def round_(idx):
                    for d in range(2):
                        tc = orders[d][idx]
                        hc, hct = HC[d]
                        ht, htt = HT[d]
                        par = idx % 2
                        if is_sample:
                            if par == 0:
                                stb = pA[:, 1024:1280] if d == 0 else pA[:, 1280:1536]
                                stk = bk[2]
                            else:
                                stb = bank4[:, 128:384] if d == 0 else bank5[:, 256:512]
                                stk = bk[4] if d == 0 else bk[5]
                        else:
                            if par == 0:
                                stb = bank4[:, 128:384] if d == 0 else bank5[:, 256:512]
                                stk = bk[4] if d == 0 else bk[5]
                            else:
                                stb = p6[:, 0:256] if d == 0 else p7[:, 0:256]
                                stk = bk[6] if d == 0 else bk[7]
                        xdb, xdbt = xdd[d][par], xddtok[d][par]
                        if have[d]:
                            hdst, hdtok = hst_slot(si, d, tc)
                            cp("act", hdst, hc, [hct], [hdtok])
                        if idx == nch - 1 and is_sample:
                            continue
                        tt("dve", xdb, v4(x_tok[:, tc, :]),
                           wde[:, d, tc, gs].unsqueeze(2).to_broadcast([128, 4, 64]), ALU.mult,
                           [xtoktok, dttok], [xdbt])
                        mm(stb, B_tok[:, tc, :], xdb.rearrange("p h q -> p (h q)"), True, True,
                           [btoktok, xdbt], [stk])
                        if have[d]:
                            tt("dve", v4(ht), v4(hc),
                               cd_bc[:, d, tc, gs].unsqueeze(2).to_broadcast([128, 4, 64]), ALU.mult,
                               [hct, dttok], [htt])
                            tt("dve", hc, stb, ht, ALU.add, [htt], [stk, hct])
                        else:
                            cp("dve", hc, stb, [], [stk, hct])
                            have[d] = True
                for idx in range(nch):
                    fns.append(lambda idx=idx: round_(idx))

                def final():
                    for d in range(2):
                        hc, hct = HC[d]
                        stv = sto[d][:].rearrange("p a b -> p (a b)")
                        cp("act", stv, hc, [hct], [stotok[d]])
                        r0 = ((si * 2 + j) * 2 + d) * 128
                        S.dma("sp", [(nstate[r0:r0 + 128, g * 256:(g + 1) * 256], stv)],
                              reads=[stotok[d]], semi=sto_sem[d])
                if not is_sample:
                    fns.append(final)
                return fns

            for si in range(2):
                for fn in state_rounds(si):
                    fn()
            pending_rounds = state_rounds(2)

            if True:
                its = list(range(NTC))
                ARG = [(p6[:], bk[6], p7[:], bk[7]), (pA[:, 0:512], bk[0], pA[:, 512:1024], bk[1])]
                YB = [(bank5, bk[5]), (pB[:, 0:512], bk[3])]

                def stage_pre_a(n):
                    tc = its[n]
                    k0 = tc * 128
                    mm(bank4[:, 0:128], BT[:, k0:k0 + 128], CT[:, k0:k0 + 128], True, True, [BTt, CTt], [bk[4]])
                    tt("dve", Rb2[:], Lfb[:].unsqueeze(2).to_broadcast([128, 2, 4, 128]),
                       a_bf[:, :, tc, gs].unsqueeze(3).to_broadcast([128, 2, 4, 128]), ALU.mult,
                       [ctok, dttok], [Rtok])
                    for d in range(2):
                        Um = Uf if d == 0 else Ub
                        pa, pat, pe_, pet = ARG[d]
                        Rflat = Rb2[:, d].rearrange("p h i -> p (h i)")
                        mm(pa, Um[:], Rflat, True, True, [ctok, Rtok], [pat])
                        mm(pe_, onesb[:], Rflat, True, True, [ctok, Rtok], [pet])

                def stage_pre_b(n):
                    for d in range(2):
                        pa, pat, pe_, pet = ARG[d]
                        act(eA2[:, d].rearrange("p h i -> p (h i)"), pa, AF.Exp, [], [pat, eAtok[d]])
                        act(eE2[:, d].rearrange("p h i -> p (h i)"), pe_, AF.Exp, [], [pet, eEtok[d]])

                def stage_cbm():
                    tt("dve", CBm2[:], bank4[:, 0:128].unsqueeze(1).to_broadcast([128, 2, 128]), Lfb[:], ALU.mult,
                       [ctok], [bk[4], CBmtok])

                def stage_mid(n):
                    tc = its[n]
                    k0 = tc * 128
                    q = n % 2
                    tt("dve", MT2[q][:], eA2[:], CBm2[:].unsqueeze(2).to_broadcast([128, 2, 4, 128]), ALU.mult,
                       [eAtok[0], eAtok[1], CBmtok], [MTtok[q]])
                    tt("dve", Cs2[q][:], eE2[:],
                       CT[:, k0:k0 + 128].unsqueeze(1).unsqueeze(1).to_broadcast([128, 2, 4, 128]), ALU.mult,
                       [eEtok[0], eEtok[1], CTt], [Cstok[q]])
                    tt("dve", xdt2[q][:], v4(x_tok[:, tc, :]).unsqueeze(1).to_broadcast([128, 2, 4, 64]),
                       dt_bf[:, :, tc, gs].unsqueeze(3).to_broadcast([128, 2, 4, 64]), ALU.mult,
                       [xtoktok, dttok], [xdttok[q]])

                def stage_y(n):
                    tc = its[n]
                    q = n % 2
                    yb_, ybt = YB[q]
                    si, tc0, nch = seq_of(tc)
                    has_f = (si == 2) or (tc != tc0)
                    has_b = (si == 2) or (tc != tc0 + nch - 1)
                    hf, hft = hst_slot(si, 0, tc)
                    hb_, hbt = hst_slot(si, 1, tc)
                    for h4 in range(4):
                        po = (h4 % 2) * 64
                        out = yb_[po:po + 64, (h4 // 2) * 128:(h4 // 2) * 128 + 128]
                        tp = (0, 64) if po else None
                        mm(out, xdt2[q][:, 0, h4, :], MT2[q][:, 0, h4, :], True, False, [xdttok[q], MTtok[q]], [ybt], tp)
                        mm(out, xdt2[q][:, 1, h4, :], MT2[q][:, 1, h4, :], False, False, [xdttok[q], MTtok[q]], [ybt], tp)
                        if has_f:
                            mm(out, hf[:, h4 * 64:(h4 + 1) * 64], Cs2[q][:, 0, h4, :], False, False,
                               [hft, Cstok[q]], [ybt], tp)
                        if has_b:
                            mm(out, hb_[:, h4 * 64:(h4 + 1) * 64], Cs2[q][:, 1, h4, :], False, False,
                               [hbt, Cstok[q]], [ybt], tp)
                        mm(out, x_tok[:, tc, h4 * 64:(h4 + 1) * 64], Dmat[:, h4, :], False, True, [xtoktok, dmtok], [ybt], tp)

                def stage_evac(n):
                    tc = its[n]
                    k0 = tc * 128
                    yb_, ybt = YB[n % 2]
                    tt("dve", yb[:, g * 2:g * 2 + 2, k0:k0 + 128], yb_[:, 0:256].rearrange("p (r i) -> p r i", r=2),
                       Hb45[:, :, k0:k0 + 128], ALU.mult, [zst[0], zst[1]], [ybt, ytok[g * 2], ytok[g * 2 + 1]])

                N = len(its)
                stage_pre_a(0)
                stage_pre_b(0)
                stage_cbm()
                for n in range(N):
                    if n == 4:
                        while pending_rounds:
                            pending_rounds.pop(0)()
                    if n + 1 < N:
                        stage_pre_a(n + 1)
                    stage_mid(n)
                    if n + 1 < N:
                        stage_pre_b(n + 1)
                    if n >= 1:
                        stage_evac(n - 1)
                    if n + 1 < N:
                        stage_cbm()
                    stage_y(n)
                    if n < 4:
                        for _ in range(3):
                            if pending_rounds:
                                pending_rounds.pop(0)()
                stage_evac(N - 1)

        def odd_layer(i):
            j = i // 2
            modulate(i)
            S.op("dve", lambda e: e.memset(Fb[0][:], 0.0), (), [Ftok[0]])
            S.op("dve", lambda e: e.memset(Fb[1][:], 0.0), (), [Ftok[1]])
            S.op("dve", lambda e: e.memset(Fb[2][:], 0.0), (), [Ftok[2]])
            for half in range(2):
                for kk in range(2):
                    k = half * 2 + kk
                    w = WINS[k]
                    invc, invct = invc_alias, hsttok
                    S.op("dve", lambda e: e.memset(invc, 1.0 / w), (), [invct])
                    for si, (s0, L) in enumerate(SEQS):
                        o = POFF[si]
                        cp("dve", invc[:, o:o + 8], invtab[:, k, 0:8], [invtok], [invct])
                        cp("dve", invc[:, o + L - 8:o + L], invtab[:, k, 8:16], [invtok], [invct])
                    def pairsV(slot, k=k):
                        return [(slot.rearrange("p (k n) -> p k n", k=8),
                                 w_in_odd[j, :, k * 512:(k + 1) * 512].rearrange("(k p) n -> p k n", p=128))]

                    def pairsG(slot, k=k):
                        return [(slot.rearrange("p (k n) -> p k n", k=8),
                                 w_in_odd[j, :, 2048 + k * 512:2048 + (k + 1) * 512].rearrange("(k p) n -> p k n", p=128))]
                    slV, stokV = ring.next(pairsV)
                    wvV = slV.rearrange("p (k n) -> p k n", k=8)
                    slG, stokG = ring.next(pairsG, prefetch=False)
                    wvG = slG.rearrange("p (k n) -> p k n", k=8)
                    for cc in range(4):
                        pg, ptoks = proj_chunk(wvV, stokV, 128, cc * 128, KC, u_rhs, u_toks)
                        vp, vpt = Fb[0], Ftok[0]
                        for si, (s0, L) in enumerate(SEQS):
                            o = POFF[si]
                            bt = ptoks[0:1] if si < 2 else ptoks[1:3]
                            cp("act", vp[:, o:o + L], pg[:, s0:s0 + L], [], bt + [vpt])
                        pg2, ptoks2 = proj_chunk(wvG, stokG, 128, cc * 128, KC, u_rhs, u_toks)
                        act(Hb[cc][:, :], pg2[:, 0:T], AF.Silu, [], ptoks2 + [Htok[cc]])
                        src, srct = vp, vpt
                        bufs = [(Fb[1], Ftok[1]), (Fb[2], Ftok[2])]
                        sh = [(1, 0), (1, 1), (2, 2), (4, 4)]
                        for lev in range(k + 1):
                            dst, dstt = bufs[lev % 2]
                            a_, b_ = sh[lev]
                            tt("dve", dst[:, 8:PW - 8], src[:, 8 - a_:PW - 8 - a_], src[:, 8 + b_:PW - 8 + b_], ALU.add,
                               [srct], [dstt])
                            src, srct = dst, dstt
                        oth, otht = bufs[(k + 1) % 2]
                        tt("dve", oth[:, 8:PW - 8], src[:, 8:PW - 8], invc[:, 8:PW - 8], ALU.mult, [srct, invct], [otht])
                        for si, (s0, L) in enumerate(SEQS):
                            o = POFF[si]
                            tt("dve", Hb[4 + cc][:, s0:s0 + L], oth[:, o:o + L], vp[:, o:o + L], ALU.subtract,
                               [otht, vpt], [Htok[4 + cc]])
                    def pairsP(slot, k=k):
                        return [(slot[:, 0:2048].rearrange("p (k n) -> p k n", k=4),
                                 w_pool[j, k, :, :].rearrange("(k p) n -> p k n", p=128))]
                    sl, stok = ring.next(pairsP)
                    wv = sl[:, 0:2048].rearrange("p (k n) -> p k n", k=4)
                    for dd in range(4):
                        pg, ptoks = proj_chunk(wv, stok, 128, dd * 128, 4,
                                               lambda kc, t0, tl: Hb[4 + kc][:, t0:t0 + tl], lambda kc: [Htok[4 + kc]])
                        yc = kk * 4 + dd
                        stt("dve", yb[:, yc, :], pg[:, 0:T], pvc(c_psc(j, k * 4 + dd)), Hb[dd][:, :], ALU.mult, ALU.mult,
                            [pvtok, Htok[dd]], ptoks + [ytok[yc]])
                    ring.release()
                out_proj_partial(i, w_out_odd[j], half * 1024)

        S.dry = True
        emit()
        S.dry = False
        S.plan = True
        pg_state[0] = 0
        emit()
        S.plan = False
        S.reset()
        ring.issued = 0
        ring.cur = 0
        pg_state[0] = 0
        emit()
        for semi in list(xs_sem) + list(sto_sem):
            ent = S.dma_sems[semi]
            if ent[1] > 0:
                nc.sync.wait_ge(ent[0], ent[1])
        print("program: insts", S.cnt, "incs", S.semval, "waits", S.nwaits, "pieces", len(ring.plan))
    return nc


_NC_CACHE = {}


def kernel(x_prompt, x_sample, state_ssd, c, c_ctx, w_mod, b_mod, ln_g, ln_b, w_in_even, conv_a_w,
           conv_a_b, a_log, dt_bias, d_skip, norm_a_g, conv_b_w, w_out_even, w_in_odd, w_pool,
           pool_scale, w_out_odd):
    f = lambda a: np.ascontiguousarray(np.asarray(a), dtype=np.float32)
    if "nc" not in _NC_CACHE:
        _NC_CACHE["nc"] = build_program()
    nc = _NC_CACHE["nc"]
    x_prompt = f(x_prompt)
    x_sample = f(x_sample)
    state_ssd = f(state_ssd)
    c = f(c)
    c_ctx = f(c_ctx)
    shared = dict(w_mod=f(w_mod), b_mod=f(b_mod), ln_g=f(ln_g), ln_b=f(ln_b), w_in_even=f(w_in_even),
                  conv_a_w=f(conv_a_w), conv_a_b=f(conv_a_b), a_log=f(a_log), dt_bias=f(dt_bias), d_skip=f(d_skip),
                  norm_a_g=f(norm_a_g), conv_b_w=f(conv_b_w), w_out_even=f(w_out_even), w_in_odd=f(w_in_odd),
                  w_pool=f(w_pool), pool_scale=f(pool_scale), w_out_odd=f(w_out_odd))
    in_maps = []
    for k in range(NCORES):
        m = dict(shared)
        m["xin"] = np.ascontiguousarray(np.concatenate([x_prompt[2 * k], x_prompt[2 * k + 1], x_sample[k]], axis=0).T)
        m["state"] = np.ascontiguousarray(state_ssd[k].transpose(0, 1, 4, 2, 3).reshape(512, 1024))
        m["cvec"] = np.ascontiguousarray(np.stack([c_ctx, c[k]], axis=0))
        in_maps.append(m)
    res = run_bass_kernel_spmd(nc, in_maps, core_ids=list(range(NCORES)))
    y_prompt = np.empty((16, 256, D), np.float32)
    y_sample = np.empty((8, 1024, D), np.float32)
    new_state = np.empty((16, 2, 2, 16, 64, 128), np.float32)
    for k in range(NCORES):
        r = res.results[k]
        yo = np.asarray(r["yout"]).T
        y_prompt[2 * k] = yo[0:256]
        y_prompt[2 * k + 1] = yo[256:512]
        y_sample[k] = yo[512:1536]
        ns = np.asarray(r["nstate"]).reshape(2, 2, 2, 128, 16, 64).transpose(0, 1, 2, 4, 5, 3)
        new_state[2 * k] = ns[0]
        new_state[2 * k + 1] = ns[1]
    return (y_prompt, y_sample, new_state)
```

```python
import math
from contextlib import ExitStack

import numpy as np
import concourse.bass as bass
import concourse.mybir as mybir
from concourse.bass_utils import run_bass_kernel_spmd

F32 = mybir.dt.float32
BF16 = mybir.dt.bfloat16
I32 = mybir.dt.int32
AF = mybir.ActivationFunctionType
ALU = mybir.AluOpType

NCORES = 8
D = 1024
T = 1536
KC = 8
NT = 3
SEQS = [(0, 256), (256, 256), (512, 1024)]
NTC = 12
ALPHA = (2 * 4) ** 0.25
LN_EPS = 1e-5
RMS_EPS = 1e-5
IN_EVEN = 7200
PW = 1600
POFF = [16, 280, 544]
WINS = (2, 4, 8, 16)
NSLOT = 2
SLOT = 4096

DEBUG_STOP = None


class Tok:
    __slots__ = ("w", "r", "name")
    ALL = []

    def __init__(self, name=""):
        self.w = None
        self.r = {}
        self.name = name
        Tok.ALL.append(self)


class Sched:
    ENG = ("pe", "dve", "act", "pool")

    def __init__(self, nc, es):
        self.nc = nc
        self.es = es
        self.engs = {"pe": nc.tensor, "dve": nc.vector, "act": nc.scalar, "pool": nc.gpsimd, "sp": nc.sync}
        self.sem = {k: es.enter_context(nc.semaphore("sem_" + k)) for k in self.ENG}
        self.cnt = {k: 0 for k in self.ENG}
        self.waited = {k: {} for k in list(self.ENG) + ["sp"]}
        self.dma_sems = []
        self.dry = False
        self.plan = False
        self.needed = {k: set() for k in self.ENG}
        self.semval = {k: 0 for k in self.ENG}
        self.valof = {k: {} for k in self.ENG}
        self.nwaits = 0

    def reset(self):
        for t in Tok.ALL:
            t.w = None
            t.r = {}
        self.cnt = {k: 0 for k in self.ENG}
        self.waited = {k: {} for k in list(self.ENG) + ["sp"]}
        self.semval = {k: 0 for k in self.ENG}
        self.valof = {k: {} for k in self.ENG}
        self.nwaits = 0
        for ent in self.dma_sems:
            ent[1] = 0

    def _wait(self, eng, dep):
        if dep is None:
            return
        src, val = dep
        if isinstance(src, str):
            if src == eng and eng == "pe":
                return
            key = src
            sem = self.sem[src]
        else:
            key = ("dma", src.num)
            sem = src
        if self.waited[eng].get(key, 0) >= val:
            return
        self.nwaits += 1
        self.waited[eng][key] = val
        if self.plan:
            if isinstance(src, str):
                self.needed[src].add(val)
            return
        if isinstance(src, str):
            self.engs[eng].wait_ge(sem, self.valof[src][val])
        else:
            self.engs[eng].wait_ge(sem, val)

    def op(self, eng, fn, reads=(), writes=()):
        if self.dry:
            return
        for t in reads:
            self._wait(eng, t.w)
        for t in writes:
            self._wait(eng, t.w)
            for src, val in t.r.items():
                if src == eng and eng == "pe":
                    continue
                self._wait(eng, (src, val))
        self.cnt[eng] += 1
        c = self.cnt[eng]
        if not self.plan:
            ins = fn(self.engs[eng])
            if c in self.needed[eng]:
                ins.then_inc(self.sem[eng], 1)
                self.semval[eng] += 1
                self.valof[eng][c] = self.semval[eng]
        for t in reads:
            t.r[eng] = c
        for t in writes:
            t.w = (eng, c)
            t.r = {}

    def new_dma_sem(self):
        s = self.es.enter_context(self.nc.semaphore("dsem%d" % len(self.dma_sems)))
        self.dma_sems.append([s, 0])
        return len(self.dma_sems) - 1

    def dma(self, q, pairs, reads=(), writes=(), semi=None, **kw):
        if self.dry:
            return
        for t in reads:
            self._wait(q, t.w)
        for t in writes:
            self._wait(q, t.w)
            for src_, val in t.r.items():
                self._wait(q, (src_, val))
        ent = self.dma_sems[semi]
        for (o, i_) in pairs:
            ent[1] += 16
            if not self.plan:
                ins = self.engs[q].dma_start(out=o, in_=i_, **kw)
                ins.then_inc(ent[0], 16)
        dep = (ent[0], ent[1])
        for t in reads:
            t.r[ent[0]] = ent[1]
        for t in writes:
            t.w = dep
            t.r = {}


class Ring:
    def __init__(self, S, buf):
        self.S = S
        self.buf = buf
        self.toks = [Tok("ring%d" % i) for i in range(NSLOT)]
        self.sems = [S.new_dma_sem() for _ in range(NSLOT)]
        self.plan = []
        self.issued = 0
        self.cur = 0

    def slot_ap(self, s):
        return self.buf[:, s, :]

    def _issue(self):
        p = self.issued
        s = p % NSLOT
        pairs = self.plan[p](self.slot_ap(s))
        self.S.dma("pool", pairs, writes=[self.toks[s]], semi=self.sems[s])
        self.issued += 1

    def next(self, pairs_fn, prefetch=True):
        if self.S.dry:
            self.plan.append(pairs_fn)
            return self.slot_ap(0), self.toks[0]
        p = self.cur
        while self.issued < min(len(self.plan), (p + NSLOT) if prefetch else (p + 1)):
            self._issue()
        s = p % NSLOT
        self.cur += 1
        return self.slot_ap(s), self.toks[s]

    def release(self):
        pass

    def kick(self):
        if self.S.dry:
            return
        while self.issued < min(len(self.plan), NSLOT):
            self._issue()


def build_program():
    nc = bass.Bass("TRN2", target_bir_lowering=False)

    def din(name, shape):
        return nc.dram_tensor(name, list(shape), F32, kind="ExternalInput").ap()

    xin = din("xin", [D, T])
    state = din("state", [512, 1024])
    cvec = din("cvec", [2, D])
    w_mod = din("w_mod", [4, D, 3072])
    b_mod = din("b_mod", [4, 3072])
    ln_g = din("ln_g", [4, D])
    ln_b = din("ln_b", [4, D])
    w_in_even = din("w_in_even", [2, D, IN_EVEN])
    conv_a_w = din("conv_a_w", [2, 3, 2048])
    conv_a_b = din("conv_a_b", [2, 2048])
    a_log = din("a_log", [2, 2, 16])
    dt_bias = din("dt_bias", [2, 2, 16])
    d_skip = din("d_skip", [2, 16])
    norm_a_g = din("norm_a_g", [2, D])
    conv_b_w = din("conv_b_w", [2, 3, D])
    w_out_even = din("w_out_even", [2, 2048, D])
    w_in_odd = din("w_in_odd", [2, D, 4096])
    w_pool = din("w_pool", [2, 4, 512, 512])
    pool_scale = din("pool_scale", [2, 2048])
    w_out_odd = din("w_out_odd", [2, 2048, D])
    yout = nc.dram_tensor("yout", [D, T], F32, kind="ExternalOutput").ap()
    nstate = nc.dram_tensor("nstate", [1024, 1024], F32, kind="ExternalOutput").ap()

    with ExitStack() as es:
        S = Sched(nc, es)

        def sb(name, shape, dt):
            return es.enter_context(nc.sbuf_tensor(name, list(shape), dt))

        x = sb("x", [128, KC, T], F32)
        xtok = [Tok("x%d" % c) for c in range(KC)]
        u = sb("u", [128, KC, T], BF16)
        utok = [Tok("u%d" % c) for c in range(KC)]
        yb = sb("ybuf", [128, KC, T], BF16)
        ytok = [Tok("y%d" % c) for c in range(KC)]
        ringbuf = sb("ring", [128, NSLOT, SLOT], BF16)
        ring = Ring(S, ringbuf)
        Fb = [sb("F%d" % i, [128, PW], F32) for i in range(3)]
        Ftok = [Tok("F%d" % i) for i in range(3)]
        Hb = [sb("H%d" % i, [128, T], BF16) for i in range(4)]
        Hb45 = sb("H45", [128, 2, T], BF16)
        Hb += [Hb45[:, 0, :], Hb45[:, 1, :]]
        Hb += [sb("H%d" % i, [128, T], BF16) for i in (6, 7)]
        Htok = [Tok("H%d" % i) for i in range(8)]
        pv = sb("pv", [128, 384], F32)
        pvtok = Tok("pv")
        pvA = sb("pvA", [128, 64], F32)
        modv = sb("modv", [128, 4, 24, 2], F32)
        modtok = Tok("modv")
        ident = sb("ident", [128, 128], F32)
        identb = sb("identb", [128, 128], BF16)
        onesb = sb("onesb", [128, 128], BF16)
        onesm = sb("onesm", [128, 128], BF16)
        onesf = sb("onesf", [128, 128], F32)
        Lfb = sb("Lfb", [128, 2, 128], BF16)
        Lf = Lfb[:, 0, :]
        Lb = Lfb[:, 1, :]
        Uf = sb("Uf", [128, 128], BF16)
        Ub = sb("Ub", [128, 128], BF16)
        ctok = Tok("consts")
        mtmp = Fb[0][:, 1024:1152]
        mtmptok = Tok("mtmp")
        A_bc = sb("A_bc", [128, 64], F32)
        D_bc = sb("D_bc", [128, 32], F32)
        dtb_bc = sb("dtb_bc", [128, 64], F32)
        abtok = Tok("abc")
        posc = sb("posc", [128, 4, 64], F32)
        postok = Tok("pos")
        invtab = sb("invtab", [128, 4, 16], F32)
        invtok = Tok("invtab")
        scb = sb("scb", [128, 8, 2], BF16)
        scbtok = Tok("scb")
        dt_tok = sb("dt_tok", [128, 2, NTC, 16], F32)
        a_bf = sb("a_bf", [128, 2, NTC, 16], BF16)
        dt_bf = sb("dt_bf", [128, 2, NTC, 16], BF16)
        wde = sb("wde", [128, 2, NTC, 16], F32)
        cd_bc = sb("cd_bc", [128, 2, NTC, 16], F32)
        dttok = Tok("dtstuff")
        Dmat = sb("Dmat", [128, 4, 128], BF16)
        dmtok = Tok("Dmat")
        stin_sem = [S.new_dma_sem(), S.new_dma_sem()]
        x_tok = sb("x_tok", [128, NTC, 256], BF16)
        xtoktok = Tok("x_tok")
        B_tok = sb("B_tok", [128, NTC, 128], BF16)
        btoktok = Tok("B_tok")
        hbuf = sb("hbuf", [128, 2048], F32)
        hst = hbuf[:].bitcast(BF16).rearrange("p (d t q) -> p d t q", d=2, t=8)
        invc_alias = hbuf[:, 0:PW]
        hsttok = Tok("hst")
        hcur = sb("hcur", [128, 256], F32)
        hcurtok = Tok("hcur")
        htmp = sb("htmp", [128, 256], F32)
        htmptok = Tok("htmp")
        NB = 2
        Rb2 = sb("Rb2", [128, 2, 4, 128], BF16)
        Rtok = Tok()
        eA2 = sb("eA2", [128, 2, 4, 128], BF16)
        eAtok = [Tok() for _ in range(2)]
        eE2 = sb("eE2", [128, 2, 4, 128], BF16)
        eEtok = [Tok() for _ in range(2)]
        MT2 = [sb("MT2%d" % q, [128, 2, 4, 128], BF16) for q in range(2)]
        MTtok = [Tok() for q in range(2)]
        Cs2 = [sb("Cs2%d" % q, [128, 2, 4, 128], BF16) for q in range(2)]
        Cstok = [Tok() for q in range(2)]
        xdt2 = [sb("xdt2%d" % q, [128, 2, 4, 64], BF16) for q in range(2)]
        xdttok = [Tok() for q in range(2)]
        h7f = Hb[7][:, :].bitcast(F32)
        hcur2 = h7f[:, 0:256]
        htmp2 = h7f[:, 256:512]
        xdd = [[sb("xdd", [128, 4, 64], BF16)[:], sb("xddb", [128, 4, 64], BF16)[:]],
               [Hb[7][:, 1024:1280].rearrange("p (h q) -> p h q", h=4), Hb[7][:, 1280:1536].rearrange("p (h q) -> p h q", h=4)]]
        xddtok = [[Tok("xdd00"), Tok("xdd01")], [Tok("xdd10"), Tok("xdd11")]]
        hcur2tok = Tok("hcur2")
        htmp2tok = Tok("htmp2")
        CBm2 = sb("CBm2", [128, 2, 128], BF16)
        CBmtok = Tok()
        sto = [sb("sto%d" % i, [128, 2, 128], F32) for i in range(2)]
        stotok = [Tok() for _ in range(2)]
        sto_sem = [S.new_dma_sem() for _ in range(2)]
        xs = [Fb[1][:, 0:D], Fb[2][:, 0:D]]
        xstok = [Ftok[1], Ftok[2]]
        xs_sem = [S.new_dma_sem() for _ in range(2)]
        pst = [Fb[2][:, i * 128:(i + 1) * 128] for i in range(3)]
        cst = Fb[2][0:16, 384:512]
        psttok = Ftok[2]
        pst_sem = S.new_dma_sem()
        misc_sem = S.new_dma_sem()

        pA = es.enter_context(nc.psum_tensor("pA", [128, 1536], F32))
        pB = es.enter_context(nc.psum_tensor("pB", [128, 1536], F32))
        p6 = es.enter_context(nc.psum_tensor("p6", [128, 512], F32))
        p7 = es.enter_context(nc.psum_tensor("p7", [128, 512], F32))
        bk = [Tok("bank%d" % i) for i in range(8)]
        PG = [(pA, bk[0:3]), (pB, bk[3:6])]
        pg_state = [0]
        trB = pB[:, 0:512].bitcast(BF16)
        trP = p6[:, :].bitcast(BF16)
        bank4 = pB[:, 512:1024]
        bank5 = pB[:, 1024:1536]

        def next_pg():
            g = PG[pg_state[0] % 2]
            pg_state[0] += 1
            return g

        def tt(eng, out, in0, in1, op, R, W):
            S.op(eng, lambda e: e.tensor_tensor(out, in0, in1, op), R, W)

        def ts(eng, out, in0, s1, s2, op0, op1, R, W):
            if s2 is None:
                S.op(eng, lambda e: e.tensor_scalar(out, in0, s1, None, op0), R, W)
            else:
                S.op(eng, lambda e: e.tensor_scalar(out, in0, s1, s2, op0, op1), R, W)

        def stt(eng, out, in0, scalar, in1, op0, op1, R, W):
            S.op(eng, lambda e: e.scalar_tensor_tensor(out=out, in0=in0, scalar=scalar, in1=in1, op0=op0, op1=op1), R, W)

        def act(out, in_, func, R, W, bias=None, scale=None):
            kw = {}
            if bias is not None:
                kw["bias"] = bias
            if scale is not None:
                kw["scale"] = scale
            S.op("act", lambda e: e.activation(out=out, in_=in_, func=func, **kw), R, W)

        def cp(eng, out, in_, R, W):
            if eng == "act":
                act(out, in_, AF.Copy, R, W)
            else:
                S.op(eng, lambda e: e.tensor_copy(out, in_), R, W)

        def mm(out, lhsT, rhs, start, stop, R, W, tp=None):
            if tp is None:
                S.op("pe", lambda e: e.matmul(out, lhsT, rhs, start=start, stop=stop), R, W)
            else:
                S.op("pe", lambda e: e.matmul(out, lhsT, rhs, start=start, stop=stop, tile_position=tp), R, W)

        def tr(out, in_, idn, R, W):
            S.op("pe", lambda e: e.transpose(out, in_, idn), R, W)

        def pvc(col):
            return pv[:, col:col + 1]

        def c_bmod(i, m): return i * 24 + m
        def c_lng(i, c): return 96 + i * 8 + c
        def c_caw(j, k, c): return 128 + j * 48 + k * 16 + c
        def c_lnb(i, c): return 128 + 96 + i * 8 + c
        def c_cab(j, c): return 256 + j * 16 + c
        def c_nag(j, c): return 256 + 32 + j * 8 + c
        def c_cbw(j, k, c): return 256 + 48 + j * 24 + k * 8 + c
        def c_psc(j, c): return 256 + 96 + j * 16 + c

        tiles = [(n * 512, 512) for n in range(NT)]

        def emit():
            mod_reset()
            ring.kick()
            def mask(dst, pattern, cm, cmp_):
                S.op("pool", lambda e: e.memset(mtmp[:], 1.0), (), [mtmptok])
                S.op("pool", lambda e: e.affine_select(out=mtmp[:], in_=mtmp[:], pattern=pattern, compare_op=cmp_,
                                                       fill=0.0, base=0, channel_multiplier=cm), (), [mtmptok])
                cp("dve", dst[:], mtmp[:], [mtmptok], [ctok])

            S.op("pool", lambda e: e.memset(ident[:], 0.0), (), [ctok])
            S.op("pool", lambda e: e.affine_select(out=ident[:], in_=ident[:], pattern=[[-1, 128]], compare_op=ALU.not_equal,
                                                   fill=1.0, base=0, channel_multiplier=1), (), [ctok])
            cp("dve", identb[:], ident[:], [ctok], [ctok])
            S.op("pool", lambda e: e.memset(onesb[:], 1.0), (), [ctok])
            S.op("pool", lambda e: e.memset(onesm[:], 1.0 / 1024), (), [ctok])
            S.op("pool", lambda e: e.memset(onesf[:], 1.0 / 1024), (), [ctok])
            mask(Lf, [[1, 128]], -1, ALU.is_ge)
            mask(Lb, [[-1, 128]], 1, ALU.is_ge)
            mask(Uf, [[-1, 128]], 1, ALU.is_gt)
            mask(Ub, [[1, 128]], -1, ALU.is_gt)

            def rows(ap2d):
                return ap2d
            loads = [
                (pst[0][0:96, :], b_mod.rearrange("i (m p) -> (i m) p", p=128)),
                (pst[0][96:128, :], ln_g.rearrange("i (c p) -> (i c) p", p=128)),
                (pst[1][0:96, :], conv_a_w.rearrange("j k (c p) -> (j k c) p", p=128)),
                (pst[1][96:128, :], ln_b.rearrange("i (c p) -> (i c) p", p=128)),
                (pst[2][0:32, :], conv_a_b.rearrange("j (c p) -> (j c) p", p=128)),
                (pst[2][32:48, :], norm_a_g.rearrange("j (c p) -> (j c) p", p=128)),
                (pst[2][48:96, :], conv_b_w.rearrange("j k (c p) -> (j k c) p", p=128)),
                (pst[2][96:128, :], pool_scale.rearrange("j (c p) -> (j c) p", p=128)),
                (cst[:, :], cvec.rearrange("v (k p) -> (v k) p", p=128)),
            ]
            S.dma("sp", loads, writes=[psttok], semi=pst_sem)
            S.dma("sp", [(A_bc[:], a_log.rearrange("j d h -> (j d h)").partition_broadcast(128)),
                         (D_bc[:], d_skip.rearrange("j h -> (j h)").partition_broadcast(128))],
                  writes=[abtok], semi=misc_sem)
            S.dma("sp", [(dtb_bc[:], dt_bias.rearrange("j d h -> (j d h)").partition_broadcast(128))],
                  writes=[abtok], semi=misc_sem)
            for i in range(3):
                tr(pA[:, i * 128:(i + 1) * 128], pst[i][:], ident[:], [psttok, ctok], [bk[0]])
            tr(pA[:, 384:400], cst[:], ident[0:16, 0:16], [psttok, ctok], [bk[0]])
            cp("dve", pv[:], pA[:, 0:384], [], [bk[0], pvtok])
            ts("dve", pvA[:, 0:32], pv[:, 96:128], ALPHA, None, ALU.mult, None, [], [pvtok])
            ts("dve", pvA[:, 32:64], pv[:, 224:256], ALPHA, None, ALU.mult, None, [], [pvtok])
            act(scb[:].rearrange("p k v -> p v k"), pA[:, 384:400].rearrange("p (v k) -> p v k", v=2), AF.Silu,
                [], [bk[0], scbtok])
            act(A_bc[:], A_bc[:], AF.Exp, [], [abtok])
            ts("dve", A_bc[:], A_bc[:], -1.0, None, ALU.mult, None, [], [abtok])

            kidx = Fb[0][:, 0:2]
            pvals = Fb[0][:, 64:128]
            S.op("pool", lambda e: e.iota(kidx, pattern=[[128, 2]], base=0, channel_multiplier=1,
                                          allow_small_or_imprecise_dtypes=True), (), [Ftok[0]])
            S.op("pool", lambda e: e.iota(pvals, pattern=[[1, 64]], base=0, channel_multiplier=0,
                                          allow_small_or_imprecise_dtypes=True), (), [Ftok[0]])
            omega = Fb[0][:, 2:4]
            act(omega, kidx, AF.Exp, [], [Ftok[0]], scale=-math.log(10000.0) / 256.0)
            ang = Fb[1][:, 0:256].rearrange("p (a b) -> p a b", a=4)
            for ch in range(4):
                cc = ch % 2
                ts("dve", ang[:, ch, :], pvals, omega[:, cc:cc + 1], 1.0 / (2 * math.pi), ALU.mult, ALU.mult,
                   [Ftok[0]], [Ftok[1]])
                if ch >= 2:
                    ts("dve", ang[:, ch, :], ang[:, ch, :], 0.25, None, ALU.add, None, [], [Ftok[1]])
            angi = Fb[1][:, 256:512].bitcast(I32)
            angf = Fb[1][:, 512:768]
            cp("dve", angi, Fb[1][:, 0:256], [], [Ftok[1]])
            cp("dve", angf, angi, [], [Ftok[1]])
            tt("dve", Fb[1][:, 0:256], Fb[1][:, 0:256], angf, ALU.subtract, [], [Ftok[1]])
            act(posc[:].rearrange("p a b -> p (a b)"), Fb[1][:, 0:256], AF.Sin, [Ftok[1]], [postok], scale=6.283185)

            ev = Fb[0][:, 128:136]
            S.op("pool", lambda e: e.iota(ev, pattern=[[1, 8]], base=0, channel_multiplier=0,
                                          allow_small_or_imprecise_dtypes=True), (), [Ftok[0]])
            for k, w in enumerate(WINS):
                ts("dve", invtab[:, k, 0:8], ev, float(w // 2), float(w), ALU.add, ALU.min, [Ftok[0]], [invtok])
                ts("dve", invtab[:, k, 8:16], ev, -1.0, float(8 + w // 2), ALU.mult, ALU.add, [Ftok[0]], [invtok])
                ts("dve", invtab[:, k, 8:16], invtab[:, k, 8:16], float(w), None, ALU.min, None, [], [invtok])
            S.op("dve", lambda e: e.reciprocal(invtab[:].rearrange("p a b -> p (a b)"),
                                               invtab[:].rearrange("p a b -> p (a b)")), [], [invtok])

            S.dma("sp", [(x[:, c, :], xin[c * 128:(c + 1) * 128, :]) for c in range(KC)], writes=xtok, semi=xs_sem[0])
            xs4 = x[:, 0:4, 512:T].rearrange("p c (r q) -> p c r q", q=64)
            tt("dve", xs4, xs4, posc[:, :, 0:16].unsqueeze(3).to_broadcast([128, 4, 16, 64]), ALU.add,
               [postok], xtok[0:4])
            xs8 = x[:, 4:8, 512:T].rearrange("p c (r q) -> p c r q", q=64)
            tt("dve", xs8, xs8, posc[:, :, :].unsqueeze(2).to_broadcast([128, 4, 16, 64]), ALU.add,
               [postok], xtok[4:8])

            for _ in range(6):
                mod_unit()

            for i in range(4):
                if i % 2 == 0:
                    even_layer(i)
                else:
                    odd_layer(i)
                layer_norm(i)
                if DEBUG_STOP is not None and i == DEBUG_STOP:
                    break

            if DEBUG_STOP is not None:
                S.dma("sp", [(yout[c * 128:(c + 1) * 128, :], x[:, c, :]) for c in range(KC)], reads=xtok, semi=xs_sem[1])

        mod_pending = []

        def mod_reset():
            del mod_pending[:]
            for i in range(4):
                for m4 in range(6):
                    mod_pending.append((i, m4))

        def mod_unit():
            if not mod_pending:
                return
            i, m4 = mod_pending.pop(0)

            def pairs(slot, i=i, m4=m4):
                return [(slot.rearrange("p (k n) -> p k n", k=8),
                         w_mod[i, :, m4 * 512:(m4 + 1) * 512].rearrange("(k p) n -> p k n", p=128))]
            sl, stok = ring.next(pairs)
            w = sl.rearrange("p (k n) -> p k n", k=8)
            for mmi in range(4):
                m = m4 * 4 + mmi
                for kc in range(KC):
                    mm(p7[:, m * 2:(m + 1) * 2], w[:, kc, mmi * 128:(mmi + 1) * 128], scb[:, kc, :],
                       kc == 0, kc == KC - 1, [stok, scbtok], [bk[7]])
            if m4 == 5:
                tt("dve", modv[:, i, :, :], p7[:, 0:48].rearrange("p (m v) -> p m v", v=2),
                   pv[:, i * 24:(i + 1) * 24].unsqueeze(2).to_broadcast([128, 24, 2]), ALU.add,
                   [pvtok], [bk[7], modtok])
                ts("dve", modv[:, i, 8:24, :], modv[:, i, 8:24, :], 1.0, None, ALU.add, None, [], [modtok])
                if i >= 1:
                    ts("dve", modv[:, i, 8:16, :], modv[:, i, 8:16, :], 1.0 / ALPHA, None, ALU.mult, None, [], [modtok])

        def modulate(i):
            for c in range(KC):
                for v, (t0, tl) in enumerate([(0, 512), (512, 1024)]):
                    act(u[:, c, t0:t0 + tl], x[:, c, t0:t0 + tl], AF.Identity, [xtok[c], modtok], [utok[c]],
                        bias=modv[:, i, c, v:v + 1], scale=modv[:, i, 8 + c, v:v + 1])
                if i == 0:
                    ts("dve", x[:, c, :], x[:, c, :], ALPHA, None, ALU.mult, None, [], [xtok[c]])

        def proj_chunk(wv, wtok, ncols, col0, kchunks, rhs_fn, rtoks):
            pg, ptoks = next_pg()
            for n, (t0, tl) in enumerate(tiles):
                for kc in range(kchunks):
                    mm(pg[0:ncols, t0:t0 + tl], wv[:, kc, col0:col0 + ncols], rhs_fn(kc, t0, tl),
                       kc == 0, kc == kchunks - 1, [wtok] + rtoks(kc), [ptoks[n]])
            return pg, ptoks

        def u_rhs(kc, t0, tl):
            return u[:, kc, t0:t0 + tl]

        def u_toks(kc):
            return [utok[kc]]

        def y_rhs(kc, t0, tl):
            return yb[:, kc, t0:t0 + tl]

        def y_toks(kc):
            return [ytok[kc]]

        def conv3(P, ptoks, dstF, dstFtok, w0, w1, w2, bias, srctoks=()):
            if bias is not None:
                act(dstF[:, 0:T], P[:, 0:T], AF.Identity, list(srctoks), list(ptoks) + [dstFtok], bias=bias, scale=w1)
            else:
                act(dstF[:, 0:T], P[:, 0:T], AF.Identity, list(srctoks), list(ptoks) + [dstFtok], bias=0.0, scale=w1)
            for (s0, L) in SEQS:
                stt("dve", dstF[:, s0 + 1:s0 + L], P[:, s0:s0 + L - 1], w0, dstF[:, s0 + 1:s0 + L], ALU.mult, ALU.add,
                    list(srctoks), list(ptoks) + [dstFtok])
                stt("dve", dstF[:, s0:s0 + L - 1], P[:, s0 + 1:s0 + L], w2, dstF[:, s0:s0 + L - 1], ALU.mult, ALU.add,
                    list(srctoks), list(ptoks) + [dstFtok])

        conv_ctr = [0]

        def conv_evac(pg, ptoks):
            k = conv_ctr[0] % 2
            conv_ctr[0] += 1
            praw, prt = (Fb[0], Ftok[0]) if k == 0 else (Fb[2], Ftok[2])
            cp("act", praw[:, 0:T], pg[:, 0:T], [], list(ptoks) + [prt])
            return k

        def conv_taps(k, w0, w1, w2, bias, dst, dsttok):
            praw, prt = (Fb[0], Ftok[0]) if k == 0 else (Fb[2], Ftok[2])
            cvb, cvt = (Fb[1][:, 0:T], Ftok[1]) if k == 0 else (hbuf[:, 0:T], hsttok)
            ts("dve", cvb, praw[:, 0:T], w1, bias, ALU.mult, ALU.add, [prt, pvtok], [cvt])
            for (s0, L) in SEQS:
                stt("dve", cvb[:, s0 + 1:s0 + L], praw[:, s0:s0 + L - 1], w0, cvb[:, s0 + 1:s0 + L], ALU.mult, ALU.add,
                    [prt], [cvt])
                stt("dve", cvb[:, s0:s0 + L - 1], praw[:, s0 + 1:s0 + L], w2, cvb[:, s0:s0 + L - 1], ALU.mult, ALU.add,
                    [prt], [cvt])
            act(dst, cvb, AF.Silu, [cvt], [dsttok])

        def out_proj_partial(i, wdram, row0):
            for half in range(2):
                def pairs(slot, half=half):
                    return [(slot.rearrange("p (k n) -> p k n", k=8),
                             wdram[row0:row0 + 1024, half * 512:(half + 1) * 512].rearrange("(k p) n -> p k n", p=128))]
                sl, stok = ring.next(pairs)
                wv = sl.rearrange("p (k n) -> p k n", k=8)
                for dd in range(4):
                    dc = half * 4 + dd
                    pg, ptoks = proj_chunk(wv, stok, 128, dd * 128, KC, y_rhs, y_toks)
                    for v, (t0, tl) in enumerate([(0, 512), (512, 1024)]):
                        bt = ptoks[0:1] if v == 0 else ptoks[1:3]
                        stt("dve", x[:, dc, t0:t0 + tl], pg[:, t0:t0 + tl], modv[:, i, 16 + dc, v:v + 1],
                            x[:, dc, t0:t0 + tl], ALU.mult, ALU.add, [modtok], bt + [xtok[dc]])
                ring.release()

        def layer_norm(i):
            mean_sb = Fb[0]
            rstd_sb = Fb[1]
            sq = [Fb[2], Fb[2]]
            sqt = [Ftok[2], Ftok[2]]
            for c in range(KC):
                q = c % 2
                xbf, xbft = Hb[q], Htok[q]
                sqf, sqft = Hb[2 + q], Htok[2 + q]
                cp("dve", xbf[:, :], x[:, c, :], [xtok[c]], [xbft])
                act(sqf[:, :], x[:, c, :], AF.Square, [xtok[c]], [sqft])
                for n, (t0, tl) in enumerate(tiles):
                    mm(pA[:, t0:t0 + tl], onesm[:], xbf[:, t0:t0 + tl], c == 0, c == KC - 1, [ctok, xbft], [bk[n]])
                    mm(pB[:, t0:t0 + tl], onesm[:], sqf[:, t0:t0 + tl], c == 0, c == KC - 1, [ctok, sqft], [bk[3 + n]])
            if i < 3:
                for _ in range(6):
                    mod_unit()
            cp("act", mean_sb[:, 0:T], pA[:, 0:T], [], bk[0:3] + [Ftok[0]])
            tt("dve", rstd_sb[:, 0:T], mean_sb[:, 0:T], mean_sb[:, 0:T], ALU.mult, [Ftok[0]], [Ftok[1]])
            tt("dve", rstd_sb[:, 0:T], pB[:, 0:T], rstd_sb[:, 0:T], ALU.subtract, [], bk[3:6] + [Ftok[1]])
            act(rstd_sb[:, 0:T], rstd_sb[:, 0:T], AF.Ln, [], [Ftok[1]], bias=LN_EPS)
            act(rstd_sb[:, 0:T], rstd_sb[:, 0:T], AF.Exp, [], [Ftok[1]], scale=-0.5)
            for c in range(KC):
                tt("dve", x[:, c, :], x[:, c, :], mean_sb[:, 0:T], ALU.subtract, [Ftok[0]], [xtok[c]])
                tt("dve", x[:, c, :], x[:, c, :], rstd_sb[:, 0:T], ALU.mult, [Ftok[1]], [xtok[c]])
                if i < 3:
                    act(x[:, c, :], x[:, c, :], AF.Identity, [pvtok], [xtok[c]],
                        bias=pvA[:, 32 + i * 8 + c:32 + i * 8 + c + 1], scale=pvA[:, i * 8 + c:i * 8 + c + 1])
                else:
                    act(x[:, c, :], x[:, c, :], AF.Identity, [pvtok], [xtok[c]], bias=pvc(c_lnb(i, c)), scale=pvc(c_lng(i, c)))
                    if DEBUG_STOP is None:
                        S.dma("sp", [(yout[c * 128:(c + 1) * 128, :], x[:, c, :])], reads=[xtok[c]], semi=xs_sem[1])

        def even_layer(i):
            j = i // 2
            modulate(i)
            def pairs_dt(slot):
                return [(slot[:, 0:256].rearrange("p (k n) -> p k n", k=8),
                         w_in_even[j, :, 3072:3104].rearrange("(k p) n -> p k n", p=128))]
            sl, stok = ring.next(pairs_dt)
            wv = sl[:, 0:256].rearrange("p (k n) -> p k n", k=8)
            for tc in range(NTC):
                for kc in range(KC):
                    mm(p6[:, tc * 32:(tc + 1) * 32], u[:, kc, tc * 128:(tc + 1) * 128], wv[:, kc, 0:32],
                       kc == 0, kc == KC - 1, [stok, utok[kc]], [bk[6]])
            ring.release()
            dtmp = wde[:].rearrange("p d t h -> p (d t h)").rearrange("p (t c) -> p t c", c=32)
            tt("dve", dtmp, p6[:, 0:384].rearrange("p (t c) -> p t c", c=32),
               dtb_bc[:, j * 32:(j + 1) * 32].unsqueeze(1).to_broadcast([128, NTC, 32]), ALU.add,
               [abtok], [bk[6], dttok])
            act(dtmp, dtmp, AF.Exp, [], [dttok])
            act(dt_tok[:], dtmp.rearrange("p t (d h) -> p d t h", d=2), AF.Ln, [], [dttok], bias=1.0)
            cp("act", dt_bf[:], dt_tok[:], [], [dttok])
            tt("dve", a_bf[:], dt_tok[:],
               A_bc[:, j * 32:(j + 1) * 32].rearrange("p (d h) -> p d h", d=2).unsqueeze(2).to_broadcast([128, 2, NTC, 16]),
               ALU.mult, [abtok], [dttok])
            def dt_tables():
                mm(p6[:, 0:192], Uf[:], a_bf[:, 0, :, :].rearrange("p t h -> p (t h)"), True, True, [ctok, dttok], [bk[6]])
                mm(p6[:, 192:384], Ub[:], a_bf[:, 1, :, :].rearrange("p t h -> p (t h)"), True, True, [ctok, dttok], [bk[6]])
                mm(p7[:, 0:384], onesb[:], a_bf[:].rearrange("p d t h -> p (d t h)"), True, True, [ctok, dttok], [bk[7]])
                act(wde[:].rearrange("p d t h -> p (d t h)"), p6[:, 0:384], AF.Exp, [], [bk[6], dttok])
                act(cd_bc[:].rearrange("p d t h -> p (d t h)"), p7[:, 0:384], AF.Exp, [], [bk[7], dttok])
                tt("dve", wde[:], wde[:], dt_tok[:], ALU.mult, [], [dttok])

            zs = [Hb[4], Hb[5]]
            zst = [Htok[4], Htok[5]]
            xT = [Hb[0], Hb[1]]
            xTt = [Htok[0], Htok[1]]
            BT, BTt = Hb[2], Htok[2]
            CT, CTt = Hb[3], Htok[3]
            cv, cvt = Fb[1], Ftok[1]
            def tr_block(srcH, srcT, dst_fn, dtok):
                for q in range(3):
                    hq = q % 2
                    for t4 in range(4):
                        tc = q * 4 + t4
                        tr(trP[:, hq * 512 + t4 * 128:hq * 512 + (t4 + 1) * 128], srcH[:, tc * 128:(tc + 1) * 128],
                           identb[:], [srcT, ctok], [bk[6]])
                    cp("act", dst_fn(q), trP[:, hq * 512:(hq + 1) * 512].rearrange("p (t f) -> p t f", t=4),
                       [], [bk[6], dtok])

            for g in range(4):
                def pairsX(slot, g=g):
                    sv = slot.rearrange("p (k n) -> p k n", k=8)
                    return [(sv[:, :, 0:256], w_in_even[j, :, 1024 + g * 256:1024 + (g + 1) * 256].rearrange("(k p) n -> p k n", p=128)),
                            (sv[:, :, 256:384], w_in_even[j, :, 2048 + g * 128:2048 + (g + 1) * 128].rearrange("(k p) n -> p k n", p=128)),
                            (sv[:, :, 384:512], w_in_even[j, :, 2560 + g * 128:2560 + (g + 1) * 128].rearrange("(k p) n -> p k n", p=128))]
                sl, stok = ring.next(pairsX)
                wv = sl.rearrange("p (k n) -> p k n", k=8)
                chunks = [(0, g * 2, xT[0][:, :], xTt[0]), (128, g * 2 + 1, xT[1][:, :], xTt[1]),
                          (256, 8 + g, BT[:, :], BTt), (384, 12 + g, CT[:, :], CTt)]
                kbuf = [None] * 4

                def conv_of(ci):
                    col0, ch, dst, dtok = chunks[ci]
                    conv_taps(kbuf[ci], pvc(c_caw(j, 0, ch)), pvc(c_caw(j, 1, ch)), pvc(c_caw(j, 2, ch)), pvc(c_cab(j, ch)),
                              dst, dtok)

                def tr_x(cc):
                    tr_block(xT[cc], xTt[cc], lambda q, cc=cc: x_tok[:, q * 4:(q + 1) * 4, cc * 128:(cc + 1) * 128], xtoktok)

                for ci in range(4):
                    pg, ptoks = proj_chunk(wv, stok, 128, chunks[ci][0], KC, u_rhs, u_toks)
                    kbuf[ci] = conv_evac(pg, ptoks)
                    if ci >= 1:
                        conv_of(ci - 1)
                    if ci == 3:
                        tr_x(0)
                ring.release()
                if g == 0:
                    dt_tables()
                def pairsZ(slot, g=g):
                    sv = slot[:, 0:2048].rearrange("p (k n) -> p k n", k=8)
                    return [(sv, w_in_even[j, :, g * 256:(g + 1) * 256].rearrange("(k p) n -> p k n", p=128))]
                sl, stok = ring.next(pairsZ)
                wv = sl[:, 0:2048].rearrange("p (k n) -> p k n", k=8)
                for cc in range(2):
                    pg, ptoks = proj_chunk(wv, stok, 128, cc * 128, KC, u_rhs, u_toks)
                    act(zs[cc][:, :], pg[:, 0:T], AF.Silu, [], ptoks + [zst[cc]])
                    if cc == 0:
                        conv_of(3)
                        tr_x(1)
                    else:
                        tr_block(BT, BTt, lambda q: B_tok[:, q * 4:(q + 1) * 4, :], btoktok)
                ring.release()
                ssd_group(j, g, zs, zst, BT, BTt, CT, CTt)

            for c in range(KC):
                sqb, sqbt = Hb[c % 2], Htok[c % 2]
                act(sqb[:, :], yb[:, c, :], AF.Square, [ytok[c]], [sqbt])
                for n, (t0, tl) in enumerate(tiles):
                    mm(pA[:, t0:t0 + tl], onesm[:], sqb[:, t0:t0 + tl], c == 0, c == KC - 1, [ctok, sqbt], [bk[n]])
            act(Fb[2][:, 0:T], pA[:, 0:T], AF.Ln, [], bk[0:3] + [Ftok[2]], bias=RMS_EPS)
            act(Fb[2][:, 0:T], Fb[2][:, 0:T], AF.Exp, [], [Ftok[2]], scale=-0.5)
            for c in range(KC):
                stt("dve", yb[:, c, :], yb[:, c, :], pvc(c_nag(j, c)), Fb[2][:, 0:T], ALU.mult, ALU.mult,
                    [pvtok, Ftok[2]], [ytok[c]])
            def mixb_part1(c):
                def pairsM(slot, c=c):
                    sv = slot.rearrange("p (k n) -> p k n", k=8)
                    return [(sv[:, :, q * 128:(q + 1) * 128],
                             w_in_even[j, :, 3104 + q * 1024 + c * 128:3104 + q * 1024 + (c + 1) * 128].rearrange("(k p) n -> p k n", p=128))
                            for q in range(4)]
                sl, stok = ring.next(pairsM)
                wv = sl.rearrange("p (k n) -> p k n", k=8)
                sg, sgt = Hb[0], Htok[0]
                pg, ptoks = proj_chunk(wv, stok, 128, 0, KC, u_rhs, u_toks)
                act(sg[:, :], pg[:, 0:T], AF.Silu, [], ptoks + [sgt])
                pg, ptoks = proj_chunk(wv, stok, 128, 128, KC, u_rhs, u_toks)
                tt("dve", Fb[0][:, 0:T], pg[:, 0:T], sg[:, :], ALU.mult, [sgt], ptoks + [Ftok[0]])
                pg, ptoks = proj_chunk(wv, stok, 128, 256, KC, u_rhs, u_toks)
                cp("act", Hb[1][:, :], pg[:, 0:T], [], ptoks + [Htok[1]])
                pg, ptoks = proj_chunk(wv, stok, 128, 384, KC, u_rhs, u_toks)
                tt("dve", Fb[1][:, 0:T], pg[:, 0:T], Hb[1][:, :], ALU.mult, [Htok[1]], ptoks + [Ftok[1]])
                ring.release()

            def mixb_part2(c):
                conv3(Fb[1], [], Fb[2], Ftok[2], pvc(c_cbw(j, 0, c)), pvc(c_cbw(j, 1, c)), pvc(c_cbw(j, 2, c)), None,
                      srctoks=[Ftok[1], pvtok])
                tt("dve", yb[:, c, :], Fb[2][:, 0:T], Fb[0][:, 0:T], ALU.mult, [Ftok[2], Ftok[0]], [ytok[c]])

            mixb_part1(0)
            out_proj_partial(i, w_out_even[j], 0)
            mixb_part2(0)
            for c in range(1, KC):
                mixb_part1(c)
                mixb_part2(c)
            out_proj_partial(i, w_out_even[j], 1024)

        def ssd_group(j, g, zs, zst, BT, BTt, CT, CTt):
            gs = slice(g * 4, g * 4 + 4)
            for h4 in range(4):
                hh = j * 16 + g * 4 + h4
                ts("dve", Dmat[:, h4, :], identb[:], D_bc[:, hh:hh + 1], None, ALU.mult, None, [ctok, abtok], [dmtok])
            HC = [(hcur[:], hcurtok), (hcur2, hcur2tok)]
            HT = [(htmp[:], htmptok), (htmp2, htmp2tok)]

            def v4(ap):
                return ap.rearrange("p (h q) -> p h q", h=4)

            hstP = Hb[6][:, 0:1024].rearrange("p (d s q) -> p d s q", d=2, s=2)

            def hst_slot(si, d, tc):
                if si == 2:
                    return hst[:, d, tc - 4, :], hsttok
                return hstP[:, d, si, :], Htok[6]

            def seq_of(tc):
                si = 0 if tc < 2 else (1 if tc < 4 else 2)
                s0, L = SEQS[si]
                return si, s0 // 128, L // 128

            def state_rounds(si):
                s0, L = SEQS[si]
                nch = L // 128
                tc0 = s0 // 128
                is_sample = (si == 2)
                orders = [list(range(tc0, tc0 + nch)), list(range(tc0 + nch - 1, tc0 - 1, -1))]
                have = [False, False]
                fns = []

                def init():
                    for d in range(2):
                        r0 = (j * 2 + d) * 128
                        S.dma("sp", [(HC[d][0], state[r0:r0 + 128, g * 256:(g + 1) * 256])],
                              writes=[HC[d][1]], semi=stin_sem[d])
                        have[d] = True
                if is_sample:
                    fns.append(init)

                def round_(idx):
                    for d in range(2):
                        tc = orders[d][idx]
                        hc, hct = HC[d]
                        ht, htt = HT[d]
                        par = idx % 2
                        if is_sample:
                            if par == 0:
                                stb = pA[:, 1024:1280] if d == 0 else pA[:, 1280:1536]
                                stk = bk[2]
                            else:
                                stb = bank4[:, 128:384] if d == 0 else bank5[:, 256:512]
                                stk = bk[4] if d == 0 else bk[5]
                        else:
                            if par == 0:
                                stb = bank4[:, 128:384] if d == 0 else bank5[:, 256:512]
                                stk = bk[4] if d == 0 else bk[5]
                            else:
                                stb = p6[:, 0:256] if d == 0 else p7[:, 0:256]
                                stk = bk[6] if d == 0 else bk[7]
                        xdb, xdbt = xdd[d][par], xddtok[d][par]
                        if have[d]:
                            hdst, hdtok = hst_slot(si, d, tc)
                            cp("act", hdst, hc, [hct], [hdtok])
                        if idx == nch - 1 and is_sample:
                            continue
                        tt("dve", xdb, v4(x_tok[:, tc, :]),
                           wde[:, d, tc, gs].unsqueeze(2).to_broadcast([128, 4, 64]), ALU.mult,
                           [xtoktok, dttok], [xdbt])
                        mm(stb, B_tok[:, tc, :], xdb.rearrange("p h q -> p (h q)"), True, True,
                           [btoktok, xdbt], [stk])
                        if have[d]:
                            tt("dve", v4(ht), v4(hc),
                               cd_bc[:, d, tc, gs].unsqueeze(2).to_broadcast([128, 4, 64]), ALU.mult,
                               [hct, dttok], [htt])
                            tt("dve", hc, stb, ht, ALU.add, [htt], [stk, hct])
                        else:
                            cp("dve", hc, stb, [], [stk, hct])
                            have[d] = True
                for idx in range(nch):
                    fns.append(lambda idx=idx: round_(idx))

                def final():
                    for d in range(2):
                        hc, hct = HC[d]
                        stv = sto[d][:].rearrange("p a b -> p (a b)")
                        cp("act", stv, hc, [hct], [stotok[d]])
                        r0 = ((si * 2 + j) * 2 + d) * 128
                        S.dma("sp", [(nstate[r0:r0 + 128, g * 256:(g + 1) * 256], stv)],
                              reads=[stotok[d]], semi=sto_sem[d])
                if not is_sample:
                    fns.append(final)
                return fns

            for si in range(2):
                for fn in state_rounds(si):
                    fn()
            pending_rounds = state_rounds(2)

            if True:
                its = list(range(NTC))
                ARG = [(p6[:], bk[6], p7[:], bk[7]), (pA[:, 0:512], bk[0], pA[:, 512:1024], bk[1])]
                YB = [(bank5, bk[5]), (pB[:, 0:512], bk[3])]

                def stage_pre_a(n):
                    tc = its[n]
                    k0 = tc * 128
                    mm(bank4[:, 0:128], BT[:, k0:k0 + 128], CT[:, k0:k0 + 128], True, True, [BTt, CTt], [bk[4]])
                    tt("dve", Rb2[:], Lfb[:].unsqueeze(2).to_broadcast([128, 2, 4, 128]),
                       a_bf[:, :, tc, gs].unsqueeze(3).to_broadcast([128, 2, 4, 128]), ALU.mult,
                       [ctok, dttok], [Rtok])
                    for d in range(2):
                        Um = Uf if d == 0 else Ub
                        pa, pat, pe_, pet = ARG[d]
                        Rflat = Rb2[:, d].rearrange("p h i -> p (h i)")
                        mm(pa, Um[:], Rflat, True, True, [ctok, Rtok], [pat])
                        mm(pe_, onesb[:], Rflat, True, True, [ctok, Rtok], [pet])

                def stage_pre_b(n):
                    for d in range(2):
                        pa, pat, pe_, pet = ARG[d]
                        act(eA2[:, d].rearrange("p h i -> p (h i)"), pa, AF.Exp, [], [pat, eAtok[d]])
                    for d in range(2):
                        pa, pat, pe_, pet = ARG[d]
                        act(eE2[:, d].rearrange("p h i -> p (h i)"), pe_, AF.Exp, [], [pet, eEtok[d]])

                def stage_cbm():
                    tt("dve", CBm2[:], bank4[:, 0:128].unsqueeze(1).to_broadcast([128, 2, 128]), Lfb[:], ALU.mult,
                       [ctok], [bk[4], CBmtok])

                def stage_mid(n):
                    tc = its[n]
                    k0 = tc * 128
                    q = n % 2
                    tt("dve", MT2[q][:], eA2[:], CBm2[:].unsqueeze(2).to_broadcast([128, 2, 4, 128]), ALU.mult,
                       [eAtok[0], eAtok[1], CBmtok], [MTtok[q]])
                    tt("dve", Cs2[q][:], eE2[:],
                       CT[:, k0:k0 + 128].unsqueeze(1).unsqueeze(1).to_broadcast([128, 2, 4, 128]), ALU.mult,
                       [eEtok[0], eEtok[1], CTt], [Cstok[q]])
                    tt("dve", xdt2[q][:], v4(x_tok[:, tc, :]).unsqueeze(1).to_broadcast([128, 2, 4, 64]),
                       dt_bf[:, :, tc, gs].unsqueeze(3).to_broadcast([128, 2, 4, 64]), ALU.mult,
                       [xtoktok, dttok], [xdttok[q]])

                def stage_y(n):
                    tc = its[n]
                    q = n % 2
                    yb_, ybt = YB[q]
                    si, tc0, nch = seq_of(tc)
                    has_f = (si == 2) or (tc != tc0)
                    has_b = (si == 2) or (tc != tc0 + nch - 1)
                    hf, hft = hst_slot(si, 0, tc)
                    hb_, hbt = hst_slot(si, 1, tc)
                    for h4 in range(4):
                        po = (h4 % 2) * 64
                        out = yb_[po:po + 64, (h4 // 2) * 128:(h4 // 2) * 128 + 128]
                        tp = (0, 64) if po else None
                        mm(out, xdt2[q][:, 0, h4, :], MT2[q][:, 0, h4, :], True, False, [xdttok[q], MTtok[q]], [ybt], tp)
                        mm(out, xdt2[q][:, 1, h4, :], MT2[q][:, 1, h4, :], False, False, [xdttok[q], MTtok[q]], [ybt], tp)
                        if has_f:
                            mm(out, hf[:, h4 * 64:(h4 + 1) * 64], Cs2[q][:, 0, h4, :], False, False,
                               [hft, Cstok[q]], [ybt], tp)
                        if has_b:
                            mm(out, hb_[:, h4 * 64:(h4 + 1) * 64], Cs2[q][:, 1, h4, :], False, False,
                               [hbt, Cstok[q]], [ybt], tp)
                        mm(out, x_tok[:, tc, h4 * 64:(h4 + 1) * 64], Dmat[:, h4, :], False, True, [xtoktok, dmtok], [ybt], tp)

                def stage_evac(n):
                    tc = its[n]
                    k0 = tc * 128
                    yb_, ybt = YB[n % 2]
                    tt("dve", yb[:, g * 2:g * 2 + 2, k0:k0 + 128], yb_[:, 0:256].rearrange("p (r i) -> p r i", r=2),
                       Hb45[:, :, k0:k0 + 128], ALU.mult, [zst[0], zst[1]], [ybt, ytok[g * 2], ytok[g * 2 + 1]])

                N = len(its)
                stage_pre_a(0)
                stage_pre_b(0)
                stage_cbm()
                for n in range(N):
                    if n == 4:
                        while pending_rounds:
                            pending_rounds.pop(0)()
                    if n + 1 < N:
                        stage_pre_a(n + 1)
                    stage_mid(n)
                    if n + 1 < N:
                        stage_pre_b(n + 1)
                    if n >= 1:
                        stage_evac(n - 1)
                    if n + 1 < N:
                        stage_cbm()
                    stage_y(n)
                    if n < 4:
                        for _ in range(3):
                            if pending_rounds:
                                pending_rounds.pop(0)()
                stage_evac(N - 1)

        def odd_layer(i):
            j = i // 2
            modulate(i)
            S.op("dve", lambda e: e.memset(Fb[0][:], 0.0), (), [Ftok[0]])
            S.op("dve", lambda e: e.memset(Fb[1][:], 0.0), (), [Ftok[1]])
            S.op("dve", lambda e: e.memset(Fb[2][:], 0.0), (), [Ftok[2]])
            for half in range(2):
                for kk in range(2):
                    k = half * 2 + kk
                    w = WINS[k]
                    invc, invct = invc_alias, hsttok
                    S.op("dve", lambda e: e.memset(invc, 1.0 / w), (), [invct])
                    for si, (s0, L) in enumerate(SEQS):
                        o = POFF[si]
                        cp("dve", invc[:, o:o + 8], invtab[:, k, 0:8], [invtok], [invct])
                        cp("dve", invc[:, o + L - 8:o + L], invtab[:, k, 8:16], [invtok], [invct])
                    def pairsV(slot, k=k):
                        return [(slot.rearrange("p (k n) -> p k n", k=8),
                                 w_in_odd[j, :, k * 512:(k + 1) * 512].rearrange("(k p) n -> p k n", p=128))]

                    def pairsG(slot, k=k):
                        return [(slot.rearrange("p (k n) -> p k n", k=8),
                                 w_in_odd[j, :, 2048 + k * 512:2048 + (k + 1) * 512].rearrange("(k p) n -> p k n", p=128))]
                    slV, stokV = ring.next(pairsV)
                    wvV = slV.rearrange("p (k n) -> p k n", k=8)
                    slG, stokG = ring.next(pairsG, prefetch=False)
                    wvG = slG.rearrange("p (k n) -> p k n", k=8)
                    for cc in range(4):
                        pg, ptoks = proj_chunk(wvV, stokV, 128, cc * 128, KC, u_rhs, u_toks)
                        vp, vpt = Fb[0], Ftok[0]
                        for si, (s0, L) in enumerate(SEQS):
                            o = POFF[si]
                            bt = ptoks[0:1] if si < 2 else ptoks[1:3]
                            cp("act", vp[:, o:o + L], pg[:, s0:s0 + L], [], bt + [vpt])
                        pg2, ptoks2 = proj_chunk(wvG, stokG, 128, cc * 128, KC, u_rhs, u_toks)
                        act(Hb[cc][:, :], pg2[:, 0:T], AF.Silu, [], ptoks2 + [Htok[cc]])
                        src, srct = vp, vpt
                        bufs = [(Fb[1], Ftok[1]), (Fb[2], Ftok[2])]
                        sh = [(1, 0), (1, 1), (2, 2), (4, 4)]
                        for lev in range(k + 1):
                            dst, dstt = bufs[lev % 2]
                            a_, b_ = sh[lev]
                            tt("dve", dst[:, 8:PW - 8], src[:, 8 - a_:PW - 8 - a_], src[:, 8 + b_:PW - 8 + b_], ALU.add,
                               [srct], [dstt])
                            src, srct = dst, dstt
                        oth, otht = bufs[(k + 1) % 2]
                        tt("dve", oth[:, 8:PW - 8], src[:, 8:PW - 8], invc[:, 8:PW - 8], ALU.mult, [srct, invct], [otht])
                        for si, (s0, L) in enumerate(SEQS):
                            o = POFF[si]
                            tt("dve", Hb[4 + cc][:, s0:s0 + L], oth[:, o:o + L], vp[:, o:o + L], ALU.subtract,
                               [otht, vpt], [Htok[4 + cc]])
                    def pairsP(slot, k=k):
                        return [(slot[:, 0:2048].rearrange("p (k n) -> p k n", k=4),
                                 w_pool[j, k, :, :].rearrange("(k p) n -> p k n", p=128))]
                    sl, stok = ring.next(pairsP)
                    wv = sl[:, 0:2048].rearrange("p (k n) -> p k n", k=4)
                    for dd in range(4):
                        pg, ptoks = proj_chunk(wv, stok, 128, dd * 128, 4,
                                               lambda kc, t0, tl: Hb[4 + kc][:, t0:t0 + tl], lambda kc: [Htok[4 + kc]])
                        yc = kk * 4 + dd
                        stt("dve", yb[:, yc, :], pg[:, 0:T], pvc(c_psc(j, k * 4 + dd)), Hb[dd][:, :], ALU.mult, ALU.mult,
                            [pvtok, Htok[dd]], ptoks + [ytok[yc]])
                    ring.release()
                out_proj_partial(i, w_out_odd[j], half * 1024)

        S.dry = True
        emit()
        S.dry = False
        S.plan = True
        pg_state[0] = 0
        emit()
        S.plan = False
        S.reset()
        ring.issued = 0
        ring.cur = 0
        pg_state[0] = 0
        emit()
        for semi in list(xs_sem) + list(sto_sem):
            ent = S.dma_sems[semi]
            if ent[1] > 0:
                nc.sync.wait_ge(ent[0], ent[1])
        print("program: insts", S.cnt, "incs", S.semval, "waits", S.nwaits, "pieces", len(ring.plan))
    return nc


_NC_CACHE = {}


def kernel(x_prompt, x_sample, state_ssd, c, c_ctx, w_mod, b_mod, ln_g, ln_b, w_in_even, conv_a_w,
           conv_a_b, a_log, dt_bias, d_skip, norm_a_g, conv_b_w, w_out_even, w_in_odd, w_pool,
           pool_scale, w_out_odd):
    f = lambda a: np.ascontiguousarray(np.asarray(a), dtype=np.float32)
    if "nc" not in _NC_CACHE:
        _NC_CACHE["nc"] = build_program()
    nc = _NC_CACHE["nc"]
    x_prompt = f(x_prompt)
    x_sample = f(x_sample)
    state_ssd = f(state_ssd)
    c = f(c)
    c_ctx = f(c_ctx)
    shared = dict(w_mod=f(w_mod), b_mod=f(b_mod), ln_g=f(ln_g), ln_b=f(ln_b), w_in_even=f(w_in_even),
                  conv_a_w=f(conv_a_w), conv_a_b=f(conv_a_b), a_log=f(a_log), dt_bias=f(dt_bias), d_skip=f(d_skip),
                  norm_a_g=f(norm_a_g), conv_b_w=f(conv_b_w), w_out_even=f(w_out_even), w_in_odd=f(w_in_odd),
                  w_pool=f(w_pool), pool_scale=f(pool_scale), w_out_odd=f(w_out_odd))
    in_maps = []
    for k in range(NCORES):
        m = dict(shared)
        m["xin"] = np.ascontiguousarray(np.concatenate([x_prompt[2 * k], x_prompt[2 * k + 1], x_sample[k]], axis=0).T)
        m["state"] = np.ascontiguousarray(state_ssd[k].transpose(0, 1, 4, 2, 3).reshape(512, 1024))
        m["cvec"] = np.ascontiguousarray(np.stack([c_ctx, c[k]], axis=0))
        in_maps.append(m)
    res = run_bass_kernel_spmd(nc, in_maps, core_ids=list(range(NCORES)))
    y_prompt = np.empty((16, 256, D), np.float32)
    y_sample = np.empty((8, 1024, D), np.float32)
    new_state = np.empty((16, 2, 2, 16, 64, 128), np.float32)
    for k in range(NCORES):
        r = res.results[k]
        yo = np.asarray(r["yout"]).T
        y_prompt[2 * k] = yo[0:256]
        y_prompt[2 * k + 1] = yo[256:512]
        y_sample[k] = yo[512:1536]
        ns = np.asarray(r["nstate"]).reshape(2, 2, 2, 128, 16, 64).transpose(0, 1, 2, 4, 5, 3)
        new_state[2 * k] = ns[0]
        new_state[2 * k + 1] = ns[1]
    return (y_prompt, y_sample, new_state)
```

```python
import math
from contextlib import ExitStack

import numpy as np
import concourse.bass as bass
import concourse.mybir as mybir
from concourse.bass_utils import run_bass_kernel_spmd

F32 = mybir.dt.float32
BF16 = mybir.dt.bfloat16
I32 = mybir.dt.int32
AF = mybir.ActivationFunctionType
ALU = mybir.AluOpType

NCORES = 8
D = 1024
T = 1536
KC = 8
NT = 3
SEQS = [(0, 256), (256, 256), (512, 1024)]
NTC = 12
ALPHA = (2 * 4) ** 0.25
LN_EPS = 1e-5
RMS_EPS = 1e-5
IN_EVEN = 7200
PW = 1600
POFF = [16, 280, 544]
WINS = (2, 4, 8, 16)
NSLOT = 2
SLOT = 4096

DEBUG_STOP = None


class Tok:
    __slots__ = ("w", "r", "name")
    ALL = []

    def __init__(self, name=""):
        self.w = None
        self.r = {}
        self.name = name
        Tok.ALL.append(self)


class Sched:
    ENG = ("pe", "dve", "act", "pool")

    def __init__(self, nc, es):
        self.nc = nc
        self.es = es
        self.engs = {"pe": nc.tensor, "dve": nc.vector, "act": nc.scalar, "pool": nc.gpsimd, "sp": nc.sync}
        self.sem = {k: es.enter_context(nc.semaphore("sem_" + k)) for k in self.ENG}
        self.cnt = {k: 0 for k in self.ENG}
        self.waited = {k: {} for k in list(self.ENG) + ["sp"]}
        self.dma_sems = []
        self.dry = False
        self.plan = False
        self.needed = {k: set() for k in self.ENG}
        self.semval = {k: 0 for k in self.ENG}
        self.valof = {k: {} for k in self.ENG}
        self.nwaits = 0

    def reset(self):
        for t in Tok.ALL:
            t.w = None
            t.r = {}
        self.cnt = {k: 0 for k in self.ENG}
        self.waited = {k: {} for k in list(self.ENG) + ["sp"]}
        self.semval = {k: 0 for k in self.ENG}
        self.valof = {k: {} for k in self.ENG}
        self.nwaits = 0
        for ent in self.dma_sems:
            ent[1] = 0

    def _wait(self, eng, dep):
        if dep is None:
            return
        src, val = dep
        if isinstance(src, str):
            if src == eng and eng == "pe":
                return
            key = src
            sem = self.sem[src]
        else:
            key = ("dma", src.num)
            sem = src
        if self.waited[eng].get(key, 0) >= val:
            return
        self.nwaits += 1
        self.waited[eng][key] = val
        if self.plan:
            if isinstance(src, str):
                self.needed[src].add(val)
            return
        if isinstance(src, str):
            self.engs[eng].wait_ge(sem, self.valof[src][val])
        else:
            self.engs[eng].wait_ge(sem, val)

    def op(self, eng, fn, reads=(), writes=()):
        if self.dry:
            return
        for t in reads:
            self._wait(eng, t.w)
        for t in writes:
            self._wait(eng, t.w)
            for src, val in t.r.items():
                if src == eng and eng == "pe":
                    continue
                self._wait(eng, (src, val))
        self.cnt[eng] += 1
        c = self.cnt[eng]
        if not self.plan:
            ins = fn(self.engs[eng])
            if c in self.needed[eng]:
                ins.then_inc(self.sem[eng], 1)
                self.semval[eng] += 1
                self.valof[eng][c] = self.semval[eng]
        for t in reads:
            t.r[eng] = c
        for t in writes:
            t.w = (eng, c)
            t.r = {}

    def new_dma_sem(self):
        s = self.es.enter_context(self.nc.semaphore("dsem%d" % len(self.dma_sems)))
        self.dma_sems.append([s, 0])
        return len(self.dma_sems) - 1

    def dma(self, q, pairs, reads=(), writes=(), semi=None, **kw):
        if self.dry:
            return
        for t in reads:
            self._wait(q, t.w)
        for t in writes:
            self._wait(q, t.w)
            for src_, val in t.r.items():
                self._wait(q, (src_, val))
        ent = self.dma_sems[semi]
        for (o, i_) in pairs:
            ent[1] += 16
            if not self.plan:
                ins = self.engs[q].dma_start(out=o, in_=i_, **kw)
                ins.then_inc(ent[0], 16)
        dep = (ent[0], ent[1])
        for t in reads:
            t.r[ent[0]] = ent[1]
        for t in writes:
            t.w = dep
            t.r = {}


class Ring:
    def __init__(self, S, buf):
        self.S = S
        self.buf = buf
        self.toks = [Tok("ring%d" % i) for i in range(NSLOT)]
        self.sems = [S.new_dma_sem() for _ in range(NSLOT)]
        self.plan = []
        self.issued = 0
        self.cur = 0

    def slot_ap(self, s):
        return self.buf[:, s, :]

    def _issue(self):
        p = self.issued
        s = p % NSLOT
        pairs = self.plan[p](self.slot_ap(s))
        self.S.dma("pool", pairs, writes=[self.toks[s]], semi=self.sems[s])
        self.issued += 1

    def next(self, pairs_fn, prefetch=True):
        if self.S.dry:
            self.plan.append(pairs_fn)
            return self.slot_ap(0), self.toks[0]
        p = self.cur
        while self.issued < min(len(self.plan), (p + NSLOT) if prefetch else (p + 1)):
            self._issue()
        s = p % NSLOT
        self.cur += 1
        return self.slot_ap(s), self.toks[s]

    def release(self):
        pass

    def kick(self):
        if self.S.dry:
            return
        while self.issued < min(len(self.plan), NSLOT):
            self._issue()


def build_program():
    nc = bass.Bass("TRN2", target_bir_lowering=False)

    def din(name, shape):
        return nc.dram_tensor(name, list(shape), F32, kind="ExternalInput").ap()

    xin = din("xin", [D, T])
    state = din("state", [512, 1024])
    cvec = din("cvec", [2, D])
    w_mod = din("w_mod", [4, D, 3072])
    b_mod = din("b_mod", [4, 3072])
    ln_g = din("ln_g", [4, D])
    ln_b = din("ln_b", [4, D])
    w_in_even = din("w_in_even", [2, D, IN_EVEN])
    conv_a_w = din("conv_a_w", [2, 3, 2048])
    conv_a_b = din("conv_a_b", [2, 2048])
    a_log = din("a_log", [2, 2, 16])
    dt_bias = din("dt_bias", [2, 2, 16])
    d_skip = din("d_skip", [2, 16])
    norm_a_g = din("norm_a_g", [2, D])
    conv_b_w = din("conv_b_w", [2, 3, D])
    w_out_even = din("w_out_even", [2, 2048, D])
    w_in_odd = din("w_in_odd", [2, D, 4096])
    w_pool = din("w_pool", [2, 4, 512, 512])
    pool_scale = din("pool_scale", [2, 2048])
    w_out_odd = din("w_out_odd", [2, 2048, D])
    yout = nc.dram_tensor("yout", [D, T], F32, kind="ExternalOutput").ap()
    nstate = nc.dram_tensor("nstate", [1024, 1024], F32, kind="ExternalOutput").ap()

    with ExitStack() as es:
        S = Sched(nc, es)

        def sb(name, shape, dt):
            return es.enter_context(nc.sbuf_tensor(name, list(shape), dt))

        x = sb("x", [128, KC, T], F32)
        xtok = [Tok("x%d" % c) for c in range(KC)]
        u = sb("u", [128, KC, T], BF16)
        utok = [Tok("u%d" % c) for c in range(KC)]
        yb = sb("ybuf", [128, KC, T], BF16)
        ytok = [Tok("y%d" % c) for c in range(KC)]
        ringbuf = sb("ring", [128, NSLOT, SLOT], BF16)
        ring = Ring(S, ringbuf)
        Fb = [sb("F%d" % i, [128, PW], F32) for i in range(3)]
        Ftok = [Tok("F%d" % i) for i in range(3)]
        Hb = [sb("H%d" % i, [128, T], BF16) for i in range(4)]
        Hb45 = sb("H45", [128, 2, T], BF16)
        Hb += [Hb45[:, 0, :], Hb45[:, 1, :]]
        Hb += [sb("H%d" % i, [128, T], BF16) for i in (6, 7)]
        Htok = [Tok("H%d" % i) for i in range(8)]
        pv = sb("pv", [128, 384], F32)
        pvtok = Tok("pv")
        pvA = sb("pvA", [128, 64], F32)
        modv = sb("modv", [128, 4, 24, 2], F32)
        modtok = Tok("modv")
        ident = sb("ident", [128, 128], F32)
        identb = sb("identb", [128, 128], BF16)
        onesb = sb("onesb", [128, 128], BF16)
        onesm = sb("onesm", [128, 128], BF16)
        onesf = sb("onesf", [128, 128], F32)
        Lfb = sb("Lfb", [128, 2, 128], BF16)
        Lf = Lfb[:, 0, :]
        Lb = Lfb[:, 1, :]
        Uf = sb("Uf", [128, 128], BF16)
        Ub = sb("Ub", [128, 128], BF16)
        ctok = Tok("consts")
        mtmp = Fb[0][:, 1024:1152]
        mtmptok = Tok("mtmp")
        A_bc = sb("A_bc", [128, 64], F32)
        D_bc = sb("D_bc", [128, 32], F32)
        dtb_bc = sb("dtb_bc", [128, 64], F32)
        abtok = Tok("abc")
        posc = sb("posc", [128, 4, 64], F32)
        postok = Tok("pos")
        invtab = sb("invtab", [128, 4, 16], F32)
        invtok = Tok("invtab")
        scb = sb("scb", [128, 8, 2], BF16)
        scbtok = Tok("scb")
        dt_tok = sb("dt_tok", [128, 2, NTC, 16], F32)
        a_bf = sb("a_bf", [128, 2, NTC, 16], BF16)
        dt_bf = sb("dt_bf", [128, 2, NTC, 16], BF16)
        wde = sb("wde", [128, 2, NTC, 16], F32)
        cd_bc = sb("cd_bc", [128, 2, NTC, 16], F32)
        dttok = Tok("dtstuff")
        Dmat = sb("Dmat", [128, 4, 128], BF16)
        dmtok = Tok("Dmat")
        stin_sem = [S.new_dma_sem(), S.new_dma_sem()]
        x_tok = sb("x_tok", [128, NTC, 256], BF16)
        xtoktok = Tok("x_tok")
        B_tok = sb("B_tok", [128, NTC, 128], BF16)
        btoktok = Tok("B_tok")
        hbuf = sb("hbuf", [128, 2048], F32)
        hst = hbuf[:].bitcast(BF16).rearrange("p (d t q) -> p d t q", d=2, t=8)
        invc_alias = hbuf[:, 0:PW]
        hsttok = Tok("hst")
        hcur = sb("hcur", [128, 256], F32)
        hcurtok = Tok("hcur")
        htmp = sb("htmp", [128, 256], F32)
        htmptok = Tok("htmp")
        NB = 2
        Rb2 = sb("Rb2", [128, 2, 4, 128], BF16)
        Rtok = Tok()
        eA2 = sb("eA2", [128, 2, 4, 128], BF16)
        eAtok = [Tok() for _ in range(2)]
        eE2 = sb("eE2", [128, 2, 4, 128], BF16)
        eEtok = [Tok() for _ in range(2)]
        MT2 = [sb("MT2%d" % q, [128, 2, 4, 128], BF16) for q in range(2)]
        MTtok = [Tok() for q in range(2)]
        Cs2 = [sb("Cs2%d" % q, [128, 2, 4, 128], BF16) for q in range(2)]
        Cstok = [Tok() for q in range(2)]
        xdt2 = [sb("xdt2%d" % q, [128, 2, 4, 64], BF16) for q in range(2)]
        xdttok = [Tok() for q in range(2)]
        h7f = Hb[7][:, :].bitcast(F32)
        hcur2 = h7f[:, 0:256]
        htmp2 = h7f[:, 256:512]
        xdd = [[sb("xdd", [128, 4, 64], BF16)[:], sb("xddb", [128, 4, 64], BF16)[:]],
               [Hb[7][:, 1024:1280].rearrange("p (h q) -> p h q", h=4), Hb[7][:, 1280:1536].rearrange("p (h q) -> p h q", h=4)]]
        xddtok = [[Tok("xdd00"), Tok("xdd01")], [Tok("xdd10"), Tok("xdd11")]]
        hcur2tok = Tok("hcur2")
        htmp2tok = Tok("htmp2")
        CBm2 = sb("CBm2", [128, 2, 128], BF16)
        CBmtok = Tok()
        sto = [sb("sto%d" % i, [128, 2, 128], F32) for i in range(2)]
        stotok = [Tok() for _ in range(2)]
        sto_sem = [S.new_dma_sem() for _ in range(2)]
        xs = [Fb[1][:, 0:D], Fb[2][:, 0:D]]
        xstok = [Ftok[1], Ftok[2]]
        xs_sem = [S.new_dma_sem() for _ in range(2)]
        pst = [Fb[2][:, i * 128:(i + 1) * 128] for i in range(3)]
        cst = Fb[2][0:16, 384:512]
        psttok = Ftok[2]
        pst_sem = S.new_dma_sem()
        misc_sem = S.new_dma_sem()

        pA = es.enter_context(nc.psum_tensor("pA", [128, 1536], F32))
        pB = es.enter_context(nc.psum_tensor("pB", [128, 1536], F32))
        p6 = es.enter_context(nc.psum_tensor("p6", [128, 512], F32))
        p7 = es.enter_context(nc.psum_tensor("p7", [128, 512], F32))
        bk = [Tok("bank%d" % i) for i in range(8)]
        PG = [(pA, bk[0:3]), (pB, bk[3:6])]
        pg_state = [0]
        trB = pB[:, 0:512].bitcast(BF16)
        trP = p6[:, :].bitcast(BF16)
        bank4 = pB[:, 512:1024]
        bank5 = pB[:, 1024:1536]

        def next_pg():
            g = PG[pg_state[0] % 2]
            pg_state[0] += 1
            return g

        def tt(eng, out, in0, in1, op, R, W):
            S.op(eng, lambda e: e.tensor_tensor(out, in0, in1, op), R, W)

        def ts(eng, out, in0, s1, s2, op0, op1, R, W):
            if s2 is None:
                S.op(eng, lambda e: e.tensor_scalar(out, in0, s1, None, op0), R, W)
            else:
                S.op(eng, lambda e: e.tensor_scalar(out, in0, s1, s2, op0, op1), R, W)

        def stt(eng, out, in0, scalar, in1, op0, op1, R, W):
            S.op(eng, lambda e: e.scalar_tensor_tensor(out=out, in0=in0, scalar=scalar, in1=in1, op0=op0, op1=op1), R, W)

        def act(out, in_, func, R, W, bias=None, scale=None):
            kw = {}
            if bias is not None:
                kw["bias"] = bias
            if scale is not None:
                kw["scale"] = scale
            S.op("act", lambda e: e.activation(out=out, in_=in_, func=func, **kw), R, W)

        def cp(eng, out, in_, R, W):
            if eng == "act":
                act(out, in_, AF.Copy, R, W)
            else:
                S.op(eng, lambda e: e.tensor_copy(out, in_), R, W)

        def mm(out, lhsT, rhs, start, stop, R, W, tp=None):
            if tp is None:
                S.op("pe", lambda e: e.matmul(out, lhsT, rhs, start=start, stop=stop), R, W)
            else:
                S.op("pe", lambda e: e.matmul(out, lhsT, rhs, start=start, stop=stop, tile_position=tp), R, W)

        def tr(out, in_, idn, R, W):
            S.op("pe", lambda e: e.transpose(out, in_, idn), R, W)

        def pvc(col):
            return pv[:, col:col + 1]

        def c_bmod(i, m): return i * 24 + m
        def c_lng(i, c): return 96 + i * 8 + c
        def c_caw(j, k, c): return 128 + j * 48 + k * 16 + c
        def c_lnb(i, c): return 128 + 96 + i * 8 + c
        def c_cab(j, c): return 256 + j * 16 + c
        def c_nag(j, c): return 256 + 32 + j * 8 + c
        def c_cbw(j, k, c): return 256 + 48 + j * 24 + k * 8 + c
        def c_psc(j, c): return 256 + 96 + j * 16 + c

        tiles = [(n * 512, 512) for n in range(NT)]

        def emit():
            mod_reset()
            ring.kick()
            def mask(dst, pattern, cm, cmp_):
                S.op("pool", lambda e: e.memset(mtmp[:], 1.0), (), [mtmptok])
                S.op("pool", lambda e: e.affine_select(out=mtmp[:], in_=mtmp[:], pattern=pattern, compare_op=cmp_,
                                                       fill=0.0, base=0, channel_multiplier=cm), (), [mtmptok])
                cp("dve", dst[:], mtmp[:], [mtmptok], [ctok])

            S.op("pool", lambda e: e.memset(ident[:], 0.0), (), [ctok])
            S.op("pool", lambda e: e.affine_select(out=ident[:], in_=ident[:], pattern=[[-1, 128]], compare_op=ALU.not_equal,
                                                   fill=1.0, base=0, channel_multiplier=1), (), [ctok])
            cp("dve", identb[:], ident[:], [ctok], [ctok])
            S.op("pool", lambda e: e.memset(onesb[:], 1.0), (), [ctok])
            S.op("pool", lambda e: e.memset(onesm[:], 1.0 / 1024), (), [ctok])
            S.op("pool", lambda e: e.memset(onesf[:], 1.0 / 1024), (), [ctok])
            mask(Lf, [[1, 128]], -1, ALU.is_ge)
            mask(Lb, [[-1, 128]], 1, ALU.is_ge)
            mask(Uf, [[-1, 128]], 1, ALU.is_gt)
            mask(Ub, [[1, 128]], -1, ALU.is_gt)

            def rows(ap2d):
                return ap2d
            loads = [
                (pst[0][0:96, :], b_mod.rearrange("i (m p) -> (i m) p", p=128)),
                (pst[0][96:128, :], ln_g.rearrange("i (c p) -> (i c) p", p=128)),
                (pst[1][0:96, :], conv_a_w.rearrange("j k (c p) -> (j k c) p", p=128)),
                (pst[1][96:128, :], ln_b.rearrange("i (c p) -> (i c) p", p=128)),
                (pst[2][0:32, :], conv_a_b.rearrange("j (c p) -> (j c) p", p=128)),
                (pst[2][32:48, :], norm_a_g.rearrange("j (c p) -> (j c) p", p=128)),
                (pst[2][48:96, :], conv_b_w.rearrange("j k (c p) -> (j k c) p", p=128)),
                (pst[2][96:128, :], pool_scale.rearrange("j (c p) -> (j c) p", p=128)),
                (cst[:, :], cvec.rearrange("v (k p) -> (v k) p", p=128)),
            ]
            S.dma("sp", loads, writes=[psttok], semi=pst_sem)
            S.dma("sp", [(A_bc[:], a_log.rearrange("j d h -> (j d h)").partition_broadcast(128)),
                         (D_bc[:], d_skip.rearrange("j h -> (j h)").partition_broadcast(128))],
                  writes=[abtok], semi=misc_sem)
            S.dma("sp", [(dtb_bc[:], dt_bias.rearrange("j d h -> (j d h)").partition_broadcast(128))],
                  writes=[abtok], semi=misc_sem)
            for i in range(3):
                tr(pA[:, i * 128:(i + 1) * 128], pst[i][:], ident[:], [psttok, ctok], [bk[0]])
            tr(pA[:, 384:400], cst[:], ident[0:16, 0:16], [psttok, ctok], [bk[0]])
            cp("dve", pv[:], pA[:, 0:384], [], [bk[0], pvtok])
            ts("dve", pvA[:, 0:32], pv[:, 96:128], ALPHA, None, ALU.mult, None, [], [pvtok])
            ts("dve", pvA[:, 32:64], pv[:, 224:256], ALPHA, None, ALU.mult, None, [], [pvtok])
            act(scb[:].rearrange("p k v -> p v k"), pA[:, 384:400].rearrange("p (v k) -> p v k", v=2), AF.Silu,
                [], [bk[0], scbtok])
            act(A_bc[:], A_bc[:], AF.Exp, [], [abtok])
            ts("dve", A_bc[:], A_bc[:], -1.0, None, ALU.mult, None, [], [abtok])

            kidx = Fb[0][:, 0:2]
            pvals = Fb[0][:, 64:128]
            S.op("pool", lambda e: e.iota(kidx, pattern=[[128, 2]], base=0, channel_multiplier=1,
                                          allow_small_or_imprecise_dtypes=True), (), [Ftok[0]])
            S.op("pool", lambda e: e.iota(pvals, pattern=[[1, 64]], base=0, channel_multiplier=0,
                                          allow_small_or_imprecise_dtypes=True), (), [Ftok[0]])
            omega = Fb[0][:, 2:4]
            act(omega, kidx, AF.Exp, [], [Ftok[0]], scale=-math.log(10000.0) / 256.0)
            ang = Fb[1][:, 0:256].rearrange("p (a b) -> p a b", a=4)
            for ch in range(4):
                cc = ch % 2
                ts("dve", ang[:, ch, :], pvals, omega[:, cc:cc + 1], 1.0 / (2 * math.pi), ALU.mult, ALU.mult,
                   [Ftok[0]], [Ftok[1]])
                if ch >= 2:
                    ts("dve", ang[:, ch, :], ang[:, ch, :], 0.25, None, ALU.add, None, [], [Ftok[1]])
            angi = Fb[1][:, 256:512].bitcast(I32)
            angf = Fb[1][:, 512:768]
            cp("dve", angi, Fb[1][:, 0:256], [], [Ftok[1]])
            cp("dve", angf, angi, [], [Ftok[1]])
            tt("dve", Fb[1][:, 0:256], Fb[1][:, 0:256], angf, ALU.subtract, [], [Ftok[1]])
            act(posc[:].rearrange("p a b -> p (a b)"), Fb[1][:, 0:256], AF.Sin, [Ftok[1]], [postok], scale=6.283185)

            ev = Fb[0][:, 128:136]
            S.op("pool", lambda e: e.iota(ev, pattern=[[1, 8]], base=0, channel_multiplier=0,
                                          allow_small_or_imprecise_dtypes=True), (), [Ftok[0]])
            for k, w in enumerate(WINS):
                ts("dve", invtab[:, k, 0:8], ev, float(w // 2), float(w), ALU.add, ALU.min, [Ftok[0]], [invtok])
                ts("dve", invtab[:, k, 8:16], ev, -1.0, float(8 + w // 2), ALU.mult, ALU.add, [Ftok[0]], [invtok])
                ts("dve", invtab[:, k, 8:16], invtab[:, k, 8:16], float(w), None, ALU.min, None, [], [invtok])
            S.op("dve", lambda e: e.reciprocal(invtab[:].rearrange("p a b -> p (a b)"),
                                               invtab[:].rearrange("p a b -> p (a b)")), [], [invtok])

            S.dma("sp", [(x[:, c, :], xin[c * 128:(c + 1) * 128, :]) for c in range(KC)], writes=xtok, semi=xs_sem[0])
            xs4 = x[:, 0:4, 512:T].rearrange("p c (r q) -> p c r q", q=64)
            tt("dve", xs4, xs4, posc[:, :, 0:16].unsqueeze(3).to_broadcast([128, 4, 16, 64]), ALU.add,
               [postok], xtok[0:4])
            xs8 = x[:, 4:8, 512:T].rearrange("p c (r q) -> p c r q", q=64)
            tt("dve", xs8, xs8, posc[:, :, :].unsqueeze(2).to_broadcast([128, 4, 16, 64]), ALU.add,
               [postok], xtok[4:8])

            for _ in range(6):
                mod_unit()

            for i in range(4):
                if i % 2 == 0:
                    even_layer(i)
                else:
                    odd_layer(i)
                layer_norm(i)
                if DEBUG_STOP is not None and i == DEBUG_STOP:
                    break

            if DEBUG_STOP is not None:
                S.dma("sp", [(yout[c * 128:(c + 1) * 128, :], x[:, c, :]) for c in range(KC)], reads=xtok, semi=xs_sem[1])

        mod_pending = []

        def mod_reset():
            del mod_pending[:]
            for i in range(4):
                for m4 in range(6):
                    mod_pending.append((i, m4))

        def mod_unit():
            if not mod_pending:
                return
            i, m4 = mod_pending.pop(0)

            def pairs(slot, i=i, m4=m4):
                return [(slot.rearrange("p (k n) -> p k n", k=8),
                         w_mod[i, :, m4 * 512:(m4 + 1) * 512].rearrange("(k p) n -> p k n", p=128))]
            sl, stok = ring.next(pairs)
            w = sl.rearrange("p (k n) -> p k n", k=8)
            for mmi in range(4):
                m = m4 * 4 + mmi
                for kc in range(KC):
                    mm(p7[:, m * 2:(m + 1) * 2], w[:, kc, mmi * 128:(mmi + 1) * 128], scb[:, kc, :],
                       kc == 0, kc == KC - 1, [stok, scbtok], [bk[7]])
            if m4 == 5:
                tt("dve", modv[:, i, :, :], p7[:, 0:48].rearrange("p (m v) -> p m v", v=2),
                   pv[:, i * 24:(i + 1) * 24].unsqueeze(2).to_broadcast([128, 24, 2]), ALU.add,
                   [pvtok], [bk[7], modtok])
                ts("dve", modv[:, i, 8:24, :], modv[:, i, 8:24, :], 1.0, None, ALU.add, None, [], [modtok])
                if i >= 1:
                    ts("dve", modv[:, i, 8:16, :], modv[:, i, 8:16, :], 1.0 / ALPHA, None, ALU.mult, None, [], [modtok])

        def modulate(i):
            for c in range(KC):
                for v, (t0, tl) in enumerate([(0, 512), (512, 1024)]):
                    act(u[:, c, t0:t0 + tl], x[:, c, t0:t0 + tl], AF.Identity, [xtok[c], modtok], [utok[c]],
                        bias=modv[:, i, c, v:v + 1], scale=modv[:, i, 8 + c, v:v + 1])
                if i == 0:
                    ts("dve", x[:, c, :], x[:, c, :], ALPHA, None, ALU.mult, None, [], [xtok[c]])

        def proj_chunk(wv, wtok, ncols, col0, kchunks, rhs_fn, rtoks):
            pg, ptoks = next_pg()
            for n, (t0, tl) in enumerate(tiles):
                for kc in range(kchunks):
                    mm(pg[0:ncols, t0:t0 + tl], wv[:, kc, col0:col0 + ncols], rhs_fn(kc, t0, tl),
                       kc == 0, kc == kchunks - 1, [wtok] + rtoks(kc), [ptoks[n]])
            return pg, ptoks

        def u_rhs(kc, t0, tl):
            return u[:, kc, t0:t0 + tl]

        def u_toks(kc):
            return [utok[kc]]

        def y_rhs(kc, t0, tl):
            return yb[:, kc, t0:t0 + tl]

        def y_toks(kc):
            return [ytok[kc]]

        def conv3(P, ptoks, dstF, dstFtok, w0, w1, w2, bias, srctoks=()):
            if bias is not None:
                act(dstF[:, 0:T], P[:, 0:T], AF.Identity, list(srctoks), list(ptoks) + [dstFtok], bias=bias, scale=w1)
            else:
                act(dstF[:, 0:T], P[:, 0:T], AF.Identity, list(srctoks), list(ptoks) + [dstFtok], bias=0.0, scale=w1)
            for (s0, L) in SEQS:
                stt("dve", dstF[:, s0 + 1:s0 + L], P[:, s0:s0 + L - 1], w0, dstF[:, s0 + 1:s0 + L], ALU.mult, ALU.add,
                    list(srctoks), list(ptoks) + [dstFtok])
                stt("dve", dstF[:, s0:s0 + L - 1], P[:, s0 + 1:s0 + L], w2, dstF[:, s0:s0 + L - 1], ALU.mult, ALU.add,
                    list(srctoks), list(ptoks) + [dstFtok])

        conv_ctr = [0]

        def conv_evac(pg, ptoks):
            k = conv_ctr[0] % 2
            conv_ctr[0] += 1
            praw, prt = (Fb[0], Ftok[0]) if k == 0 else (Fb[2], Ftok[2])
            cp("act", praw[:, 0:T], pg[:, 0:T], [], list(ptoks) + [prt])
            return k

        def conv_taps(k, w0, w1, w2, bias, dst, dsttok):
            praw, prt = (Fb[0], Ftok[0]) if k == 0 else (Fb[2], Ftok[2])
            cvb, cvt = (Fb[1][:, 0:T], Ftok[1]) if k == 0 else (hbuf[:, 0:T], hsttok)
            ts("dve", cvb, praw[:, 0:T], w1, bias, ALU.mult, ALU.add, [prt, pvtok], [cvt])
            for (s0, L) in SEQS:
                stt("dve", cvb[:, s0 + 1:s0 + L], praw[:, s0:s0 + L - 1], w0, cvb[:, s0 + 1:s0 + L], ALU.mult, ALU.add,
                    [prt], [cvt])
                stt("dve", cvb[:, s0:s0 + L - 1], praw[:, s0 + 1:s0 + L], w2, cvb[:, s0:s0 + L - 1], ALU.mult, ALU.add,
                    [prt], [cvt])
            act(dst, cvb, AF.Silu, [cvt], [dsttok])

        def out_proj_partial(i, wdram, row0):
            for half in range(2):
                def pairs(slot, half=half):
                    return [(slot.rearrange("p (k n) -> p k n", k=8),
                             wdram[row0:row0 + 1024, half * 512:(half + 1) * 512].rearrange("(k p) n -> p k n", p=128))]
                sl, stok = ring.next(pairs)
                wv = sl.rearrange("p (k n) -> p k n", k=8)
                for dd in range(4):
                    dc = half * 4 + dd
                    pg, ptoks = proj_chunk(wv, stok, 128, dd * 128, KC, y_rhs, y_toks)
                    for v, (t0, tl) in enumerate([(0, 512), (512, 1024)]):
                        bt = ptoks[0:1] if v == 0 else ptoks[1:3]
                        stt("dve", x[:, dc, t0:t0 + tl], pg[:, t0:t0 + tl], modv[:, i, 16 + dc, v:v + 1],
                            x[:, dc, t0:t0 + tl], ALU.mult, ALU.add, [modtok], bt + [xtok[dc]])
                ring.release()

        def layer_norm(i):
            mean_sb = Fb[0]
            rstd_sb = Fb[1]
            sq = [Fb[2], Fb[2]]
            sqt = [Ftok[2], Ftok[2]]
            for c in range(KC):
                q = c % 2
                xbf, xbft = Hb[q], Htok[q]
                sqf, sqft = Hb[2 + q], Htok[2 + q]
                cp("dve", xbf[:, :], x[:, c, :], [xtok[c]], [xbft])
                act(sqf[:, :], x[:, c, :], AF.Square, [xtok[c]], [sqft])
                for n, (t0, tl) in enumerate(tiles):
                    mm(pA[:, t0:t0 + tl], onesm[:], xbf[:, t0:t0 + tl], c == 0, c == KC - 1, [ctok, xbft], [bk[n]])
                    mm(pB[:, t0:t0 + tl], onesm[:], sqf[:, t0:t0 + tl], c == 0, c == KC - 1, [ctok, sqft], [bk[3 + n]])
            if i < 3:
                for _ in range(6):
                    mod_unit()
            cp("act", mean_sb[:, 0:T], pA[:, 0:T], [], bk[0:3] + [Ftok[0]])
            tt("dve", rstd_sb[:, 0:T], mean_sb[:, 0:T], mean_sb[:, 0:T], ALU.mult, [Ftok[0]], [Ftok[1]])
            tt("dve", rstd_sb[:, 0:T], pB[:, 0:T], rstd_sb[:, 0:T], ALU.subtract, [], bk[3:6] + [Ftok[1]])
            act(rstd_sb[:, 0:T], rstd_sb[:, 0:T], AF.Ln, [], [Ftok[1]], bias=LN_EPS)
            act(rstd_sb[:, 0:T], rstd_sb[:, 0:T], AF.Exp, [], [Ftok[1]], scale=-0.5)
            for c in range(KC):
                tt("dve", x[:, c, :], x[:, c, :], mean_sb[:, 0:T], ALU.subtract, [Ftok[0]], [xtok[c]])
                tt("dve", x[:, c, :], x[:, c, :], rstd_sb[:, 0:T], ALU.mult, [Ftok[1]], [xtok[c]])
                if i < 3:
                    act(x[:, c, :], x[:, c, :], AF.Identity, [pvtok], [xtok[c]],
                        bias=pvA[:, 32 + i * 8 + c:32 + i * 8 + c + 1], scale=pvA[:, i * 8 + c:i * 8 + c + 1])
                else:
                    act(x[:, c, :], x[:, c, :], AF.Identity, [pvtok], [xtok[c]], bias=pvc(c_lnb(i, c)), scale=pvc(c_lng(i, c)))
                    if DEBUG_STOP is None:
                        S.dma("sp", [(yout[c * 128:(c + 1) * 128, :], x[:, c, :])], reads=[xtok[c]], semi=xs_sem[1])

        def even_layer(i):
            j = i // 2
            modulate(i)
            def pairs_dt(slot):
                return [(slot[:, 0:256].rearrange("p (k n) -> p k n", k=8),
                         w_in_even[j, :, 3072:3104].rearrange("(k p) n -> p k n", p=128))]
            sl, stok = ring.next(pairs_dt)
            wv = sl[:, 0:256].rearrange("p (k n) -> p k n", k=8)
            for tc in range(NTC):
                for kc in range(KC):
                    mm(p6[:, tc * 32:(tc + 1) * 32], u[:, kc, tc * 128:(tc + 1) * 128], wv[:, kc, 0:32],
                       kc == 0, kc == KC - 1, [stok, utok[kc]], [bk[6]])
            ring.release()
            dtmp = wde[:].rearrange("p d t h -> p (d t h)").rearrange("p (t c) -> p t c", c=32)
            tt("dve", dtmp, p6[:, 0:384].rearrange("p (t c) -> p t c", c=32),
               dtb_bc[:, j * 32:(j + 1) * 32].unsqueeze(1).to_broadcast([128, NTC, 32]), ALU.add,
               [abtok], [bk[6], dttok])
            act(dtmp, dtmp, AF.Exp, [], [dttok])
            act(dt_tok[:], dtmp.rearrange("p t (d h) -> p d t h", d=2), AF.Ln, [], [dttok], bias=1.0)
            cp("act", dt_bf[:], dt_tok[:], [], [dttok])
            tt("dve", a_bf[:], dt_tok[:],
               A_bc[:, j * 32:(j + 1) * 32].rearrange("p (d h) -> p d h", d=2).unsqueeze(2).to_broadcast([128, 2, NTC, 16]),
               ALU.mult, [abtok], [dttok])
            def dt_tables():
                mm(p6[:, 0:192], Uf[:], a_bf[:, 0, :, :].rearrange("p t h -> p (t h)"), True, True, [ctok, dttok], [bk[6]])
                mm(p6[:, 192:384], Ub[:], a_bf[:, 1, :, :].rearrange("p t h -> p (t h)"), True, True, [ctok, dttok], [bk[6]])
                mm(p7[:, 0:384], onesb[:], a_bf[:].rearrange("p d t h -> p (d t h)"), True, True, [ctok, dttok], [bk[7]])
                act(wde[:].rearrange("p d t h -> p (d t h)"), p6[:, 0:384], AF.Exp, [], [bk[6], dttok])
                act(cd_bc[:].rearrange("p d t h -> p (d t h)"), p7[:, 0:384], AF.Exp, [], [bk[7], dttok])
                tt("dve", wde[:], wde[:], dt_tok[:], ALU.mult, [], [dttok])

            zs = [Hb[4], Hb[5]]
            zst = [Htok[4], Htok[5]]
            xT = [Hb[0], Hb[1]]
            xTt = [Htok[0], Htok[1]]
            BT, BTt = Hb[2], Htok[2]
            CT, CTt = Hb[3], Htok[3]
            cv, cvt = Fb[1], Ftok[1]
            def tr_block(srcH, srcT, dst_fn, dtok):
                for q in range(3):
                    hq = q % 2
                    for t4 in range(4):
                        tc = q * 4 + t4
                        tr(trP[:, hq * 512 + t4 * 128:hq * 512 + (t4 + 1) * 128], srcH[:, tc * 128:(tc + 1) * 128],
                           identb[:], [srcT, ctok], [bk[6]])
                    cp("act", dst_fn(q), trP[:, hq * 512:(hq + 1) * 512].rearrange("p (t f) -> p t f", t=4),
                       [], [bk[6], dtok])

            for g in range(4):
                def pairsX(slot, g=g):
                    sv = slot.rearrange("p (k n) -> p k n", k=8)
                    return [(sv[:, :, 0:256], w_in_even[j, :, 1024 + g * 256:1024 + (g + 1) * 256].rearrange("(k p) n -> p k n", p=128)),
                            (sv[:, :, 256:384], w_in_even[j, :, 2048 + g * 128:2048 + (g + 1) * 128].rearrange("(k p) n -> p k n", p=128)),
                            (sv[:, :, 384:512], w_in_even[j, :, 2560 + g * 128:2560 + (g + 1) * 128].rearrange("(k p) n -> p k n", p=128))]
                sl, stok = ring.next(pairsX)
                wv = sl.rearrange("p (k n) -> p k n", k=8)
                chunks = [(0, g * 2, xT[0][:, :], xTt[0]), (128, g * 2 + 1, xT[1][:, :], xTt[1]),
                          (256, 8 + g, BT[:, :], BTt), (384, 12 + g, CT[:, :], CTt)]
                kbuf = [None] * 4

                def conv_of(ci):
                    col0, ch, dst, dtok = chunks[ci]
                    conv_taps(kbuf[ci], pvc(c_caw(j, 0, ch)), pvc(c_caw(j, 1, ch)), pvc(c_caw(j, 2, ch)), pvc(c_cab(j, ch)),
                              dst, dtok)

                def tr_x(cc):
                    tr_block(xT[cc], xTt[cc], lambda q, cc=cc: x_tok[:, q * 4:(q + 1) * 4, cc * 128:(cc + 1) * 128], xtoktok)

                for ci in range(4):
                    pg, ptoks = proj_chunk(wv, stok, 128, chunks[ci][0], KC, u_rhs, u_toks)
                    kbuf[ci] = conv_evac(pg, ptoks)
                    if ci >= 1:
                        conv_of(ci - 1)
                    if ci == 3:
                        tr_x(0)
                ring.release()
                if g == 0:
                    dt_tables()
                def pairsZ(slot, g=g):
                    sv = slot[:, 0:2048].rearrange("p (k n) -> p k n", k=8)
                    return [(sv, w_in_even[j, :, g * 256:(g + 1) * 256].rearrange("(k p) n -> p k n", p=128))]
                sl, stok = ring.next(pairsZ)
                wv = sl[:, 0:2048].rearrange("p (k n) -> p k n", k=8)
                for cc in range(2):
                    pg, ptoks = proj_chunk(wv, stok, 128, cc * 128, KC, u_rhs, u_toks)
                    act(zs[cc][:, :], pg[:, 0:T], AF.Silu, [], ptoks + [zst[cc]])
                    if cc == 0:
                        conv_of(3)
                        tr_x(1)
                    else:
                        tr_block(BT, BTt, lambda q: B_tok[:, q * 4:(q + 1) * 4, :], btoktok)
                ring.release()
                ssd_group(j, g, zs, zst, BT, BTt, CT, CTt)

            for c in range(KC):
                sqb, sqbt = Hb[c % 2], Htok[c % 2]
                act(sqb[:, :], yb[:, c, :], AF.Square, [ytok[c]], [sqbt])
                for n, (t0, tl) in enumerate(tiles):
                    mm(pA[:, t0:t0 + tl], onesm[:], sqb[:, t0:t0 + tl], c == 0, c == KC - 1, [ctok, sqbt], [bk[n]])
            act(Fb[2][:, 0:T], pA[:, 0:T], AF.Ln, [], bk[0:3] + [Ftok[2]], bias=RMS_EPS)
            act(Fb[2][:, 0:T], Fb[2][:, 0:T], AF.Exp, [], [Ftok[2]], scale=-0.5)
            for c in range(KC):
                stt("dve", yb[:, c, :], yb[:, c, :], pvc(c_nag(j, c)), Fb[2][:, 0:T], ALU.mult, ALU.mult,
                    [pvtok, Ftok[2]], [ytok[c]])
            def mixb_part1(c):
                def pairsM(slot, c=c):
                    sv = slot.rearrange("p (k n) -> p k n", k=8)
                    return [(sv[:, :, q * 128:(q + 1) * 128],
                             w_in_even[j, :, 3104 + q * 1024 + c * 128:3104 + q * 1024 + (c + 1) * 128].rearrange("(k p) n -> p k n", p=128))
                            for q in range(4)]
                sl, stok = ring.next(pairsM)
                wv = sl.rearrange("p (k n) -> p k n", k=8)
                sg, sgt = Hb[0], Htok[0]
                pg, ptoks = proj_chunk(wv, stok, 128, 0, KC, u_rhs, u_toks)
                act(sg[:, :], pg[:, 0:T], AF.Silu, [], ptoks + [sgt])
                pg, ptoks = proj_chunk(wv, stok, 128, 128, KC, u_rhs, u_toks)
                tt("dve", Fb[0][:, 0:T], pg[:, 0:T], sg[:, :], ALU.mult, [sgt], ptoks + [Ftok[0]])
                pg, ptoks = proj_chunk(wv, stok, 128, 256, KC, u_rhs, u_toks)
                cp("act", Hb[1][:, :], pg[:, 0:T], [], ptoks + [Htok[1]])
                pg, ptoks = proj_chunk(wv, stok, 128, 384, KC, u_rhs, u_toks)
                tt("dve", Fb[1][:, 0:T], pg[:, 0:T], Hb[1][:, :], ALU.mult, [Htok[1]], ptoks + [Ftok[1]])
                ring.release()

            def mixb_part2(c):
                conv3(Fb[1], [], Fb[2], Ftok[2], pvc(c_cbw(j, 0, c)), pvc(c_cbw(j, 1, c)), pvc(c_cbw(j, 2, c)), None,
                      srctoks=[Ftok[1], pvtok])
                tt("dve", yb[:, c, :], Fb[2][:, 0:T], Fb[0][:, 0:T], ALU.mult, [Ftok[2], Ftok[0]], [ytok[c]])

            mixb_part1(0)
            out_proj_partial(i, w_out_even[j], 0)
            mixb_part2(0)
            for c in range(1, KC):
                mixb_part1(c)
                mixb_part2(c)
            out_proj_partial(i, w_out_even[j], 1024)

        def ssd_group(j, g, zs, zst, BT, BTt, CT, CTt):
            gs = slice(g * 4, g * 4 + 4)
            for h4 in range(4):
                hh = j * 16 + g * 4 + h4
                ts("dve", Dmat[:, h4, :], identb[:], D_bc[:, hh:hh + 1], None, ALU.mult, None, [ctok, abtok], [dmtok])
            HC = [(hcur[:], hcurtok), (hcur2, hcur2tok)]
            HT = [(htmp[:], htmptok), (htmp2, htmp2tok)]

            def v4(ap):
                return ap.rearrange("p (h q) -> p h q", h=4)

            hstP = Hb[6][:, 0:1024].rearrange("p (d s q) -> p d s q", d=2, s=2)

            def hst_slot(si, d, tc):
                if si == 2:
                    return hst[:, d, tc - 4, :], hsttok
                return hstP[:, d, si, :], Htok[6]

            def seq_of(tc):
                si = 0 if tc < 2 else (1 if tc < 4 else 2)
                s0, L = SEQS[si]
                return si, s0 // 128, L // 128

            def state_rounds(si):
                s0, L = SEQS[si]
                nch = L // 128
                tc0 = s0 // 128
                is_sample = (si == 2)
                orders = [list(range(tc0, tc0 + nch)), list(range(tc0 + nch - 1, tc0 - 1, -1))]
                have = [False, False]
                fns = []

                def init():
                    for d in range(2):
                        r0 = (j * 2 + d) * 128
                        S.dma("sp", [(HC[d][0], state[r0:r0 + 128, g * 256:(g + 1) * 256])],
                              writes=[HC[d][1]], semi=stin_sem[d])
                        have[d] = True
                if is_sample:
                    fns.append(init)

                def round_(idx):
                    for d in range(2):
                        tc = orders[d][idx]
                        hc, hct = HC[d]
                        ht, htt = HT[d]
                        par = idx % 2
                        if is_sample:
                            if par == 0:
                                stb = pA[:, 1024:1280] if d == 0 else pA[:, 1280:1536]
                                stk = bk[2]
                            else:
                                stb = bank4[:, 128:384] if d == 0 else bank5[:, 256:512]
                                stk = bk[4] if d == 0 else bk[5]
                        else:
                            if par == 0:
                                stb = bank4[:, 128:384] if d == 0 else bank5[:, 256:512]
                                stk = bk[4] if d == 0 else bk[5]
                            else:
                                stb = p6[:, 0:256] if d == 0 else p7[:, 0:256]
                                stk = bk[6] if d == 0 else bk[7]
                        xdb, xdbt = xdd[d][par], xddtok[d][par]
                        if have[d]:
                            hdst, hdtok = hst_slot(si, d, tc)
                            cp("act", hdst, hc, [hct], [hdtok])
                        if idx == nch - 1 and is_sample:
                            continue
                        tt("dve", xdb, v4(x_tok[:, tc, :]),
                           wde[:, d, tc, gs].unsqueeze(2).to_broadcast([128, 4, 64]), ALU.mult,
                           [xtoktok, dttok], [xdbt])
                        mm(stb, B_tok[:, tc, :], xdb.rearrange("p h q -> p (h q)"), True, True,
                           [btoktok, xdbt], [stk])
                        if have[d]:
                            tt("dve", v4(ht), v4(hc),
                               cd_bc[:, d, tc, gs].unsqueeze(2).to_broadcast([128, 4, 64]), ALU.mult,
                               [hct, dttok], [htt])
                            tt("dve", hc, stb, ht, ALU.add, [htt], [stk, hct])
                        else:
                            cp("dve", hc, stb, [], [stk, hct])
                            have[d] = True
                for idx in range(nch):
                    fns.append(lambda idx=idx: round_(idx))

                def final():
                    for d in range(2):
                        hc, hct = HC[d]
                        stv = sto[d][:].rearrange("p a b -> p (a b)")
                        cp("act", stv, hc, [hct], [stotok[d]])
                        r0 = ((si * 2 + j) * 2 + d) * 128
                        S.dma("sp", [(nstate[r0:r0 + 128, g * 256:(g + 1) * 256], stv)],
                              reads=[stotok[d]], semi=sto_sem[d])
                if not is_sample:
                    fns.append(final)
                return fns

            for si in range(2):
                for fn in state_rounds(si):
                    fn()
            pending_rounds = state_rounds(2)

            if True:
                its = list(range(NTC))
                ARG = [(p6[:], bk[6], p7[:], bk[7]), (pA[:, 0:512], bk[0], pA[:, 512:1024], bk[1])]
                YB = [(bank5, bk[5]), (pB[:, 0:512], bk[3])]

                def stage_pre_a(n):
                    tc = its[n]
                    k0 = tc * 128
                    mm(bank4[:, 0:128], BT[:, k0:k0 + 128], CT[:, k0:k0 + 128], True, True, [BTt, CTt], [bk[4]])
                    tt("dve", Rb2[:], Lfb[:].unsqueeze(2).to_broadcast([128, 2, 4, 128]),
                       a_bf[:, :, tc, gs].unsqueeze(3).to_broadcast([128, 2, 4, 128]), ALU.mult,
                       [ctok, dttok], [Rtok])
                    for d in range(2):
                        Um = Uf if d == 0 else Ub
                        pa, pat, pe_, pet = ARG[d]
                        Rflat = Rb2[:, d].rearrange("p h i -> p (h i)")
                        mm(pa, Um[:], Rflat, True, True, [ctok, Rtok], [pat])
                        mm(pe_, onesb[:], Rflat, True, True, [ctok, Rtok], [pet])

                def stage_pre_b(n):
                    for d in range(2):
                        pa, pat, pe_, pet = ARG[d]
                        act(eA2[:, d].rearrange("p h i -> p (h i)"), pa, AF.Exp, [], [pat, eAtok[d]])
                    for d in range(2):
                        pa, pat, pe_, pet = ARG[d]
                        act(eE2[:, d].rearrange("p h i -> p (h i)"), pe_, AF.Exp, [], [pet, eEtok[d]])

                def stage_cbm():
                    tt("dve", CBm2[:], bank4[:, 0:128].unsqueeze(1).to_broadcast([128, 2, 128]), Lfb[:], ALU.mult,
                       [ctok], [bk[4], CBmtok])

                def stage_mid(n):
                    tc = its[n]
                    k0 = tc * 128
                    q = n % 2
                    tt("dve", xdt2[q][:], v4(x_tok[:, tc, :]).unsqueeze(1).to_broadcast([128, 2, 4, 64]),
                       dt_bf[:, :, tc, gs].unsqueeze(3).to_broadcast([128, 2, 4, 64]), ALU.mult,
                       [xtoktok, dttok], [xdttok[q]])
                    tt("dve", MT2[q][:], eA2[:], CBm2[:].unsqueeze(2).to_broadcast([128, 2, 4, 128]), ALU.mult,
                       [eAtok[0], eAtok[1], CBmtok], [MTtok[q]])
                    tt("dve", Cs2[q][:], eE2[:],
                       CT[:, k0:k0 + 128].unsqueeze(1).unsqueeze(1).to_broadcast([128, 2, 4, 128]), ALU.mult,
                       [eEtok[0], eEtok[1], CTt], [Cstok[q]])

                def stage_y(n):
                    tc = its[n]
                    q = n % 2
                    yb_, ybt = YB[q]
                    si, tc0, nch = seq_of(tc)
                    has_f = (si == 2) or (tc != tc0)
                    has_b = (si == 2) or (tc != tc0 + nch - 1)
                    hf, hft = hst_slot(si, 0, tc)
                    hb_, hbt = hst_slot(si, 1, tc)
                    for h4 in range(4):
                        po = (h4 % 2) * 64
                        out = yb_[po:po + 64, (h4 // 2) * 128:(h4 // 2) * 128 + 128]
                        tp = (0, 64) if po else None
                        mm(out, xdt2[q][:, 0, h4, :], MT2[q][:, 0, h4, :], True, False, [xdttok[q], MTtok[q]], [ybt], tp)
                        mm(out, xdt2[q][:, 1, h4, :], MT2[q][:, 1, h4, :], False, False, [xdttok[q], MTtok[q]], [ybt], tp)
                        if has_f:
                            mm(out, hf[:, h4 * 64:(h4 + 1) * 64], Cs2[q][:, 0, h4, :], False, False,
                               [hft, Cstok[q]], [ybt], tp)
                        if has_b:
                            mm(out, hb_[:, h4 * 64:(h4 + 1) * 64], Cs2[q][:, 1, h4, :], False, False,
                               [hbt, Cstok[q]], [ybt], tp)
                        mm(out, x_tok[:, tc, h4 * 64:(h4 + 1) * 64], Dmat[:, h4, :], False, True, [xtoktok, dmtok], [ybt], tp)

                def stage_evac(n):
                    tc = its[n]
                    k0 = tc * 128
                    yb_, ybt = YB[n % 2]
                    tt("dve", yb[:, g * 2:g * 2 + 2, k0:k0 + 128], yb_[:, 0:256].rearrange("p (r i) -> p r i", r=2),
                       Hb45[:, :, k0:k0 + 128], ALU.mult, [zst[0], zst[1]], [ybt, ytok[g * 2], ytok[g * 2 + 1]])

                N = len(its)
                stage_pre_a(0)
                stage_pre_b(0)
                stage_cbm()
                for n in range(N):
                    if n == 4:
                        while pending_rounds:
                            pending_rounds.pop(0)()
                    if n + 1 < N:
                        stage_pre_a(n + 1)
                    stage_mid(n)
                    if n + 1 < N:
                        stage_pre_b(n + 1)
                    if n >= 1:
                        stage_evac(n - 1)
                    if n + 1 < N:
                        stage_cbm()
                    stage_y(n)
                    if n < 4:
                        for _ in range(3):
                            if pending_rounds:
                                pending_rounds.pop(0)()
                stage_evac(N - 1)

        def odd_layer(i):
            j = i // 2
            modulate(i)
            S.op("dve", lambda e: e.memset(Fb[0][:], 0.0), (), [Ftok[0]])
            S.op("dve", lambda e: e.memset(Fb[1][:], 0.0), (), [Ftok[1]])
            S.op("dve", lambda e: e.memset(Fb[2][:], 0.0), (), [Ftok[2]])
            for half in range(2):
                for kk in range(2):
                    k = half * 2 + kk
                    w = WINS[k]
                    invc, invct = invc_alias, hsttok
                    S.op("dve", lambda e: e.memset(invc, 1.0 / w), (), [invct])
                    for si, (s0, L) in enumerate(SEQS):
                        o = POFF[si]
                        cp("dve", invc[:, o:o + 8], invtab[:, k, 0:8], [invtok], [invct])
                        cp("dve", invc[:, o + L - 8:o + L], invtab[:, k, 8:16], [invtok], [invct])
                    def pairsV(slot, k=k):
                        return [(slot.rearrange("p (k n) -> p k n", k=8),
                                 w_in_odd[j, :, k * 512:(k + 1) * 512].rearrange("(k p) n -> p k n", p=128))]

                    def pairsG(slot, k=k):
                        return [(slot.rearrange("p (k n) -> p k n", k=8),
                                 w_in_odd[j, :, 2048 + k * 512:2048 + (k + 1) * 512].rearrange("(k p) n -> p k n", p=128))]
                    slV, stokV = ring.next(pairsV)
                    wvV = slV.rearrange("p (k n) -> p k n", k=8)
                    slG, stokG = ring.next(pairsG, prefetch=False)
                    wvG = slG.rearrange("p (k n) -> p k n", k=8)
                    for cc in range(4):
                        pg, ptoks = proj_chunk(wvV, stokV, 128, cc * 128, KC, u_rhs, u_toks)
                        vp, vpt = Fb[0], Ftok[0]
                        for si, (s0, L) in enumerate(SEQS):
                            o = POFF[si]
                            bt = ptoks[0:1] if si < 2 else ptoks[1:3]
                            cp("act", vp[:, o:o + L], pg[:, s0:s0 + L], [], bt + [vpt])
                        pg2, ptoks2 = proj_chunk(wvG, stokG, 128, cc * 128, KC, u_rhs, u_toks)
                        act(Hb[cc][:, :], pg2[:, 0:T], AF.Silu, [], ptoks2 + [Htok[cc]])
                        src, srct = vp, vpt
                        bufs = [(Fb[1], Ftok[1]), (Fb[2], Ftok[2])]
                        sh = [(1, 0), (1, 1), (2, 2), (4, 4)]
                        for lev in range(k + 1):
                            dst, dstt = bufs[lev % 2]
                            a_, b_ = sh[lev]
                            tt("dve", dst[:, 8:PW - 8], src[:, 8 - a_:PW - 8 - a_], src[:, 8 + b_:PW - 8 + b_], ALU.add,
                               [srct], [dstt])
                            src, srct = dst, dstt
                        oth, otht = bufs[(k + 1) % 2]
                        tt("dve", oth[:, 8:PW - 8], src[:, 8:PW - 8], invc[:, 8:PW - 8], ALU.mult, [srct, invct], [otht])
                        for si, (s0, L) in enumerate(SEQS):
                            o = POFF[si]
                            tt("dve", Hb[4 + cc][:, s0:s0 + L], oth[:, o:o + L], vp[:, o:o + L], ALU.subtract,
                               [otht, vpt], [Htok[4 + cc]])
                    def pairsP(slot, k=k):
                        return [(slot[:, 0:2048].rearrange("p (k n) -> p k n", k=4),
                                 w_pool[j, k, :, :].rearrange("(k p) n -> p k n", p=128))]
                    sl, stok = ring.next(pairsP)
                    wv = sl[:, 0:2048].rearrange("p (k n) -> p k n", k=4)
                    for dd in range(4):
                        pg, ptoks = proj_chunk(wv, stok, 128, dd * 128, 4,
                                               lambda kc, t0, tl: Hb[4 + kc][:, t0:t0 + tl], lambda kc: [Htok[4 + kc]])
                        yc = kk * 4 + dd
                        stt("dve", yb[:, yc, :], pg[:, 0:T], pvc(c_psc(j, k * 4 + dd)), Hb[dd][:, :], ALU.mult, ALU.mult,
                            [pvtok, Htok[dd]], ptoks + [ytok[yc]])
                    ring.release()
                out_proj_partial(i, w_out_odd[j], half * 1024)

        S.dry = True
        emit()
        S.dry = False
        S.plan = True
        pg_state[0] = 0
        emit()
        S.plan = False
        S.reset()
        ring.issued = 0
        ring.cur = 0
        pg_state[0] = 0
        emit()
        for semi in list(xs_sem) + list(sto_sem):
            ent = S.dma_sems[semi]
            if ent[1] > 0:
                nc.sync.wait_ge(ent[0], ent[1])
        print("program: insts", S.cnt, "incs", S.semval, "waits", S.nwaits, "pieces", len(ring.plan))
    return nc


_NC_CACHE = {}


def kernel(x_prompt, x_sample, state_ssd, c, c_ctx, w_mod, b_mod, ln_g, ln_b, w_in_even, conv_a_w,
           conv_a_b, a_log, dt_bias, d_skip, norm_a_g, conv_b_w, w_out_even, w_in_odd, w_pool,
           pool_scale, w_out_odd):
    f = lambda a: np.ascontiguousarray(np.asarray(a), dtype=np.float32)
    if "nc" not in _NC_CACHE:
        _NC_CACHE["nc"] = build_program()
    nc = _NC_CACHE["nc"]
    x_prompt = f(x_prompt)
    x_sample = f(x_sample)
    state_ssd = f(state_ssd)
    c = f(c)
    c_ctx = f(c_ctx)
    shared = dict(w_mod=f(w_mod), b_mod=f(b_mod), ln_g=f(ln_g), ln_b=f(ln_b), w_in_even=f(w_in_even),
                  conv_a_w=f(conv_a_w), conv_a_b=f(conv_a_b), a_log=f(a_log), dt_bias=f(dt_bias), d_skip=f(d_skip),
                  norm_a_g=f(norm_a_g), conv_b_w=f(conv_b_w), w_out_even=f(w_out_even), w_in_odd=f(w_in_odd),
                  w_pool=f(w_pool), pool_scale=f(pool_scale), w_out_odd=f(w_out_odd))
    in_maps = []
    for k in range(NCORES):
        m = dict(shared)
        m["xin"] = np.ascontiguousarray(np.concatenate([x_prompt[2 * k], x_prompt[2 * k + 1], x_sample[k]], axis=0).T)
        m["state"] = np.ascontiguousarray(state_ssd[k].transpose(0, 1, 4, 2, 3).reshape(512, 1024))
        m["cvec"] = np.ascontiguousarray(np.stack([c_ctx, c[k]], axis=0))
        in_maps.append(m)
    res = run_bass_kernel_spmd(nc, in_maps, core_ids=list(range(NCORES)))
    y_prompt = np.empty((16, 256, D), np.float32)
    y_sample = np.empty((8, 1024, D), np.float32)
    new_state = np.empty((16, 2, 2, 16, 64, 128), np.float32)
    for k in range(NCORES):
        r = res.results[k]
        yo = np.asarray(r["yout"]).T
        y_prompt[2 * k] = yo[0:256]
        y_prompt[2 * k + 1] = yo[256:512]
        y_sample[k] = yo[512:1536]
        ns = np.asarray(r["nstate"]).reshape(2, 2, 2, 128, 16, 64).transpose(0, 1, 2, 4, 5, 3)
        new_state[2 * k] = ns[0]
        new_state[2 * k + 1] = ns[1]
    return (y_prompt, y_sample, new_state)
```
